# Optimizing a Trainium2 kernel written in Bass

```python
import math
import jax
import jax.numpy as jnp
from jax import lax
import numpy as np

D_MODEL = 1024
BATCH = 8
SEQ = 4096
DEPTH = 2

GRID_W = 64
CTX_LEN = 256
N_EVEN = (DEPTH + 1) // 2
N_ODD = DEPTH // 2
N_MOD = 9
D_FF = 2816
NORM_EPS = 1e-6
MIX_WIDTH = D_MODEL
A_WIDTH = MIX_WIDTH // 2
A_HEADS = 4
A_QK_DIM = A_WIDTH // (2 * A_HEADS)
A_V_DIM = 2 * A_QK_DIM
B_WIDTH = MIX_WIDTH - A_WIDTH
AB_IN = 3 * A_WIDTH + 3 * B_WIDTH
HY_IN = 3 * B_WIDTH
HY_ORDER = 2
HY_SHORT = 3
HY_EMB = 33
HY_BANDS = (HY_EMB - 1) // 2
HY_FILTER_HID = 64
HY_INNER = 2
HY_FILTER_STD = 0.03
HY_TARGET = 1e-2
HY_MAX_DECAY = math.log(HY_TARGET) / 0.3
HY_MIN_DECAY = math.log(HY_TARGET) / 1.5
C_WIDTH = MIX_WIDTH
C_HEADS = 16
C_HEAD_DIM = C_WIDTH // C_HEADS
NA_ROWS_MAX = 8
NA_COLS = 16
ROPE_BASE = 10000.0
QBLOCK = 128

kernel_name = 'hybrid_diffattn_hyena_natten_macaron_dit'


def rmsnorm(x, w):
    xf = x.astype(jnp.float32)
    y = xf * lax.rsqrt(jnp.mean(xf * xf, axis=-1, keepdims=True) + NORM_EPS)
    return (y * w.astype(jnp.float32)).astype(x.dtype)


def adaln_in(h, gain, mod, k):
    return rmsnorm(h, gain) * (1.0 + mod[:, 3 * k + 1]) + mod[:, 3 * k]


def swiglu(h, w1, w3, w2):
    return (jax.nn.silu(h @ w1) * (h @ w3)) @ w2


def macaron(h, gain, mod, k, w1, w3, w2):
    return h + 0.5 * mod[:, 3 * k + 2] * swiglu(adaln_in(h, gain, mod, k), w1, w3, w2)


def axial_rope(x):
    n = x.shape[1]
    t = jnp.arange(n)
    pos = (t // GRID_W, t % GRID_W)
    half = x.shape[-1] // 2
    nf = half // 2
    inv_freq = ROPE_BASE ** (-jnp.arange(nf, dtype=jnp.float32) / nf)
    bshape = (1, n) + (1,) * (x.ndim - 3) + (nf,)
    parts = []
    for a in range(2):
        ang = pos[a].astype(jnp.float32)[:, None] * inv_freq
        cos = jnp.cos(ang).reshape(bshape).astype(x.dtype)
        sin = jnp.sin(ang).reshape(bshape).astype(x.dtype)
        xa = x[..., a * half:(a + 1) * half]
        x1, x2 = xa[..., :nf], xa[..., nf:]
        parts += [x1 * cos - x2 * sin, x1 * sin + x2 * cos]
    return jnp.concatenate(parts, axis=-1)


def diff_attend(q, k, v, lam):
    s = jnp.einsum('bqhmd,bkhmd->bhmqk', q, k).astype(jnp.float32) * (A_QK_DIM ** -0.5)
    p = jax.nn.softmax(s, axis=-1)
    a = (p[:, :, 0] - lam * p[:, :, 1]).astype(v.dtype)
    return jnp.einsum('bhqk,bkhd->bqhd', a, v)


def plain_attend(q, k, v):
    s = jnp.einsum('bqhd,bkhd->bhqk', q, k).astype(jnp.float32) * (q.shape[-1] ** -0.5)
    p = jax.nn.softmax(s, axis=-1).astype(v.dtype)
    return jnp.einsum('bhqk,bkhd->bqhd', p, v)


def short_conv(u, w, b):
    n = u.shape[1]
    pad = HY_SHORT // 2
    up = jnp.pad(u, ((0, 0), (pad, pad), (0, 0)))
    return sum(up[:, j:j + n] * w[j] for j in range(HY_SHORT)) + b


def hyena_filters(n, f_w0, f_b0, f_w1, f_b1, f_freq, f_wout):
    t = jnp.linspace(0.0, 1.0, n, dtype=jnp.float32)[:, None]
    w = (2.0 * math.pi / n) * jnp.arange(n, dtype=jnp.float32)[:, None]
    bands = jnp.linspace(1e-4, HY_BANDS - 1, HY_BANDS, dtype=jnp.float32)[None, :]
    z = jnp.concatenate([t, jnp.cos(bands * w), -jnp.sin(bands * w)], axis=-1)
    hid = jnp.sin(f_freq * (z @ f_w0 + f_b0))
    for i in range(HY_INNER):
        hid = jnp.sin(f_freq * (hid @ f_w1[i] + f_b1[i]))
    filt = (hid @ f_wout).reshape(n, HY_ORDER, 2, B_WIDTH)
    deltas = jnp.abs(jnp.linspace(HY_MIN_DECAY, HY_MAX_DECAY, B_WIDTH, dtype=jnp.float32))
    window = jnp.exp(-t * deltas)
    return filt * window[:, None, None, :]


def long_conv_bidir(v, h_fwd, h_bwd, skip):
    n = v.shape[1]
    taps = jnp.concatenate([h_fwd, jnp.zeros_like(h_fwd[:1]), h_bwd[:0:-1]], axis=0)
    taps_f = jnp.fft.rfft(taps.astype(jnp.float32), n=2 * n, axis=0)
    v_f = jnp.fft.rfft(v.astype(jnp.float32), n=2 * n, axis=1)
    y = jnp.fft.irfft(v_f * taps_f[None], n=2 * n, axis=1)[:, :n]
    return (y + v.astype(jnp.float32) * skip.astype(jnp.float32)).astype(v.dtype)


def hyena(u, conv_w, conv_b, f_w0, f_b0, f_w1, f_b1, f_freq, f_wout, hy_b):
    n = u.shape[1]
    u = short_conv(u, conv_w, conv_b)
    x1, x2, v = jnp.split(u, 3, axis=-1)
    filt = hyena_filters(n, f_w0, f_b0, f_w1, f_b1, f_freq, f_wout)
    z = x1 * long_conv_bidir(v, filt[:, 0, 0], filt[:, 0, 1], hy_b[0])
    z = x2 * long_conv_bidir(z, filt[:, 1, 0], filt[:, 1, 1], hy_b[1])
    return z


def mixer_ab(h_lat, h_ctx, w_in, w_out, lam_p, subln_w, conv_w, conv_b,
             f_w0, f_b0, f_w1, f_b1, f_freq, f_wout, hy_b, layer, need_ctx):
    bsz, n, _ = h_lat.shape
    n_ctx = h_ctx.shape[1]
    lam_init = 0.8 - 0.6 * math.exp(-0.3 * layer)
    lam = (jnp.exp(jnp.sum(lam_p[0] * lam_p[1]).astype(jnp.float32))
           - jnp.exp(jnp.sum(lam_p[2] * lam_p[3]).astype(jnp.float32)) + lam_init)

    def head_out(o):
        return (rmsnorm(o, subln_w) * (1.0 - lam_init)).reshape(o.shape[0], o.shape[1], A_WIDTH)

    def hy(u):
        return hyena(u, conv_w, conv_b, f_w0, f_b0, f_w1, f_b1, f_freq, f_wout, hy_b)

    u_l = h_lat @ w_in
    q_l = axial_rope(u_l[..., :A_WIDTH].reshape(bsz, n, A_HEADS, 2, A_QK_DIM))
    k_l = axial_rope(u_l[..., A_WIDTH:2 * A_WIDTH].reshape(bsz, n, A_HEADS, 2, A_QK_DIM))
    v_l = u_l[..., 2 * A_WIDTH:3 * A_WIDTH].reshape(bsz, n, A_HEADS, A_V_DIM)
    if need_ctx:
        u_c = h_ctx @ w_in
        kv_c = u_c[..., A_WIDTH:3 * A_WIDTH]
    else:
        kv_c = h_ctx @ w_in[:, A_WIDTH:3 * A_WIDTH]
    k_c = kv_c[..., :A_WIDTH].reshape(bsz, n_ctx, A_HEADS, 2, A_QK_DIM)
    v_c = kv_c[..., A_WIDTH:].reshape(bsz, n_ctx, A_HEADS, A_V_DIM)
    k_all = jnp.concatenate([k_c, k_l], axis=1)
    v_all = jnp.concatenate([v_c, v_l], axis=1)
    nb = n // QBLOCK
    q_blk = q_l.reshape(bsz, nb, QBLOCK, A_HEADS, 2, A_QK_DIM).swapaxes(0, 1)
    o_l = lax.map(lambda qb: diff_attend(qb, k_all, v_all, lam), q_blk)
    a_l = head_out(o_l.swapaxes(0, 1).reshape(bsz, n, A_HEADS, A_V_DIM))
    b_l = hy(u_l[..., 3 * A_WIDTH:])
    y_l = jnp.concatenate([a_l, b_l], axis=-1) @ w_out
    y_c = None
    if need_ctx:
        q_c = u_c[..., :A_WIDTH].reshape(bsz, n_ctx, A_HEADS, 2, A_QK_DIM)
        a_c = head_out(diff_attend(q_c, k_c, v_c, lam))
        b_c = hy(u_c[..., 3 * A_WIDTH:])
        y_c = jnp.concatenate([a_c, b_c], axis=-1) @ w_out
    return y_l, y_c


def neighbourhood_attention(q, k, v, k_ctx, v_ctx, rpb):
    bsz, n, nh, dh = q.shape
    rows = n // GRID_W
    kr = min(NA_ROWS_MAX, rows)
    kc = NA_COLS
    scale = dh ** -0.5
    qg = q.reshape(bsz, rows, GRID_W, nh, dh)
    kg = k.reshape(bsz, rows, GRID_W, nh, dh)
    vg = v.reshape(bsz, rows, GRID_W, nh, dh)
    cols = jnp.arange(GRID_W)
    col_start = jnp.clip(cols - kc // 2, 0, GRID_W - kc)
    col_idx = col_start[:, None] + jnp.arange(kc)[None, :]
    col_off = col_idx - cols[:, None] + (NA_COLS - 1)
    rpb_cols = rpb[:, :, col_off]

    def row_step(args):
        r, q_row = args
        rs = jnp.clip(r - kr // 2, 0, rows - kr)
        k_nb = lax.dynamic_slice_in_dim(kg, rs, kr, axis=1)[:, :, col_idx]
        v_nb = lax.dynamic_slice_in_dim(vg, rs, kr, axis=1)[:, :, col_idx]
        row_off = rs + jnp.arange(kr) - r + (NA_ROWS_MAX - 1)
        bias = rpb_cols[:, row_off].transpose(0, 2, 1, 3)
        s_lat = jnp.einsum('bchd,brckhd->bhcrk', q_row, k_nb).astype(jnp.float32) * scale + bias
        s_ctx = jnp.einsum('bchd,bkhd->bhck', q_row, k_ctx).astype(jnp.float32) * scale
        s = jnp.concatenate([s_lat.reshape(bsz, nh, GRID_W, kr * kc), s_ctx], axis=-1)
        p = jax.nn.softmax(s, axis=-1).astype(v.dtype)
        p_lat = p[..., :kr * kc].reshape(bsz, nh, GRID_W, kr, kc)
        return (jnp.einsum('bhcrk,brckhd->bchd', p_lat, v_nb)
                + jnp.einsum('bhck,bkhd->bchd', p[..., kr * kc:], v_ctx))

    out = lax.map(row_step, (jnp.arange(rows), qg.swapaxes(0, 1)))
    return out.swapaxes(0, 1).reshape(bsz, n, nh * dh)


def mixer_c(h_lat, h_ctx, w_in, w_out, rpb, need_ctx):
    bsz, n, _ = h_lat.shape
    n_ctx = h_ctx.shape[1]
    u_l = (h_lat @ w_in).reshape(bsz, n, 3, C_HEADS, C_HEAD_DIM)
    if need_ctx:
        u_c = (h_ctx @ w_in).reshape(bsz, n_ctx, 3, C_HEADS, C_HEAD_DIM)
        k_c, v_c = u_c[:, :, 1], u_c[:, :, 2]
    else:
        kv_c = (h_ctx @ w_in[:, C_WIDTH:]).reshape(bsz, n_ctx, 2, C_HEADS, C_HEAD_DIM)
        k_c, v_c = kv_c[:, :, 0], kv_c[:, :, 1]
    o_l = neighbourhood_attention(u_l[:, :, 0], u_l[:, :, 1], u_l[:, :, 2], k_c, v_c, rpb)
    y_l = o_l @ w_out
    y_c = None
    if need_ctx:
        y_c = plain_attend(u_c[:, :, 0], k_c, v_c).reshape(bsz, n_ctx, C_WIDTH) @ w_out
    return y_l, y_c


def setup_inputs(seed: int = 0) -> dict:
    key = jax.random.key(seed)
    ks = jax.random.split(key, 27)
    f32 = jnp.float32
    D = D_MODEL

    def nrm(k, shape, scale):
        return jax.random.normal(k, shape, f32) * scale

    return {
        'x': nrm(ks[0], (BATCH, SEQ, D), 1.0),
        'c': nrm(ks[1], (BATCH, D), 1.0),
        'ctx': nrm(ks[2], (BATCH, CTX_LEN, D), 1.0),
        'c_ctx': nrm(ks[3], (D,), 1.0),
        'mod_w': nrm(ks[4], (DEPTH, D, N_MOD * D), 0.5 * D ** -0.5),
        'mod_b': nrm(ks[5], (DEPTH, N_MOD * D), 0.02),
        'norm_w': 1.0 + nrm(ks[6], (DEPTH, 3, D), 0.02),
        'ffn_w1': nrm(ks[7], (DEPTH, 2, D, D_FF), D ** -0.5),
        'ffn_w3': nrm(ks[8], (DEPTH, 2, D, D_FF), D ** -0.5),
        'ffn_w2': nrm(ks[9], (DEPTH, 2, D_FF, D), D_FF ** -0.5),
        'ab_w_in': nrm(ks[10], (N_EVEN, D, AB_IN), D ** -0.5),
        'ab_w_out': nrm(ks[11], (N_EVEN, MIX_WIDTH, D), MIX_WIDTH ** -0.5),
        'diff_lambda': nrm(ks[12], (N_EVEN, 4, A_QK_DIM), 0.1),
        'diff_subln_w': 1.0 + nrm(ks[13], (N_EVEN, A_V_DIM), 0.02),
        'hy_conv_w': nrm(ks[14], (N_EVEN, HY_SHORT, HY_IN), HY_SHORT ** -0.5),
        'hy_conv_b': nrm(ks[15], (N_EVEN, HY_IN), 0.02),
        'hy_f_w0': nrm(ks[16], (N_EVEN, HY_EMB, HY_FILTER_HID), HY_EMB ** -0.5),
        'hy_f_b0': nrm(ks[17], (N_EVEN, HY_FILTER_HID), 0.1),
        'hy_f_w1': nrm(ks[18], (N_EVEN, HY_INNER, HY_FILTER_HID, HY_FILTER_HID), HY_FILTER_HID ** -0.5),
        'hy_f_b1': nrm(ks[19], (N_EVEN, HY_INNER, HY_FILTER_HID), 0.1),
        'hy_f_freq': 1.0 + nrm(ks[20], (N_EVEN, HY_FILTER_HID), 0.1),
        'hy_f_wout': nrm(ks[21], (N_EVEN, HY_FILTER_HID, HY_ORDER * 2 * B_WIDTH), HY_FILTER_STD * HY_FILTER_HID ** -0.5),
        'hy_bias': nrm(ks[22], (N_EVEN, HY_ORDER, B_WIDTH), 0.5),
        'na_w_in': nrm(ks[23], (N_ODD, D, 3 * C_WIDTH), D ** -0.5),
        'na_w_out': nrm(ks[24], (N_ODD, C_WIDTH, D), C_WIDTH ** -0.5),
        'na_rpb': nrm(ks[25], (N_ODD, C_HEADS, 2 * NA_ROWS_MAX - 1, 2 * NA_COLS - 1), 0.02),
        'final_norm_w': 1.0 + nrm(ks[26], (D,), 0.02),
    }


def reference(x, c, ctx, c_ctx, mod_w, mod_b, norm_w, ffn_w1, ffn_w3, ffn_w2,
              ab_w_in, ab_w_out, diff_lambda, diff_subln_w, hy_conv_w, hy_conv_b,
              hy_f_w0, hy_f_b0, hy_f_w1, hy_f_b1, hy_f_freq, hy_f_wout, hy_bias,
              na_w_in, na_w_out, na_rpb, final_norm_w):
    bsz = x.shape[0]
    lat, cx = x, ctx
    s_l = jax.nn.silu(c)
    s_c = jax.nn.silu(c_ctx)[None]
    for layer in range(DEPTH):
        need_ctx = layer < DEPTH - 1
        mod_l = (s_l @ mod_w[layer] + mod_b[layer]).reshape(bsz, N_MOD, 1, D_MODEL)
        mod_c = (s_c @ mod_w[layer] + mod_b[layer]).reshape(1, N_MOD, 1, D_MODEL)
        g = norm_w[layer]
        f0 = (ffn_w1[layer, 0], ffn_w3[layer, 0], ffn_w2[layer, 0])
        f1 = (ffn_w1[layer, 1], ffn_w3[layer, 1], ffn_w2[layer, 1])
        lat = macaron(lat, g[0], mod_l, 0, *f0)
        cx = macaron(cx, g[0], mod_c, 0, *f0)
        h_l = adaln_in(lat, g[1], mod_l, 1)
        h_c = adaln_in(cx, g[1], mod_c, 1)
        if layer % 2 == 0:
            i = layer // 2
            y_l, y_c = mixer_ab(h_l, h_c, ab_w_in[i], ab_w_out[i], diff_lambda[i], diff_subln_w[i],
                                hy_conv_w[i], hy_conv_b[i], hy_f_w0[i], hy_f_b0[i], hy_f_w1[i],
                                hy_f_b1[i], hy_f_freq[i], hy_f_wout[i], hy_bias[i], layer, need_ctx)
        else:
            i = layer // 2
            y_l, y_c = mixer_c(h_l, h_c, na_w_in[i], na_w_out[i], na_rpb[i], need_ctx)
        lat = macaron(lat + mod_l[:, 5] * y_l, g[2], mod_l, 2, *f1)
        if need_ctx:
            cx = macaron(cx + mod_c[:, 5] * y_c, g[2], mod_c, 2, *f1)
    return rmsnorm(lat, final_norm_w)
```

```python
import math
from contextlib import ExitStack
import numpy as np
import ml_dtypes
import concourse.bass as bass
import concourse.mybir as mybir
from concourse.bass_utils import run_bass_kernel_spmd

F32 = mybir.dt.float32
BF16 = mybir.dt.bfloat16
AF = mybir.ActivationFunctionType
ALU = mybir.AluOpType
AX = mybir.AxisListType

D = 1024; SEQ = 4096; NCTX = 256; TT = SEQ + NCTX; DFF = 2816; NJ = DFF // 128
GRID_W = 64; EPS = 1e-6
GS = 512
SAME_ENGINE_SYNC = True


class Res:
    __slots__ = ("w", "r")

    def __init__(self):
        self.w = None
        self.r = {}


class Sched:
    NSLOT = 10

    def __init__(self, nc, es):
        self.nc = nc
        self.eng = {"pe": nc.tensor, "act": nc.scalar, "dve": nc.vector, "pool": nc.gpsimd, "sp": nc.sync}
        self.sem = {e: es.enter_context(nc.semaphore("s_" + e)) for e in ("pe", "act", "dve", "pool")}
        self.cnt = {e: 0 for e in self.sem}
        self.seen = {e: {} for e in self.eng}
        self.dsem = {q: [es.enter_context(nc.semaphore("d_%s_%d" % (q, i))) for i in range(self.NSLOT)]
                     for q in ("sp", "pool")}
        self.dcnt = {q: [0] * self.NSLOT for q in self.dsem}
        self.dnext = {q: 0 for q in self.dsem}
        self.dram = {}

    def dres(self, *key):
        r = self.dram.get(key)
        if r is None:
            r = self.dram[key] = Res()
        return r

    def _semof(self, key):
        return self.sem[key[1]] if key[0] == "c" else self.dsem[key[1]][key[2]]

    def _wait(self, eng, key, val):
        if self.seen[eng].get(key, 0) >= val:
            return
        self.seen[eng][key] = val
        self.eng[eng].wait_ge(self._semof(key), val)

    def _deps(self, eng, reads, writes):
        best = {}
        for r in reads:
            if r.w is not None:
                k, v = r.w
                if best.get(k, 0) < v:
                    best[k] = v
        for w in writes:
            if w.w is not None:
                k, v = w.w
                if best.get(k, 0) < v:
                    best[k] = v
            for k, v in w.r.items():
                if best.get(k, 0) < v:
                    best[k] = v
        for k, v in best.items():
            if k == ("c", eng) and (eng == "pe" or not SAME_ENGINE_SYNC):
                continue
            self._wait(eng, k, v)

    def _mark(self, key, val, reads, writes):
        for r in reads:
            if r.r.get(key, 0) < val:
                r.r[key] = val
        for w in writes:
            w.w = (key, val)
            w.r = {}

    def op(self, eng, fn, reads=(), writes=()):
        self._deps(eng, reads, writes)
        self.cnt[eng] += 1
        fn(self.eng[eng]).then_inc(self.sem[eng], 1)
        self._mark(("c", eng), self.cnt[eng], reads, writes)

    def mm(self, fns, reads=(), writes=()):
        self._deps("pe", reads, writes)
        ins = None
        for f in fns:
            ins = f(self.nc.tensor)
        self.cnt["pe"] += 1
        ins.then_inc(self.sem["pe"], 1)
        self._mark(("c", "pe"), self.cnt["pe"], reads, writes)

    def dma(self, q, out, in_, reads=(), writes=(), **kw):
        slot = self.dnext[q]
        self.dnext[q] = (slot + 1) % self.NSLOT
        key = ("d", q, slot)
        if self.dcnt[q][slot] > 0:
            self._wait(q, key, self.dcnt[q][slot])
        self._deps(q, reads, writes)
        self.dcnt[q][slot] += 16
        self.eng[q].dma_start(out=out, in_=in_, **kw).then_inc(self.dsem[q][slot], 16)
        self._mark(key, self.dcnt[q][slot], reads, writes)

    def barrier(self):
        keys = [(("c", e), self.cnt[e]) for e in self.cnt]
        for q in self.dsem:
            for i in range(self.NSLOT):
                keys.append((("d", q, i), self.dcnt[q][i]))
        for e in self.eng:
            for k, v in keys:
                if v > 0:
                    self._wait(e, k, v)


class Tl:
    def __init__(self, t, nres=1):
        self.t = t
        self.res = [Res() for _ in range(nres)]

    @property
    def r(self):
        return self.res[0]


def _bf(a):
    return np.asarray(a, dtype=np.float32).astype(ml_dtypes.bfloat16)


def _dft_blocks(n):
    ne = n + 128
    idx = np.arange(n, dtype=np.int64)
    m = (idx[:, None] * idx[None, :]) % (2 * n)
    ang = m.astype(np.float64) * (math.pi / n)
    C = np.zeros((ne, ne), np.float64)
    S = np.zeros((ne, ne), np.float64)
    C[:n, :n] = np.cos(ang)
    S[:n, :n] = np.sin(ang)
    alt = np.where(idx % 2 == 0, 1.0, -1.0)
    C[:n, n] = alt
    C[n, :n] = alt
    nt = ne // 128

    def blk(M):
        return np.ascontiguousarray(M.reshape(nt, 128, nt, 128).transpose(2, 1, 0, 3)).reshape(nt, 128, nt * 128)
    wf = np.full((ne,), 1.0 / n, np.float32)
    wf[0] = 0.5 / n
    wf[n] = 0.5 / n
    wf[n + 1:] = 0.0
    return _bf(blk(C)), _bf(blk(S)), np.ascontiguousarray(wf.reshape(nt, 128).T)


def _filter_consts(n):
    t = np.linspace(0.0, 1.0, n, dtype=np.float32)[:, None]
    w = (2.0 * math.pi / n) * np.arange(n, dtype=np.float32)[:, None]
    bands = np.linspace(1e-4, 15, 16, dtype=np.float32)[None, :]
    z = np.concatenate([t, np.cos(bands * w), -np.sin(bands * w)], axis=-1).astype(np.float32)
    tl = np.ascontiguousarray((-t[:, 0]).reshape(n // 128, 128).T).astype(np.float32)
    return np.ascontiguousarray(z.T), tl


def _rope_tables():
    t = np.arange(SEQ)
    pos = (t // GRID_W, t % GRID_W)
    inv = (10000.0 ** (-np.arange(16, dtype=np.float32) / 16)).astype(np.float32)
    C = np.zeros((128, SEQ), np.float32)
    Sg = np.zeros((128, SEQ), np.float32)
    for m in range(2):
        for a in range(2):
            ang = pos[a].astype(np.float32)[None, :] * inv[:, None]
            c = np.cos(ang).astype(np.float32)
            s = np.sin(ang).astype(np.float32)
            b = m * 64 + a * 32
            C[b:b + 16] = c
            C[b + 16:b + 32] = c
            Sg[b:b + 16] = -s
            Sg[b + 16:b + 32] = s
    perm = np.zeros(128, np.int64)
    for m in range(2):
        for a in range(2):
            b = m * 64 + a * 32
            perm[b:b + 16] = np.arange(b + 16, b + 32)
            perm[b + 16:b + 32] = np.arange(b, b + 16)
    return C, Sg, perm


def _na_geometry():
    blocks = []
    types = {}
    for qb in range(32):
        r0 = 2 * qb
        rs = [min(max(r - 4, 0), 56) for r in (r0, r0 + 1)]
        lo, hi = min(rs), max(rs) + 8
        sig = (hi - lo, rs[0] - lo, rs[1] - lo, r0 - lo)
        if sig not in types:
            types[sig] = len(types)
        blocks.append((lo, hi, types[sig]))
    return blocks, types


def _na_bias(rpb):
    blocks, types = _na_geometry()
    nt = len(types)
    out = np.zeros((nt, 16, 7 * 128, 128), np.float32)
    out[:, :, 256:, :] = -30000.0
    kk = np.arange(640)
    ki, kc = kk // 64, kk % 64
    qq = np.arange(128)
    qj, qc = qq // 64, qq % 64
    cs = np.clip(qc - 8, 0, 48)
    for sig, ti in types.items():
        nrows, rs0, rs1, r0l = sig
        rs = np.array([rs0, rs1])[qj]
        qr = r0l + qj
        valid = (ki[:, None] < nrows) & (ki[:, None] >= rs[None, :]) & (ki[:, None] < rs[None, :] + 8) \
            & (kc[:, None] >= cs[None, :]) & (kc[:, None] < cs[None, :] + 16)
        dr = np.clip(ki[:, None] - qr[None, :] + 7, 0, 14)
        dc = np.clip(kc[:, None] - qc[None, :] + 15, 0, 30)
        g = rpb[:, dr, dc]
        out[ti, :, 256:, :] = np.where(valid[None], g, np.float32(-30000.0))
    o = out.reshape(nt, 8, 2, 7, 128, 128).transpose(1, 4, 0, 2, 3, 5)
    return np.ascontiguousarray(o).reshape(8, 128, nt * 2 * 7 * 128), blocks, nt


_CONSTS = {}


def _consts():
    if _CONSTS:
        return _CONSTS
    c = _CONSTS
    c["dftC"], c["dftS"], c["wf"] = _dft_blocks(SEQ)
    c["dftCc"], c["dftSc"], c["wfc"] = _dft_blocks(NCTX)
    c["zT"], c["tl"] = _filter_consts(SEQ)
    c["zTc"], c["tlc"] = _filter_consts(NCTX)
    hy_min = math.log(1e-2) / 1.5
    hy_max = math.log(1e-2) / 0.3
    deltas = np.abs(np.linspace(hy_min, hy_max, 512, dtype=np.float32))
    c["deltas"] = np.ascontiguousarray(np.broadcast_to(deltas[None, :], (128, 512))).astype(np.float32)
    c["ropeC"], c["ropeS"], c["perm"] = _rope_tables()
    c["ident"] = np.eye(128, dtype=np.float32)
    return c


def _fm(v, nch):
    return np.ascontiguousarray(np.asarray(v, np.float32).reshape(nch, 128).T)


class Prog:
    def __init__(self, dbg=None, nab_cols=0, na_blocks=None):
        self.dbg = dbg
        self.nab_cols = nab_cols
        self.na_blocks = na_blocks
        self.nc = bass.Bass("TRN2", target_bir_lowering=False)
        self.din = {}

    def inp(self, name, shape, dt=F32):
        self.din[name] = self.nc.dram_tensor(name, list(shape), dt, kind="ExternalInput").ap()
        return self.din[name]

    def scr(self, name, shape, dt):
        return self.nc.dram_tensor(name, list(shape), dt).ap()

    def sb(self, name, shape, dt=F32, nres=1):
        self.uid = getattr(self, "uid", 0) + 1
        return Tl(self.es.enter_context(self.nc.sbuf_tensor("sb%d_%s" % (self.uid, name), list(shape), dt)), nres)

    def build(self):
        nc = self.nc
        I = self.inp
        x = I("x", [SEQ, D]); ctxi = I("ctxi", [NCTX, D])
        I("sv", [128, 16]); I("mod_w", [2, D, 9 * D]); I("mod_b2", [128, 2 * 2 * 72]); I("normT2", [128, 2 * 3 * 2 * 8])
        I("fnormT", [128, 8])
        I("ffn_w1", [2, 2, D, DFF]); I("ffn_w3", [2, 2, D, DFF]); I("ffn_w2", [2, 2, DFF, D])
        I("ab_w_in", [D, 3072]); I("ab_w_inp", [D, 1024]); I("ab_w_out", [D, D])
        I("lamp", [128, 256]); I("subw", [128, 1])
        I("hcw", [128, 36]); I("hcb", [128, 12])
        I("fw0", [33, 64]); I("fb0", [64, 1]); I("fw1", [2, 64, 64]); I("fb1", [64, 2]); I("ffreq", [64, 1])
        I("fwout", [64, 2048]); I("hskip", [1, 1024])
        I("na_w_in", [D, 3072]); I("na_w_out", [D, D]); I("nab", [8, 128, self.nab_cols])
        I("ident", [128, 128]); I("ropeC", [128, SEQ]); I("ropeS", [128, SEQ])
        I("dftC", [33, 128, 33 * 128], BF16); I("dftS", [33, 128, 33 * 128], BF16); I("wf", [128, 33])
        I("dftCc", [3, 128, 3 * 128], BF16); I("dftSc", [3, 128, 3 * 128], BF16); I("wfc", [128, 3])
        I("zT", [33, SEQ]); I("zTc", [33, NCTX]); I("tl", [128, 32]); I("tlc", [128, 2]); I("deltas", [128, 512])
        self.out = nc.dram_tensor("out", [SEQ, D], F32, kind="ExternalOutput").ap()
        if self.dbg:
            self.dbgo = nc.dram_tensor("dbg", [D, TT], F32, kind="ExternalOutput").ap()
        S_ = self.scr
        self.latT = S_("latT", [D, TT], F32)
        self.w1b = [[S_("w1b%d%d" % (l, i), [D, DFF], BF16) for i in range(2)] for l in range(2)]
        self.w3b = [[S_("w3b%d%d" % (l, i), [D, DFF], BF16) for i in range(2)] for l in range(2)]
        self.w2b = [[S_("w2b%d%d" % (l, i), [DFF, D], BF16) for i in range(2)] for l in range(2)]
        self.abinb = S_("abinb", [D, 3072], BF16); self.abinpb = S_("abinpb", [D, 1024], BF16)
        self.aboutb = S_("aboutb", [D, D], BF16)
        self.nainb = S_("nainb", [D, 3072], BF16); self.naoutb = S_("naoutb", [D, D], BF16)
        self.QT = S_("QT", [D, TT], BF16); self.KT = S_("KT", [D, TT], BF16)
        self.VA = S_("VA", [TT, 1040], BF16)
        self.UH = S_("UH", [1536, TT], F32)
        self.UTM = S_("UTM", [TT, 1024], F32)
        self.VTM = S_("VTM", [TT, 512], BF16)
        self.HSD = S_("HSD", [2, 2, SEQ, 512], BF16)
        self.HCS = S_("HCS", [2, 2, 33 * 128, 512], F32)
        self.catT = S_("catT", [D, TT], BF16)
        with ExitStack() as es:
            self.es = es
            self.S = Sched(nc, es)
            self.ps = [Tl(es.enter_context(nc.psum_tensor("ps%d" % i, [128, 512], F32))) for i in range(8)]
            self.persist()
            self.phase(self.ph_setup)
            self.phase(self.ph_xT)
            for l in range(2):
                self.phase(lambda: self.ph_ffn(l, 0))
                if self.dbg == "ffn%d0" % l:
                    break
                if l == 0:
                    self.phase(self.ph_abproj)
                    self.phase(self.ph_hyena_prep)
                    self.phase(self.ph_diffattn)
                    for nn in (NCTX, SEQ):
                        self.phase(lambda: self.ph_filters(nn))
                        self.phase(lambda: self.ph_hyena(nn))
                    if self.dbg == "cat":
                        break
                    self.phase(lambda: self.ph_outproj(0))
                else:
                    self.phase(self.ph_naproj)
                    self.phase(self.ph_na)
                    self.phase(lambda: self.ph_outproj(1))
                if self.dbg == "mix%d" % l:
                    break
                self.phase(lambda: self.ph_ffn(l, 1))
                if self.dbg == "ffn%d1" % l:
                    break
            if self.dbg:
                src = self.latT
                if self.dbg == "cat":
                    src = None
                if src is not None:
                    self.S.dma("sp", self.dbgo, src, reads=[], writes=[self.S.dres("dbgo")])
                else:
                    self.S.dma("pool", self.dbgo, self.catT, reads=[], writes=[self.S.dres("dbgo")])
            else:
                self.phase(self.ph_final)
            self.S.barrier()
        return nc

    def phase(self, fn):
        with ExitStack() as es:
            old = self.es
            self.es = es
            fn()
            self.S.barrier()
            self.es = old

    def persist(self):
        S = self.S
        self.ident = self.sb("ident", [128, 128])
        self.onesb = self.sb("onesb", [128, 128], BF16)
        self.msc = self.sb("msc", [128, 2 * 3 * 3 * 2 * 8])
        S.dma("sp", self.ident.t[:], self.din["ident"], writes=[self.ident.r])
        S.op("dve", lambda e: e.memset(self.onesb.t[:], 1.0), writes=[self.onesb.r])
        self.svs = self.sb("svs", [128, 16]); self.modT = self.sb("modT", [128, 2 * 2 * 72])
        self.mb2 = self.sb("mb2", [128, 2 * 2 * 72]); self.nT2 = self.sb("nT2", [128, 96])
        S.dma("sp", self.svs.t[:], self.din["sv"], writes=[self.svs.r])
        S.dma("sp", self.mb2.t[:], self.din["mod_b2"], writes=[self.mb2.r])
        S.dma("sp", self.nT2.t[:], self.din["normT2"], writes=[self.nT2.r])

    def mcol(self, l, k, ty, s):
        o = (((l * 3 + k) * 3 + ty) * 2 + s) * 8
        return self.msc.t[:, o:o + 8]

    def cast_weights(self, l):
        S, din = self.S, self.din
        for i in range(2):
            S.dma("pool", self.w1b[l][i], din["ffn_w1"][l, i], writes=[S.dres("w1b", l, i)])
            S.dma("pool", self.w3b[l][i], din["ffn_w3"][l, i], writes=[S.dres("w3b", l, i)])
            S.dma("pool", self.w2b[l][i], din["ffn_w2"][l, i], writes=[S.dres("w2b", l, i)])
        pairs = ((self.abinb, "ab_w_in"), (self.abinpb, "ab_w_inp"), (self.aboutb, "ab_w_out")) if l == 0 else \
            ((self.nainb, "na_w_in"), (self.naoutb, "na_w_out"))
        for dst, src in pairs:
            S.dma("pool", dst, din[src], writes=[S.dres(src)])

    def mod_alloc(self):
        self.mw = [self.sb("mw%d" % i, [128, 8, 1024]) for i in range(2)]
        self.mwi = 0

    def mod_block(self, l, b):
        S, din = self.S, self.din
        svs, modT, mb2 = self.svs, self.modT, self.mb2
        svv = svs.t[:].rearrange("p (k s) -> p k s", s=2)
        mwv = din["mod_w"][l].rearrange("(k p) n -> p k n", p=128)
        buf = self.mw[self.mwi % 2]
        self.mwi += 1
        S.dma("sp", buf.t[:], mwv[:, :, b * 1024:(b + 1) * 1024], writes=[buf.r])
        ps = self.ps[7]
        fns = []
        for jj in range(8):
            for k in range(8):
                fns.append(lambda e, jj=jj, k=k, buf=buf, ps=ps: e.matmul(
                    ps.t[:, 2 * jj:2 * jj + 2], lhsT=buf.t[:, k, jj * 128:(jj + 1) * 128], rhs=svv[:, k, :],
                    start=(k == 0), stop=(k == 7)))
        S.mm(fns, reads=[buf.r, svs.r], writes=[ps.r])
        psv = ps.t[:, 0:16].rearrange("p (j s) -> p s j", s=2)
        for s in range(2):
            o = (l * 2 + s) * 72 + b * 8
            S.op("dve", lambda e, s=s, o=o, psv=psv: e.tensor_tensor(
                out=modT.t[:, o:o + 8], in0=psv[:, s, :], in1=mb2.t[:, o:o + 8], op=ALU.add),
                reads=[ps.r, mb2.r], writes=[modT.r])

    def mod_finish(self, l):
        S = self.S
        modT, nT2 = self.modT, self.nT2
        for k in range(3):
            for s in range(2):
                mo = (l * 2 + s) * 72
                no = ((l * 3 + k) * 2 + s) * 8
                sc = modT.t[:, mo + (3 * k + 1) * 8: mo + (3 * k + 2) * 8]
                sh = modT.t[:, mo + (3 * k) * 8: mo + (3 * k + 1) * 8]
                gt = modT.t[:, mo + (3 * k + 2) * 8: mo + (3 * k + 3) * 8]
                S.op("dve", lambda e, sc=sc, no=no, l=l, k=k, s=s: e.scalar_tensor_tensor(
                    out=self.mcol(l, k, 0, s), in0=sc, scalar=1.0, in1=nT2.t[:, no:no + 8],
                    op0=ALU.add, op1=ALU.mult), reads=[modT.r, nT2.r], writes=[self.msc.r])
                S.op("dve", lambda e, sh=sh, l=l, k=k, s=s: e.tensor_copy(out=self.mcol(l, k, 1, s), in_=sh),
                     reads=[modT.r], writes=[self.msc.r])
                S.op("dve", lambda e, gt=gt, l=l, k=k, s=s: e.tensor_scalar(
                    out=self.mcol(l, k, 2, s), in0=gt, scalar1=(1.0 if k == 1 else 0.5), scalar2=None,
                    op0=ALU.mult), reads=[modT.r], writes=[self.msc.r])

    def ph_setup(self):
        S, din = self.S, self.din
        self.cast_weights(0)
        S.op("act", lambda e: e.activation(out=self.svs.t[:], in_=self.svs.t[:], func=AF.Silu), reads=[self.svs.r], writes=[self.svs.r])
        self.mod_alloc()
        for b in range(9):
            self.mod_block(0, b)
        self.mod_finish(0)

    def ph_xT(self):
        S = self.S
        xin = [self.sb("xin%d" % i, [128, D]) for i in range(2)]
        xo = [self.sb("xo%d" % i, [128, 8, 128]) for i in range(2)]
        for ti in range(TT // 128):
            src = self.din["ctxi"][ti * 128:(ti + 1) * 128, :] if ti < 2 else self.din["x"][(ti - 2) * 128:(ti - 1) * 128, :]
            xi, o = xin[ti % 2], xo[ti % 2]
            S.dma("sp", xi.t[:], src, writes=[xi.r])
            for h in range(2):
                ps = self.ps[(ti % 2) * 2 + h]
                S.mm([lambda e, c=c, ps=ps, xi=xi, h=h: e.transpose(
                    ps.t[:, c * 128:(c + 1) * 128], xi.t[:, (h * 4 + c) * 128:(h * 4 + c + 1) * 128], self.ident.t[:])
                    for c in range(4)], reads=[xi.r, self.ident.r], writes=[ps.r])
                eng = "act" if h == 0 else "dve"
                ov = o.t[:, h * 4:(h + 1) * 4, :]
                pv = ps.t[:, :].rearrange("p (c t) -> p c t", t=128)
                if eng == "act":
                    S.op("act", lambda e, ov=ov, pv=pv: e.copy(out=ov, in_=pv), reads=[ps.r], writes=[o.r])
                else:
                    S.op("dve", lambda e, ov=ov, pv=pv: e.tensor_copy(out=ov, in_=pv), reads=[ps.r], writes=[o.r])
            dst = self.latT.rearrange("(c p) t -> p c t", p=128)[:, :, ti * 128:(ti + 1) * 128]
            S.dma("sp", dst, o.t[:], reads=[o.r], writes=[S.dres("latT", ti)])

    def groups(self, with_ctx=True):
        g = [(0, NCTX, 1)] if with_ctx else []
        return g + [(NCTX + GS * i, GS, 0) for i in range(SEQ // GS)]

    def alloc_pro(self):
        self.xt = [self.sb("xt%d" % i, [128, 8, GS]) for i in range(2)]
        self.sq = self.sb("sq", [128, 8, GS], BF16)
        self.rstd = self.sb("rstd", [128, GS])
        self.ptmp = [self.sb("ptmp%d" % i, [128, GS]) for i in range(2)]
        self.hTs = [self.sb("hT%d" % i, [128, 8, GS], BF16) for i in range(2)]
        self.hT = self.hTs[0]
        self.gi = 0

    def prologue(self, l, k, t0, n, s, psb, want_h=True):
        S = self.S
        xt = self.xt[self.gi % 2]
        self.hT = self.hTs[self.gi % 2]
        self.gi += 1
        lv = self.latT.rearrange("(c p) t -> p c t", p=128)[:, :, t0:t0 + n]
        S.dma("sp", xt.t[:, :, :n], lv, writes=[xt.r])
        sq, rstd, hT = self.sq, self.rstd, self.hT
        S.op("act", lambda e: e.activation(out=sq.t[:, :, :n], in_=xt.t[:, :, :n], func=AF.Square),
             reads=[xt.r], writes=[sq.r])
        ps = self.ps[psb]
        S.mm([lambda e, c=c: e.matmul(ps.t[:, :n], lhsT=self.onesb.t[:], rhs=sq.t[:, c, :n], start=(c == 0), stop=(c == 7))
              for c in range(8)], reads=[sq.r, self.onesb.r], writes=[ps.r])
        S.op("act", lambda e: e.activation(out=rstd.t[:, :n], in_=ps.t[:, :n], func=AF.Sqrt, bias=EPS, scale=1.0 / D),
             reads=[ps.r], writes=[rstd.r])
        S.op("dve", lambda e: e.reciprocal(out=rstd.t[:, :n], in_=rstd.t[:, :n]), reads=[rstd.r], writes=[rstd.r])
        if want_h:
            gs, sh = self.mcol(l, k, 0, s), self.mcol(l, k, 1, s)
            for c in range(8):
                tmp = self.ptmp[c % 2]
                S.op("dve", lambda e, c=c, tmp=tmp: e.scalar_tensor_tensor(
                    out=tmp.t[:, :n], in0=xt.t[:, c, :n], scalar=gs[:, c:c + 1], in1=rstd.t[:, :n],
                    op0=ALU.mult, op1=ALU.mult), reads=[xt.r, rstd.r, self.msc.r], writes=[tmp.r])
                S.op("act", lambda e, c=c, tmp=tmp: e.activation(
                    out=hT.t[:, c, :n], in_=tmp.t[:, :n], func=AF.Identity, bias=sh[:, c:c + 1], scale=1.0),
                    reads=[tmp.r, self.msc.r], writes=[hT.r])
        return xt

    def store_group(self, xt, t0, n):
        lv = self.latT.rearrange("(c p) t -> p c t", p=128)[:, :, t0:t0 + n]
        self.S.dma("sp", lv, xt.t[:, :, :n], reads=[xt.r], writes=[self.S.dres("latT", t0)])

    def ph_ffn(self, l, i):
        S = self.S
        k = 0 if i == 0 else 2
        self.alloc_pro()
        w2t = self.sb("w2t", [128, NJ, D], BF16)
        w2v = self.w2b[l][i].rearrange("(j p) n -> p j n", p=128)
        for a in range(2):
            S.dma("sp", w2t.t[:, a * 11:(a + 1) * 11, :], w2v[:, a * 11:(a + 1) * 11, :], writes=[w2t.r] if a == 0 else [w2t.r])
        wb1 = [self.sb("wb1_%d" % a, [128, 8, 256], BF16) for a in range(3)]
        wb3 = [self.sb("wb3_%d" % a, [128, 8, 256], BF16) for a in range(3)]
        actT = self.sb("actT", [128, NJ, GS], BF16)
        sg = [self.sb("sg%d" % a, [128, GS]) for a in range(2)]
        w1v = self.w1b[l][i].rearrange("(k p) n -> p k n", p=128)
        w3v = self.w3b[l][i].rearrange("(k p) n -> p k n", p=128)
        need_ctx = (l == 0) or (i == 0)
        bi = 0
        grps = self.groups(need_ctx)
        nxt = (self.prologue(l, k, grps[0][0], grps[0][1], grps[0][2], 6), self.hT)
        for gidx, (t0, n, s) in enumerate(grps):
            xt, hT = nxt
            for nb in range(11):
                b1, b3 = wb1[bi % 3], wb3[bi % 3]
                bi += 1
                S.dma("sp", b1.t[:], w1v[:, :, nb * 256:(nb + 1) * 256], writes=[b1.r])
                S.dma("sp", b3.t[:], w3v[:, :, nb * 256:(nb + 1) * 256], writes=[b3.r])
                for jj in range(2):
                    j = nb * 2 + jj
                    pg, pu = self.ps[2 * (j % 2)], self.ps[2 * (j % 2) + 1]
                    S.mm([lambda e, kc=kc, b1=b1, pg=pg, jj=jj: e.matmul(
                        pg.t[:, :n], lhsT=b1.t[:, kc, jj * 128:(jj + 1) * 128], rhs=hT.t[:, kc, :n],
                        start=(kc == 0), stop=(kc == 7)) for kc in range(8)], reads=[b1.r, hT.r], writes=[pg.r])
                    S.mm([lambda e, kc=kc, b3=b3, pu=pu, jj=jj: e.matmul(
                        pu.t[:, :n], lhsT=b3.t[:, kc, jj * 128:(jj + 1) * 128], rhs=hT.t[:, kc, :n],
                        start=(kc == 0), stop=(kc == 7)) for kc in range(8)], reads=[b3.r, hT.r], writes=[pu.r])
                    sgt = sg[j % 2]
                    S.op("act", lambda e, pg=pg, sgt=sgt: e.activation(out=sgt.t[:, :n], in_=pg.t[:, :n], func=AF.Silu),
                         reads=[pg.r], writes=[sgt.r])
                    S.op("dve", lambda e, pu=pu, sgt=sgt, j=j: e.tensor_tensor(
                        out=actT.t[:, j, :n], in0=pu.t[:, :n], in1=sgt.t[:, :n], op=ALU.mult),
                        reads=[pu.r, sgt.r], writes=[actT.r])
            if gidx + 1 < len(grps):
                g2 = grps[gidx + 1]
                nxt = (self.prologue(l, k, g2[0], g2[1], g2[2], 6), self.hT)
            gate = self.mcol(l, k, 2, s)
            for dc in range(8):
                py = self.ps[4 + dc % 2]
                S.mm([lambda e, j=j, py=py, dc=dc: e.matmul(
                    py.t[:, :n], lhsT=w2t.t[:, j, dc * 128:(dc + 1) * 128], rhs=actT.t[:, j, :n],
                    start=(j == 0), stop=(j == NJ - 1)) for j in range(NJ)], reads=[w2t.r, actT.r], writes=[py.r])
                S.op("dve", lambda e, py=py, dc=dc: e.scalar_tensor_tensor(
                    out=xt.t[:, dc, :n], in0=py.t[:, :n], scalar=gate[:, dc:dc + 1], in1=xt.t[:, dc, :n],
                    op0=ALU.mult, op1=ALU.add), reads=[py.r, xt.r, self.msc.r], writes=[xt.r])
            self.store_group(xt, t0, n)

    def ph_abproj(self):
        S, din = self.S, self.din
        self.alloc_pro()
        win = self.sb("win", [128, 8, 3072], BF16)
        winp = self.sb("winp", [128, 8, 1024], BF16)
        wv = self.abinb.rearrange("(k p) n -> p k n", p=128)
        for a in range(3):
            S.dma("sp", win.t[:, :, a * 1024:(a + 1) * 1024], wv[:, :, a * 1024:(a + 1) * 1024], writes=[win.r])
        S.dma("sp", winp.t[:], self.abinpb.rearrange("(k p) n -> p k n", p=128), writes=[winp.r])
        rc = self.sb("rc", [128, GS]); rs = self.sb("rs", [128, GS])
        qo = [self.sb("qo%d" % a, [128, GS], BF16) for a in range(2)]
        t1 = [self.sb("t1_%d" % a, [128, GS]) for a in range(2)]
        t2 = [self.sb("t2_%d" % a, [128, GS]) for a in range(2)]
        vo = [self.sb("vo%d" % a, [128, 4, 129], BF16) for a in range(2)]
        uo = [self.sb("uo%d" % a, [128, GS]) for a in range(2)]
        for v in vo:
            S.op("dve", lambda e, v=v: e.memset(v.t[:], 1.0), writes=[v.r])
        ci = 0
        for (t0, n, s) in self.groups(True):
            self.prologue(0, 1, t0, n, s, 6)
            hT = self.hT
            if s == 0:
                S.dma("sp", rc.t[:, :n], din["ropeC"][:, t0 - NCTX:t0 - NCTX + n], writes=[rc.r])
                S.dma("sp", rs.t[:, :n], din["ropeS"][:, t0 - NCTX:t0 - NCTX + n], writes=[rs.r])
            for which in range(2):
                for hc in range(4):
                    col = which * 512 + hc * 128
                    pa, pb = self.ps[(ci % 2) * 2], self.ps[(ci % 2) * 2 + 1]
                    q = qo[ci % 2]; a1 = t1[ci % 2]; a2 = t2[ci % 2]
                    ci += 1
                    S.mm([lambda e, k=k, pa=pa, col=col: e.matmul(
                        pa.t[:, :n], lhsT=win.t[:, k, col:col + 128], rhs=hT.t[:, k, :n], start=(k == 0), stop=(k == 7))
                        for k in range(8)], reads=[win.r, hT.r], writes=[pa.r])
                    if s == 0:
                        S.mm([lambda e, k=k, pb=pb, col=col: e.matmul(
                            pb.t[:, :n], lhsT=winp.t[:, k, col:col + 128], rhs=hT.t[:, k, :n], start=(k == 0), stop=(k == 7))
                            for k in range(8)], reads=[winp.r, hT.r], writes=[pb.r])
                        S.op("dve", lambda e, pa=pa, a1=a1: e.tensor_tensor(out=a1.t[:, :n], in0=pa.t[:, :n], in1=rc.t[:, :n], op=ALU.mult),
                             reads=[pa.r, rc.r], writes=[a1.r])
                        S.op("dve", lambda e, pb=pb, a2=a2: e.tensor_tensor(out=a2.t[:, :n], in0=pb.t[:, :n], in1=rs.t[:, :n], op=ALU.mult),
                             reads=[pb.r, rs.r], writes=[a2.r])
                        S.op("pool", lambda e, q=q, a1=a1, a2=a2: e.tensor_tensor(out=q.t[:, :n], in0=a1.t[:, :n], in1=a2.t[:, :n], op=ALU.add),
                             reads=[a1.r, a2.r], writes=[q.r])
                    else:
                        S.op("act", lambda e, q=q, pa=pa: e.copy(out=q.t[:, :n], in_=pa.t[:, :n]), reads=[pa.r], writes=[q.r])
                    dst = (self.QT if which == 0 else self.KT)[hc * 128:(hc + 1) * 128, t0:t0 + n]
                    S.dma("sp", dst, q.t[:, :n], reads=[q.r], writes=[S.dres("qk", which, hc, t0)])
            for tt in range(n // 128):
                pv = self.ps[4 + tt % 2]
                v = vo[tt % 2]
                S.mm([lambda e, k=k, pv=pv, tt=tt: e.matmul(
                    pv.t[:, :512], lhsT=hT.t[:, k, tt * 128:(tt + 1) * 128], rhs=win.t[:, k, 1024:1536], start=(k == 0), stop=(k == 7))
                    for k in range(8)], reads=[win.r, hT.r], writes=[pv.r])
                S.op("act", lambda e, pv=pv, v=v: e.copy(out=v.t[:, :, 0:128], in_=pv.t[:, :].rearrange("p (h d) -> p h d", d=128)),
                     reads=[pv.r], writes=[v.r])
                S.dma("sp", self.VA[t0 + tt * 128:t0 + (tt + 1) * 128, 0:516].rearrange("p (h d) -> p h d", d=129), v.t[:],
                      reads=[v.r], writes=[S.dres("va", t0, tt)])
            for c in range(12):
                pu = self.ps[(c % 4)]
                u = uo[c % 2]
                S.mm([lambda e, k=k, pu=pu, c=c: e.matmul(
                    pu.t[:, :n], lhsT=win.t[:, k, 1536 + c * 128:1536 + (c + 1) * 128], rhs=hT.t[:, k, :n], start=(k == 0), stop=(k == 7))
                    for k in range(8)], reads=[win.r, hT.r], writes=[pu.r])
                if c % 2 == 0:
                    S.op("act", lambda e, pu=pu, u=u: e.copy(out=u.t[:, :n], in_=pu.t[:, :n]), reads=[pu.r], writes=[u.r])
                else:
                    S.op("dve", lambda e, pu=pu, u=u: e.tensor_copy(out=u.t[:, :n], in_=pu.t[:, :n]), reads=[pu.r], writes=[u.r])
                S.dma("sp", self.UH[c * 128:(c + 1) * 128, t0:t0 + n], u.t[:, :n], reads=[u.r], writes=[S.dres("uh", c, t0)])

    def ph_hyena_prep(self):
        S, din = self.S, self.din
        cw = self.sb("cw", [128, 36]); cb = self.sb("cb", [128, 12])
        S.dma("sp", cw.t[:], din["hcw"], writes=[cw.r]); S.dma("sp", cb.t[:], din["hcb"], writes=[cb.r])
        ub = [self.sb("ub%d" % a, [128, SEQ + 2]) for a in range(2)]
        uc = [self.sb("uc%d" % a, [128, SEQ]) for a in range(2)]
        st = [self.sb("st%d" % a, [128, 4, 128]) for a in range(2)]
        stb = [self.sb("stb%d" % a, [128, 4, 128], BF16) for a in range(2)]
        ci = 0
        gi = 0
        self.mod_alloc()
        mblk = 0
        for (T0, n) in ((0, NCTX), (NCTX, SEQ)):
            for c in range(12):
                if n == SEQ and mblk < 9:
                    self.mod_block(1, mblk)
                    mblk += 1
                    if mblk == 9:
                        self.mod_finish(1)
                b, u = ub[ci % 2], uc[ci % 2]
                ci += 1
                S.op("dve", lambda e, b=b: e.memset(b.t[:, 0:1], 0.0), writes=[b.r])
                S.op("dve", lambda e, b=b: e.memset(b.t[:, n + 1:n + 2], 0.0), writes=[b.r])
                S.dma("sp", b.t[:, 1:n + 1], self.UH[c * 128:(c + 1) * 128, T0:T0 + n], writes=[b.r])
                S.op("act", lambda e, b=b, u=u, c=c: e.activation(out=u.t[:, :n], in_=b.t[:, 1:n + 1], func=AF.Identity,
                                                                 bias=cb.t[:, c:c + 1], scale=cw.t[:, 3 * c + 1:3 * c + 2]),
                     reads=[b.r, cw.r, cb.r], writes=[u.r])
                S.op("dve", lambda e, b=b, u=u, c=c: e.scalar_tensor_tensor(out=u.t[:, :n], in0=b.t[:, 0:n], scalar=cw.t[:, 3 * c:3 * c + 1],
                                                                        in1=u.t[:, :n], op0=ALU.mult, op1=ALU.add),
                     reads=[b.r, u.r, cw.r], writes=[u.r])
                S.op("dve", lambda e, b=b, u=u, c=c: e.scalar_tensor_tensor(out=u.t[:, :n], in0=b.t[:, 2:n + 2], scalar=cw.t[:, 3 * c + 2:3 * c + 3],
                                                                        in1=u.t[:, :n], op0=ALU.mult, op1=ALU.add),
                     reads=[b.r, u.r, cw.r], writes=[u.r])
                for g4 in range(n // 512 if n >= 512 else 1):
                    ntl = min(4, n // 128)
                    ps = self.ps[gi % 4]
                    S.mm([lambda e, a=a, ps=ps, u=u, g4=g4: e.transpose(
                        ps.t[:, a * 128:(a + 1) * 128], u.t[:, (g4 * 4 + a) * 128:(g4 * 4 + a + 1) * 128], self.ident.t[:])
                        for a in range(ntl)], reads=[u.r, self.ident.r], writes=[ps.r])
                    pvw = ps.t[:, :ntl * 128].rearrange("p (a w) -> p a w", w=128)
                    r0 = T0 + g4 * 512
                    if c < 8:
                        sx = st[gi % 2]
                        S.op("act" if gi % 2 == 0 else "dve",
                             (lambda e, sx=sx, pvw=pvw: e.copy(out=sx.t[:, :ntl, :], in_=pvw)) if gi % 2 == 0 else
                             (lambda e, sx=sx, pvw=pvw: e.tensor_copy(out=sx.t[:, :ntl, :], in_=pvw)),
                             reads=[ps.r], writes=[sx.r])
                        dst = self.UTM[r0:r0 + ntl * 128, c * 128:(c + 1) * 128].rearrange("(a p) w -> p a w", p=128)
                        S.dma("sp", dst, sx.t[:, :ntl, :], reads=[sx.r], writes=[S.dres("utm", c, r0)])
                    else:
                        sx = stb[gi % 2]
                        S.op("act" if gi % 2 == 0 else "dve",
                             (lambda e, sx=sx, pvw=pvw: e.copy(out=sx.t[:, :ntl, :], in_=pvw)) if gi % 2 == 0 else
                             (lambda e, sx=sx, pvw=pvw: e.tensor_copy(out=sx.t[:, :ntl, :], in_=pvw)),
                             reads=[ps.r], writes=[sx.r])
                        dst = self.VTM[r0:r0 + ntl * 128, (c - 8) * 128:(c - 7) * 128].rearrange("(a p) w -> p a w", p=128)
                        S.dma("sp", dst, sx.t[:, :ntl, :], reads=[sx.r], writes=[S.dres("vtm", c, r0)])
                    gi += 1

    def ph_diffattn(self):
        S, din = self.S, self.din
        self.cast_weights(1)
        lam_init = 0.8 - 0.6 * math.exp(-0.3 * 0)
        lp = self.sb("lp", [128, 256]); pr = self.sb("pr", [128, 128]); sm = self.sb("sm", [128, 2])
        nlam = self.sb("nlam", [128, 1]); wsub = self.sb("wsub", [128, 1])
        S.dma("sp", lp.t[:], din["lamp"], writes=[lp.r]); S.dma("sp", wsub.t[:], din["subw"], writes=[wsub.r])
        S.op("dve", lambda e: e.tensor_tensor(out=pr.t[:, 0:64], in0=lp.t[:, 0:64], in1=lp.t[:, 64:128], op=ALU.mult), reads=[lp.r], writes=[pr.r])
        S.op("dve", lambda e: e.tensor_tensor(out=pr.t[:, 64:128], in0=lp.t[:, 128:192], in1=lp.t[:, 192:256], op=ALU.mult), reads=[lp.r], writes=[pr.r])
        S.op("dve", lambda e: e.reduce_sum(out=sm.t[:, 0:2], in_=pr.t[:, :].rearrange("p (a b) -> p a b", b=64), axis=AX.X), reads=[pr.r], writes=[sm.r])
        S.op("act", lambda e: e.activation(out=sm.t[:], in_=sm.t[:], func=AF.Exp), reads=[sm.r], writes=[sm.r])
        S.op("dve", lambda e: e.tensor_tensor(out=nlam.t[:], in0=sm.t[:, 1:2], in1=sm.t[:, 0:1], op=ALU.subtract), reads=[sm.r], writes=[nlam.r])
        S.op("dve", lambda e: e.tensor_scalar(out=nlam.t[:], in0=nlam.t[:], scalar1=-lam_init, scalar2=None, op0=ALU.add), reads=[nlam.r], writes=[nlam.r])
        S.op("dve", lambda e: e.tensor_scalar(out=wsub.t[:], in0=wsub.t[:], scalar1=1.0 - lam_init, scalar2=None, op0=ALU.mult), reads=[wsub.r], writes=[wsub.r])
        ktb = [self.sb("ktb%d" % a, [128, TT], BF16) for a in range(2)]
        qtb = [self.sb("qtb%d" % a, [128, TT], BF16) for a in range(2)]
        vab = [self.sb("vab%d" % a, [128, TT // 128, 129], BF16) for a in range(2)]
        cab = [self.sb("cab%d" % a, [128, TT], BF16) for a in range(2)]
        pts = [self.sb("pt%d" % a, [128, 512], BF16) for a in range(3)]
        sacc = [self.sb("sacc%d" % a, [128, 512]) for a in range(2)]
        rsb = self.sb("rsb", [128, 512]); osb = self.sb("osb", [128, 512]); o1 = self.sb("o1", [128, 512])
        sqb = self.sb("sqb", [128, 512], BF16); rst = self.sb("rst", [128, 512])
        onesf = self.sb("onesf", [128, 128])
        S.op("dve", lambda e: e.memset(onesf.t[:], 1.0), writes=[onesf.r])
        cnt = 0
        mi = 0
        for h in range(4):
            kt_, qt_, va_, ca_ = ktb[h % 2], qtb[h % 2], vab[h % 2], cab[h % 2]
            S.dma("sp", kt_.t[:], self.KT[h * 128:(h + 1) * 128, :], writes=[kt_.r])
            S.dma("sp", qt_.t[:], self.QT[h * 128:(h + 1) * 128, :], writes=[qt_.r])
            S.dma("sp", va_.t[:], self.VA[:, h * 129:(h + 1) * 129].rearrange("(i p) w -> p i w", p=128), writes=[va_.r])
            qgroups = [(0, NCTX, list(range(2)))] + [(NCTX + 512 * g, 512, list(range(TT // 128))) for g in range(SEQ // 512)]
            for (q0, qn, kts) in qgroups:
                for m in range(2):
                    OT = self.ps[0 if mi % 2 == 0 else 3]
                    SM = self.ps[1 if mi % 2 == 0 else 6]
                    sa = sacc[mi % 2]
                    mi += 1

                    def score(i, m=m):
                        pS = self.ps[4 + i % 2]
                        kt = kts[i]
                        S.mm([lambda e, pS=pS, kt=kt, m=m: e.matmul(
                            pS.t[:, :qn], lhsT=kt_.t[m * 64:(m + 1) * 64, kt * 128:(kt + 1) * 128],
                            rhs=qt_.t[m * 64:(m + 1) * 64, q0:q0 + qn], start=True, stop=True)],
                            reads=[kt_.r, qt_.r], writes=[pS.r])
                    score(0)
                    for i, kt in enumerate(kts):
                        if i + 1 < len(kts):
                            score(i + 1)
                        pS = self.ps[4 + i % 2]
                        pt = pts[cnt % 3]
                        cnt += 1
                        S.op("act", lambda e, pS=pS, pt=pt: e.activation(out=pt.t[:, :qn], in_=pS.t[:, :qn], func=AF.Exp, scale=0.125),
                             reads=[pS.r], writes=[pt.r])
                        S.mm([lambda e, pt=pt, kt=kt, OT=OT: e.matmul(
                            OT.t[:, :qn], lhsT=va_.t[:, kt, 0:128], rhs=pt.t[:, :qn], start=(kt == kts[0]), stop=(kt == kts[-1]))],
                            reads=[pt.r, va_.r], writes=[OT.r])
                        if i == 0:
                            S.op("pool", lambda e, pt=pt, sa=sa: e.tensor_copy(out=sa.t[:, :qn], in_=pt.t[:, :qn]), reads=[pt.r], writes=[sa.r])
                        else:
                            S.op("pool", lambda e, pt=pt, sa=sa: e.tensor_tensor(out=sa.t[:, :qn], in0=sa.t[:, :qn], in1=pt.t[:, :qn], op=ALU.add),
                                 reads=[pt.r, sa.r], writes=[sa.r])
                    S.mm([lambda e, SM=SM, sa=sa: e.matmul(SM.t[:, :qn], lhsT=onesf.t[:], rhs=sa.t[:, :qn], start=True, stop=True)],
                         reads=[onesf.r, sa.r], writes=[SM.r])
                    S.op("dve", lambda e, SM=SM: e.reciprocal(out=rsb.t[:, :qn], in_=SM.t[:, :qn]), reads=[SM.r], writes=[rsb.r])
                    if m == 0:
                        S.op("dve", lambda e, OT=OT: e.tensor_tensor(out=osb.t[:, :qn], in0=OT.t[:, :qn], in1=rsb.t[:, :qn], op=ALU.mult),
                             reads=[OT.r, rsb.r], writes=[osb.r])
                        continue
                    S.op("dve", lambda e, OT=OT: e.tensor_tensor(out=o1.t[:, :qn], in0=OT.t[:, :qn], in1=rsb.t[:, :qn], op=ALU.mult),
                         reads=[OT.r, rsb.r], writes=[o1.r])
                    S.op("dve", lambda e: e.scalar_tensor_tensor(out=osb.t[:, :qn], in0=o1.t[:, :qn], scalar=nlam.t[:, 0:1], in1=osb.t[:, :qn],
                                                                 op0=ALU.mult, op1=ALU.add), reads=[o1.r, nlam.r, osb.r], writes=[osb.r])
                    S.op("act", lambda e: e.activation(out=sqb.t[:, :qn], in_=osb.t[:, :qn], func=AF.Square), reads=[osb.r], writes=[sqb.r])
                    SS = self.ps[2]
                    S.mm([lambda e, SS=SS: e.matmul(SS.t[:, :qn], lhsT=self.onesb.t[:], rhs=sqb.t[:, :qn], start=True, stop=True)],
                         reads=[self.onesb.r, sqb.r], writes=[SS.r])
                    S.op("act", lambda e, SS=SS: e.activation(out=rst.t[:, :qn], in_=SS.t[:, :qn], func=AF.Sqrt, bias=EPS, scale=1.0 / 128),
                         reads=[SS.r], writes=[rst.r])
                    S.op("dve", lambda e: e.reciprocal(out=rst.t[:, :qn], in_=rst.t[:, :qn]), reads=[rst.r], writes=[rst.r])
                    S.op("dve", lambda e: e.scalar_tensor_tensor(out=ca_.t[:, q0:q0 + qn], in0=osb.t[:, :qn], scalar=wsub.t[:, 0:1], in1=rst.t[:, :qn],
                                                                 op0=ALU.mult, op1=ALU.mult), reads=[osb.r, wsub.r, rst.r], writes=[ca_.r])
            S.dma("sp", self.catT[h * 128:(h + 1) * 128, :], ca_.t[:], reads=[ca_.r], writes=[S.dres("catA", h)])

    def ph_filters(self, n):
        S, din = self.S, self.din
        cm = (n == NCTX)
        zt = self.sb("zt", [33, n]); w0 = self.sb("w0", [33, 64]); b0 = self.sb("b0", [64, 1])
        w1 = self.sb("w1", [64, 2, 64]); b1 = self.sb("b1", [64, 2]); fr = self.sb("fr", [64, 1]); wo = self.sb("wo", [64, 2048])
        skp = self.sb("skp", [1, 1024]); dl = self.sb("dl", [128, 512]); tl = self.sb("tl", [128, n // 128])
        S.dma("sp", zt.t[:], din["zTc" if cm else "zT"], writes=[zt.r])
        S.dma("sp", w0.t[:], din["fw0"], writes=[w0.r]); S.dma("sp", b0.t[:], din["fb0"], writes=[b0.r])
        S.dma("sp", w1.t[:], din["fw1"].rearrange("i k m -> k i m"), writes=[w1.r]); S.dma("sp", b1.t[:], din["fb1"], writes=[b1.r])
        S.dma("sp", fr.t[:], din["ffreq"], writes=[fr.r]); S.dma("sp", wo.t[:], din["fwout"], writes=[wo.r])
        S.op("dve", lambda e: e.tensor_scalar(out=fr.t[:], in0=fr.t[:], scalar1=1.0 / (2.0 * math.pi), scalar2=None, op0=ALU.mult), reads=[fr.r], writes=[fr.r])
        S.dma("sp", skp.t[:], din["hskip"], writes=[skp.r]); S.dma("sp", dl.t[:], din["deltas"], writes=[dl.r])
        S.dma("sp", tl.t[:], din["tlc" if cm else "tl"], writes=[tl.r])
        hid = [self.sb("hid%d" % a, [64, n]) for a in range(2)]
        tmp = [self.sb("ftmp%d" % a, [64, 512]) for a in range(2)]
        gsz = min(512, n)
        TWO_PI = 2.0 * math.pi
        OFFS = math.pi + 16.0 * math.pi
        ci = 0
        for layer in range(3):
            src = zt if layer == 0 else hid[(layer - 1) % 2]
            dst = hid[layer % 2]
            bias = b0.t[:, 0:1] if layer == 0 else b1.t[:, layer - 1:layer]
            for g in range(n // gsz):
                ps = self.ps[ci % 2]; tm = tmp[ci % 2]
                ci += 1
                if layer == 0:
                    S.mm([lambda e, ps=ps, g=g: e.matmul(ps.t[:64, :gsz], lhsT=w0.t[:, :], rhs=zt.t[:, g * gsz:(g + 1) * gsz], start=True, stop=True)],
                         reads=[w0.r, zt.r], writes=[ps.r])
                else:
                    S.mm([lambda e, ps=ps, g=g, src=src, layer=layer: e.matmul(ps.t[:64, :gsz], lhsT=w1.t[:, layer - 1, :], rhs=src.t[:, g * gsz:(g + 1) * gsz],
                                                                             start=True, stop=True)], reads=[w1.r, src.r], writes=[ps.r])
                S.op("dve", lambda e, ps=ps, tm=tm, bias=bias: e.tensor_scalar(out=tm.t[:, :gsz], in0=ps.t[:64, :gsz], scalar1=bias, scalar2=fr.t[:, 0:1],
                                                                           op0=ALU.add, op1=ALU.mult), reads=[ps.r, b0.r, b1.r, fr.r], writes=[tm.r])
                for rnd in range(2):
                    S.op("dve", lambda e, tm=tm: e.scalar_tensor_tensor(out=tm.t[:, :gsz], in0=tm.t[:, :gsz], scalar=-0.5, in1=tm.t[:, :gsz],
                                                                    op0=ALU.is_lt, op1=ALU.add), reads=[tm.r], writes=[tm.r])
                    S.op("dve", lambda e, tm=tm: e.scalar_tensor_tensor(out=tm.t[:, :gsz], in0=tm.t[:, :gsz], scalar=0.5, in1=tm.t[:, :gsz],
                                                                    op0=ALU.is_gt, op1=ALU.subtract), reads=[tm.r], writes=[tm.r])
                S.op("act", lambda e, tm=tm, dst=dst, g=g: e.activation(out=dst.t[:, g * gsz:(g + 1) * gsz], in_=tm.t[:, :gsz], func=AF.Sin, scale=TWO_PI),
                     reads=[tm.r], writes=[dst.r])
        hfin = hid[0]
        wnd = [self.sb("wnd%d" % a, [128, 512]) for a in range(2)]
        ff = [self.sb("ff%d" % a, [128, 512]) for a in range(4)]
        ho = [self.sb("ho%d" % a, [128, 512], BF16) for a in range(4)]
        hi_ = 0
        for tt in range(n // 128):
            wn = wnd[tt % 2]
            S.op("act", lambda e, wn=wn, tt=tt: e.activation(out=wn.t[:], in_=dl.t[:], func=AF.Exp, scale=tl.t[:, tt:tt + 1]),
                 reads=[dl.r, tl.r], writes=[wn.r])
            for o in range(2):
                for d in range(2):
                    cb = o * 2 + d
                    ps = self.ps[2 + cb]
                    S.mm([lambda e, ps=ps, cb=cb, tt=tt: e.matmul(ps.t[:, :512], lhsT=hfin.t[:, tt * 128:(tt + 1) * 128], rhs=wo.t[:, cb * 512:(cb + 1) * 512],
                                                                 start=True, stop=True)], reads=[hfin.r, wo.r], writes=[ps.r])
                    f = ff[cb]
                    S.op("dve", lambda e, ps=ps, f=f, wn=wn: e.tensor_tensor(out=f.t[:], in0=ps.t[:, :512], in1=wn.t[:], op=ALU.mult),
                         reads=[ps.r, wn.r], writes=[f.r])
                    if tt == 0:
                        if d == 0:
                            S.op("dve", lambda e, f=f, o=o: e.tensor_tensor(out=f.t[0:1, :], in0=f.t[0:1, :], in1=skp.t[0:1, o * 512:(o + 1) * 512], op=ALU.add),
                                 reads=[f.r, skp.r], writes=[f.r])
                        else:
                            S.op("dve", lambda e, f=f: e.memset(f.t[0:1, :], 0.0), reads=[f.r], writes=[f.r])
                f0, f1 = ff[o * 2], ff[o * 2 + 1]
                for sd in range(2):
                    h_ = ho[hi_ % 4]
                    hi_ += 1
                    S.op("pool", lambda e, h_=h_, f0=f0, f1=f1, sd=sd: e.tensor_tensor(out=h_.t[:], in0=f0.t[:], in1=f1.t[:],
                                                                                    op=(ALU.add if sd == 0 else ALU.subtract)),
                         reads=[f0.r, f1.r], writes=[h_.r])
                    S.dma("sp", self.HSD[o, sd, tt * 128:(tt + 1) * 128, :], h_.t[:], reads=[h_.r], writes=[S.dres("hsd", n, o, sd, tt)])

    def ph_hyena(self, n):
        S, din = self.S, self.din
        cm = (n == NCTX)
        T0 = 0 if cm else NCTX
        nt = n // 128
        nf = nt + 1
        dC, dS = (din["dftCc"], din["dftSc"]) if cm else (din["dftC"], din["dftS"])
        wft = self.sb("wft", [128, nf])
        S.dma("sp", wft.t[:], din["wfc" if cm else "wf"], writes=[wft.r])
        vt = self.sb("vt", [128, nt, 512], BF16)
        Y = [self.sb("Y%d" % a, [128, nf, 512], BF16) for a in range(2)]
        blk = [self.sb("blk%d" % a, [128, nf, 128], BF16) for a in range(4)]
        hst = [self.sb("hst%d" % a, [128, 512]) for a in range(2)]
        bi = 0

        def load_blk(src, b, rows):
            nonlocal bi
            t = blk[bi % 4]
            bi += 1
            S.dma("sp", t.t[:, :rows, :], src[b][:, 0:rows * 128].rearrange("p (i w) -> p i w", w=128), writes=[t.r])
            return t
        pi_ = 0
        for o in range(2):
            for cs in range(2):
                S.dma("sp", vt.t[:], self.HSD[o, cs, 0:n, :].rearrange("(i p) c -> p i c", p=128), writes=[vt.r])
                for fb in range(nf):
                    t = load_blk(dC if cs == 0 else dS, fb, nt)
                    ps = self.ps[pi_ % 2]; hs_ = hst[pi_ % 2]
                    pi_ += 1
                    S.mm([lambda e, i=i, t=t, ps=ps: e.matmul(ps.t[:, :512], lhsT=t.t[:, i, :], rhs=vt.t[:, i, :], start=(i == 0), stop=(i == nt - 1))
                          for i in range(nt)], reads=[t.r, vt.r], writes=[ps.r])
                    S.op("act", lambda e, ps=ps, hs_=hs_, fb=fb: e.activation(out=hs_.t[:], in_=ps.t[:, :512], func=AF.Copy, scale=wft.t[:, fb:fb + 1]),
                         reads=[ps.r, wft.r], writes=[hs_.r])
                    S.dma("sp", self.HCS[o, cs, fb * 128:(fb + 1) * 128, :], hs_.t[:], reads=[hs_.r], writes=[S.dres("hcs", o, cs, fb)])
        S.dma("sp", vt.t[:], self.VTM[T0:T0 + n, :].rearrange("(i p) c -> p i c", p=128), reads=[S.dres("hcs", 1, 1, nf - 1)], writes=[vt.r])
        Hc = [self.sb("Hc%d" % a, [128, 512]) for a in range(2)]
        Hs = [self.sb("Hs%d" % a, [128, 512]) for a in range(2)]
        tq = [self.sb("tq%d" % a, [128, 512]) for a in range(4)]
        xs = [self.sb("xs%d" % a, [128, 512]) for a in range(2)]
        bt = self.sb("bt", [128, 512])
        bst = [self.sb("bst%d" % a, [128, 4, 128], BF16) for a in range(2)]
        for o in range(2):
            for fb in range(nf):
                tC = load_blk(dC, fb, nt)
                tS = load_blk(dS, fb, nt)
                hc, hs = Hc[fb % 2], Hs[fb % 2]
                S.dma("sp", hc.t[:], self.HCS[o, 0, fb * 128:(fb + 1) * 128, :], reads=[S.dres("hcs", o, 0, fb)], writes=[hc.r])
                S.dma("sp", hs.t[:], self.HCS[o, 1, fb * 128:(fb + 1) * 128, :], reads=[S.dres("hcs", o, 1, fb)], writes=[hs.r])
                pC, pS = self.ps[2 * (fb % 2)], self.ps[2 * (fb % 2) + 1]
                S.mm([lambda e, i=i, tC=tC, pC=pC: e.matmul(pC.t[:, :512], lhsT=tC.t[:, i, :], rhs=vt.t[:, i, :], start=(i == 0), stop=(i == nt - 1))
                      for i in range(nt)], reads=[tC.r, vt.r], writes=[pC.r])
                S.mm([lambda e, i=i, tS=tS, pS=pS: e.matmul(pS.t[:, :512], lhsT=tS.t[:, i, :], rhs=vt.t[:, i, :], start=(i == 0), stop=(i == nt - 1))
                      for i in range(nt)], reads=[tS.r, vt.r], writes=[pS.r])
                a1, a2, a3, a4 = tq
                S.op("dve", lambda e, pC=pC, hc=hc: e.tensor_tensor(out=a1.t[:], in0=pC.t[:, :512], in1=hc.t[:], op=ALU.mult), reads=[pC.r, hc.r], writes=[a1.r])
                S.op("dve", lambda e, pS=pS, hs=hs: e.tensor_tensor(out=a2.t[:], in0=pS.t[:, :512], in1=hs.t[:], op=ALU.mult), reads=[pS.r, hs.r], writes=[a2.r])
                S.op("pool", lambda e, fb=fb: e.tensor_tensor(out=Y[0].t[:, fb, :], in0=a1.t[:], in1=a2.t[:], op=ALU.subtract), reads=[a1.r, a2.r], writes=[Y[0].r])
                S.op("dve", lambda e, pC=pC, hs=hs: e.tensor_tensor(out=a3.t[:], in0=pC.t[:, :512], in1=hs.t[:], op=ALU.mult), reads=[pC.r, hs.r], writes=[a3.r])
                S.op("dve", lambda e, pS=pS, hc=hc: e.tensor_tensor(out=a4.t[:], in0=pS.t[:, :512], in1=hc.t[:], op=ALU.mult), reads=[pS.r, hc.r], writes=[a4.r])
                S.op("pool", lambda e, fb=fb: e.tensor_tensor(out=Y[1].t[:, fb, :], in0=a3.t[:], in1=a4.t[:], op=ALU.add), reads=[a3.r, a4.r], writes=[Y[1].r])
            for tb in range(nt):
                tC = load_blk(dC, tb, nf)
                tS = load_blk(dS, tb, nf)
                py = self.ps[4 + tb % 2]
                x_ = xs[tb % 2]
                S.dma("sp", x_.t[:], self.UTM[T0 + tb * 128:T0 + (tb + 1) * 128, o * 512:(o + 1) * 512], writes=[x_.r])
                fns = []
                for j in range(nf):
                    fns.append(lambda e, j=j, tC=tC, py=py: e.matmul(py.t[:, :512], lhsT=tC.t[:, j, :], rhs=Y[0].t[:, j, :], start=(j == 0), stop=False))
                    fns.append(lambda e, j=j, tS=tS, py=py: e.matmul(py.t[:, :512], lhsT=tS.t[:, j, :], rhs=Y[1].t[:, j, :], start=False, stop=(j == nf - 1)))
                S.mm(fns, reads=[tC.r, tS.r, Y[0].r, Y[1].r], writes=[py.r])
                if o == 0:
                    S.op("dve", lambda e, py=py, x_=x_, tb=tb: e.tensor_tensor(out=vt.t[:, tb, :], in0=py.t[:, :512], in1=x_.t[:], op=ALU.mult),
                         reads=[py.r, x_.r], writes=[vt.r])
                else:
                    S.op("dve", lambda e, py=py, x_=x_: e.tensor_tensor(out=bt.t[:], in0=py.t[:, :512], in1=x_.t[:], op=ALU.mult),
                         reads=[py.r, x_.r], writes=[bt.r])
                    pT = self.ps[6 + tb % 2]
                    S.mm([lambda e, c=c, pT=pT: e.transpose(pT.t[:, c * 128:(c + 1) * 128], bt.t[:, c * 128:(c + 1) * 128], self.ident.t[:])
                          for c in range(4)], reads=[bt.r, self.ident.r], writes=[pT.r])
                    b_ = bst[tb % 2]
                    S.op("act", lambda e, pT=pT, b_=b_: e.copy(out=b_.t[:], in_=pT.t[:, :].rearrange("p (c w) -> p c w", w=128)), reads=[pT.r], writes=[b_.r])
                    dst = self.catT[512:1024, T0 + tb * 128:T0 + (tb + 1) * 128].rearrange("(c p) t -> p c t", p=128)
                    S.dma("sp", dst, b_.t[:], reads=[b_.r], writes=[S.dres("catB", n, tb)])

    def ph_outproj(self, l):
        S = self.S
        wo = self.sb("wo", [128, 8, D], BF16)
        S.dma("sp", wo.t[:], (self.aboutb if l == 0 else self.naoutb).rearrange("(k p) n -> p k n", p=128), writes=[wo.r])
        xts = [self.sb("oxt%d" % a, [128, 8, GS]) for a in range(2)]
        cts = [self.sb("oct%d" % a, [128, 8, GS], BF16) for a in range(2)]
        lv = self.latT.rearrange("(c p) t -> p c t", p=128)
        cv = self.catT.rearrange("(c p) t -> p c t", p=128)
        gi = 0
        for (t0, n, s) in self.groups(l == 0):
            xt, ct = xts[gi % 2], cts[gi % 2]
            gi += 1
            S.dma("sp", xt.t[:, :, :n], lv[:, :, t0:t0 + n], writes=[xt.r])
            S.dma("sp", ct.t[:, :, :n], cv[:, :, t0:t0 + n], writes=[ct.r])
            gate = self.mcol(l, 1, 2, s)
            for dc in range(8):
                py = self.ps[dc % 4]
                S.mm([lambda e, k=k, py=py, dc=dc: e.matmul(py.t[:, :n], lhsT=wo.t[:, k, dc * 128:(dc + 1) * 128], rhs=ct.t[:, k, :n],
                                                          start=(k == 0), stop=(k == 7)) for k in range(8)], reads=[wo.r, ct.r], writes=[py.r])
                S.op("dve", lambda e, py=py, dc=dc: e.scalar_tensor_tensor(out=xt.t[:, dc, :n], in0=py.t[:, :n], scalar=gate[:, dc:dc + 1], in1=xt.t[:, dc, :n],
                                                                       op0=ALU.mult, op1=ALU.add), reads=[py.r, xt.r, self.msc.r], writes=[xt.r])
            self.store_group(xt, t0, n)

    def ph_naproj(self):
        S = self.S
        self.alloc_pro()
        win = self.sb("win", [128, 8, 3072], BF16)
        wv = self.nainb.rearrange("(k p) n -> p k n", p=128)
        for a in range(3):
            S.dma("sp", win.t[:, :, a * 1024:(a + 1) * 1024], wv[:, :, a * 1024:(a + 1) * 1024], writes=[win.r])
        qo = [self.sb("qo%d" % a, [128, GS], BF16) for a in range(2)]
        vo = [self.sb("vo%d" % a, [128, 16, 65], BF16) for a in range(2)]
        for v in vo:
            S.op("dve", lambda e, v=v: e.memset(v.t[:], 1.0), writes=[v.r])
        ci = 0
        for (t0, n, s) in self.groups(True):
            self.prologue(1, 1, t0, n, s, 6)
            hT = self.hT
            for which in range(2):
                if which == 0 and s == 1:
                    continue
                for hc in range(8):
                    col = which * 1024 + hc * 128
                    pa = self.ps[ci % 4]; q = qo[ci % 2]
                    ci += 1
                    S.mm([lambda e, k=k, pa=pa, col=col: e.matmul(pa.t[:, :n], lhsT=win.t[:, k, col:col + 128], rhs=hT.t[:, k, :n],
                                                                start=(k == 0), stop=(k == 7)) for k in range(8)], reads=[win.r, hT.r], writes=[pa.r])
                    if ci % 2 == 0:
                        S.op("act", lambda e, q=q, pa=pa: e.copy(out=q.t[:, :n], in_=pa.t[:, :n]), reads=[pa.r], writes=[q.r])
                    else:
                        S.op("dve", lambda e, q=q, pa=pa: e.tensor_copy(out=q.t[:, :n], in_=pa.t[:, :n]), reads=[pa.r], writes=[q.r])
                    dst = (self.QT if which == 0 else self.KT)[hc * 128:(hc + 1) * 128, t0:t0 + n]
                    S.dma("sp", dst, q.t[:, :n], reads=[q.r], writes=[S.dres("qk2", which, hc, t0)])
            for tt in range(n // 128):
                v = vo[tt % 2]
                for hf in range(2):
                    pv = self.ps[4 + hf]
                    S.mm([lambda e, k=k, pv=pv, tt=tt, hf=hf: e.matmul(pv.t[:, :512], lhsT=hT.t[:, k, tt * 128:(tt + 1) * 128],
                                                                      rhs=win.t[:, k, 2048 + hf * 512:2048 + (hf + 1) * 512], start=(k == 0), stop=(k == 7))
                          for k in range(8)], reads=[win.r, hT.r], writes=[pv.r])
                    S.op("act" if hf == 0 else "dve",
                         (lambda e, pv=pv, v=v, hf=hf: e.copy(out=v.t[:, hf * 8:(hf + 1) * 8, 0:64], in_=pv.t[:, :].rearrange("p (h d) -> p h d", d=64))) if hf == 0 else
                         (lambda e, pv=pv, v=v, hf=hf: e.tensor_copy(out=v.t[:, hf * 8:(hf + 1) * 8, 0:64], in_=pv.t[:, :].rearrange("p (h d) -> p h d", d=64))),
                         reads=[pv.r], writes=[v.r])
                S.dma("sp", self.VA[t0 + tt * 128:t0 + (tt + 1) * 128, :].rearrange("p (h d) -> p h d", d=65), v.t[:],
                      reads=[v.r], writes=[S.dres("va2", t0, tt)])

    def ph_na(self):
        S, din = self.S, self.din
        blocks = self.na_blocks
        ntp = self.ntypes
        ktb = [self.sb("ktb%d" % a, [128, TT], BF16) for a in range(2)]
        qtb = [self.sb("qtb%d" % a, [128, TT], BF16) for a in range(2)]
        vab = [self.sb("vab%d" % a, [128, TT // 128, 130], BF16) for a in range(2)]
        bib = [self.sb("bib%d" % a, [128, ntp * 2 * 7 * 128]) for a in range(2)]
        obb = [self.sb("obb%d" % a, [128, SEQ], BF16) for a in range(2)]
        tmA = [self.sb("tmA%d" % a, [128, 512]) for a in range(2)]
        tmB = [self.sb("tmB%d" % a, [128, 384]) for a in range(2)]
        PA = [self.sb("PA%d" % a, [128, 512], BF16) for a in range(2)]
        PB = [self.sb("PB%d" % a, [128, 384], BF16) for a in range(2)]
        o2 = [self.sb("o2_%d" % a, [128, 128]) for a in range(2)]
        rr = self.sb("rr", [128, 2])
        its = [(qb, hh) for qb in range(32) for hh in range(2)]

        def geom(qb):
            lo, hi, ty = blocks[qb]
            nk = (hi - lo) * 64
            tile0 = (NCTX + lo * 64) // 128
            nfull = nk // 128
            return ty, tile0, nfull, (nk % 128 != 0)
        for hc in range(8):
            kt_, qt_, va_, bi_, ob_ = ktb[hc % 2], qtb[hc % 2], vab[hc % 2], bib[hc % 2], obb[hc % 2]
            S.dma("sp", kt_.t[:], self.KT[hc * 128:(hc + 1) * 128, :], writes=[kt_.r])
            S.dma("sp", qt_.t[:, NCTX:], self.QT[hc * 128:(hc + 1) * 128, NCTX:], writes=[qt_.r])
            S.dma("sp", va_.t[:], self.VA[:, hc * 130:(hc + 1) * 130].rearrange("(i p) w -> p i w", p=128), writes=[va_.r])
            S.dma("sp", bi_.t[:], din["nab"][hc], writes=[bi_.r])

            def tiles(qb):
                ty, tile0, nfull, half = geom(qb)
                tl = [(0, 128), (1, 128)] + [(tile0 + a, 128) for a in range(nfull)]
                if half:
                    tl.append((tile0 + nfull, 64))
                return tl

            def scores(it):
                qb, hh = its[it]
                A, B = self.ps[2 + (it % 2) * 2], self.ps[3 + (it % 2) * 2]
                q0 = NCTX + qb * 128
                fns = []
                for idx, (tile, sz) in enumerate(tiles(qb)):
                    dst = A.t[:sz, idx * 128:(idx + 1) * 128] if idx < 4 else B.t[:sz, (idx - 4) * 128:(idx - 3) * 128]
                    fns.append(lambda e, dst=dst, tile=tile, sz=sz, hh=hh, q0=q0: e.matmul(
                        dst, lhsT=kt_.t[hh * 64:(hh + 1) * 64, tile * 128:tile * 128 + sz],
                        rhs=qt_.t[hh * 64:(hh + 1) * 64, q0:q0 + 128], start=True, stop=True))
                S.mm(fns, reads=[kt_.r, qt_.r], writes=[A.r, B.r])
            scores(0)
            for it, (qb, hh) in enumerate(its):
                if it + 1 < len(its):
                    scores(it + 1)
                ty = geom(qb)[0]
                tl = tiles(qb)
                nb = (len(tl) - 4) * 128
                A, B = self.ps[2 + (it % 2) * 2], self.ps[3 + (it % 2) * 2]
                ta, tb_, pa, pb = tmA[it % 2], tmB[it % 2], PA[it % 2], PB[it % 2]
                bo = (ty * 2 + hh) * 896
                S.op("dve", lambda e, A=A, ta=ta, bo=bo: e.scalar_tensor_tensor(
                    out=ta.t[:], in0=A.t[:, 0:512], scalar=0.125, in1=bi_.t[:, bo:bo + 512], op0=ALU.mult, op1=ALU.add),
                    reads=[A.r, bi_.r], writes=[ta.r])
                S.op("act", lambda e, ta=ta, pa=pa: e.activation(out=pa.t[:], in_=ta.t[:], func=AF.Exp), reads=[ta.r], writes=[pa.r])
                S.op("dve", lambda e, B=B, tb_=tb_, bo=bo, nb=nb: e.scalar_tensor_tensor(
                    out=tb_.t[:, :nb], in0=B.t[:, 0:nb], scalar=0.125, in1=bi_.t[:, bo + 512:bo + 512 + nb], op0=ALU.mult, op1=ALU.add),
                    reads=[B.r, bi_.r], writes=[tb_.r])
                S.op("act", lambda e, tb_=tb_, pb=pb, nb=nb: e.activation(out=pb.t[:, :nb], in_=tb_.t[:, :nb], func=AF.Exp), reads=[tb_.r], writes=[pb.r])
                po = self.ps[hh]
                fns = []
                for idx, (tile, sz) in enumerate(tl):
                    src = pa.t[:sz, idx * 128:(idx + 1) * 128] if idx < 4 else pb.t[:sz, (idx - 4) * 128:(idx - 3) * 128]
                    fns.append(lambda e, po=po, src=src, tile=tile, sz=sz, hh=hh, idx=idx, n_=len(tl): e.matmul(
                        po.t[:, 0:65], lhsT=src, rhs=va_.t[:sz, tile, hh * 65:(hh + 1) * 65], start=(idx == 0), stop=(idx == n_ - 1)))
                S.mm(fns, reads=[pa.r, pb.r, va_.r], writes=[po.r])
                oo = o2[qb % 2]
                S.op("dve", lambda e, po=po, hh=hh: e.reciprocal(out=rr.t[:, hh:hh + 1], in_=po.t[:, 64:65]), reads=[po.r], writes=[rr.r])
                S.op("dve", lambda e, po=po, hh=hh, oo=oo: e.tensor_scalar(out=oo.t[:, hh * 64:(hh + 1) * 64], in0=po.t[:, 0:64], scalar1=rr.t[:, hh:hh + 1],
                                                                       scalar2=None, op0=ALU.mult), reads=[po.r, rr.r], writes=[oo.r])
                if hh == 1:
                    pT = self.ps[6 + qb % 2]
                    S.mm([lambda e, pT=pT, oo=oo: e.transpose(pT.t[:, 0:128], oo.t[:], self.ident.t[:])], reads=[oo.r, self.ident.r], writes=[pT.r])
                    S.op("act", lambda e, pT=pT, qb=qb: e.copy(out=ob_.t[:, qb * 128:(qb + 1) * 128], in_=pT.t[:, 0:128]), reads=[pT.r], writes=[ob_.r])
            S.dma("sp", self.catT[hc * 128:(hc + 1) * 128, NCTX:], ob_.t[:], reads=[ob_.r], writes=[S.dres("catN", hc)])

    def ph_final(self):
        S = self.S
        self.alloc_pro()
        fw = self.sb("fw", [128, 8])
        S.dma("sp", fw.t[:], self.din["fnormT"], writes=[fw.r])
        yT = [self.sb("yT%d" % a, [128, 8, GS]) for a in range(2)]
        ot = [self.sb("ot%d" % a, [128, D]) for a in range(2)]
        gi = 0
        oi = 0
        for (t0, n, s) in self.groups(False):
            xt = self.prologue(1, 2, t0, n, s, 6, want_h=False)
            y = yT[gi % 2]
            gi += 1
            for c in range(8):
                S.op("dve", lambda e, c=c, y=y, xt=xt: e.scalar_tensor_tensor(
                    out=y.t[:, c, :n], in0=xt.t[:, c, :n], scalar=fw.t[:, c:c + 1], in1=self.rstd.t[:, :n], op0=ALU.mult, op1=ALU.mult),
                    reads=[xt.r, fw.r, self.rstd.r], writes=[y.r])
            for tt in range(n // 128):
                o = ot[oi % 2]
                oi += 1
                for hf in range(2):
                    ps = self.ps[hf * 2 + (tt % 2)]
                    S.mm([lambda e, c=c, ps=ps, hf=hf, tt=tt, y=y: e.transpose(ps.t[:, c * 128:(c + 1) * 128], y.t[:, hf * 4 + c, tt * 128:(tt + 1) * 128],
                                                                            self.ident.t[:]) for c in range(4)], reads=[y.r, self.ident.r], writes=[ps.r])
                    if hf == 0:
                        S.op("act", lambda e, ps=ps, o=o: e.copy(out=o.t[:, 0:512], in_=ps.t[:, :]), reads=[ps.r], writes=[o.r])
                    else:
                        S.op("dve", lambda e, ps=ps, o=o: e.tensor_copy(out=o.t[:, 512:1024], in_=ps.t[:, :]), reads=[ps.r], writes=[o.r])
                r0 = t0 - NCTX + tt * 128
                S.dma("sp", self.out[r0:r0 + 128, :], o.t[:], reads=[o.r], writes=[S.dres("out", r0)])


def _host_inputs(inputs, b, consts, nab):
    f32 = np.float32
    g = lambda k: np.asarray(inputs[k], dtype=f32)
    m = {}
    m["x"] = np.ascontiguousarray(g("x")[b])
    m["ctxi"] = np.ascontiguousarray(g("ctx")[b])
    sv = np.stack([_fm(g("c")[b], 8), _fm(g("c_ctx"), 8)], axis=-1)
    m["sv"] = np.ascontiguousarray(sv.reshape(128, 16))
    m["mod_w"] = g("mod_w")
    mb = np.stack([_fm(g("mod_b")[l], 72) for l in range(2)], axis=1)
    m["mod_b2"] = np.ascontiguousarray(np.repeat(mb[:, :, None, :], 2, axis=2).reshape(128, 288))
    nw = g("norm_w")
    nt = np.stack([np.stack([_fm(nw[l, k], 8) for k in range(3)], axis=1) for l in range(2)], axis=1)
    m["normT2"] = np.ascontiguousarray(np.repeat(nt[:, :, :, None, :], 2, axis=3).reshape(128, 96))
    m["fnormT"] = _fm(g("final_norm_w"), 8)
    m["ffn_w1"] = g("ffn_w1"); m["ffn_w3"] = g("ffn_w3"); m["ffn_w2"] = g("ffn_w2")
    wi = g("ab_w_in")[0]
    m["ab_w_in"] = wi
    perm = consts["perm"]
    cols = np.concatenate([hc * 128 + perm for hc in range(4)] + [512 + hc * 128 + perm for hc in range(4)])
    m["ab_w_inp"] = np.ascontiguousarray(wi[:, cols])
    m["ab_w_out"] = g("ab_w_out")[0]
    m["lamp"] = np.ascontiguousarray(np.broadcast_to(g("diff_lambda")[0].reshape(1, 256), (128, 256)))
    m["subw"] = np.ascontiguousarray(g("diff_subln_w")[0].reshape(128, 1))
    cw = g("hy_conv_w")[0]
    m["hcw"] = np.ascontiguousarray(np.stack([_fm(cw[j], 12) for j in range(3)], axis=-1).reshape(128, 36))
    m["hcb"] = _fm(g("hy_conv_b")[0], 12)
    m["fw0"] = g("hy_f_w0")[0]; m["fb0"] = np.ascontiguousarray(g("hy_f_b0")[0].reshape(64, 1))
    m["fw1"] = g("hy_f_w1")[0]; m["fb1"] = np.ascontiguousarray(g("hy_f_b1")[0].T)
    m["ffreq"] = np.ascontiguousarray(g("hy_f_freq")[0].reshape(64, 1))
    m["fwout"] = g("hy_f_wout")[0]
    m["hskip"] = np.ascontiguousarray(g("hy_bias")[0].reshape(1, 1024))
    m["na_w_in"] = g("na_w_in")[0]; m["na_w_out"] = g("na_w_out")[0]
    m["nab"] = nab
    for k in ("ident", "ropeC", "ropeS", "dftC", "dftS", "wf", "dftCc", "dftSc", "wfc", "zT", "zTc", "tl", "tlc", "deltas"):
        m[k] = consts[k]
    return m


_PROG = {}


def run(inputs, cores, dbg=None):
    consts = _consts()
    nab, blocks, nt = _na_bias(np.asarray(inputs["na_rpb"], np.float32)[0])
    key = (dbg,)
    if key not in _PROG:
        p = Prog(dbg=dbg, nab_cols=nab.shape[2], na_blocks=blocks)
        p.ntypes = nt
        _PROG[key] = p.build()
    nc = _PROG[key]
    in_maps = [_host_inputs(inputs, b, consts, nab) for b in cores]
    res = run_bass_kernel_spmd(nc, in_maps, core_ids=list(range(len(cores))))
    return res


def kernel(**inputs):
    res = run(inputs, list(range(8)))
    return np.stack([np.asarray(r["out"], dtype=np.float32) for r in res.results], axis=0)
```

```python
import math
from contextlib import ExitStack
import numpy as np
import ml_dtypes
import concourse.bass as bass
import concourse.mybir as mybir
from concourse.bass_utils import run_bass_kernel_spmd

F32 = mybir.dt.float32
BF16 = mybir.dt.bfloat16
AF = mybir.ActivationFunctionType
ALU = mybir.AluOpType
AX = mybir.AxisListType

D = 1024; SEQ = 4096; NCTX = 256; TT = SEQ + NCTX; DFF = 2816; NJ = DFF // 128
GRID_W = 64; EPS = 1e-6
GS = 512
SAME_ENGINE_SYNC = True


class Res:
    __slots__ = ("w", "r")

    def __init__(self):
        self.w = None
        self.r = {}


class Sched:
    NSLOT = 10

    def __init__(self, nc, es):
        self.nc = nc
        self.eng = {"pe": nc.tensor, "act": nc.scalar, "dve": nc.vector, "pool": nc.gpsimd, "sp": nc.sync}
        self.sem = {e: es.enter_context(nc.semaphore("s_" + e)) for e in ("pe", "act", "dve", "pool")}
        self.cnt = {e: 0 for e in self.sem}
        self.seen = {e: {} for e in self.eng}
        self.dsem = {q: [es.enter_context(nc.semaphore("d_%s_%d" % (q, i))) for i in range(self.NSLOT)]
                     for q in ("sp", "pool")}
        self.dcnt = {q: [0] * self.NSLOT for q in self.dsem}
        self.dnext = {q: 0 for q in self.dsem}
        self.dram = {}

    def dres(self, *key):
        r = self.dram.get(key)
        if r is None:
            r = self.dram[key] = Res()
        return r

    def _semof(self, key):
        return self.sem[key[1]] if key[0] == "c" else self.dsem[key[1]][key[2]]

    def _wait(self, eng, key, val):
        if self.seen[eng].get(key, 0) >= val:
            return
        self.seen[eng][key] = val
        self.eng[eng].wait_ge(self._semof(key), val)

    def _deps(self, eng, reads, writes):
        best = {}
        for r in reads:
            if r.w is not None:
                k, v = r.w
                if best.get(k, 0) < v:
                    best[k] = v
        for w in writes:
            if w.w is not None:
                k, v = w.w
                if best.get(k, 0) < v:
                    best[k] = v
            for k, v in w.r.items():
                if best.get(k, 0) < v:
                    best[k] = v
        for k, v in best.items():
            if k == ("c", eng) and (eng == "pe" or not SAME_ENGINE_SYNC):
                continue
            self._wait(eng, k, v)

    def _mark(self, key, val, reads, writes):
        for r in reads:
            if r.r.get(key, 0) < val:
                r.r[key] = val
        for w in writes:
            w.w = (key, val)
            w.r = {}

    def op(self, eng, fn, reads=(), writes=()):
        self._deps(eng, reads, writes)
        self.cnt[eng] += 1
        fn(self.eng[eng]).then_inc(self.sem[eng], 1)
        self._mark(("c", eng), self.cnt[eng], reads, writes)

    def mm(self, fns, reads=(), writes=()):
        self._deps("pe", reads, writes)
        ins = None
        for f in fns:
            ins = f(self.nc.tensor)
        self.cnt["pe"] += 1
        ins.then_inc(self.sem["pe"], 1)
        self._mark(("c", "pe"), self.cnt["pe"], reads, writes)

    def dma(self, q, out, in_, reads=(), writes=(), **kw):
        slot = self.dnext[q]
        self.dnext[q] = (slot + 1) % self.NSLOT
        key = ("d", q, slot)
        if self.dcnt[q][slot] > 0:
            self._wait(q, key, self.dcnt[q][slot])
        self._deps(q, reads, writes)
        self.dcnt[q][slot] += 16
        self.eng[q].dma_start(out=out, in_=in_, **kw).then_inc(self.dsem[q][slot], 16)
        self._mark(key, self.dcnt[q][slot], reads, writes)

    def barrier(self):
        keys = [(("c", e), self.cnt[e]) for e in self.cnt]
        for q in self.dsem:
            for i in range(self.NSLOT):
                keys.append((("d", q, i), self.dcnt[q][i]))
        for e in self.eng:
            for k, v in keys:
                if v > 0:
                    self._wait(e, k, v)


class Tl:
    def __init__(self, t, nres=1):
        self.t = t
        self.res = [Res() for _ in range(nres)]

    @property
    def r(self):
        return self.res[0]


def _bf(a):
    return np.asarray(a, dtype=np.float32).astype(ml_dtypes.bfloat16)


def _dft_blocks(n):
    ne = n + 128
    idx = np.arange(n, dtype=np.int64)
    m = (idx[:, None] * idx[None, :]) % (2 * n)
    ang = m.astype(np.float64) * (math.pi / n)
    C = np.zeros((ne, ne), np.float64)
    S = np.zeros((ne, ne), np.float64)
    C[:n, :n] = np.cos(ang)
    S[:n, :n] = np.sin(ang)
    alt = np.where(idx % 2 == 0, 1.0, -1.0)
    C[:n, n] = alt
    C[n, :n] = alt
    nt = ne // 128

    def blk(M):
        return np.ascontiguousarray(M.reshape(nt, 128, nt, 128).transpose(2, 1, 0, 3)).reshape(nt, 128, nt * 128)
    wf = np.full((ne,), 1.0 / n, np.float32)
    wf[0] = 0.5 / n
    wf[n] = 0.5 / n
    wf[n + 1:] = 0.0
    return _bf(blk(C)), _bf(blk(S)), np.ascontiguousarray(wf.reshape(nt, 128).T)


def _filter_consts(n):
    t = np.linspace(0.0, 1.0, n, dtype=np.float32)[:, None]
    w = (2.0 * math.pi / n) * np.arange(n, dtype=np.float32)[:, None]
    bands = np.linspace(1e-4, 15, 16, dtype=np.float32)[None, :]
    z = np.concatenate([t, np.cos(bands * w), -np.sin(bands * w)], axis=-1).astype(np.float32)
    tl = np.ascontiguousarray((-t[:, 0]).reshape(n // 128, 128).T).astype(np.float32)
    return np.ascontiguousarray(z.T), tl


def _rope_tables():
    t = np.arange(SEQ)
    pos = (t // GRID_W, t % GRID_W)
    inv = (10000.0 ** (-np.arange(16, dtype=np.float32) / 16)).astype(np.float32)
    C = np.zeros((128, SEQ), np.float32)
    Sg = np.zeros((128, SEQ), np.float32)
    for m in range(2):
        for a in range(2):
            ang = pos[a].astype(np.float32)[None, :] * inv[:, None]
            c = np.cos(ang).astype(np.float32)
            s = np.sin(ang).astype(np.float32)
            b = m * 64 + a * 32
            C[b:b + 16] = c
            C[b + 16:b + 32] = c
            Sg[b:b + 16] = -s
            Sg[b + 16:b + 32] = s
    perm = np.zeros(128, np.int64)
    for m in range(2):
        for a in range(2):
            b = m * 64 + a * 32
            perm[b:b + 16] = np.arange(b + 16, b + 32)
            perm[b + 16:b + 32] = np.arange(b, b + 16)
    return C, Sg, perm


def _na_geometry():
    blocks = []
    types = {}
    for qb in range(32):
        r0 = 2 * qb
        rs = [min(max(r - 4, 0), 56) for r in (r0, r0 + 1)]
        lo, hi = min(rs), max(rs) + 8
        sig = (hi - lo, rs[0] - lo, rs[1] - lo, r0 - lo)
        if sig not in types:
            types[sig] = len(types)
        blocks.append((lo, hi, types[sig]))
    return blocks, types


def _na_bias(rpb):
    blocks, types = _na_geometry()
    nt = len(types)
    out = np.zeros((nt, 16, 7 * 128, 128), np.float32)
    out[:, :, 256:, :] = -30000.0
    kk = np.arange(640)
    ki, kc = kk // 64, kk % 64
    qq = np.arange(128)
    qj, qc = qq // 64, qq % 64
    cs = np.clip(qc - 8, 0, 48)
    for sig, ti in types.items():
        nrows, rs0, rs1, r0l = sig
        rs = np.array([rs0, rs1])[qj]
        qr = r0l + qj
        valid = (ki[:, None] < nrows) & (ki[:, None] >= rs[None, :]) & (ki[:, None] < rs[None, :] + 8) \
            & (kc[:, None] >= cs[None, :]) & (kc[:, None] < cs[None, :] + 16)
        dr = np.clip(ki[:, None] - qr[None, :] + 7, 0, 14)
        dc = np.clip(kc[:, None] - qc[None, :] + 15, 0, 30)
        g = rpb[:, dr, dc]
        out[ti, :, 256:, :] = np.where(valid[None], g, np.float32(-30000.0))
    o = out.reshape(nt, 8, 2, 7, 128, 128).transpose(1, 4, 0, 2, 3, 5)
    return np.ascontiguousarray(o).reshape(8, 128, nt * 2 * 7 * 128), blocks, nt


_CONSTS = {}


def _consts():
    if _CONSTS:
        return _CONSTS
    c = _CONSTS
    c["dftC"], c["dftS"], c["wf"] = _dft_blocks(SEQ)
    c["dftCc"], c["dftSc"], c["wfc"] = _dft_blocks(NCTX)
    c["zT"], c["tl"] = _filter_consts(SEQ)
    c["zTc"], c["tlc"] = _filter_consts(NCTX)
    hy_min = math.log(1e-2) / 1.5
    hy_max = math.log(1e-2) / 0.3
    deltas = np.abs(np.linspace(hy_min, hy_max, 512, dtype=np.float32))
    c["deltas"] = np.ascontiguousarray(np.broadcast_to(deltas[None, :], (128, 512))).astype(np.float32)
    c["ropeC"], c["ropeS"], c["perm"] = _rope_tables()
    c["ident"] = np.eye(128, dtype=np.float32)
    return c


def _fm(v, nch):
    return np.ascontiguousarray(np.asarray(v, np.float32).reshape(nch, 128).T)


class Prog:
    def __init__(self, dbg=None, nab_cols=0, na_blocks=None):
        self.dbg = dbg
        self.nab_cols = nab_cols
        self.na_blocks = na_blocks
        self.nc = bass.Bass("TRN2", target_bir_lowering=False)
        self.din = {}

    def inp(self, name, shape, dt=F32):
        self.din[name] = self.nc.dram_tensor(name, list(shape), dt, kind="ExternalInput").ap()
        return self.din[name]

    def scr(self, name, shape, dt):
        return self.nc.dram_tensor(name, list(shape), dt).ap()

    def sb(self, name, shape, dt=F32, nres=1):
        self.uid = getattr(self, "uid", 0) + 1
        return Tl(self.es.enter_context(self.nc.sbuf_tensor("sb%d_%s" % (self.uid, name), list(shape), dt)), nres)

    def build(self):
        nc = self.nc
        I = self.inp
        x = I("x", [SEQ, D]); ctxi = I("ctxi", [NCTX, D])
        I("sv", [128, 16]); I("mod_w", [2, D, 9 * D]); I("mod_b2", [128, 2 * 2 * 72]); I("normT2", [128, 2 * 3 * 2 * 8])
        I("fnormT", [128, 8])
        I("ffn_w1", [2, 2, D, DFF]); I("ffn_w3", [2, 2, D, DFF]); I("ffn_w2", [2, 2, DFF, D])
        I("ab_w_in", [D, 3072]); I("ab_w_inp", [D, 1024]); I("ab_w_out", [D, D])
        I("lamp", [128, 256]); I("subw", [128, 1])
        I("hcw", [128, 36]); I("hcb", [128, 12])
        I("fw0", [33, 64]); I("fb0", [64, 1]); I("fw1", [2, 64, 64]); I("fb1", [64, 2]); I("ffreq", [64, 1])
        I("fwout", [64, 2048]); I("hskip", [1, 1024])
        I("na_w_in", [D, 3072]); I("na_w_out", [D, D]); I("nab", [8, 128, self.nab_cols])
        I("ident", [128, 128]); I("ropeC", [128, SEQ]); I("ropeS", [128, SEQ])
        I("dftC", [33, 128, 33 * 128], BF16); I("dftS", [33, 128, 33 * 128], BF16); I("wf", [128, 33])
        I("dftCc", [3, 128, 3 * 128], BF16); I("dftSc", [3, 128, 3 * 128], BF16); I("wfc", [128, 3])
        I("zT", [33, SEQ]); I("zTc", [33, NCTX]); I("tl", [128, 32]); I("tlc", [128, 2]); I("deltas", [128, 512])
        self.out = nc.dram_tensor("out", [SEQ, D], F32, kind="ExternalOutput").ap()
        if self.dbg:
            self.dbgo = nc.dram_tensor("dbg", [D, TT], F32, kind="ExternalOutput").ap()
        S_ = self.scr
        self.latT = S_("latT", [D, TT], F32)
        self.w1b = [[S_("w1b%d%d" % (l, i), [D, DFF], BF16) for i in range(2)] for l in range(2)]
        self.w3b = [[S_("w3b%d%d" % (l, i), [D, DFF], BF16) for i in range(2)] for l in range(2)]
        self.w2b = [[S_("w2b%d%d" % (l, i), [DFF, D], BF16) for i in range(2)] for l in range(2)]
        self.abinb = S_("abinb", [D, 3072], BF16); self.abinpb = S_("abinpb", [D, 1024], BF16)
        self.aboutb = S_("aboutb", [D, D], BF16)
        self.nainb = S_("nainb", [D, 3072], BF16); self.naoutb = S_("naoutb", [D, D], BF16)
        self.QT = S_("QT", [D, TT], BF16); self.KT = S_("KT", [D, TT], BF16)
        self.VA = S_("VA", [TT, 1040], BF16)
        self.UH = S_("UH", [1536, TT], F32)
        self.UTM = S_("UTM", [TT, 1024], F32)
        self.VTM = S_("VTM", [TT, 512], BF16)
        self.HSD = S_("HSD", [2, 2, SEQ, 512], BF16)
        self.HCS = S_("HCS", [2, 2, 33 * 128, 512], F32)
        self.catT = S_("catT", [D, TT], BF16)
        with ExitStack() as es:
            self.es = es
            self.S = Sched(nc, es)
            self.ps = [Tl(es.enter_context(nc.psum_tensor("ps%d" % i, [128, 512], F32))) for i in range(8)]
            self.persist()
            self.phase(self.ph_setup)
            self.phase(self.ph_xT)
            for l in range(2):
                self.phase(lambda: self.ph_ffn(l, 0))
                if self.dbg == "ffn%d0" % l:
                    break
                if l == 0:
                    self.phase(self.ph_abproj)
                    self.phase(self.ph_hyena_prep)
                    self.phase(self.ph_diffattn)
                    for nn in (NCTX, SEQ):
                        self.phase(lambda: self.ph_filters(nn))
                        self.phase(lambda: self.ph_hyena(nn))
                    if self.dbg == "cat":
                        break
                    self.phase(lambda: self.ph_outproj(0))
                else:
                    self.phase(self.ph_naproj)
                    self.phase(self.ph_na)
                    self.phase(lambda: self.ph_outproj(1))
                if self.dbg == "mix%d" % l:
                    break
                self.phase(lambda: self.ph_ffn(l, 1))
                if self.dbg == "ffn%d1" % l:
                    break
            if self.dbg:
                src = self.latT
                if self.dbg == "cat":
                    src = None
                if src is not None:
                    self.S.dma("sp", self.dbgo, src, reads=[], writes=[self.S.dres("dbgo")])
                else:
                    self.S.dma("pool", self.dbgo, self.catT, reads=[], writes=[self.S.dres("dbgo")])
            else:
                self.phase(self.ph_final)
            self.S.barrier()
        return nc

    def phase(self, fn):
        with ExitStack() as es:
            old = self.es
            self.es = es
            fn()
            self.S.barrier()
            self.es = old

    def persist(self):
        S = self.S
        self.ident = self.sb("ident", [128, 128])
        self.onesb = self.sb("onesb", [128, 128], BF16)
        self.msc = self.sb("msc", [128, 2 * 3 * 3 * 2 * 8])
        S.dma("sp", self.ident.t[:], self.din["ident"], writes=[self.ident.r])
        S.op("dve", lambda e: e.memset(self.onesb.t[:], 1.0), writes=[self.onesb.r])
        self.svs = self.sb("svs", [128, 16]); self.modT = self.sb("modT", [128, 2 * 2 * 72])
        self.mb2 = self.sb("mb2", [128, 2 * 2 * 72]); self.nT2 = self.sb("nT2", [128, 96])
        S.dma("sp", self.svs.t[:], self.din["sv"], writes=[self.svs.r])
        S.dma("sp", self.mb2.t[:], self.din["mod_b2"], writes=[self.mb2.r])
        S.dma("sp", self.nT2.t[:], self.din["normT2"], writes=[self.nT2.r])

    def mcol(self, l, k, ty, s):
        o = (((l * 3 + k) * 3 + ty) * 2 + s) * 8
        return self.msc.t[:, o:o + 8]

    def cast_weights(self, l):
        S, din = self.S, self.din
        for i in range(2):
            S.dma("pool", self.w1b[l][i], din["ffn_w1"][l, i], writes=[S.dres("w1b", l, i)])
            S.dma("pool", self.w3b[l][i], din["ffn_w3"][l, i], writes=[S.dres("w3b", l, i)])
            S.dma("pool", self.w2b[l][i], din["ffn_w2"][l, i], writes=[S.dres("w2b", l, i)])
        pairs = ((self.abinb, "ab_w_in"), (self.abinpb, "ab_w_inp"), (self.aboutb, "ab_w_out")) if l == 0 else \
            ((self.nainb, "na_w_in"), (self.naoutb, "na_w_out"))
        for dst, src in pairs:
            S.dma("pool", dst, din[src], writes=[S.dres(src)])

    def mod_alloc(self):
        self.mw = [self.sb("mw%d" % i, [128, 8, 1024]) for i in range(2)]
        self.mwi = 0

    def mod_block(self, l, b):
        S, din = self.S, self.din
        svs, modT, mb2 = self.svs, self.modT, self.mb2
        svv = svs.t[:].rearrange("p (k s) -> p k s", s=2)
        mwv = din["mod_w"][l].rearrange("(k p) n -> p k n", p=128)
        buf = self.mw[self.mwi % 2]
        self.mwi += 1
        S.dma("sp", buf.t[:], mwv[:, :, b * 1024:(b + 1) * 1024], writes=[buf.r])
        ps = self.ps[7]
        fns = []
        for jj in range(8):
            for k in range(8):
                fns.append(lambda e, jj=jj, k=k, buf=buf, ps=ps: e.matmul(
                    ps.t[:, 2 * jj:2 * jj + 2], lhsT=buf.t[:, k, jj * 128:(jj + 1) * 128], rhs=svv[:, k, :],
                    start=(k == 0), stop=(k == 7)))
        S.mm(fns, reads=[buf.r, svs.r], writes=[ps.r])
        psv = ps.t[:, 0:16].rearrange("p (j s) -> p s j", s=2)
        for s in range(2):
            o = (l * 2 + s) * 72 + b * 8
            S.op("dve", lambda e, s=s, o=o, psv=psv: e.tensor_tensor(
                out=modT.t[:, o:o + 8], in0=psv[:, s, :], in1=mb2.t[:, o:o + 8], op=ALU.add),
                reads=[ps.r, mb2.r], writes=[modT.r])

    def mod_finish(self, l):
        S = self.S
        modT, nT2 = self.modT, self.nT2
        for k in range(3):
            for s in range(2):
                mo = (l * 2 + s) * 72
                no = ((l * 3 + k) * 2 + s) * 8
                sc = modT.t[:, mo + (3 * k + 1) * 8: mo + (3 * k + 2) * 8]
                sh = modT.t[:, mo + (3 * k) * 8: mo + (3 * k + 1) * 8]
                gt = modT.t[:, mo + (3 * k + 2) * 8: mo + (3 * k + 3) * 8]
                S.op("dve", lambda e, sc=sc, no=no, l=l, k=k, s=s: e.scalar_tensor_tensor(
                    out=self.mcol(l, k, 0, s), in0=sc, scalar=1.0, in1=nT2.t[:, no:no + 8],
                    op0=ALU.add, op1=ALU.mult), reads=[modT.r, nT2.r], writes=[self.msc.r])
                S.op("dve", lambda e, sh=sh, l=l, k=k, s=s: e.tensor_copy(out=self.mcol(l, k, 1, s), in_=sh),
                     reads=[modT.r], writes=[self.msc.r])
                S.op("dve", lambda e, gt=gt, l=l, k=k, s=s: e.tensor_scalar(
                    out=self.mcol(l, k, 2, s), in0=gt, scalar1=(1.0 if k == 1 else 0.5), scalar2=None,
                    op0=ALU.mult), reads=[modT.r], writes=[self.msc.r])

    def ph_setup(self):
        S, din = self.S, self.din
        self.cast_weights(0)
        S.op("act", lambda e: e.activation(out=self.svs.t[:], in_=self.svs.t[:], func=AF.Silu), reads=[self.svs.r], writes=[self.svs.r])
        self.mod_alloc()
        for b in range(9):
            self.mod_block(0, b)
        self.mod_finish(0)

    def ph_xT(self):
        S = self.S
        xin = [self.sb("xin%d" % i, [128, D]) for i in range(2)]
        xo = [self.sb("xo%d" % i, [128, 8, 128]) for i in range(2)]
        for ti in range(TT // 128):
            src = self.din["ctxi"][ti * 128:(ti + 1) * 128, :] if ti < 2 else self.din["x"][(ti - 2) * 128:(ti - 1) * 128, :]
            xi, o = xin[ti % 2], xo[ti % 2]
            S.dma("sp", xi.t[:], src, writes=[xi.r])
            for h in range(2):
                ps = self.ps[(ti % 2) * 2 + h]
                S.mm([lambda e, c=c, ps=ps, xi=xi, h=h: e.transpose(
                    ps.t[:, c * 128:(c + 1) * 128], xi.t[:, (h * 4 + c) * 128:(h * 4 + c + 1) * 128], self.ident.t[:])
                    for c in range(4)], reads=[xi.r, self.ident.r], writes=[ps.r])
                eng = "act" if h == 0 else "dve"
                ov = o.t[:, h * 4:(h + 1) * 4, :]
                pv = ps.t[:, :].rearrange("p (c t) -> p c t", t=128)
                if eng == "act":
                    S.op("act", lambda e, ov=ov, pv=pv: e.copy(out=ov, in_=pv), reads=[ps.r], writes=[o.r])
                else:
                    S.op("dve", lambda e, ov=ov, pv=pv: e.tensor_copy(out=ov, in_=pv), reads=[ps.r], writes=[o.r])
            dst = self.latT.rearrange("(c p) t -> p c t", p=128)[:, :, ti * 128:(ti + 1) * 128]
            S.dma("sp", dst, o.t[:], reads=[o.r], writes=[S.dres("latT", ti)])

    def groups(self, with_ctx=True):
        g = [(0, NCTX, 1)] if with_ctx else []
        return g + [(NCTX + GS * i, GS, 0) for i in range(SEQ // GS)]

    def alloc_pro(self):
        self.xt = [self.sb("xt%d" % i, [128, 8, GS]) for i in range(2)]
        self.sq = self.sb("sq", [128, 8, GS], BF16)
        self.rstd = self.sb("rstd", [128, GS])
        self.ptmp = [self.sb("ptmp%d" % i, [128, GS]) for i in range(2)]
        self.hTs = [self.sb("hT%d" % i, [128, 8, GS], BF16) for i in range(2)]
        self.hT = self.hTs[0]
        self.gi = 0

    def prologue(self, l, k, t0, n, s, psb, want_h=True):
        S = self.S
        xt = self.xt[self.gi % 2]
        self.hT = self.hTs[self.gi % 2]
        self.gi += 1
        lv = self.latT.rearrange("(c p) t -> p c t", p=128)[:, :, t0:t0 + n]
        S.dma("sp", xt.t[:, :, :n], lv, writes=[xt.r])
        sq, rstd, hT = self.sq, self.rstd, self.hT
        S.op("act", lambda e: e.activation(out=sq.t[:, :, :n], in_=xt.t[:, :, :n], func=AF.Square),
             reads=[xt.r], writes=[sq.r])
        ps = self.ps[psb]
        S.mm([lambda e, c=c: e.matmul(ps.t[:, :n], lhsT=self.onesb.t[:], rhs=sq.t[:, c, :n], start=(c == 0), stop=(c == 7))
              for c in range(8)], reads=[sq.r, self.onesb.r], writes=[ps.r])
        S.op("act", lambda e: e.activation(out=rstd.t[:, :n], in_=ps.t[:, :n], func=AF.Sqrt, bias=EPS, scale=1.0 / D),
             reads=[ps.r], writes=[rstd.r])
        S.op("dve", lambda e: e.reciprocal(out=rstd.t[:, :n], in_=rstd.t[:, :n]), reads=[rstd.r], writes=[rstd.r])
        if want_h:
            gs, sh = self.mcol(l, k, 0, s), self.mcol(l, k, 1, s)
            for c in range(8):
                tmp = self.ptmp[c % 2]
                S.op("dve", lambda e, c=c, tmp=tmp: e.scalar_tensor_tensor(
                    out=tmp.t[:, :n], in0=xt.t[:, c, :n], scalar=gs[:, c:c + 1], in1=rstd.t[:, :n],
                    op0=ALU.mult, op1=ALU.mult), reads=[xt.r, rstd.r, self.msc.r], writes=[tmp.r])
                S.op("act", lambda e, c=c, tmp=tmp: e.activation(
                    out=hT.t[:, c, :n], in_=tmp.t[:, :n], func=AF.Identity, bias=sh[:, c:c + 1], scale=1.0),
                    reads=[tmp.r, self.msc.r], writes=[hT.r])
        return xt

    def store_group(self, xt, t0, n):
        lv = self.latT.rearrange("(c p) t -> p c t", p=128)[:, :, t0:t0 + n]
        self.S.dma("sp", lv, xt.t[:, :, :n], reads=[xt.r], writes=[self.S.dres("latT", t0)])

    def ph_ffn(self, l, i):
        S = self.S
        k = 0 if i == 0 else 2
        self.alloc_pro()
        w2t = self.sb("w2t", [128, NJ, D], BF16)
        w2v = self.w2b[l][i].rearrange("(j p) n -> p j n", p=128)
        for a in range(2):
            S.dma("sp", w2t.t[:, a * 11:(a + 1) * 11, :], w2v[:, a * 11:(a + 1) * 11, :], writes=[w2t.r] if a == 0 else [w2t.r])
        wb1 = [self.sb("wb1_%d" % a, [128, 8, 256], BF16) for a in range(3)]
        wb3 = [self.sb("wb3_%d" % a, [128, 8, 256], BF16) for a in range(3)]
        actT = self.sb("actT", [128, NJ, GS], BF16)
        sg = [self.sb("sg%d" % a, [128, GS]) for a in range(2)]
        w1v = self.w1b[l][i].rearrange("(k p) n -> p k n", p=128)
        w3v = self.w3b[l][i].rearrange("(k p) n -> p k n", p=128)
        need_ctx = (l == 0) or (i == 0)
        bi = 0
        grps = self.groups(need_ctx)
        nxt = (self.prologue(l, k, grps[0][0], grps[0][1], grps[0][2], 6), self.hT)
        for gidx, (t0, n, s) in enumerate(grps):
            xt, hT = nxt
            for nb in range(11):
                b1, b3 = wb1[bi % 3], wb3[bi % 3]
                bi += 1
                S.dma("sp", b1.t[:], w1v[:, :, nb * 256:(nb + 1) * 256], writes=[b1.r])
                S.dma("sp", b3.t[:], w3v[:, :, nb * 256:(nb + 1) * 256], writes=[b3.r])
                for jj in range(2):
                    j = nb * 2 + jj
                    pg, pu = self.ps[2 * (j % 2)], self.ps[2 * (j % 2) + 1]
                    S.mm([lambda e, kc=kc, b1=b1, pg=pg, jj=jj: e.matmul(
                        pg.t[:, :n], lhsT=b1.t[:, kc, jj * 128:(jj + 1) * 128], rhs=hT.t[:, kc, :n],
                        start=(kc == 0), stop=(kc == 7)) for kc in range(8)], reads=[b1.r, hT.r], writes=[pg.r])
                    S.mm([lambda e, kc=kc, b3=b3, pu=pu, jj=jj: e.matmul(
                        pu.t[:, :n], lhsT=b3.t[:, kc, jj * 128:(jj + 1) * 128], rhs=hT.t[:, kc, :n],
                        start=(kc == 0), stop=(kc == 7)) for kc in range(8)], reads=[b3.r, hT.r], writes=[pu.r])
                    sgt = sg[j % 2]
                    S.op("act", lambda e, pg=pg, sgt=sgt: e.activation(out=sgt.t[:, :n], in_=pg.t[:, :n], func=AF.Silu),
                         reads=[pg.r], writes=[sgt.r])
                    S.op("dve", lambda e, pu=pu, sgt=sgt, j=j: e.tensor_tensor(
                        out=actT.t[:, j, :n], in0=pu.t[:, :n], in1=sgt.t[:, :n], op=ALU.mult),
                        reads=[pu.r, sgt.r], writes=[actT.r])
            if gidx + 1 < len(grps):
                g2 = grps[gidx + 1]
                nxt = (self.prologue(l, k, g2[0], g2[1], g2[2], 6), self.hT)
            gate = self.mcol(l, k, 2, s)
            for dc in range(8):
                py = self.ps[4 + dc % 2]
                S.mm([lambda e, j=j, py=py, dc=dc: e.matmul(
                    py.t[:, :n], lhsT=w2t.t[:, j, dc * 128:(dc + 1) * 128], rhs=actT.t[:, j, :n],
                    start=(j == 0), stop=(j == NJ - 1)) for j in range(NJ)], reads=[w2t.r, actT.r], writes=[py.r])
                S.op("dve", lambda e, py=py, dc=dc: e.scalar_tensor_tensor(
                    out=xt.t[:, dc, :n], in0=py.t[:, :n], scalar=gate[:, dc:dc + 1], in1=xt.t[:, dc, :n],
                    op0=ALU.mult, op1=ALU.add), reads=[py.r, xt.r, self.msc.r], writes=[xt.r])
            self.store_group(xt, t0, n)

    def ph_abproj(self):
        S, din = self.S, self.din
        self.alloc_pro()
        win = self.sb("win", [128, 8, 3072], BF16)
        winp = self.sb("winp", [128, 8, 1024], BF16)
        wv = self.abinb.rearrange("(k p) n -> p k n", p=128)
        for a in range(3):
            S.dma("sp", win.t[:, :, a * 1024:(a + 1) * 1024], wv[:, :, a * 1024:(a + 1) * 1024], writes=[win.r])
        S.dma("sp", winp.t[:], self.abinpb.rearrange("(k p) n -> p k n", p=128), writes=[winp.r])
        rc = self.sb("rc", [128, GS]); rs = self.sb("rs", [128, GS])
        qo = [self.sb("qo%d" % a, [128, GS], BF16) for a in range(2)]
        t1 = [self.sb("t1_%d" % a, [128, GS]) for a in range(2)]
        t2 = [self.sb("t2_%d" % a, [128, GS]) for a in range(2)]
        vo = [self.sb("vo%d" % a, [128, 4, 129], BF16) for a in range(2)]
        uo = [self.sb("uo%d" % a, [128, GS]) for a in range(2)]
        for v in vo:
            S.op("dve", lambda e, v=v: e.memset(v.t[:], 1.0), writes=[v.r])
        ci = 0
        for (t0, n, s) in self.groups(True):
            self.prologue(0, 1, t0, n, s, 6)
            hT = self.hT
            if s == 0:
                S.dma("sp", rc.t[:, :n], din["ropeC"][:, t0 - NCTX:t0 - NCTX + n], writes=[rc.r])
                S.dma("sp", rs.t[:, :n], din["ropeS"][:, t0 - NCTX:t0 - NCTX + n], writes=[rs.r])
            for which in range(2):
                for hc in range(4):
                    col = which * 512 + hc * 128
                    pa, pb = self.ps[(ci % 2) * 2], self.ps[(ci % 2) * 2 + 1]
                    q = qo[ci % 2]; a1 = t1[ci % 2]; a2 = t2[ci % 2]
                    ci += 1
                    S.mm([lambda e, k=k, pa=pa, col=col: e.matmul(
                        pa.t[:, :n], lhsT=win.t[:, k, col:col + 128], rhs=hT.t[:, k, :n], start=(k == 0), stop=(k == 7))
                        for k in range(8)], reads=[win.r, hT.r], writes=[pa.r])
                    if s == 0:
                        S.mm([lambda e, k=k, pb=pb, col=col: e.matmul(
                            pb.t[:, :n], lhsT=winp.t[:, k, col:col + 128], rhs=hT.t[:, k, :n], start=(k == 0), stop=(k == 7))
                            for k in range(8)], reads=[winp.r, hT.r], writes=[pb.r])
                        S.op("dve", lambda e, pa=pa, a1=a1: e.tensor_tensor(out=a1.t[:, :n], in0=pa.t[:, :n], in1=rc.t[:, :n], op=ALU.mult),
                             reads=[pa.r, rc.r], writes=[a1.r])
                        S.op("dve", lambda e, pb=pb, a2=a2: e.tensor_tensor(out=a2.t[:, :n], in0=pb.t[:, :n], in1=rs.t[:, :n], op=ALU.mult),
                             reads=[pb.r, rs.r], writes=[a2.r])
                        S.op("pool", lambda e, q=q, a1=a1, a2=a2: e.tensor_tensor(out=q.t[:, :n], in0=a1.t[:, :n], in1=a2.t[:, :n], op=ALU.add),
                             reads=[a1.r, a2.r], writes=[q.r])
                    else:
                        S.op("act", lambda e, q=q, pa=pa: e.copy(out=q.t[:, :n], in_=pa.t[:, :n]), reads=[pa.r], writes=[q.r])
                    dst = (self.QT if which == 0 else self.KT)[hc * 128:(hc + 1) * 128, t0:t0 + n]
                    S.dma("sp", dst, q.t[:, :n], reads=[q.r], writes=[S.dres("qk", which, hc, t0)])
            for tt in range(n // 128):
                pv = self.ps[4 + tt % 2]
                v = vo[tt % 2]
                S.mm([lambda e, k=k, pv=pv, tt=tt: e.matmul(
                    pv.t[:, :512], lhsT=hT.t[:, k, tt * 128:(tt + 1) * 128], rhs=win.t[:, k, 1024:1536], start=(k == 0), stop=(k == 7))
                    for k in range(8)], reads=[win.r, hT.r], writes=[pv.r])
                S.op("act", lambda e, pv=pv, v=v: e.copy(out=v.t[:, :, 0:128], in_=pv.t[:, :].rearrange("p (h d) -> p h d", d=128)),
                     reads=[pv.r], writes=[v.r])
                S.dma("sp", self.VA[t0 + tt * 128:t0 + (tt + 1) * 128, 0:516].rearrange("p (h d) -> p h d", d=129), v.t[:],
                      reads=[v.r], writes=[S.dres("va", t0, tt)])
            for c in range(12):
                pu = self.ps[(c % 4)]
                u = uo[c % 2]
                S.mm([lambda e, k=k, pu=pu, c=c: e.matmul(
                    pu.t[:, :n], lhsT=win.t[:, k, 1536 + c * 128:1536 + (c + 1) * 128], rhs=hT.t[:, k, :n], start=(k == 0), stop=(k == 7))
                    for k in range(8)], reads=[win.r, hT.r], writes=[pu.r])
                if c % 2 == 0:
                    S.op("act", lambda e, pu=pu, u=u: e.copy(out=u.t[:, :n], in_=pu.t[:, :n]), reads=[pu.r], writes=[u.r])
                else:
                    S.op("dve", lambda e, pu=pu, u=u: e.tensor_copy(out=u.t[:, :n], in_=pu.t[:, :n]), reads=[pu.r], writes=[u.r])
                S.dma("sp", self.UH[c * 128:(c + 1) * 128, t0:t0 + n], u.t[:, :n], reads=[u.r], writes=[S.dres("uh", c, t0)])

    def ph_hyena_prep(self):
        S, din = self.S, self.din
        cw = self.sb("cw", [128, 36]); cb = self.sb("cb", [128, 12])
        S.dma("sp", cw.t[:], din["hcw"], writes=[cw.r]); S.dma("sp", cb.t[:], din["hcb"], writes=[cb.r])
        ub = [self.sb("ub%d" % a, [128, SEQ + 2]) for a in range(2)]
        uc = [self.sb("uc%d" % a, [128, SEQ]) for a in range(2)]
        st = [self.sb("st%d" % a, [128, 4, 128]) for a in range(2)]
        stb = [self.sb("stb%d" % a, [128, 4, 128], BF16) for a in range(2)]
        ci = 0
        gi = 0
        self.mod_alloc()
        mblk = 0
        for (T0, n) in ((0, NCTX), (NCTX, SEQ)):
            for c in range(12):
                if n == SEQ and mblk < 9:
                    self.mod_block(1, mblk)
                    mblk += 1
                    if mblk == 9:
                        self.mod_finish(1)
                b, u = ub[ci % 2], uc[ci % 2]
                ci += 1
                S.op("dve", lambda e, b=b: e.memset(b.t[:, 0:1], 0.0), writes=[b.r])
                S.op("dve", lambda e, b=b: e.memset(b.t[:, n + 1:n + 2], 0.0), writes=[b.r])
                S.dma("sp", b.t[:, 1:n + 1], self.UH[c * 128:(c + 1) * 128, T0:T0 + n], writes=[b.r])
                S.op("act", lambda e, b=b, u=u, c=c: e.activation(out=u.t[:, :n], in_=b.t[:, 1:n + 1], func=AF.Identity,
                                                                 bias=cb.t[:, c:c + 1], scale=cw.t[:, 3 * c + 1:3 * c + 2]),
                     reads=[b.r, cw.r, cb.r], writes=[u.r])
                S.op("dve", lambda e, b=b, u=u, c=c: e.scalar_tensor_tensor(out=u.t[:, :n], in0=b.t[:, 0:n], scalar=cw.t[:, 3 * c:3 * c + 1],
                                                                        in1=u.t[:, :n], op0=ALU.mult, op1=ALU.add),
                     reads=[b.r, u.r, cw.r], writes=[u.r])
                S.op("dve", lambda e, b=b, u=u, c=c: e.scalar_tensor_tensor(out=u.t[:, :n], in0=b.t[:, 2:n + 2], scalar=cw.t[:, 3 * c + 2:3 * c + 3],
                                                                        in1=u.t[:, :n], op0=ALU.mult, op1=ALU.add),
                     reads=[b.r, u.r, cw.r], writes=[u.r])
                for g4 in range(n // 512 if n >= 512 else 1):
                    ntl = min(4, n // 128)
                    ps = self.ps[gi % 4]
                    S.mm([lambda e, a=a, ps=ps, u=u, g4=g4: e.transpose(
                        ps.t[:, a * 128:(a + 1) * 128], u.t[:, (g4 * 4 + a) * 128:(g4 * 4 + a + 1) * 128], self.ident.t[:])
                        for a in range(ntl)], reads=[u.r, self.ident.r], writes=[ps.r])
                    pvw = ps.t[:, :ntl * 128].rearrange("p (a w) -> p a w", w=128)
                    r0 = T0 + g4 * 512
                    if c < 8:
                        sx = st[gi % 2]
                        S.op("act" if gi % 2 == 0 else "dve",
                             (lambda e, sx=sx, pvw=pvw: e.copy(out=sx.t[:, :ntl, :], in_=pvw)) if gi % 2 == 0 else
                             (lambda e, sx=sx, pvw=pvw: e.tensor_copy(out=sx.t[:, :ntl, :], in_=pvw)),
                             reads=[ps.r], writes=[sx.r])
                        dst = self.UTM[r0:r0 + ntl * 128, c * 128:(c + 1) * 128].rearrange("(a p) w -> p a w", p=128)
                        S.dma("sp", dst, sx.t[:, :ntl, :], reads=[sx.r], writes=[S.dres("utm", c, r0)])
                    else:
                        sx = stb[gi % 2]
                        S.op("act" if gi % 2 == 0 else "dve",
                             (lambda e, sx=sx, pvw=pvw: e.copy(out=sx.t[:, :ntl, :], in_=pvw)) if gi % 2 == 0 else
                             (lambda e, sx=sx, pvw=pvw: e.tensor_copy(out=sx.t[:, :ntl, :], in_=pvw)),
                             reads=[ps.r], writes=[sx.r])
                        dst = self.VTM[r0:r0 + ntl * 128, (c - 8) * 128:(c - 7) * 128].rearrange("(a p) w -> p a w", p=128)
                        S.dma("sp", dst, sx.t[:, :ntl, :], reads=[sx.r], writes=[S.dres("vtm", c, r0)])
                    gi += 1

    def ph_diffattn(self):
        S, din = self.S, self.din
        self.cast_weights(1)
        lam_init = 0.8 - 0.6 * math.exp(-0.3 * 0)
        lp = self.sb("lp", [128, 256]); pr = self.sb("pr", [128, 128]); sm = self.sb("sm", [128, 2])
        nlam = self.sb("nlam", [128, 1]); wsub = self.sb("wsub", [128, 1])
        S.dma("sp", lp.t[:], din["lamp"], writes=[lp.r]); S.dma("sp", wsub.t[:], din["subw"], writes=[wsub.r])
        S.op("dve", lambda e: e.tensor_tensor(out=pr.t[:, 0:64], in0=lp.t[:, 0:64], in1=lp.t[:, 64:128], op=ALU.mult), reads=[lp.r], writes=[pr.r])
        S.op("dve", lambda e: e.tensor_tensor(out=pr.t[:, 64:128], in0=lp.t[:, 128:192], in1=lp.t[:, 192:256], op=ALU.mult), reads=[lp.r], writes=[pr.r])
        S.op("dve", lambda e: e.reduce_sum(out=sm.t[:, 0:2], in_=pr.t[:, :].rearrange("p (a b) -> p a b", b=64), axis=AX.X), reads=[pr.r], writes=[sm.r])
        S.op("act", lambda e: e.activation(out=sm.t[:], in_=sm.t[:], func=AF.Exp), reads=[sm.r], writes=[sm.r])
        S.op("dve", lambda e: e.tensor_tensor(out=nlam.t[:], in0=sm.t[:, 1:2], in1=sm.t[:, 0:1], op=ALU.subtract), reads=[sm.r], writes=[nlam.r])
        S.op("dve", lambda e: e.tensor_scalar(out=nlam.t[:], in0=nlam.t[:], scalar1=-lam_init, scalar2=None, op0=ALU.add), reads=[nlam.r], writes=[nlam.r])
        S.op("dve", lambda e: e.tensor_scalar(out=wsub.t[:], in0=wsub.t[:], scalar1=1.0 - lam_init, scalar2=None, op0=ALU.mult), reads=[wsub.r], writes=[wsub.r])
        ktb = [self.sb("ktb%d" % a, [128, TT], BF16) for a in range(2)]
        qtb = [self.sb("qtb%d" % a, [128, TT], BF16) for a in range(2)]
        vab = [self.sb("vab%d" % a, [128, TT // 128, 129], BF16) for a in range(2)]
        cab = [self.sb("cab%d" % a, [128, TT], BF16) for a in range(2)]
        pts = [self.sb("pt%d" % a, [128, 512], BF16) for a in range(4)]
        sacc = [self.sb("sacc%d" % a, [128, 512]) for a in range(2)]
        rsb = self.sb("rsb", [128, 512]); osb = self.sb("osb", [128, 512]); o1 = self.sb("o1", [128, 512])
        sqb = self.sb("sqb", [128, 512], BF16); rst = self.sb("rst", [128, 512])
        onesf = self.sb("onesf", [128, 128])
        S.op("dve", lambda e: e.memset(onesf.t[:], 1.0), writes=[onesf.r])
        cnt = 0
        mi = 0
        for h in range(4):
            kt_, qt_, va_, ca_ = ktb[h % 2], qtb[h % 2], vab[h % 2], cab[h % 2]
            S.dma("sp", kt_.t[:], self.KT[h * 128:(h + 1) * 128, :], writes=[kt_.r])
            S.dma("sp", qt_.t[:], self.QT[h * 128:(h + 1) * 128, :], writes=[qt_.r])
            S.dma("sp", va_.t[:], self.VA[:, h * 129:(h + 1) * 129].rearrange("(i p) w -> p i w", p=128), writes=[va_.r])
            qgroups = [(0, NCTX, list(range(2)))] + [(NCTX + 512 * g, 512, list(range(TT // 128))) for g in range(SEQ // 512)]
            for (q0, qn, kts) in qgroups:
                OT = [self.ps[0], self.ps[1]]

                def score(i):
                    kt = kts[i]
                    for m in range(2):
                        pS = self.ps[4 + 2 * m + i % 2]
                        S.mm([lambda e, pS=pS, kt=kt, m=m: e.matmul(
                            pS.t[:, :qn], lhsT=kt_.t[m * 64:(m + 1) * 64, kt * 128:(kt + 1) * 128],
                            rhs=qt_.t[m * 64:(m + 1) * 64, q0:q0 + qn], start=True, stop=True)],
                            reads=[kt_.r, qt_.r], writes=[pS.r])
                score(0)
                for i, kt in enumerate(kts):
                    if i + 1 < len(kts):
                        score(i + 1)
                    for m in range(2):
                        pS = self.ps[4 + 2 * m + i % 2]
                        pt = pts[cnt % 4]
                        cnt += 1
                        sa = sacc[m]
                        S.op("act", lambda e, pS=pS, pt=pt: e.activation(out=pt.t[:, :qn], in_=pS.t[:, :qn], func=AF.Exp, scale=0.125),
                             reads=[pS.r], writes=[pt.r])
                        S.mm([lambda e, pt=pt, kt=kt, m=m: e.matmul(
                            OT[m].t[:, :qn], lhsT=va_.t[:, kt, 0:128], rhs=pt.t[:, :qn], start=(kt == kts[0]), stop=(kt == kts[-1]))],
                            reads=[pt.r, va_.r], writes=[OT[m].r])
                        if i == 0:
                            S.op("dve", lambda e, pt=pt, sa=sa: e.tensor_copy(out=sa.t[:, :qn], in_=pt.t[:, :qn]), reads=[pt.r], writes=[sa.r])
                        else:
                            S.op("dve", lambda e, pt=pt, sa=sa: e.tensor_tensor(out=sa.t[:, :qn], in0=sa.t[:, :qn], in1=pt.t[:, :qn], op=ALU.add),
                                 reads=[pt.r, sa.r], writes=[sa.r])
                for m in range(2):
                    SM = self.ps[2]
                    sa = sacc[m]
                    S.mm([lambda e, SM=SM, sa=sa: e.matmul(SM.t[:, :qn], lhsT=onesf.t[:], rhs=sa.t[:, :qn], start=True, stop=True)],
                         reads=[onesf.r, sa.r], writes=[SM.r])
                    S.op("dve", lambda e, SM=SM: e.reciprocal(out=rsb.t[:, :qn], in_=SM.t[:, :qn]), reads=[SM.r], writes=[rsb.r])
                    if m == 0:
                        S.op("dve", lambda e: e.tensor_tensor(out=osb.t[:, :qn], in0=OT[0].t[:, :qn], in1=rsb.t[:, :qn], op=ALU.mult),
                             reads=[OT[0].r, rsb.r], writes=[osb.r])
                    else:
                        S.op("dve", lambda e: e.tensor_tensor(out=o1.t[:, :qn], in0=OT[1].t[:, :qn], in1=rsb.t[:, :qn], op=ALU.mult),
                             reads=[OT[1].r, rsb.r], writes=[o1.r])
                S.op("dve", lambda e: e.scalar_tensor_tensor(out=osb.t[:, :qn], in0=o1.t[:, :qn], scalar=nlam.t[:, 0:1], in1=osb.t[:, :qn],
                                                             op0=ALU.mult, op1=ALU.add), reads=[o1.r, nlam.r, osb.r], writes=[osb.r])
                S.op("act", lambda e: e.activation(out=sqb.t[:, :qn], in_=osb.t[:, :qn], func=AF.Square), reads=[osb.r], writes=[sqb.r])
                SS = self.ps[3]
                S.mm([lambda e, SS=SS: e.matmul(SS.t[:, :qn], lhsT=self.onesb.t[:], rhs=sqb.t[:, :qn], start=True, stop=True)],
                     reads=[self.onesb.r, sqb.r], writes=[SS.r])
                S.op("act", lambda e, SS=SS: e.activation(out=rst.t[:, :qn], in_=SS.t[:, :qn], func=AF.Sqrt, bias=EPS, scale=1.0 / 128),
                     reads=[SS.r], writes=[rst.r])
                S.op("dve", lambda e: e.reciprocal(out=rst.t[:, :qn], in_=rst.t[:, :qn]), reads=[rst.r], writes=[rst.r])
                S.op("dve", lambda e: e.scalar_tensor_tensor(out=ca_.t[:, q0:q0 + qn], in0=osb.t[:, :qn], scalar=wsub.t[:, 0:1], in1=rst.t[:, :qn],
                                                             op0=ALU.mult, op1=ALU.mult), reads=[osb.r, wsub.r, rst.r], writes=[ca_.r])
            S.dma("sp", self.catT[h * 128:(h + 1) * 128, :], ca_.t[:], reads=[ca_.r], writes=[S.dres("catA", h)])

    def ph_filters(self, n):
        S, din = self.S, self.din
        cm = (n == NCTX)
        zt = self.sb("zt", [33, n]); w0 = self.sb("w0", [33, 64]); b0 = self.sb("b0", [64, 1])
        w1 = self.sb("w1", [64, 2, 64]); b1 = self.sb("b1", [64, 2]); fr = self.sb("fr", [64, 1]); wo = self.sb("wo", [64, 2048])
        skp = self.sb("skp", [1, 1024]); dl = self.sb("dl", [128, 512]); tl = self.sb("tl", [128, n // 128])
        S.dma("sp", zt.t[:], din["zTc" if cm else "zT"], writes=[zt.r])
        S.dma("sp", w0.t[:], din["fw0"], writes=[w0.r]); S.dma("sp", b0.t[:], din["fb0"], writes=[b0.r])
        S.dma("sp", w1.t[:], din["fw1"].rearrange("i k m -> k i m"), writes=[w1.r]); S.dma("sp", b1.t[:], din["fb1"], writes=[b1.r])
        S.dma("sp", fr.t[:], din["ffreq"], writes=[fr.r]); S.dma("sp", wo.t[:], din["fwout"], writes=[wo.r])
        S.op("dve", lambda e: e.tensor_scalar(out=fr.t[:], in0=fr.t[:], scalar1=1.0 / (2.0 * math.pi), scalar2=None, op0=ALU.mult), reads=[fr.r], writes=[fr.r])
        S.dma("sp", skp.t[:], din["hskip"], writes=[skp.r]); S.dma("sp", dl.t[:], din["deltas"], writes=[dl.r])
        S.dma("sp", tl.t[:], din["tlc" if cm else "tl"], writes=[tl.r])
        hid = [self.sb("hid%d" % a, [64, n]) for a in range(2)]
        tmp = [self.sb("ftmp%d" % a, [64, 512]) for a in range(2)]
        gsz = min(512, n)
        TWO_PI = 2.0 * math.pi
        OFFS = math.pi + 16.0 * math.pi
        ci = 0
        for layer in range(3):
            src = zt if layer == 0 else hid[(layer - 1) % 2]
            dst = hid[layer % 2]
            bias = b0.t[:, 0:1] if layer == 0 else b1.t[:, layer - 1:layer]
            for g in range(n // gsz):
                ps = self.ps[ci % 2]; tm = tmp[ci % 2]
                ci += 1
                if layer == 0:
                    S.mm([lambda e, ps=ps, g=g: e.matmul(ps.t[:64, :gsz], lhsT=w0.t[:, :], rhs=zt.t[:, g * gsz:(g + 1) * gsz], start=True, stop=True)],
                         reads=[w0.r, zt.r], writes=[ps.r])
                else:
                    S.mm([lambda e, ps=ps, g=g, src=src, layer=layer: e.matmul(ps.t[:64, :gsz], lhsT=w1.t[:, layer - 1, :], rhs=src.t[:, g * gsz:(g + 1) * gsz],
                                                                             start=True, stop=True)], reads=[w1.r, src.r], writes=[ps.r])
                S.op("dve", lambda e, ps=ps, tm=tm, bias=bias: e.tensor_scalar(out=tm.t[:, :gsz], in0=ps.t[:64, :gsz], scalar1=bias, scalar2=fr.t[:, 0:1],
                                                                           op0=ALU.add, op1=ALU.mult), reads=[ps.r, b0.r, b1.r, fr.r], writes=[tm.r])
                for rnd in range(2):
                    S.op("dve", lambda e, tm=tm: e.scalar_tensor_tensor(out=tm.t[:, :gsz], in0=tm.t[:, :gsz], scalar=-0.5, in1=tm.t[:, :gsz],
                                                                    op0=ALU.is_lt, op1=ALU.add), reads=[tm.r], writes=[tm.r])
                    S.op("dve", lambda e, tm=tm: e.scalar_tensor_tensor(out=tm.t[:, :gsz], in0=tm.t[:, :gsz], scalar=0.5, in1=tm.t[:, :gsz],
                                                                    op0=ALU.is_gt, op1=ALU.subtract), reads=[tm.r], writes=[tm.r])
                S.op("act", lambda e, tm=tm, dst=dst, g=g: e.activation(out=dst.t[:, g * gsz:(g + 1) * gsz], in_=tm.t[:, :gsz], func=AF.Sin, scale=TWO_PI),
                     reads=[tm.r], writes=[dst.r])
        hfin = hid[0]
        wnd = [self.sb("wnd%d" % a, [128, 512]) for a in range(2)]
        ff = [self.sb("ff%d" % a, [128, 512]) for a in range(4)]
        ho = [self.sb("ho%d" % a, [128, 512], BF16) for a in range(4)]
        hi_ = 0
        for tt in range(n // 128):
            wn = wnd[tt % 2]
            S.op("act", lambda e, wn=wn, tt=tt: e.activation(out=wn.t[:], in_=dl.t[:], func=AF.Exp, scale=tl.t[:, tt:tt + 1]),
                 reads=[dl.r, tl.r], writes=[wn.r])
            for o in range(2):
                for d in range(2):
                    cb = o * 2 + d
                    ps = self.ps[2 + cb]
                    S.mm([lambda e, ps=ps, cb=cb, tt=tt: e.matmul(ps.t[:, :512], lhsT=hfin.t[:, tt * 128:(tt + 1) * 128], rhs=wo.t[:, cb * 512:(cb + 1) * 512],
                                                                 start=True, stop=True)], reads=[hfin.r, wo.r], writes=[ps.r])
                    f = ff[cb]
                    S.op("dve", lambda e, ps=ps, f=f, wn=wn: e.tensor_tensor(out=f.t[:], in0=ps.t[:, :512], in1=wn.t[:], op=ALU.mult),
                         reads=[ps.r, wn.r], writes=[f.r])
                    if tt == 0:
                        if d == 0:
                            S.op("dve", lambda e, f=f, o=o: e.tensor_tensor(out=f.t[0:1, :], in0=f.t[0:1, :], in1=skp.t[0:1, o * 512:(o + 1) * 512], op=ALU.add),
                                 reads=[f.r, skp.r], writes=[f.r])
                        else:
                            S.op("dve", lambda e, f=f: e.memset(f.t[0:1, :], 0.0), reads=[f.r], writes=[f.r])
                f0, f1 = ff[o * 2], ff[o * 2 + 1]
                for sd in range(2):
                    h_ = ho[hi_ % 4]
                    hi_ += 1
                    S.op("pool", lambda e, h_=h_, f0=f0, f1=f1, sd=sd: e.tensor_tensor(out=h_.t[:], in0=f0.t[:], in1=f1.t[:],
                                                                                    op=(ALU.add if sd == 0 else ALU.subtract)),
                         reads=[f0.r, f1.r], writes=[h_.r])
                    S.dma("sp", self.HSD[o, sd, tt * 128:(tt + 1) * 128, :], h_.t[:], reads=[h_.r], writes=[S.dres("hsd", n, o, sd, tt)])

    def ph_hyena(self, n):
        S, din = self.S, self.din
        cm = (n == NCTX)
        T0 = 0 if cm else NCTX
        nt = n // 128
        nf = nt + 1
        dC, dS = (din["dftCc"], din["dftSc"]) if cm else (din["dftC"], din["dftS"])
        wft = self.sb("wft", [128, nf])
        S.dma("sp", wft.t[:], din["wfc" if cm else "wf"], writes=[wft.r])
        vt = self.sb("vt", [128, nt, 512], BF16)
        Y = [self.sb("Y%d" % a, [128, nf, 512], BF16) for a in range(2)]
        blk = [self.sb("blk%d" % a, [128, nf, 128], BF16) for a in range(4)]
        hst = [self.sb("hst%d" % a, [128, 512]) for a in range(2)]
        bi = 0

        def load_blk(src, b, rows):
            nonlocal bi
            t = blk[bi % 4]
            bi += 1
            S.dma("sp", t.t[:, :rows, :], src[b][:, 0:rows * 128].rearrange("p (i w) -> p i w", w=128), writes=[t.r])
            return t
        pi_ = 0
        for o in range(2):
            for cs in range(2):
                S.dma("sp", vt.t[:], self.HSD[o, cs, 0:n, :].rearrange("(i p) c -> p i c", p=128), writes=[vt.r])
                for fb in range(nf):
                    t = load_blk(dC if cs == 0 else dS, fb, nt)
                    ps = self.ps[pi_ % 2]; hs_ = hst[pi_ % 2]
                    pi_ += 1
                    S.mm([lambda e, i=i, t=t, ps=ps: e.matmul(ps.t[:, :512], lhsT=t.t[:, i, :], rhs=vt.t[:, i, :], start=(i == 0), stop=(i == nt - 1))
                          for i in range(nt)], reads=[t.r, vt.r], writes=[ps.r])
                    S.op("act", lambda e, ps=ps, hs_=hs_, fb=fb: e.activation(out=hs_.t[:], in_=ps.t[:, :512], func=AF.Copy, scale=wft.t[:, fb:fb + 1]),
                         reads=[ps.r, wft.r], writes=[hs_.r])
                    S.dma("sp", self.HCS[o, cs, fb * 128:(fb + 1) * 128, :], hs_.t[:], reads=[hs_.r], writes=[S.dres("hcs", o, cs, fb)])
        S.dma("sp", vt.t[:], self.VTM[T0:T0 + n, :].rearrange("(i p) c -> p i c", p=128), reads=[S.dres("hcs", 1, 1, nf - 1)], writes=[vt.r])
        Hc = [self.sb("Hc%d" % a, [128, 512]) for a in range(2)]
        Hs = [self.sb("Hs%d" % a, [128, 512]) for a in range(2)]
        tq = [self.sb("tq%d" % a, [128, 512]) for a in range(4)]
        xs = [self.sb("xs%d" % a, [128, 512]) for a in range(2)]
        bt = self.sb("bt", [128, 512])
        bst = [self.sb("bst%d" % a, [128, 4, 128], BF16) for a in range(2)]
        for o in range(2):
            for fb in range(nf):
                tC = load_blk(dC, fb, nt)
                tS = load_blk(dS, fb, nt)
                hc, hs = Hc[fb % 2], Hs[fb % 2]
                S.dma("sp", hc.t[:], self.HCS[o, 0, fb * 128:(fb + 1) * 128, :], reads=[S.dres("hcs", o, 0, fb)], writes=[hc.r])
                S.dma("sp", hs.t[:], self.HCS[o, 1, fb * 128:(fb + 1) * 128, :], reads=[S.dres("hcs", o, 1, fb)], writes=[hs.r])
                pC, pS = self.ps[2 * (fb % 2)], self.ps[2 * (fb % 2) + 1]
                S.mm([lambda e, i=i, tC=tC, pC=pC: e.matmul(pC.t[:, :512], lhsT=tC.t[:, i, :], rhs=vt.t[:, i, :], start=(i == 0), stop=(i == nt - 1))
                      for i in range(nt)], reads=[tC.r, vt.r], writes=[pC.r])
                S.mm([lambda e, i=i, tS=tS, pS=pS: e.matmul(pS.t[:, :512], lhsT=tS.t[:, i, :], rhs=vt.t[:, i, :], start=(i == 0), stop=(i == nt - 1))
                      for i in range(nt)], reads=[tS.r, vt.r], writes=[pS.r])
                a1, a2, a3, a4 = tq
                S.op("dve", lambda e, pC=pC, hc=hc: e.tensor_tensor(out=a1.t[:], in0=pC.t[:, :512], in1=hc.t[:], op=ALU.mult), reads=[pC.r, hc.r], writes=[a1.r])
                S.op("dve", lambda e, pS=pS, hs=hs: e.tensor_tensor(out=a2.t[:], in0=pS.t[:, :512], in1=hs.t[:], op=ALU.mult), reads=[pS.r, hs.r], writes=[a2.r])
                S.op("pool", lambda e, fb=fb: e.tensor_tensor(out=Y[0].t[:, fb, :], in0=a1.t[:], in1=a2.t[:], op=ALU.subtract), reads=[a1.r, a2.r], writes=[Y[0].r])
                S.op("dve", lambda e, pC=pC, hs=hs: e.tensor_tensor(out=a3.t[:], in0=pC.t[:, :512], in1=hs.t[:], op=ALU.mult), reads=[pC.r, hs.r], writes=[a3.r])
                S.op("dve", lambda e, pS=pS, hc=hc: e.tensor_tensor(out=a4.t[:], in0=pS.t[:, :512], in1=hc.t[:], op=ALU.mult), reads=[pS.r, hc.r], writes=[a4.r])
                S.op("pool", lambda e, fb=fb: e.tensor_tensor(out=Y[1].t[:, fb, :], in0=a3.t[:], in1=a4.t[:], op=ALU.add), reads=[a3.r, a4.r], writes=[Y[1].r])
            for tb in range(nt):
                tC = load_blk(dC, tb, nf)
                tS = load_blk(dS, tb, nf)
                py = self.ps[4 + tb % 2]
                x_ = xs[tb % 2]
                S.dma("sp", x_.t[:], self.UTM[T0 + tb * 128:T0 + (tb + 1) * 128, o * 512:(o + 1) * 512], writes=[x_.r])
                fns = []
                for j in range(nf):
                    fns.append(lambda e, j=j, tC=tC, py=py: e.matmul(py.t[:, :512], lhsT=tC.t[:, j, :], rhs=Y[0].t[:, j, :], start=(j == 0), stop=False))
                    fns.append(lambda e, j=j, tS=tS, py=py: e.matmul(py.t[:, :512], lhsT=tS.t[:, j, :], rhs=Y[1].t[:, j, :], start=False, stop=(j == nf - 1)))
                S.mm(fns, reads=[tC.r, tS.r, Y[0].r, Y[1].r], writes=[py.r])
                if o == 0:
                    S.op("dve", lambda e, py=py, x_=x_, tb=tb: e.tensor_tensor(out=vt.t[:, tb, :], in0=py.t[:, :512], in1=x_.t[:], op=ALU.mult),
                         reads=[py.r, x_.r], writes=[vt.r])
                else:
                    S.op("dve", lambda e, py=py, x_=x_: e.tensor_tensor(out=bt.t[:], in0=py.t[:, :512], in1=x_.t[:], op=ALU.mult),
                         reads=[py.r, x_.r], writes=[bt.r])
                    pT = self.ps[6 + tb % 2]
                    S.mm([lambda e, c=c, pT=pT: e.transpose(pT.t[:, c * 128:(c + 1) * 128], bt.t[:, c * 128:(c + 1) * 128], self.ident.t[:])
                          for c in range(4)], reads=[bt.r, self.ident.r], writes=[pT.r])
                    b_ = bst[tb % 2]
                    S.op("act", lambda e, pT=pT, b_=b_: e.copy(out=b_.t[:], in_=pT.t[:, :].rearrange("p (c w) -> p c w", w=128)), reads=[pT.r], writes=[b_.r])
                    dst = self.catT[512:1024, T0 + tb * 128:T0 + (tb + 1) * 128].rearrange("(c p) t -> p c t", p=128)
                    S.dma("sp", dst, b_.t[:], reads=[b_.r], writes=[S.dres("catB", n, tb)])

    def ph_outproj(self, l):
        S = self.S
        wo = self.sb("wo", [128, 8, D], BF16)
        S.dma("sp", wo.t[:], (self.aboutb if l == 0 else self.naoutb).rearrange("(k p) n -> p k n", p=128), writes=[wo.r])
        xts = [self.sb("oxt%d" % a, [128, 8, GS]) for a in range(2)]
        cts = [self.sb("oct%d" % a, [128, 8, GS], BF16) for a in range(2)]
        lv = self.latT.rearrange("(c p) t -> p c t", p=128)
        cv = self.catT.rearrange("(c p) t -> p c t", p=128)
        gi = 0
        for (t0, n, s) in self.groups(l == 0):
            xt, ct = xts[gi % 2], cts[gi % 2]
            gi += 1
            S.dma("sp", xt.t[:, :, :n], lv[:, :, t0:t0 + n], writes=[xt.r])
            S.dma("sp", ct.t[:, :, :n], cv[:, :, t0:t0 + n], writes=[ct.r])
            gate = self.mcol(l, 1, 2, s)
            for dc in range(8):
                py = self.ps[dc % 4]
                S.mm([lambda e, k=k, py=py, dc=dc: e.matmul(py.t[:, :n], lhsT=wo.t[:, k, dc * 128:(dc + 1) * 128], rhs=ct.t[:, k, :n],
                                                          start=(k == 0), stop=(k == 7)) for k in range(8)], reads=[wo.r, ct.r], writes=[py.r])
                S.op("dve", lambda e, py=py, dc=dc: e.scalar_tensor_tensor(out=xt.t[:, dc, :n], in0=py.t[:, :n], scalar=gate[:, dc:dc + 1], in1=xt.t[:, dc, :n],
                                                                       op0=ALU.mult, op1=ALU.add), reads=[py.r, xt.r, self.msc.r], writes=[xt.r])
            self.store_group(xt, t0, n)

    def ph_naproj(self):
        S = self.S
        self.alloc_pro()
        win = self.sb("win", [128, 8, 3072], BF16)
        wv = self.nainb.rearrange("(k p) n -> p k n", p=128)
        for a in range(3):
            S.dma("sp", win.t[:, :, a * 1024:(a + 1) * 1024], wv[:, :, a * 1024:(a + 1) * 1024], writes=[win.r])
        qo = [self.sb("qo%d" % a, [128, GS], BF16) for a in range(2)]
        vo = [self.sb("vo%d" % a, [128, 16, 65], BF16) for a in range(2)]
        for v in vo:
            S.op("dve", lambda e, v=v: e.memset(v.t[:], 1.0), writes=[v.r])
        ci = 0
        for (t0, n, s) in self.groups(True):
            self.prologue(1, 1, t0, n, s, 6)
            hT = self.hT
            for which in range(2):
                if which == 0 and s == 1:
                    continue
                for hc in range(8):
                    col = which * 1024 + hc * 128
                    pa = self.ps[ci % 4]; q = qo[ci % 2]
                    ci += 1
                    S.mm([lambda e, k=k, pa=pa, col=col: e.matmul(pa.t[:, :n], lhsT=win.t[:, k, col:col + 128], rhs=hT.t[:, k, :n],
                                                                start=(k == 0), stop=(k == 7)) for k in range(8)], reads=[win.r, hT.r], writes=[pa.r])
                    if ci % 2 == 0:
                        S.op("act", lambda e, q=q, pa=pa: e.copy(out=q.t[:, :n], in_=pa.t[:, :n]), reads=[pa.r], writes=[q.r])
                    else:
                        S.op("dve", lambda e, q=q, pa=pa: e.tensor_copy(out=q.t[:, :n], in_=pa.t[:, :n]), reads=[pa.r], writes=[q.r])
                    dst = (self.QT if which == 0 else self.KT)[hc * 128:(hc + 1) * 128, t0:t0 + n]
                    S.dma("sp", dst, q.t[:, :n], reads=[q.r], writes=[S.dres("qk2", which, hc, t0)])
            for tt in range(n // 128):
                v = vo[tt % 2]
                for hf in range(2):
                    pv = self.ps[4 + hf]
                    S.mm([lambda e, k=k, pv=pv, tt=tt, hf=hf: e.matmul(pv.t[:, :512], lhsT=hT.t[:, k, tt * 128:(tt + 1) * 128],
                                                                      rhs=win.t[:, k, 2048 + hf * 512:2048 + (hf + 1) * 512], start=(k == 0), stop=(k == 7))
                          for k in range(8)], reads=[win.r, hT.r], writes=[pv.r])
                    S.op("act" if hf == 0 else "dve",
                         (lambda e, pv=pv, v=v, hf=hf: e.copy(out=v.t[:, hf * 8:(hf + 1) * 8, 0:64], in_=pv.t[:, :].rearrange("p (h d) -> p h d", d=64))) if hf == 0 else
                         (lambda e, pv=pv, v=v, hf=hf: e.tensor_copy(out=v.t[:, hf * 8:(hf + 1) * 8, 0:64], in_=pv.t[:, :].rearrange("p (h d) -> p h d", d=64))),
                         reads=[pv.r], writes=[v.r])
                S.dma("sp", self.VA[t0 + tt * 128:t0 + (tt + 1) * 128, :].rearrange("p (h d) -> p h d", d=65), v.t[:],
                      reads=[v.r], writes=[S.dres("va2", t0, tt)])

    def ph_na(self):
        S, din = self.S, self.din
        blocks = self.na_blocks
        ntp = self.ntypes
        ktb = [self.sb("ktb%d" % a, [128, TT], BF16) for a in range(2)]
        qtb = [self.sb("qtb%d" % a, [128, TT], BF16) for a in range(2)]
        vab = [self.sb("vab%d" % a, [128, TT // 128, 130], BF16) for a in range(2)]
        bib = [self.sb("bib%d" % a, [128, ntp * 2 * 7 * 128]) for a in range(2)]
        obb = [self.sb("obb%d" % a, [128, SEQ], BF16) for a in range(2)]
        tmA = [self.sb("tmA%d" % a, [128, 512]) for a in range(2)]
        tmB = [self.sb("tmB%d" % a, [128, 384]) for a in range(2)]
        PA = [self.sb("PA%d" % a, [128, 512], BF16) for a in range(2)]
        PB = [self.sb("PB%d" % a, [128, 384], BF16) for a in range(2)]
        o2 = [self.sb("o2_%d" % a, [128, 128]) for a in range(2)]
        rr = self.sb("rr", [128, 2])
        its = [(qb, hh) for qb in range(32) for hh in range(2)]

        def geom(qb):
            lo, hi, ty = blocks[qb]
            nk = (hi - lo) * 64
            tile0 = (NCTX + lo * 64) // 128
            nfull = nk // 128
            return ty, tile0, nfull, (nk % 128 != 0)
        for hc in range(8):
            kt_, qt_, va_, bi_, ob_ = ktb[hc % 2], qtb[hc % 2], vab[hc % 2], bib[hc % 2], obb[hc % 2]
            S.dma("sp", kt_.t[:], self.KT[hc * 128:(hc + 1) * 128, :], writes=[kt_.r])
            S.dma("sp", qt_.t[:, NCTX:], self.QT[hc * 128:(hc + 1) * 128, NCTX:], writes=[qt_.r])
            S.dma("sp", va_.t[:], self.VA[:, hc * 130:(hc + 1) * 130].rearrange("(i p) w -> p i w", p=128), writes=[va_.r])
            S.dma("sp", bi_.t[:], din["nab"][hc], writes=[bi_.r])

            def tiles(qb):
                ty, tile0, nfull, half = geom(qb)
                tl = [(0, 128), (1, 128)] + [(tile0 + a, 128) for a in range(nfull)]
                if half:
                    tl.append((tile0 + nfull, 64))
                return tl

            def scores(it):
                qb, hh = its[it]
                A, B = self.ps[2 + (it % 2) * 2], self.ps[3 + (it % 2) * 2]
                q0 = NCTX + qb * 128
                fns = []
                for idx, (tile, sz) in enumerate(tiles(qb)):
                    dst = A.t[:sz, idx * 128:(idx + 1) * 128] if idx < 4 else B.t[:sz, (idx - 4) * 128:(idx - 3) * 128]
                    fns.append(lambda e, dst=dst, tile=tile, sz=sz, hh=hh, q0=q0: e.matmul(
                        dst, lhsT=kt_.t[hh * 64:(hh + 1) * 64, tile * 128:tile * 128 + sz],
                        rhs=qt_.t[hh * 64:(hh + 1) * 64, q0:q0 + 128], start=True, stop=True))
                S.mm(fns, reads=[kt_.r, qt_.r], writes=[A.r, B.r])
            scores(0)
            for it, (qb, hh) in enumerate(its):
                if it + 1 < len(its):
                    scores(it + 1)
                ty = geom(qb)[0]
                tl = tiles(qb)
                nb = (len(tl) - 4) * 128
                A, B = self.ps[2 + (it % 2) * 2], self.ps[3 + (it % 2) * 2]
                ta, tb_, pa, pb = tmA[it % 2], tmB[it % 2], PA[it % 2], PB[it % 2]
                bo = (ty * 2 + hh) * 896
                S.op("dve", lambda e, A=A, ta=ta, bo=bo: e.scalar_tensor_tensor(
                    out=ta.t[:], in0=A.t[:, 0:512], scalar=0.125, in1=bi_.t[:, bo:bo + 512], op0=ALU.mult, op1=ALU.add),
                    reads=[A.r, bi_.r], writes=[ta.r])
                S.op("act", lambda e, ta=ta, pa=pa: e.activation(out=pa.t[:], in_=ta.t[:], func=AF.Exp), reads=[ta.r], writes=[pa.r])
                S.op("dve", lambda e, B=B, tb_=tb_, bo=bo, nb=nb: e.scalar_tensor_tensor(
                    out=tb_.t[:, :nb], in0=B.t[:, 0:nb], scalar=0.125, in1=bi_.t[:, bo + 512:bo + 512 + nb], op0=ALU.mult, op1=ALU.add),
                    reads=[B.r, bi_.r], writes=[tb_.r])
                S.op("act", lambda e, tb_=tb_, pb=pb, nb=nb: e.activation(out=pb.t[:, :nb], in_=tb_.t[:, :nb], func=AF.Exp), reads=[tb_.r], writes=[pb.r])
                po = self.ps[hh]
                fns = []
                for idx, (tile, sz) in enumerate(tl):
                    src = pa.t[:sz, idx * 128:(idx + 1) * 128] if idx < 4 else pb.t[:sz, (idx - 4) * 128:(idx - 3) * 128]
                    fns.append(lambda e, po=po, src=src, tile=tile, sz=sz, hh=hh, idx=idx, n_=len(tl): e.matmul(
                        po.t[:, 0:65], lhsT=src, rhs=va_.t[:sz, tile, hh * 65:(hh + 1) * 65], start=(idx == 0), stop=(idx == n_ - 1)))
                S.mm(fns, reads=[pa.r, pb.r, va_.r], writes=[po.r])
                oo = o2[qb % 2]
                S.op("dve", lambda e, po=po, hh=hh: e.reciprocal(out=rr.t[:, hh:hh + 1], in_=po.t[:, 64:65]), reads=[po.r], writes=[rr.r])
                S.op("dve", lambda e, po=po, hh=hh, oo=oo: e.tensor_scalar(out=oo.t[:, hh * 64:(hh + 1) * 64], in0=po.t[:, 0:64], scalar1=rr.t[:, hh:hh + 1],
                                                                       scalar2=None, op0=ALU.mult), reads=[po.r, rr.r], writes=[oo.r])
                if hh == 1:
                    pT = self.ps[6 + qb % 2]
                    S.mm([lambda e, pT=pT, oo=oo: e.transpose(pT.t[:, 0:128], oo.t[:], self.ident.t[:])], reads=[oo.r, self.ident.r], writes=[pT.r])
                    S.op("act", lambda e, pT=pT, qb=qb: e.copy(out=ob_.t[:, qb * 128:(qb + 1) * 128], in_=pT.t[:, 0:128]), reads=[pT.r], writes=[ob_.r])
            S.dma("sp", self.catT[hc * 128:(hc + 1) * 128, NCTX:], ob_.t[:], reads=[ob_.r], writes=[S.dres("catN", hc)])

    def ph_final(self):
        S = self.S
        self.alloc_pro()
        fw = self.sb("fw", [128, 8])
        S.dma("sp", fw.t[:], self.din["fnormT"], writes=[fw.r])
        yT = [self.sb("yT%d" % a, [128, 8, GS]) for a in range(2)]
        ot = [self.sb("ot%d" % a, [128, D]) for a in range(2)]
        gi = 0
        oi = 0
        for (t0, n, s) in self.groups(False):
            xt = self.prologue(1, 2, t0, n, s, 6, want_h=False)
            y = yT[gi % 2]
            gi += 1
            for c in range(8):
                S.op("dve", lambda e, c=c, y=y, xt=xt: e.scalar_tensor_tensor(
                    out=y.t[:, c, :n], in0=xt.t[:, c, :n], scalar=fw.t[:, c:c + 1], in1=self.rstd.t[:, :n], op0=ALU.mult, op1=ALU.mult),
                    reads=[xt.r, fw.r, self.rstd.r], writes=[y.r])
            for tt in range(n // 128):
                o = ot[oi % 2]
                oi += 1
                for hf in range(2):
                    ps = self.ps[hf * 2 + (tt % 2)]
                    S.mm([lambda e, c=c, ps=ps, hf=hf, tt=tt, y=y: e.transpose(ps.t[:, c * 128:(c + 1) * 128], y.t[:, hf * 4 + c, tt * 128:(tt + 1) * 128],
                                                                            self.ident.t[:]) for c in range(4)], reads=[y.r, self.ident.r], writes=[ps.r])
                    if hf == 0:
                        S.op("act", lambda e, ps=ps, o=o: e.copy(out=o.t[:, 0:512], in_=ps.t[:, :]), reads=[ps.r], writes=[o.r])
                    else:
                        S.op("dve", lambda e, ps=ps, o=o: e.tensor_copy(out=o.t[:, 512:1024], in_=ps.t[:, :]), reads=[ps.r], writes=[o.r])
                r0 = t0 - NCTX + tt * 128
                S.dma("sp", self.out[r0:r0 + 128, :], o.t[:], reads=[o.r], writes=[S.dres("out", r0)])


def _host_inputs(inputs, b, consts, nab):
    f32 = np.float32
    g = lambda k: np.asarray(inputs[k], dtype=f32)
    m = {}
    m["x"] = np.ascontiguousarray(g("x")[b])
    m["ctxi"] = np.ascontiguousarray(g("ctx")[b])
    sv = np.stack([_fm(g("c")[b], 8), _fm(g("c_ctx"), 8)], axis=-1)
    m["sv"] = np.ascontiguousarray(sv.reshape(128, 16))
    m["mod_w"] = g("mod_w")
    mb = np.stack([_fm(g("mod_b")[l], 72) for l in range(2)], axis=1)
    m["mod_b2"] = np.ascontiguousarray(np.repeat(mb[:, :, None, :], 2, axis=2).reshape(128, 288))
    nw = g("norm_w")
    nt = np.stack([np.stack([_fm(nw[l, k], 8) for k in range(3)], axis=1) for l in range(2)], axis=1)
    m["normT2"] = np.ascontiguousarray(np.repeat(nt[:, :, :, None, :], 2, axis=3).reshape(128, 96))
    m["fnormT"] = _fm(g("final_norm_w"), 8)
    m["ffn_w1"] = g("ffn_w1"); m["ffn_w3"] = g("ffn_w3"); m["ffn_w2"] = g("ffn_w2")
    wi = g("ab_w_in")[0]
    m["ab_w_in"] = wi
    perm = consts["perm"]
    cols = np.concatenate([hc * 128 + perm for hc in range(4)] + [512 + hc * 128 + perm for hc in range(4)])
    m["ab_w_inp"] = np.ascontiguousarray(wi[:, cols])
    m["ab_w_out"] = g("ab_w_out")[0]
    m["lamp"] = np.ascontiguousarray(np.broadcast_to(g("diff_lambda")[0].reshape(1, 256), (128, 256)))
    m["subw"] = np.ascontiguousarray(g("diff_subln_w")[0].reshape(128, 1))
    cw = g("hy_conv_w")[0]
    m["hcw"] = np.ascontiguousarray(np.stack([_fm(cw[j], 12) for j in range(3)], axis=-1).reshape(128, 36))
    m["hcb"] = _fm(g("hy_conv_b")[0], 12)
    m["fw0"] = g("hy_f_w0")[0]; m["fb0"] = np.ascontiguousarray(g("hy_f_b0")[0].reshape(64, 1))
    m["fw1"] = g("hy_f_w1")[0]; m["fb1"] = np.ascontiguousarray(g("hy_f_b1")[0].T)
    m["ffreq"] = np.ascontiguousarray(g("hy_f_freq")[0].reshape(64, 1))
    m["fwout"] = g("hy_f_wout")[0]
    m["hskip"] = np.ascontiguousarray(g("hy_bias")[0].reshape(1, 1024))
    m["na_w_in"] = g("na_w_in")[0]; m["na_w_out"] = g("na_w_out")[0]
    m["nab"] = nab
    for k in ("ident", "ropeC", "ropeS", "dftC", "dftS", "wf", "dftCc", "dftSc", "wfc", "zT", "zTc", "tl", "tlc", "deltas"):
        m[k] = consts[k]
    return m


_PROG = {}


def run(inputs, cores, dbg=None):
    consts = _consts()
    nab, blocks, nt = _na_bias(np.asarray(inputs["na_rpb"], np.float32)[0])
    key = (dbg,)
    if key not in _PROG:
        p = Prog(dbg=dbg, nab_cols=nab.shape[2], na_blocks=blocks)
        p.ntypes = nt
        _PROG[key] = p.build()
    nc = _PROG[key]
    in_maps = [_host_inputs(inputs, b, consts, nab) for b in cores]
    res = run_bass_kernel_spmd(nc, in_maps, core_ids=list(range(len(cores))))
    return res


def kernel(**inputs):
    res = run(inputs, list(range(8)))
    return np.stack([np.asarray(r["out"], dtype=np.float32) for r in res.results], axis=0)
```

```python
import math
from contextlib import ExitStack
import numpy as np
import ml_dtypes
import concourse.bass as bass
import concourse.mybir as mybir
from concourse.bass_utils import run_bass_kernel_spmd

F32 = mybir.dt.float32
BF16 = mybir.dt.bfloat16
AF = mybir.ActivationFunctionType
ALU = mybir.AluOpType
AX = mybir.AxisListType

D = 1024; SEQ = 4096; NCTX = 256; TT = SEQ + NCTX; DFF = 2816; NJ = DFF // 128
GRID_W = 64; EPS = 1e-6
GS = 512
SAME_ENGINE_SYNC = True


class Res:
    __slots__ = ("w", "r")

    def __init__(self):
        self.w = None
        self.r = {}


class Sched:
    NSLOT = 10

    def __init__(self, nc, es):
        self.nc = nc
        self.eng = {"pe": nc.tensor, "act": nc.scalar, "dve": nc.vector, "pool": nc.gpsimd, "sp": nc.sync}
        self.sem = {e: es.enter_context(nc.semaphore("s_" + e)) for e in ("pe", "act", "dve", "pool")}
        self.cnt = {e: 0 for e in self.sem}
        self.seen = {e: {} for e in self.eng}
        self.dsem = {q: [es.enter_context(nc.semaphore("d_%s_%d" % (q, i))) for i in range(self.NSLOT)]
                     for q in ("sp", "pool")}
        self.dcnt = {q: [0] * self.NSLOT for q in self.dsem}
        self.dnext = {q: 0 for q in self.dsem}
        self.dram = {}

    def dres(self, *key):
        r = self.dram.get(key)
        if r is None:
            r = self.dram[key] = Res()
        return r

    def _semof(self, key):
        return self.sem[key[1]] if key[0] == "c" else self.dsem[key[1]][key[2]]

    def _wait(self, eng, key, val):
        if self.seen[eng].get(key, 0) >= val:
            return
        self.seen[eng][key] = val
        self.eng[eng].wait_ge(self._semof(key), val)

    def _deps(self, eng, reads, writes):
        best = {}
        for r in reads:
            if r.w is not None:
                k, v = r.w
                if best.get(k, 0) < v:
                    best[k] = v
        for w in writes:
            if w.w is not None:
                k, v = w.w
                if best.get(k, 0) < v:
                    best[k] = v
            for k, v in w.r.items():
                if best.get(k, 0) < v:
                    best[k] = v
        for k, v in best.items():
            if k == ("c", eng) and (eng == "pe" or not SAME_ENGINE_SYNC):
                continue
            self._wait(eng, k, v)

    def _mark(self, key, val, reads, writes):
        for r in reads:
            if r.r.get(key, 0) < val:
                r.r[key] = val
        for w in writes:
            w.w = (key, val)
            w.r = {}

    def op(self, eng, fn, reads=(), writes=()):
        self._deps(eng, reads, writes)
        self.cnt[eng] += 1
        fn(self.eng[eng]).then_inc(self.sem[eng], 1)
        self._mark(("c", eng), self.cnt[eng], reads, writes)

    def mm(self, fns, reads=(), writes=()):
        self._deps("pe", reads, writes)
        ins = None
        for f in fns:
            ins = f(self.nc.tensor)
        self.cnt["pe"] += 1
        ins.then_inc(self.sem["pe"], 1)
        self._mark(("c", "pe"), self.cnt["pe"], reads, writes)

    def dma(self, q, out, in_, reads=(), writes=(), **kw):
        slot = self.dnext[q]
        self.dnext[q] = (slot + 1) % self.NSLOT
        key = ("d", q, slot)
        if self.dcnt[q][slot] > 0:
            self._wait(q, key, self.dcnt[q][slot])
        self._deps(q, reads, writes)
        self.dcnt[q][slot] += 16
        self.eng[q].dma_start(out=out, in_=in_, **kw).then_inc(self.dsem[q][slot], 16)
        self._mark(key, self.dcnt[q][slot], reads, writes)

    def barrier(self):
        keys = [(("c", e), self.cnt[e]) for e in self.cnt]
        for q in self.dsem:
            for i in range(self.NSLOT):
                keys.append((("d", q, i), self.dcnt[q][i]))
        for e in self.eng:
            for k, v in keys:
                if v > 0:
                    self._wait(e, k, v)


class Tl:
    def __init__(self, t, nres=1):
        self.t = t
        self.res = [Res() for _ in range(nres)]

    @property
    def r(self):
        return self.res[0]


def _bf(a):
    return np.asarray(a, dtype=np.float32).astype(ml_dtypes.bfloat16)


def _dft_blocks(n):
    ne = n + 128
    idx = np.arange(n, dtype=np.int64)
    m = (idx[:, None] * idx[None, :]) % (2 * n)
    ang = m.astype(np.float64) * (math.pi / n)
    C = np.zeros((ne, ne), np.float64)
    S = np.zeros((ne, ne), np.float64)
    C[:n, :n] = np.cos(ang)
    S[:n, :n] = np.sin(ang)
    alt = np.where(idx % 2 == 0, 1.0, -1.0)
    C[:n, n] = alt
    C[n, :n] = alt
    nt = ne // 128

    def blk(M):
        return np.ascontiguousarray(M.reshape(nt, 128, nt, 128).transpose(2, 1, 0, 3)).reshape(nt, 128, nt * 128)
    wf = np.full((ne,), 1.0 / n, np.float32)
    wf[0] = 0.5 / n
    wf[n] = 0.5 / n
    wf[n + 1:] = 0.0
    return _bf(blk(C)), _bf(blk(S)), np.ascontiguousarray(wf.reshape(nt, 128).T)


def _filter_consts(n):
    t = np.linspace(0.0, 1.0, n, dtype=np.float32)[:, None]
    w = (2.0 * math.pi / n) * np.arange(n, dtype=np.float32)[:, None]
    bands = np.linspace(1e-4, 15, 16, dtype=np.float32)[None, :]
    z = np.concatenate([t, np.cos(bands * w), -np.sin(bands * w)], axis=-1).astype(np.float32)
    tl = np.ascontiguousarray((-t[:, 0]).reshape(n // 128, 128).T).astype(np.float32)
    return np.ascontiguousarray(z.T), tl


def _rope_tables():
    t = np.arange(SEQ)
    pos = (t // GRID_W, t % GRID_W)
    inv = (10000.0 ** (-np.arange(16, dtype=np.float32) / 16)).astype(np.float32)
    C = np.zeros((128, SEQ), np.float32)
    Sg = np.zeros((128, SEQ), np.float32)
    for m in range(2):
        for a in range(2):
            ang = pos[a].astype(np.float32)[None, :] * inv[:, None]
            c = np.cos(ang).astype(np.float32)
            s = np.sin(ang).astype(np.float32)
            b = m * 64 + a * 32
            C[b:b + 16] = c
            C[b + 16:b + 32] = c
            Sg[b:b + 16] = -s
            Sg[b + 16:b + 32] = s
    perm = np.zeros(128, np.int64)
    for m in range(2):
        for a in range(2):
            b = m * 64 + a * 32
            perm[b:b + 16] = np.arange(b + 16, b + 32)
            perm[b + 16:b + 32] = np.arange(b, b + 16)
    return C, Sg, perm


def _na_geometry():
    blocks = []
    types = {}
    for qb in range(32):
        r0 = 2 * qb
        rs = [min(max(r - 4, 0), 56) for r in (r0, r0 + 1)]
        lo, hi = min(rs), max(rs) + 8
        sig = (hi - lo, rs[0] - lo, rs[1] - lo, r0 - lo)
        if sig not in types:
            types[sig] = len(types)
        blocks.append((lo, hi, types[sig]))
    return blocks, types


def _na_bias(rpb):
    blocks, types = _na_geometry()
    nt = len(types)
    out = np.zeros((nt, 16, 7 * 128, 128), np.float32)
    out[:, :, 256:, :] = -30000.0
    kk = np.arange(640)
    ki, kc = kk // 64, kk % 64
    qq = np.arange(128)
    qj, qc = qq // 64, qq % 64
    cs = np.clip(qc - 8, 0, 48)
    for sig, ti in types.items():
        nrows, rs0, rs1, r0l = sig
        rs = np.array([rs0, rs1])[qj]
        qr = r0l + qj
        valid = (ki[:, None] < nrows) & (ki[:, None] >= rs[None, :]) & (ki[:, None] < rs[None, :] + 8) \
            & (kc[:, None] >= cs[None, :]) & (kc[:, None] < cs[None, :] + 16)
        dr = np.clip(ki[:, None] - qr[None, :] + 7, 0, 14)
        dc = np.clip(kc[:, None] - qc[None, :] + 15, 0, 30)
        g = rpb[:, dr, dc]
        out[ti, :, 256:, :] = np.where(valid[None], g, np.float32(-30000.0))
    o = out.reshape(nt, 8, 2, 7, 128, 128).transpose(1, 4, 0, 2, 3, 5)
    return np.ascontiguousarray(o).reshape(8, 128, nt * 2 * 7 * 128), blocks, nt


_CONSTS = {}


def _consts():
    if _CONSTS:
        return _CONSTS
    c = _CONSTS
    c["dftC"], c["dftS"], c["wf"] = _dft_blocks(SEQ)
    c["dftCc"], c["dftSc"], c["wfc"] = _dft_blocks(NCTX)
    c["zT"], c["tl"] = _filter_consts(SEQ)
    c["zTc"], c["tlc"] = _filter_consts(NCTX)
    hy_min = math.log(1e-2) / 1.5
    hy_max = math.log(1e-2) / 0.3
    deltas = np.abs(np.linspace(hy_min, hy_max, 512, dtype=np.float32))
    c["deltas"] = np.ascontiguousarray(np.broadcast_to(deltas[None, :], (128, 512))).astype(np.float32)
    c["ropeC"], c["ropeS"], c["perm"] = _rope_tables()
    c["ident"] = np.eye(128, dtype=np.float32)
    return c


def _fm(v, nch):
    return np.ascontiguousarray(np.asarray(v, np.float32).reshape(nch, 128).T)


class Prog:
    def __init__(self, dbg=None, nab_cols=0, na_blocks=None):
        self.dbg = dbg
        self.nab_cols = nab_cols
        self.na_blocks = na_blocks
        self.nc = bass.Bass("TRN2", target_bir_lowering=False)
        self.din = {}

    def inp(self, name, shape, dt=F32):
        self.din[name] = self.nc.dram_tensor(name, list(shape), dt, kind="ExternalInput").ap()
        return self.din[name]

    def scr(self, name, shape, dt):
        return self.nc.dram_tensor(name, list(shape), dt).ap()

    def sb(self, name, shape, dt=F32, nres=1):
        self.uid = getattr(self, "uid", 0) + 1
        return Tl(self.es.enter_context(self.nc.sbuf_tensor("sb%d_%s" % (self.uid, name), list(shape), dt)), nres)

    def build(self):
        nc = self.nc
        I = self.inp
        x = I("x", [SEQ, D]); ctxi = I("ctxi", [NCTX, D])
        I("sv", [128, 16]); I("mod_w", [2, D, 9 * D]); I("mod_b2", [128, 2 * 2 * 72]); I("normT2", [128, 2 * 3 * 2 * 8])
        I("fnormT", [128, 8])
        I("ffn_w1", [2, 2, D, DFF]); I("ffn_w3", [2, 2, D, DFF]); I("ffn_w2", [2, 2, DFF, D])
        I("ab_w_in", [D, 3072]); I("ab_w_inp", [D, 1024]); I("ab_w_out", [D, D])
        I("lamp", [128, 256]); I("subw", [128, 1])
        I("hcw", [128, 36]); I("hcb", [128, 12])
        I("fw0", [33, 64]); I("fb0", [64, 1]); I("fw1", [2, 64, 64]); I("fb1", [64, 2]); I("ffreq", [64, 1])
        I("fwout", [64, 2048]); I("hskip", [1, 1024])
        I("na_w_in", [D, 3072]); I("na_w_out", [D, D]); I("nab", [8, 128, self.nab_cols])
        I("ident", [128, 128]); I("ropeC", [128, SEQ]); I("ropeS", [128, SEQ])
        I("dftC", [33, 128, 33 * 128], BF16); I("dftS", [33, 128, 33 * 128], BF16); I("wf", [128, 33])
        I("dftCc", [3, 128, 3 * 128], BF16); I("dftSc", [3, 128, 3 * 128], BF16); I("wfc", [128, 3])
        I("zT", [33, SEQ]); I("zTc", [33, NCTX]); I("tl", [128, 32]); I("tlc", [128, 2]); I("deltas", [128, 512])
        self.out = nc.dram_tensor("out", [SEQ, D], F32, kind="ExternalOutput").ap()
        if self.dbg:
            self.dbgo = nc.dram_tensor("dbg", [D, TT], F32, kind="ExternalOutput").ap()
        S_ = self.scr
        self.latT = S_("latT", [D, TT], F32)
        self.w1b = [[S_("w1b%d%d" % (l, i), [D, DFF], BF16) for i in range(2)] for l in range(2)]
        self.w3b = [[S_("w3b%d%d" % (l, i), [D, DFF], BF16) for i in range(2)] for l in range(2)]
        self.w2b = [[S_("w2b%d%d" % (l, i), [DFF, D], BF16) for i in range(2)] for l in range(2)]
        self.abinb = S_("abinb", [D, 3072], BF16); self.abinpb = S_("abinpb", [D, 1024], BF16)
        self.aboutb = S_("aboutb", [D, D], BF16)
        self.nainb = S_("nainb", [D, 3072], BF16); self.naoutb = S_("naoutb", [D, D], BF16)
        self.QT = S_("QT", [D, TT], BF16); self.KT = S_("KT", [D, TT], BF16)
        self.VA = S_("VA", [TT, 1040], BF16)
        self.UH = S_("UH", [1536, TT], F32)
        self.UTM = S_("UTM", [TT, 1024], F32)
        self.VTM = S_("VTM", [TT, 512], BF16)
        self.HSD = S_("HSD", [2, 2, SEQ, 512], BF16)
        self.HCS = S_("HCS", [2, 2, 33 * 128, 512], F32)
        self.catT = S_("catT", [D, TT], BF16)
        with ExitStack() as es:
            self.es = es
            self.S = Sched(nc, es)
            self.ps = [Tl(es.enter_context(nc.psum_tensor("ps%d" % i, [128, 512], F32))) for i in range(8)]
            self.persist()
            self.phase(self.ph_setup)
            for l in range(2):
                self.phase(lambda: self.ph_ffn(l, 0))
                if self.dbg == "ffn%d0" % l:
                    break
                if l == 0:
                    self.phase(self.ph_abproj)
                    self.phase(self.ph_hyena_prep)
                    self.phase(self.ph_diffattn)
                    for nn in (NCTX, SEQ):
                        self.phase(lambda: self.ph_filters(nn))
                        self.phase(lambda: self.ph_hyena(nn))
                    if self.dbg == "cat":
                        break
                    self.phase(lambda: self.ph_outproj(0))
                else:
                    self.phase(self.ph_naproj)
                    self.phase(self.ph_na)
                    self.phase(lambda: self.ph_outproj(1))
                if self.dbg == "mix%d" % l:
                    break
                self.phase(lambda: self.ph_ffn(l, 1))
                if self.dbg == "ffn%d1" % l:
                    break
            if self.dbg:
                src = self.latT
                if self.dbg == "cat":
                    src = None
                if src is not None:
                    self.S.dma("sp", self.dbgo, src, reads=[], writes=[self.S.dres("dbgo")])
                else:
                    self.S.dma("pool", self.dbgo, self.catT, reads=[], writes=[self.S.dres("dbgo")])
            else:
                self.phase(self.ph_final)
            self.S.barrier()
        return nc

    def phase(self, fn):
        with ExitStack() as es:
            old = self.es
            self.es = es
            fn()
            self.S.barrier()
            self.es = old

    def persist(self):
        S = self.S
        self.ident = self.sb("ident", [128, 128])
        self.onesb = self.sb("onesb", [128, 128], BF16)
        self.msc = self.sb("msc", [128, 2 * 3 * 3 * 2 * 8])
        S.dma("sp", self.ident.t[:], self.din["ident"], writes=[self.ident.r])
        S.op("dve", lambda e: e.memset(self.onesb.t[:], 1.0), writes=[self.onesb.r])
        self.svs = self.sb("svs", [128, 16]); self.modT = self.sb("modT", [128, 2 * 2 * 72])
        self.mb2 = self.sb("mb2", [128, 2 * 2 * 72]); self.nT2 = self.sb("nT2", [128, 96])
        S.dma("sp", self.svs.t[:], self.din["sv"], writes=[self.svs.r])
        S.dma("sp", self.mb2.t[:], self.din["mod_b2"], writes=[self.mb2.r])
        S.dma("sp", self.nT2.t[:], self.din["normT2"], writes=[self.nT2.r])

    def mcol(self, l, k, ty, s):
        o = (((l * 3 + k) * 3 + ty) * 2 + s) * 8
        return self.msc.t[:, o:o + 8]

    def cast_weights(self, l):
        S, din = self.S, self.din
        for i in range(2):
            S.dma("pool", self.w1b[l][i], din["ffn_w1"][l, i], writes=[S.dres("w1b", l, i)])
            S.dma("pool", self.w3b[l][i], din["ffn_w3"][l, i], writes=[S.dres("w3b", l, i)])
            S.dma("pool", self.w2b[l][i], din["ffn_w2"][l, i], writes=[S.dres("w2b", l, i)])
        pairs = ((self.abinb, "ab_w_in"), (self.abinpb, "ab_w_inp"), (self.aboutb, "ab_w_out")) if l == 0 else \
            ((self.nainb, "na_w_in"), (self.naoutb, "na_w_out"))
        for dst, src in pairs:
            S.dma("pool", dst, din[src], writes=[S.dres(src)])

    def mod_alloc(self):
        self.mw = [self.sb("mw%d" % i, [128, 8, 1024]) for i in range(2)]
        self.mwi = 0

    def mod_block(self, l, b):
        S, din = self.S, self.din
        svs, modT, mb2 = self.svs, self.modT, self.mb2
        svv = svs.t[:].rearrange("p (k s) -> p k s", s=2)
        mwv = din["mod_w"][l].rearrange("(k p) n -> p k n", p=128)
        buf = self.mw[self.mwi % 2]
        self.mwi += 1
        S.dma("sp", buf.t[:], mwv[:, :, b * 1024:(b + 1) * 1024], writes=[buf.r])
        ps = self.ps[7]
        fns = []
        for jj in range(8):
            for k in range(8):
                fns.append(lambda e, jj=jj, k=k, buf=buf, ps=ps: e.matmul(
                    ps.t[:, 2 * jj:2 * jj + 2], lhsT=buf.t[:, k, jj * 128:(jj + 1) * 128], rhs=svv[:, k, :],
                    start=(k == 0), stop=(k == 7)))
        S.mm(fns, reads=[buf.r, svs.r], writes=[ps.r])
        psv = ps.t[:, 0:16].rearrange("p (j s) -> p s j", s=2)
        for s in range(2):
            o = (l * 2 + s) * 72 + b * 8
            S.op("dve", lambda e, s=s, o=o, psv=psv: e.tensor_tensor(
                out=modT.t[:, o:o + 8], in0=psv[:, s, :], in1=mb2.t[:, o:o + 8], op=ALU.add),
                reads=[ps.r, mb2.r], writes=[modT.r])

    def mod_finish(self, l):
        S = self.S
        modT, nT2 = self.modT, self.nT2
        for k in range(3):
            for s in range(2):
                mo = (l * 2 + s) * 72
                no = ((l * 3 + k) * 2 + s) * 8
                sc = modT.t[:, mo + (3 * k + 1) * 8: mo + (3 * k + 2) * 8]
                sh = modT.t[:, mo + (3 * k) * 8: mo + (3 * k + 1) * 8]
                gt = modT.t[:, mo + (3 * k + 2) * 8: mo + (3 * k + 3) * 8]
                S.op("dve", lambda e, sc=sc, no=no, l=l, k=k, s=s: e.scalar_tensor_tensor(
                    out=self.mcol(l, k, 0, s), in0=sc, scalar=1.0, in1=nT2.t[:, no:no + 8],
                    op0=ALU.add, op1=ALU.mult), reads=[modT.r, nT2.r], writes=[self.msc.r])
                S.op("dve", lambda e, sh=sh, l=l, k=k, s=s: e.tensor_copy(out=self.mcol(l, k, 1, s), in_=sh),
                     reads=[modT.r], writes=[self.msc.r])
                S.op("dve", lambda e, gt=gt, l=l, k=k, s=s: e.tensor_scalar(
                    out=self.mcol(l, k, 2, s), in0=gt, scalar1=(1.0 if k == 1 else 0.5), scalar2=None,
                    op0=ALU.mult), reads=[modT.r], writes=[self.msc.r])

    def ph_setup(self):
        S, din = self.S, self.din
        self.cast_weights(0)
        S.op("act", lambda e: e.activation(out=self.svs.t[:], in_=self.svs.t[:], func=AF.Silu), reads=[self.svs.r], writes=[self.svs.r])
        self.mod_alloc()
        self.xT_alloc()
        ti = 0
        for b in range(9):
            self.mod_block(0, b)
            for _ in range(4 if b < 8 else 2):
                self.xT_tile(ti)
                ti += 1
        self.mod_finish(0)

    def xT_alloc(self):
        self.xin = [self.sb("xin%d" % i, [128, D]) for i in range(2)]
        self.xo = [self.sb("xo%d" % i, [128, 8, 128]) for i in range(2)]

    def xT_tile(self, ti):
        S = self.S
        src = self.din["ctxi"][ti * 128:(ti + 1) * 128, :] if ti < 2 else self.din["x"][(ti - 2) * 128:(ti - 1) * 128, :]
        xi, o = self.xin[ti % 2], self.xo[ti % 2]
        S.dma("sp", xi.t[:], src, writes=[xi.r])
        for h in range(2):
            ps = self.ps[(ti % 2) * 2 + h]
            S.mm([lambda e, c=c, ps=ps, xi=xi, h=h: e.transpose(
                ps.t[:, c * 128:(c + 1) * 128], xi.t[:, (h * 4 + c) * 128:(h * 4 + c + 1) * 128], self.ident.t[:])
                for c in range(4)], reads=[xi.r, self.ident.r], writes=[ps.r])
            ov = o.t[:, h * 4:(h + 1) * 4, :]
            pv = ps.t[:, :].rearrange("p (c t) -> p c t", t=128)
            if h == 0:
                S.op("act", lambda e, ov=ov, pv=pv: e.copy(out=ov, in_=pv), reads=[ps.r], writes=[o.r])
            else:
                S.op("dve", lambda e, ov=ov, pv=pv: e.tensor_copy(out=ov, in_=pv), reads=[ps.r], writes=[o.r])
        dst = self.latT.rearrange("(c p) t -> p c t", p=128)[:, :, ti * 128:(ti + 1) * 128]
        S.dma("sp", dst, o.t[:], reads=[o.r], writes=[S.dres("latT", ti)])

    def groups(self, with_ctx=True):
        g = [(0, NCTX, 1)] if with_ctx else []
        return g + [(NCTX + GS * i, GS, 0) for i in range(SEQ // GS)]

    def alloc_pro(self):
        self.xt = [self.sb("xt%d" % i, [128, 8, GS]) for i in range(2)]
        self.sq = self.sb("sq", [128, 8, GS], BF16)
        self.rstd = self.sb("rstd", [128, GS])
        self.ptmp = [self.sb("ptmp%d" % i, [128, GS]) for i in range(2)]
        self.hTs = [self.sb("hT%d" % i, [128, 8, GS], BF16) for i in range(2)]
        self.hT = self.hTs[0]
        self.gi = 0

    def prologue(self, l, k, t0, n, s, psb, want_h=True):
        S = self.S
        xt = self.xt[self.gi % 2]
        self.hT = self.hTs[self.gi % 2]
        self.gi += 1
        lv = self.latT.rearrange("(c p) t -> p c t", p=128)[:, :, t0:t0 + n]
        S.dma("sp", xt.t[:, :, :n], lv, writes=[xt.r])
        sq, rstd, hT = self.sq, self.rstd, self.hT
        S.op("act", lambda e: e.activation(out=sq.t[:, :, :n], in_=xt.t[:, :, :n], func=AF.Square),
             reads=[xt.r], writes=[sq.r])
        ps = self.ps[psb]
        S.mm([lambda e, c=c: e.matmul(ps.t[:, :n], lhsT=self.onesb.t[:], rhs=sq.t[:, c, :n], start=(c == 0), stop=(c == 7))
              for c in range(8)], reads=[sq.r, self.onesb.r], writes=[ps.r])
        S.op("act", lambda e: e.activation(out=rstd.t[:, :n], in_=ps.t[:, :n], func=AF.Sqrt, bias=EPS, scale=1.0 / D),
             reads=[ps.r], writes=[rstd.r])
        S.op("dve", lambda e: e.reciprocal(out=rstd.t[:, :n], in_=rstd.t[:, :n]), reads=[rstd.r], writes=[rstd.r])
        if want_h:
            gs, sh = self.mcol(l, k, 0, s), self.mcol(l, k, 1, s)
            for c in range(8):
                tmp = self.ptmp[c % 2]
                S.op("dve", lambda e, c=c, tmp=tmp: e.scalar_tensor_tensor(
                    out=tmp.t[:, :n], in0=xt.t[:, c, :n], scalar=gs[:, c:c + 1], in1=rstd.t[:, :n],
                    op0=ALU.mult, op1=ALU.mult), reads=[xt.r, rstd.r, self.msc.r], writes=[tmp.r])
                S.op("act", lambda e, c=c, tmp=tmp: e.activation(
                    out=hT.t[:, c, :n], in_=tmp.t[:, :n], func=AF.Identity, bias=sh[:, c:c + 1], scale=1.0),
                    reads=[tmp.r, self.msc.r], writes=[hT.r])
        return xt

    def store_group(self, xt, t0, n):
        lv = self.latT.rearrange("(c p) t -> p c t", p=128)[:, :, t0:t0 + n]
        self.S.dma("sp", lv, xt.t[:, :, :n], reads=[xt.r], writes=[self.S.dres("latT", t0)])

    def ph_ffn(self, l, i):
        S = self.S
        k = 0 if i == 0 else 2
        self.alloc_pro()
        w2t = self.sb("w2t", [128, NJ, D], BF16)
        w2v = self.w2b[l][i].rearrange("(j p) n -> p j n", p=128)
        for a in range(2):
            S.dma("sp", w2t.t[:, a * 11:(a + 1) * 11, :], w2v[:, a * 11:(a + 1) * 11, :], writes=[w2t.r] if a == 0 else [w2t.r])
        wb1 = [self.sb("wb1_%d" % a, [128, 8, 256], BF16) for a in range(3)]
        wb3 = [self.sb("wb3_%d" % a, [128, 8, 256], BF16) for a in range(3)]
        actT = self.sb("actT", [128, NJ, GS], BF16)
        sg = [self.sb("sg%d" % a, [128, GS]) for a in range(2)]
        w1v = self.w1b[l][i].rearrange("(k p) n -> p k n", p=128)
        w3v = self.w3b[l][i].rearrange("(k p) n -> p k n", p=128)
        need_ctx = (l == 0) or (i == 0)
        bi = 0
        grps = self.groups(need_ctx)
        nxt = (self.prologue(l, k, grps[0][0], grps[0][1], grps[0][2], 6), self.hT)
        for gidx, (t0, n, s) in enumerate(grps):
            xt, hT = nxt
            for nb in range(11):
                b1, b3 = wb1[bi % 3], wb3[bi % 3]
                bi += 1
                S.dma("sp", b1.t[:], w1v[:, :, nb * 256:(nb + 1) * 256], writes=[b1.r])
                S.dma("sp", b3.t[:], w3v[:, :, nb * 256:(nb + 1) * 256], writes=[b3.r])
                for jj in range(2):
                    j = nb * 2 + jj
                    pg, pu = self.ps[2 * (j % 2)], self.ps[2 * (j % 2) + 1]
                    S.mm([lambda e, kc=kc, b1=b1, pg=pg, jj=jj: e.matmul(
                        pg.t[:, :n], lhsT=b1.t[:, kc, jj * 128:(jj + 1) * 128], rhs=hT.t[:, kc, :n],
                        start=(kc == 0), stop=(kc == 7)) for kc in range(8)], reads=[b1.r, hT.r], writes=[pg.r])
                    S.mm([lambda e, kc=kc, b3=b3, pu=pu, jj=jj: e.matmul(
                        pu.t[:, :n], lhsT=b3.t[:, kc, jj * 128:(jj + 1) * 128], rhs=hT.t[:, kc, :n],
                        start=(kc == 0), stop=(kc == 7)) for kc in range(8)], reads=[b3.r, hT.r], writes=[pu.r])
                    sgt = sg[j % 2]
                    S.op("act", lambda e, pg=pg, sgt=sgt: e.activation(out=sgt.t[:, :n], in_=pg.t[:, :n], func=AF.Silu),
                         reads=[pg.r], writes=[sgt.r])
                    S.op("dve", lambda e, pu=pu, sgt=sgt, j=j: e.tensor_tensor(
                        out=actT.t[:, j, :n], in0=pu.t[:, :n], in1=sgt.t[:, :n], op=ALU.mult),
                        reads=[pu.r, sgt.r], writes=[actT.r])
            if gidx + 1 < len(grps):
                g2 = grps[gidx + 1]
                nxt = (self.prologue(l, k, g2[0], g2[1], g2[2], 6), self.hT)
            gate = self.mcol(l, k, 2, s)
            for dc in range(8):
                py = self.ps[4 + dc % 2]
                S.mm([lambda e, j=j, py=py, dc=dc: e.matmul(
                    py.t[:, :n], lhsT=w2t.t[:, j, dc * 128:(dc + 1) * 128], rhs=actT.t[:, j, :n],
                    start=(j == 0), stop=(j == NJ - 1)) for j in range(NJ)], reads=[w2t.r, actT.r], writes=[py.r])
                S.op("dve", lambda e, py=py, dc=dc: e.scalar_tensor_tensor(
                    out=xt.t[:, dc, :n], in0=py.t[:, :n], scalar=gate[:, dc:dc + 1], in1=xt.t[:, dc, :n],
                    op0=ALU.mult, op1=ALU.add), reads=[py.r, xt.r, self.msc.r], writes=[xt.r])
            self.store_group(xt, t0, n)

    def ph_abproj(self):
        S, din = self.S, self.din
        self.alloc_pro()
        win = self.sb("win", [128, 8, 3072], BF16)
        winp = self.sb("winp", [128, 8, 1024], BF16)
        wv = self.abinb.rearrange("(k p) n -> p k n", p=128)
        for a in range(3):
            S.dma("sp", win.t[:, :, a * 1024:(a + 1) * 1024], wv[:, :, a * 1024:(a + 1) * 1024], writes=[win.r])
        S.dma("sp", winp.t[:], self.abinpb.rearrange("(k p) n -> p k n", p=128), writes=[winp.r])
        rc = self.sb("rc", [128, GS]); rs = self.sb("rs", [128, GS])
        qo = [self.sb("qo%d" % a, [128, GS], BF16) for a in range(2)]
        t1 = [self.sb("t1_%d" % a, [128, GS]) for a in range(2)]
        t2 = [self.sb("t2_%d" % a, [128, GS]) for a in range(2)]
        vo = [self.sb("vo%d" % a, [128, 4, 129], BF16) for a in range(2)]
        uo = [self.sb("uo%d" % a, [128, GS]) for a in range(2)]
        for v in vo:
            S.op("dve", lambda e, v=v: e.memset(v.t[:], 1.0), writes=[v.r])
        ci = 0
        for (t0, n, s) in self.groups(True):
            self.prologue(0, 1, t0, n, s, 6)
            hT = self.hT
            if s == 0:
                S.dma("sp", rc.t[:, :n], din["ropeC"][:, t0 - NCTX:t0 - NCTX + n], writes=[rc.r])
                S.dma("sp", rs.t[:, :n], din["ropeS"][:, t0 - NCTX:t0 - NCTX + n], writes=[rs.r])
            for which in range(2):
                for hc in range(4):
                    col = which * 512 + hc * 128
                    pa, pb = self.ps[(ci % 2) * 2], self.ps[(ci % 2) * 2 + 1]
                    q = qo[ci % 2]; a1 = t1[ci % 2]; a2 = t2[ci % 2]
                    ci += 1
                    S.mm([lambda e, k=k, pa=pa, col=col: e.matmul(
                        pa.t[:, :n], lhsT=win.t[:, k, col:col + 128], rhs=hT.t[:, k, :n], start=(k == 0), stop=(k == 7))
                        for k in range(8)], reads=[win.r, hT.r], writes=[pa.r])
                    if s == 0:
                        S.mm([lambda e, k=k, pb=pb, col=col: e.matmul(
                            pb.t[:, :n], lhsT=winp.t[:, k, col:col + 128], rhs=hT.t[:, k, :n], start=(k == 0), stop=(k == 7))
                            for k in range(8)], reads=[winp.r, hT.r], writes=[pb.r])
                        S.op("dve", lambda e, pa=pa, a1=a1: e.tensor_tensor(out=a1.t[:, :n], in0=pa.t[:, :n], in1=rc.t[:, :n], op=ALU.mult),
                             reads=[pa.r, rc.r], writes=[a1.r])
                        S.op("dve", lambda e, pb=pb, a2=a2: e.tensor_tensor(out=a2.t[:, :n], in0=pb.t[:, :n], in1=rs.t[:, :n], op=ALU.mult),
                             reads=[pb.r, rs.r], writes=[a2.r])
                        S.op("pool", lambda e, q=q, a1=a1, a2=a2: e.tensor_tensor(out=q.t[:, :n], in0=a1.t[:, :n], in1=a2.t[:, :n], op=ALU.add),
                             reads=[a1.r, a2.r], writes=[q.r])
                    else:
                        S.op("act", lambda e, q=q, pa=pa: e.copy(out=q.t[:, :n], in_=pa.t[:, :n]), reads=[pa.r], writes=[q.r])
                    dst = (self.QT if which == 0 else self.KT)[hc * 128:(hc + 1) * 128, t0:t0 + n]
                    S.dma("sp", dst, q.t[:, :n], reads=[q.r], writes=[S.dres("qk", which, hc, t0)])
            for tt in range(n // 128):
                pv = self.ps[4 + tt % 2]
                v = vo[tt % 2]
                S.mm([lambda e, k=k, pv=pv, tt=tt: e.matmul(
                    pv.t[:, :512], lhsT=hT.t[:, k, tt * 128:(tt + 1) * 128], rhs=win.t[:, k, 1024:1536], start=(k == 0), stop=(k == 7))
                    for k in range(8)], reads=[win.r, hT.r], writes=[pv.r])
                S.op("act", lambda e, pv=pv, v=v: e.copy(out=v.t[:, :, 0:128], in_=pv.t[:, :].rearrange("p (h d) -> p h d", d=128)),
                     reads=[pv.r], writes=[v.r])
                S.dma("sp", self.VA[t0 + tt * 128:t0 + (tt + 1) * 128, 0:516].rearrange("p (h d) -> p h d", d=129), v.t[:],
                      reads=[v.r], writes=[S.dres("va", t0, tt)])
            for c in range(12):
                pu = self.ps[(c % 4)]
                u = uo[c % 2]
                S.mm([lambda e, k=k, pu=pu, c=c: e.matmul(
                    pu.t[:, :n], lhsT=win.t[:, k, 1536 + c * 128:1536 + (c + 1) * 128], rhs=hT.t[:, k, :n], start=(k == 0), stop=(k == 7))
                    for k in range(8)], reads=[win.r, hT.r], writes=[pu.r])
                if c % 2 == 0:
                    S.op("act", lambda e, pu=pu, u=u: e.copy(out=u.t[:, :n], in_=pu.t[:, :n]), reads=[pu.r], writes=[u.r])
                else:
                    S.op("dve", lambda e, pu=pu, u=u: e.tensor_copy(out=u.t[:, :n], in_=pu.t[:, :n]), reads=[pu.r], writes=[u.r])
                S.dma("sp", self.UH[c * 128:(c + 1) * 128, t0:t0 + n], u.t[:, :n], reads=[u.r], writes=[S.dres("uh", c, t0)])

    def ph_hyena_prep(self):
        S, din = self.S, self.din
        cw = self.sb("cw", [128, 36]); cb = self.sb("cb", [128, 12])
        S.dma("sp", cw.t[:], din["hcw"], writes=[cw.r]); S.dma("sp", cb.t[:], din["hcb"], writes=[cb.r])
        ub = [self.sb("ub%d" % a, [128, SEQ + 2]) for a in range(2)]
        uc = [self.sb("uc%d" % a, [128, SEQ]) for a in range(2)]
        st = [self.sb("st%d" % a, [128, 4, 128]) for a in range(2)]
        stb = [self.sb("stb%d" % a, [128, 4, 128], BF16) for a in range(2)]
        ci = 0
        gi = 0
        self.mod_alloc()
        mblk = 0
        for (T0, n) in ((0, NCTX), (NCTX, SEQ)):
            for c in range(12):
                if n == SEQ and mblk < 9:
                    self.mod_block(1, mblk)
                    mblk += 1
                    if mblk == 9:
                        self.mod_finish(1)
                b, u = ub[ci % 2], uc[ci % 2]
                ci += 1
                S.op("dve", lambda e, b=b: e.memset(b.t[:, 0:1], 0.0), writes=[b.r])
                S.op("dve", lambda e, b=b: e.memset(b.t[:, n + 1:n + 2], 0.0), writes=[b.r])
                S.dma("sp", b.t[:, 1:n + 1], self.UH[c * 128:(c + 1) * 128, T0:T0 + n], writes=[b.r])
                S.op("act", lambda e, b=b, u=u, c=c: e.activation(out=u.t[:, :n], in_=b.t[:, 1:n + 1], func=AF.Identity,
                                                                 bias=cb.t[:, c:c + 1], scale=cw.t[:, 3 * c + 1:3 * c + 2]),
                     reads=[b.r, cw.r, cb.r], writes=[u.r])
                S.op("dve", lambda e, b=b, u=u, c=c: e.scalar_tensor_tensor(out=u.t[:, :n], in0=b.t[:, 0:n], scalar=cw.t[:, 3 * c:3 * c + 1],
                                                                        in1=u.t[:, :n], op0=ALU.mult, op1=ALU.add),
                     reads=[b.r, u.r, cw.r], writes=[u.r])
                S.op("dve", lambda e, b=b, u=u, c=c: e.scalar_tensor_tensor(out=u.t[:, :n], in0=b.t[:, 2:n + 2], scalar=cw.t[:, 3 * c + 2:3 * c + 3],
                                                                        in1=u.t[:, :n], op0=ALU.mult, op1=ALU.add),
                     reads=[b.r, u.r, cw.r], writes=[u.r])
                for g4 in range(n // 512 if n >= 512 else 1):
                    ntl = min(4, n // 128)
                    ps = self.ps[gi % 4]
                    S.mm([lambda e, a=a, ps=ps, u=u, g4=g4: e.transpose(
                        ps.t[:, a * 128:(a + 1) * 128], u.t[:, (g4 * 4 + a) * 128:(g4 * 4 + a + 1) * 128], self.ident.t[:])
                        for a in range(ntl)], reads=[u.r, self.ident.r], writes=[ps.r])
                    pvw = ps.t[:, :ntl * 128].rearrange("p (a w) -> p a w", w=128)
                    r0 = T0 + g4 * 512
                    if c < 8:
                        sx = st[gi % 2]
                        S.op("act" if gi % 2 == 0 else "dve",
                             (lambda e, sx=sx, pvw=pvw: e.copy(out=sx.t[:, :ntl, :], in_=pvw)) if gi % 2 == 0 else
                             (lambda e, sx=sx, pvw=pvw: e.tensor_copy(out=sx.t[:, :ntl, :], in_=pvw)),
                             reads=[ps.r], writes=[sx.r])
                        dst = self.UTM[r0:r0 + ntl * 128, c * 128:(c + 1) * 128].rearrange("(a p) w -> p a w", p=128)
                        S.dma("sp", dst, sx.t[:, :ntl, :], reads=[sx.r], writes=[S.dres("utm", c, r0)])
                    else:
                        sx = stb[gi % 2]
                        S.op("act" if gi % 2 == 0 else "dve",
                             (lambda e, sx=sx, pvw=pvw: e.copy(out=sx.t[:, :ntl, :], in_=pvw)) if gi % 2 == 0 else
                             (lambda e, sx=sx, pvw=pvw: e.tensor_copy(out=sx.t[:, :ntl, :], in_=pvw)),
                             reads=[ps.r], writes=[sx.r])
                        dst = self.VTM[r0:r0 + ntl * 128, (c - 8) * 128:(c - 7) * 128].rearrange("(a p) w -> p a w", p=128)
                        S.dma("sp", dst, sx.t[:, :ntl, :], reads=[sx.r], writes=[S.dres("vtm", c, r0)])
                    gi += 1

    def ph_diffattn(self):
        S, din = self.S, self.din
        self.cast_weights(1)
        lam_init = 0.8 - 0.6 * math.exp(-0.3 * 0)
        lp = self.sb("lp", [128, 256]); pr = self.sb("pr", [128, 128]); sm = self.sb("sm", [128, 2])
        nlam = self.sb("nlam", [128, 1]); wsub = self.sb("wsub", [128, 1])
        S.dma("sp", lp.t[:], din["lamp"], writes=[lp.r]); S.dma("sp", wsub.t[:], din["subw"], writes=[wsub.r])
        S.op("dve", lambda e: e.tensor_tensor(out=pr.t[:, 0:64], in0=lp.t[:, 0:64], in1=lp.t[:, 64:128], op=ALU.mult), reads=[lp.r], writes=[pr.r])
        S.op("dve", lambda e: e.tensor_tensor(out=pr.t[:, 64:128], in0=lp.t[:, 128:192], in1=lp.t[:, 192:256], op=ALU.mult), reads=[lp.r], writes=[pr.r])
        S.op("dve", lambda e: e.reduce_sum(out=sm.t[:, 0:2], in_=pr.t[:, :].rearrange("p (a b) -> p a b", b=64), axis=AX.X), reads=[pr.r], writes=[sm.r])
        S.op("act", lambda e: e.activation(out=sm.t[:], in_=sm.t[:], func=AF.Exp), reads=[sm.r], writes=[sm.r])
        S.op("dve", lambda e: e.tensor_tensor(out=nlam.t[:], in0=sm.t[:, 1:2], in1=sm.t[:, 0:1], op=ALU.subtract), reads=[sm.r], writes=[nlam.r])
        S.op("dve", lambda e: e.tensor_scalar(out=nlam.t[:], in0=nlam.t[:], scalar1=-lam_init, scalar2=None, op0=ALU.add), reads=[nlam.r], writes=[nlam.r])
        S.op("dve", lambda e: e.tensor_scalar(out=wsub.t[:], in0=wsub.t[:], scalar1=1.0 - lam_init, scalar2=None, op0=ALU.mult), reads=[wsub.r], writes=[wsub.r])
        ktb = [self.sb("ktb%d" % a, [128, TT], BF16) for a in range(2)]
        qtb = [self.sb("qtb%d" % a, [128, TT], BF16) for a in range(2)]
        vab = [self.sb("vab%d" % a, [128, TT // 128, 129], BF16) for a in range(2)]
        cab = [self.sb("cab%d" % a, [128, TT], BF16) for a in range(2)]
        pts = [self.sb("pt%d" % a, [128, 512], BF16) for a in range(6)]
        sacc = [self.sb("sacc%d" % a, [128, 512]) for a in range(2)]
        rsb = self.sb("rsb", [128, 512]); osb = self.sb("osb", [128, 512]); o1 = self.sb("o1", [128, 512])
        sqb = self.sb("sqb", [128, 512], BF16); rst = self.sb("rst", [128, 512])
        onesf = self.sb("onesf", [128, 128])
        S.op("dve", lambda e: e.memset(onesf.t[:], 1.0), writes=[onesf.r])
        cnt = 0
        mi = 0
        for h in range(4):
            kt_, qt_, va_, ca_ = ktb[h % 2], qtb[h % 2], vab[h % 2], cab[h % 2]
            S.dma("sp", kt_.t[:], self.KT[h * 128:(h + 1) * 128, :], writes=[kt_.r])
            S.dma("sp", qt_.t[:], self.QT[h * 128:(h + 1) * 128, :], writes=[qt_.r])
            S.dma("sp", va_.t[:], self.VA[:, h * 129:(h + 1) * 129].rearrange("(i p) w -> p i w", p=128), writes=[va_.r])
            qgroups = [(0, NCTX, list(range(2)))] + [(NCTX + 512 * g, 512, list(range(TT // 128))) for g in range(SEQ // 512)]
            for (q0, qn, kts) in qgroups:
                OT = [self.ps[0], self.ps[1]]

                def score(i):
                    kt = kts[i]
                    for m in range(2):
                        pS = self.ps[4 + 2 * m + i % 2]
                        S.mm([lambda e, pS=pS, kt=kt, m=m: e.matmul(
                            pS.t[:, :qn], lhsT=kt_.t[m * 64:(m + 1) * 64, kt * 128:(kt + 1) * 128],
                            rhs=qt_.t[m * 64:(m + 1) * 64, q0:q0 + qn], start=True, stop=True)],
                            reads=[kt_.r, qt_.r], writes=[pS.r])
                score(0)
                for i, kt in enumerate(kts):
                    if i + 1 < len(kts):
                        score(i + 1)
                    for m in range(2):
                        pS = self.ps[4 + 2 * m + i % 2]
                        pt = pts[cnt % 6]
                        cnt += 1
                        sa = sacc[m]
                        S.op("act", lambda e, pS=pS, pt=pt: e.activation(out=pt.t[:, :qn], in_=pS.t[:, :qn], func=AF.Exp, scale=0.125),
                             reads=[pS.r], writes=[pt.r])
                        S.mm([lambda e, pt=pt, kt=kt, m=m: e.matmul(
                            OT[m].t[:, :qn], lhsT=va_.t[:, kt, 0:128], rhs=pt.t[:, :qn], start=(kt == kts[0]), stop=(kt == kts[-1]))],
                            reads=[pt.r, va_.r], writes=[OT[m].r])
                        aeng = "dve" if m == 0 else "pool"
                        if i == 0:
                            S.op(aeng, lambda e, pt=pt, sa=sa: e.tensor_copy(out=sa.t[:, :qn], in_=pt.t[:, :qn]), reads=[pt.r], writes=[sa.r])
                        else:
                            S.op(aeng, lambda e, pt=pt, sa=sa: e.tensor_tensor(out=sa.t[:, :qn], in0=sa.t[:, :qn], in1=pt.t[:, :qn], op=ALU.add),
                                 reads=[pt.r, sa.r], writes=[sa.r])
                for m in range(2):
                    SM = self.ps[2]
                    sa = sacc[m]
                    S.mm([lambda e, SM=SM, sa=sa: e.matmul(SM.t[:, :qn], lhsT=onesf.t[:], rhs=sa.t[:, :qn], start=True, stop=True)],
                         reads=[onesf.r, sa.r], writes=[SM.r])
                    S.op("dve", lambda e, SM=SM: e.reciprocal(out=rsb.t[:, :qn], in_=SM.t[:, :qn]), reads=[SM.r], writes=[rsb.r])
                    if m == 0:
                        S.op("dve", lambda e: e.tensor_tensor(out=osb.t[:, :qn], in0=OT[0].t[:, :qn], in1=rsb.t[:, :qn], op=ALU.mult),
                             reads=[OT[0].r, rsb.r], writes=[osb.r])
                    else:
                        S.op("dve", lambda e: e.tensor_tensor(out=o1.t[:, :qn], in0=OT[1].t[:, :qn], in1=rsb.t[:, :qn], op=ALU.mult),
                             reads=[OT[1].r, rsb.r], writes=[o1.r])
                S.op("dve", lambda e: e.scalar_tensor_tensor(out=osb.t[:, :qn], in0=o1.t[:, :qn], scalar=nlam.t[:, 0:1], in1=osb.t[:, :qn],
                                                             op0=ALU.mult, op1=ALU.add), reads=[o1.r, nlam.r, osb.r], writes=[osb.r])
                S.op("act", lambda e: e.activation(out=sqb.t[:, :qn], in_=osb.t[:, :qn], func=AF.Square), reads=[osb.r], writes=[sqb.r])
                SS = self.ps[3]
                S.mm([lambda e, SS=SS: e.matmul(SS.t[:, :qn], lhsT=self.onesb.t[:], rhs=sqb.t[:, :qn], start=True, stop=True)],
                     reads=[self.onesb.r, sqb.r], writes=[SS.r])
                S.op("act", lambda e, SS=SS: e.activation(out=rst.t[:, :qn], in_=SS.t[:, :qn], func=AF.Sqrt, bias=EPS, scale=1.0 / 128),
                     reads=[SS.r], writes=[rst.r])
                S.op("dve", lambda e: e.reciprocal(out=rst.t[:, :qn], in_=rst.t[:, :qn]), reads=[rst.r], writes=[rst.r])
                S.op("dve", lambda e: e.scalar_tensor_tensor(out=ca_.t[:, q0:q0 + qn], in0=osb.t[:, :qn], scalar=wsub.t[:, 0:1], in1=rst.t[:, :qn],
                                                             op0=ALU.mult, op1=ALU.mult), reads=[osb.r, wsub.r, rst.r], writes=[ca_.r])
            S.dma("sp", self.catT[h * 128:(h + 1) * 128, :], ca_.t[:], reads=[ca_.r], writes=[S.dres("catA", h)])

    def ph_filters(self, n):
        S, din = self.S, self.din
        cm = (n == NCTX)
        zt = self.sb("zt", [33, n]); w0 = self.sb("w0", [33, 64]); b0 = self.sb("b0", [64, 1])
        w1 = self.sb("w1", [64, 2, 64]); b1 = self.sb("b1", [64, 2]); fr = self.sb("fr", [64, 1]); wo = self.sb("wo", [64, 2048])
        skp = self.sb("skp", [1, 1024]); dl = self.sb("dl", [128, 512]); tl = self.sb("tl", [128, n // 128])
        S.dma("sp", zt.t[:], din["zTc" if cm else "zT"], writes=[zt.r])
        S.dma("sp", w0.t[:], din["fw0"], writes=[w0.r]); S.dma("sp", b0.t[:], din["fb0"], writes=[b0.r])
        S.dma("sp", w1.t[:], din["fw1"].rearrange("i k m -> k i m"), writes=[w1.r]); S.dma("sp", b1.t[:], din["fb1"], writes=[b1.r])
        S.dma("sp", fr.t[:], din["ffreq"], writes=[fr.r]); S.dma("sp", wo.t[:], din["fwout"], writes=[wo.r])
        S.op("dve", lambda e: e.tensor_scalar(out=fr.t[:], in0=fr.t[:], scalar1=1.0 / (2.0 * math.pi), scalar2=None, op0=ALU.mult), reads=[fr.r], writes=[fr.r])
        S.dma("sp", skp.t[:], din["hskip"], writes=[skp.r]); S.dma("sp", dl.t[:], din["deltas"], writes=[dl.r])
        S.dma("sp", tl.t[:], din["tlc" if cm else "tl"], writes=[tl.r])
        hid = [self.sb("hid%d" % a, [64, n]) for a in range(2)]
        tmp = [self.sb("ftmp%d" % a, [64, 512]) for a in range(2)]
        gsz = min(512, n)
        TWO_PI = 2.0 * math.pi
        OFFS = math.pi + 16.0 * math.pi
        ci = 0
        for layer in range(3):
            src = zt if layer == 0 else hid[(layer - 1) % 2]
            dst = hid[layer % 2]
            bias = b0.t[:, 0:1] if layer == 0 else b1.t[:, layer - 1:layer]
            for g in range(n // gsz):
                ps = self.ps[ci % 2]; tm = tmp[ci % 2]
                ci += 1
                if layer == 0:
                    S.mm([lambda e, ps=ps, g=g: e.matmul(ps.t[:64, :gsz], lhsT=w0.t[:, :], rhs=zt.t[:, g * gsz:(g + 1) * gsz], start=True, stop=True)],
                         reads=[w0.r, zt.r], writes=[ps.r])
                else:
                    S.mm([lambda e, ps=ps, g=g, src=src, layer=layer: e.matmul(ps.t[:64, :gsz], lhsT=w1.t[:, layer - 1, :], rhs=src.t[:, g * gsz:(g + 1) * gsz],
                                                                             start=True, stop=True)], reads=[w1.r, src.r], writes=[ps.r])
                S.op("dve", lambda e, ps=ps, tm=tm, bias=bias: e.tensor_scalar(out=tm.t[:, :gsz], in0=ps.t[:64, :gsz], scalar1=bias, scalar2=fr.t[:, 0:1],
                                                                           op0=ALU.add, op1=ALU.mult), reads=[ps.r, b0.r, b1.r, fr.r], writes=[tm.r])
                for rnd in range(2):
                    S.op("dve", lambda e, tm=tm: e.scalar_tensor_tensor(out=tm.t[:, :gsz], in0=tm.t[:, :gsz], scalar=-0.5, in1=tm.t[:, :gsz],
                                                                    op0=ALU.is_lt, op1=ALU.add), reads=[tm.r], writes=[tm.r])
                    S.op("dve", lambda e, tm=tm: e.scalar_tensor_tensor(out=tm.t[:, :gsz], in0=tm.t[:, :gsz], scalar=0.5, in1=tm.t[:, :gsz],
                                                                    op0=ALU.is_gt, op1=ALU.subtract), reads=[tm.r], writes=[tm.r])
                S.op("act", lambda e, tm=tm, dst=dst, g=g: e.activation(out=dst.t[:, g * gsz:(g + 1) * gsz], in_=tm.t[:, :gsz], func=AF.Sin, scale=TWO_PI),
                     reads=[tm.r], writes=[dst.r])
        hfin = hid[0]
        wnd = [self.sb("wnd%d" % a, [128, 512]) for a in range(2)]
        ff = [self.sb("ff%d" % a, [128, 512]) for a in range(4)]
        ho = [self.sb("ho%d" % a, [128, 512], BF16) for a in range(4)]
        hi_ = 0
        for tt in range(n // 128):
            wn = wnd[tt % 2]
            S.op("act", lambda e, wn=wn, tt=tt: e.activation(out=wn.t[:], in_=dl.t[:], func=AF.Exp, scale=tl.t[:, tt:tt + 1]),
                 reads=[dl.r, tl.r], writes=[wn.r])
            for o in range(2):
                for d in range(2):
                    cb = o * 2 + d
                    ps = self.ps[2 + cb]
                    S.mm([lambda e, ps=ps, cb=cb, tt=tt: e.matmul(ps.t[:, :512], lhsT=hfin.t[:, tt * 128:(tt + 1) * 128], rhs=wo.t[:, cb * 512:(cb + 1) * 512],
                                                                 start=True, stop=True)], reads=[hfin.r, wo.r], writes=[ps.r])
                    f = ff[cb]
                    S.op("dve", lambda e, ps=ps, f=f, wn=wn: e.tensor_tensor(out=f.t[:], in0=ps.t[:, :512], in1=wn.t[:], op=ALU.mult),
                         reads=[ps.r, wn.r], writes=[f.r])
                    if tt == 0:
                        if d == 0:
                            S.op("dve", lambda e, f=f, o=o: e.tensor_tensor(out=f.t[0:1, :], in0=f.t[0:1, :], in1=skp.t[0:1, o * 512:(o + 1) * 512], op=ALU.add),
                                 reads=[f.r, skp.r], writes=[f.r])
                        else:
                            S.op("dve", lambda e, f=f: e.memset(f.t[0:1, :], 0.0), reads=[f.r], writes=[f.r])
                f0, f1 = ff[o * 2], ff[o * 2 + 1]
                for sd in range(2):
                    h_ = ho[hi_ % 4]
                    hi_ += 1
                    S.op("pool", lambda e, h_=h_, f0=f0, f1=f1, sd=sd: e.tensor_tensor(out=h_.t[:], in0=f0.t[:], in1=f1.t[:],
                                                                                    op=(ALU.add if sd == 0 else ALU.subtract)),
                         reads=[f0.r, f1.r], writes=[h_.r])
                    S.dma("sp", self.HSD[o, sd, tt * 128:(tt + 1) * 128, :], h_.t[:], reads=[h_.r], writes=[S.dres("hsd", n, o, sd, tt)])

    def ph_hyena(self, n):
        S, din = self.S, self.din
        cm = (n == NCTX)
        T0 = 0 if cm else NCTX
        nt = n // 128
        nf = nt + 1
        dC, dS = (din["dftCc"], din["dftSc"]) if cm else (din["dftC"], din["dftS"])
        wft = self.sb("wft", [128, nf])
        S.dma("sp", wft.t[:], din["wfc" if cm else "wf"], writes=[wft.r])
        vt = self.sb("vt", [128, nt, 512], BF16)
        Y = [self.sb("Y%d" % a, [128, nf, 512], BF16) for a in range(2)]
        blk = [self.sb("blk%d" % a, [128, nf, 128], BF16) for a in range(4)]
        hst = [self.sb("hst%d" % a, [128, 512]) for a in range(2)]
        bi = 0

        def load_blk(src, b, rows):
            nonlocal bi
            t = blk[bi % 4]
            bi += 1
            S.dma("sp", t.t[:, :rows, :], src[b][:, 0:rows * 128].rearrange("p (i w) -> p i w", w=128), writes=[t.r])
            return t
        pi_ = 0
        for o in range(2):
            for cs in range(2):
                S.dma("sp", vt.t[:], self.HSD[o, cs, 0:n, :].rearrange("(i p) c -> p i c", p=128), writes=[vt.r])
                for fb in range(nf):
                    t = load_blk(dC if cs == 0 else dS, fb, nt)
                    ps = self.ps[pi_ % 2]; hs_ = hst[pi_ % 2]
                    pi_ += 1
                    S.mm([lambda e, i=i, t=t, ps=ps: e.matmul(ps.t[:, :512], lhsT=t.t[:, i, :], rhs=vt.t[:, i, :], start=(i == 0), stop=(i == nt - 1))
                          for i in range(nt)], reads=[t.r, vt.r], writes=[ps.r])
                    S.op("act", lambda e, ps=ps, hs_=hs_, fb=fb: e.activation(out=hs_.t[:], in_=ps.t[:, :512], func=AF.Copy, scale=wft.t[:, fb:fb + 1]),
                         reads=[ps.r, wft.r], writes=[hs_.r])
                    S.dma("sp", self.HCS[o, cs, fb * 128:(fb + 1) * 128, :], hs_.t[:], reads=[hs_.r], writes=[S.dres("hcs", o, cs, fb)])
        S.dma("sp", vt.t[:], self.VTM[T0:T0 + n, :].rearrange("(i p) c -> p i c", p=128), reads=[S.dres("hcs", 1, 1, nf - 1)], writes=[vt.r])
        Hc = [self.sb("Hc%d" % a, [128, 512]) for a in range(2)]
        Hs = [self.sb("Hs%d" % a, [128, 512]) for a in range(2)]
        tq = [self.sb("tq%d" % a, [128, 512]) for a in range(4)]
        xs = [self.sb("xs%d" % a, [128, 512]) for a in range(2)]
        bt = self.sb("bt", [128, 512])
        bst = [self.sb("bst%d" % a, [128, 4, 128], BF16) for a in range(2)]
        for o in range(2):
            for fb in range(nf):
                tC = load_blk(dC, fb, nt)
                tS = load_blk(dS, fb, nt)
                hc, hs = Hc[fb % 2], Hs[fb % 2]
                S.dma("sp", hc.t[:], self.HCS[o, 0, fb * 128:(fb + 1) * 128, :], reads=[S.dres("hcs", o, 0, fb)], writes=[hc.r])
                S.dma("sp", hs.t[:], self.HCS[o, 1, fb * 128:(fb + 1) * 128, :], reads=[S.dres("hcs", o, 1, fb)], writes=[hs.r])
                pC, pS = self.ps[2 * (fb % 2)], self.ps[2 * (fb % 2) + 1]
                S.mm([lambda e, i=i, tC=tC, pC=pC: e.matmul(pC.t[:, :512], lhsT=tC.t[:, i, :], rhs=vt.t[:, i, :], start=(i == 0), stop=(i == nt - 1))
                      for i in range(nt)], reads=[tC.r, vt.r], writes=[pC.r])
                S.mm([lambda e, i=i, tS=tS, pS=pS: e.matmul(pS.t[:, :512], lhsT=tS.t[:, i, :], rhs=vt.t[:, i, :], start=(i == 0), stop=(i == nt - 1))
                      for i in range(nt)], reads=[tS.r, vt.r], writes=[pS.r])
                a1, a2, a3, a4 = tq
                S.op("dve", lambda e, pC=pC, hc=hc: e.tensor_tensor(out=a1.t[:], in0=pC.t[:, :512], in1=hc.t[:], op=ALU.mult), reads=[pC.r, hc.r], writes=[a1.r])
                S.op("dve", lambda e, pS=pS, hs=hs: e.tensor_tensor(out=a2.t[:], in0=pS.t[:, :512], in1=hs.t[:], op=ALU.mult), reads=[pS.r, hs.r], writes=[a2.r])
                S.op("pool", lambda e, fb=fb: e.tensor_tensor(out=Y[0].t[:, fb, :], in0=a1.t[:], in1=a2.t[:], op=ALU.subtract), reads=[a1.r, a2.r], writes=[Y[0].r])
                S.op("dve", lambda e, pC=pC, hs=hs: e.tensor_tensor(out=a3.t[:], in0=pC.t[:, :512], in1=hs.t[:], op=ALU.mult), reads=[pC.r, hs.r], writes=[a3.r])
                S.op("dve", lambda e, pS=pS, hc=hc: e.tensor_tensor(out=a4.t[:], in0=pS.t[:, :512], in1=hc.t[:], op=ALU.mult), reads=[pS.r, hc.r], writes=[a4.r])
                S.op("pool", lambda e, fb=fb: e.tensor_tensor(out=Y[1].t[:, fb, :], in0=a3.t[:], in1=a4.t[:], op=ALU.add), reads=[a3.r, a4.r], writes=[Y[1].r])
            for tb in range(nt):
                tC = load_blk(dC, tb, nf)
                tS = load_blk(dS, tb, nf)
                py = self.ps[4 + tb % 2]
                x_ = xs[tb % 2]
                S.dma("sp", x_.t[:], self.UTM[T0 + tb * 128:T0 + (tb + 1) * 128, o * 512:(o + 1) * 512], writes=[x_.r])
                fns = []
                for j in range(nf):
                    fns.append(lambda e, j=j, tC=tC, py=py: e.matmul(py.t[:, :512], lhsT=tC.t[:, j, :], rhs=Y[0].t[:, j, :], start=(j == 0), stop=False))
                    fns.append(lambda e, j=j, tS=tS, py=py: e.matmul(py.t[:, :512], lhsT=tS.t[:, j, :], rhs=Y[1].t[:, j, :], start=False, stop=(j == nf - 1)))
                S.mm(fns, reads=[tC.r, tS.r, Y[0].r, Y[1].r], writes=[py.r])
                if o == 0:
                    S.op("dve", lambda e, py=py, x_=x_, tb=tb: e.tensor_tensor(out=vt.t[:, tb, :], in0=py.t[:, :512], in1=x_.t[:], op=ALU.mult),
                         reads=[py.r, x_.r], writes=[vt.r])
                else:
                    S.op("dve", lambda e, py=py, x_=x_: e.tensor_tensor(out=bt.t[:], in0=py.t[:, :512], in1=x_.t[:], op=ALU.mult),
                         reads=[py.r, x_.r], writes=[bt.r])
                    pT = self.ps[6 + tb % 2]
                    S.mm([lambda e, c=c, pT=pT: e.transpose(pT.t[:, c * 128:(c + 1) * 128], bt.t[:, c * 128:(c + 1) * 128], self.ident.t[:])
                          for c in range(4)], reads=[bt.r, self.ident.r], writes=[pT.r])
                    b_ = bst[tb % 2]
                    S.op("act", lambda e, pT=pT, b_=b_: e.copy(out=b_.t[:], in_=pT.t[:, :].rearrange("p (c w) -> p c w", w=128)), reads=[pT.r], writes=[b_.r])
                    dst = self.catT[512:1024, T0 + tb * 128:T0 + (tb + 1) * 128].rearrange("(c p) t -> p c t", p=128)
                    S.dma("sp", dst, b_.t[:], reads=[b_.r], writes=[S.dres("catB", n, tb)])

    def ph_outproj(self, l):
        S = self.S
        wo = self.sb("wo", [128, 8, D], BF16)
        S.dma("sp", wo.t[:], (self.aboutb if l == 0 else self.naoutb).rearrange("(k p) n -> p k n", p=128), writes=[wo.r])
        xts = [self.sb("oxt%d" % a, [128, 8, GS]) for a in range(2)]
        cts = [self.sb("oct%d" % a, [128, 8, GS], BF16) for a in range(2)]
        lv = self.latT.rearrange("(c p) t -> p c t", p=128)
        cv = self.catT.rearrange("(c p) t -> p c t", p=128)
        gi = 0
        for (t0, n, s) in self.groups(l == 0):
            xt, ct = xts[gi % 2], cts[gi % 2]
            gi += 1
            S.dma("sp", xt.t[:, :, :n], lv[:, :, t0:t0 + n], writes=[xt.r])
            S.dma("sp", ct.t[:, :, :n], cv[:, :, t0:t0 + n], writes=[ct.r])
            gate = self.mcol(l, 1, 2, s)
            for dc in range(8):
                py = self.ps[dc % 4]
                S.mm([lambda e, k=k, py=py, dc=dc: e.matmul(py.t[:, :n], lhsT=wo.t[:, k, dc * 128:(dc + 1) * 128], rhs=ct.t[:, k, :n],
                                                          start=(k == 0), stop=(k == 7)) for k in range(8)], reads=[wo.r, ct.r], writes=[py.r])
                S.op("dve", lambda e, py=py, dc=dc: e.scalar_tensor_tensor(out=xt.t[:, dc, :n], in0=py.t[:, :n], scalar=gate[:, dc:dc + 1], in1=xt.t[:, dc, :n],
                                                                       op0=ALU.mult, op1=ALU.add), reads=[py.r, xt.r, self.msc.r], writes=[xt.r])
            self.store_group(xt, t0, n)

    def ph_naproj(self):
        S = self.S
        self.alloc_pro()
        win = self.sb("win", [128, 8, 3072], BF16)
        wv = self.nainb.rearrange("(k p) n -> p k n", p=128)
        for a in range(3):
            S.dma("sp", win.t[:, :, a * 1024:(a + 1) * 1024], wv[:, :, a * 1024:(a + 1) * 1024], writes=[win.r])
        qo = [self.sb("qo%d" % a, [128, GS], BF16) for a in range(2)]
        vo = [self.sb("vo%d" % a, [128, 16, 65], BF16) for a in range(2)]
        for v in vo:
            S.op("dve", lambda e, v=v: e.memset(v.t[:], 1.0), writes=[v.r])
        ci = 0
        for (t0, n, s) in self.groups(True):
            self.prologue(1, 1, t0, n, s, 6)
            hT = self.hT
            for which in range(2):
                if which == 0 and s == 1:
                    continue
                for hc in range(8):
                    col = which * 1024 + hc * 128
                    pa = self.ps[ci % 4]; q = qo[ci % 2]
                    ci += 1
                    S.mm([lambda e, k=k, pa=pa, col=col: e.matmul(pa.t[:, :n], lhsT=win.t[:, k, col:col + 128], rhs=hT.t[:, k, :n],
                                                                start=(k == 0), stop=(k == 7)) for k in range(8)], reads=[win.r, hT.r], writes=[pa.r])
                    if ci % 2 == 0:
                        S.op("act", lambda e, q=q, pa=pa: e.copy(out=q.t[:, :n], in_=pa.t[:, :n]), reads=[pa.r], writes=[q.r])
                    else:
                        S.op("dve", lambda e, q=q, pa=pa: e.tensor_copy(out=q.t[:, :n], in_=pa.t[:, :n]), reads=[pa.r], writes=[q.r])
                    dst = (self.QT if which == 0 else self.KT)[hc * 128:(hc + 1) * 128, t0:t0 + n]
                    S.dma("sp", dst, q.t[:, :n], reads=[q.r], writes=[S.dres("qk2", which, hc, t0)])
            for tt in range(n // 128):
                v = vo[tt % 2]
                for hf in range(2):
                    pv = self.ps[4 + hf]
                    S.mm([lambda e, k=k, pv=pv, tt=tt, hf=hf: e.matmul(pv.t[:, :512], lhsT=hT.t[:, k, tt * 128:(tt + 1) * 128],
                                                                      rhs=win.t[:, k, 2048 + hf * 512:2048 + (hf + 1) * 512], start=(k == 0), stop=(k == 7))
                          for k in range(8)], reads=[win.r, hT.r], writes=[pv.r])
                    S.op("act" if hf == 0 else "dve",
                         (lambda e, pv=pv, v=v, hf=hf: e.copy(out=v.t[:, hf * 8:(hf + 1) * 8, 0:64], in_=pv.t[:, :].rearrange("p (h d) -> p h d", d=64))) if hf == 0 else
                         (lambda e, pv=pv, v=v, hf=hf: e.tensor_copy(out=v.t[:, hf * 8:(hf + 1) * 8, 0:64], in_=pv.t[:, :].rearrange("p (h d) -> p h d", d=64))),
                         reads=[pv.r], writes=[v.r])
                S.dma("sp", self.VA[t0 + tt * 128:t0 + (tt + 1) * 128, :].rearrange("p (h d) -> p h d", d=65), v.t[:],
                      reads=[v.r], writes=[S.dres("va2", t0, tt)])

    def ph_na(self):
        S, din = self.S, self.din
        blocks = self.na_blocks
        ntp = self.ntypes
        ktb = [self.sb("ktb%d" % a, [128, TT], BF16) for a in range(2)]
        qtb = [self.sb("qtb%d" % a, [128, TT], BF16) for a in range(2)]
        vab = [self.sb("vab%d" % a, [128, TT // 128, 130], BF16) for a in range(2)]
        bib = [self.sb("bib%d" % a, [128, ntp * 2 * 7 * 128]) for a in range(2)]
        obb = [self.sb("obb%d" % a, [128, SEQ], BF16) for a in range(2)]
        tmA = [self.sb("tmA%d" % a, [128, 512]) for a in range(2)]
        tmB = [self.sb("tmB%d" % a, [128, 384]) for a in range(2)]
        PA = [self.sb("PA%d" % a, [128, 512], BF16) for a in range(2)]
        PB = [self.sb("PB%d" % a, [128, 384], BF16) for a in range(2)]
        o2 = [self.sb("o2_%d" % a, [128, 128]) for a in range(2)]
        rr = self.sb("rr", [128, 2])
        its = [(qb, hh) for qb in range(32) for hh in range(2)]

        def geom(qb):
            lo, hi, ty = blocks[qb]
            nk = (hi - lo) * 64
            tile0 = (NCTX + lo * 64) // 128
            nfull = nk // 128
            return ty, tile0, nfull, (nk % 128 != 0)
        for hc in range(8):
            kt_, qt_, va_, bi_, ob_ = ktb[hc % 2], qtb[hc % 2], vab[hc % 2], bib[hc % 2], obb[hc % 2]
            S.dma("sp", kt_.t[:], self.KT[hc * 128:(hc + 1) * 128, :], writes=[kt_.r])
            S.dma("sp", qt_.t[:, NCTX:], self.QT[hc * 128:(hc + 1) * 128, NCTX:], writes=[qt_.r])
            S.dma("sp", va_.t[:], self.VA[:, hc * 130:(hc + 1) * 130].rearrange("(i p) w -> p i w", p=128), writes=[va_.r])
            S.dma("sp", bi_.t[:], din["nab"][hc], writes=[bi_.r])

            def tiles(qb):
                ty, tile0, nfull, half = geom(qb)
                tl = [(0, 128), (1, 128)] + [(tile0 + a, 128) for a in range(nfull)]
                if half:
                    tl.append((tile0 + nfull, 64))
                return tl

            def scores(it):
                qb, hh = its[it]
                A, B = self.ps[2 + (it % 2) * 2], self.ps[3 + (it % 2) * 2]
                q0 = NCTX + qb * 128
                fns = []
                for idx, (tile, sz) in enumerate(tiles(qb)):
                    dst = A.t[:sz, idx * 128:(idx + 1) * 128] if idx < 4 else B.t[:sz, (idx - 4) * 128:(idx - 3) * 128]
                    fns.append(lambda e, dst=dst, tile=tile, sz=sz, hh=hh, q0=q0: e.matmul(
                        dst, lhsT=kt_.t[hh * 64:(hh + 1) * 64, tile * 128:tile * 128 + sz],
                        rhs=qt_.t[hh * 64:(hh + 1) * 64, q0:q0 + 128], start=True, stop=True))
                S.mm(fns, reads=[kt_.r, qt_.r], writes=[A.r, B.r])
            def softmax_pv(it):
                qb, hh = its[it]
                ty = geom(qb)[0]
                tl = tiles(qb)
                nb = (len(tl) - 4) * 128
                A, B = self.ps[2 + (it % 2) * 2], self.ps[3 + (it % 2) * 2]
                ta, tb_, pa, pb = tmA[it % 2], tmB[it % 2], PA[it % 2], PB[it % 2]
                bo = (ty * 2 + hh) * 896
                S.op("dve", lambda e, A=A, ta=ta, bo=bo: e.scalar_tensor_tensor(
                    out=ta.t[:], in0=A.t[:, 0:512], scalar=0.125, in1=bi_.t[:, bo:bo + 512], op0=ALU.mult, op1=ALU.add),
                    reads=[A.r, bi_.r], writes=[ta.r])
                S.op("act", lambda e, ta=ta, pa=pa: e.activation(out=pa.t[:], in_=ta.t[:], func=AF.Exp), reads=[ta.r], writes=[pa.r])
                S.op("dve", lambda e, B=B, tb_=tb_, bo=bo, nb=nb: e.scalar_tensor_tensor(
                    out=tb_.t[:, :nb], in0=B.t[:, 0:nb], scalar=0.125, in1=bi_.t[:, bo + 512:bo + 512 + nb], op0=ALU.mult, op1=ALU.add),
                    reads=[B.r, bi_.r], writes=[tb_.r])
                S.op("act", lambda e, tb_=tb_, pb=pb, nb=nb: e.activation(out=pb.t[:, :nb], in_=tb_.t[:, :nb], func=AF.Exp), reads=[tb_.r], writes=[pb.r])
                po = self.ps[hh]
                fns = []
                for idx, (tile, sz) in enumerate(tl):
                    src = pa.t[:sz, idx * 128:(idx + 1) * 128] if idx < 4 else pb.t[:sz, (idx - 4) * 128:(idx - 3) * 128]
                    fns.append(lambda e, po=po, src=src, tile=tile, sz=sz, hh=hh, idx=idx, n_=len(tl): e.matmul(
                        po.t[:, 0:65], lhsT=src, rhs=va_.t[:sz, tile, hh * 65:(hh + 1) * 65], start=(idx == 0), stop=(idx == n_ - 1)))
                S.mm(fns, reads=[pa.r, pb.r, va_.r], writes=[po.r])

            def epilogue(it):
                qb, hh = its[it]
                po = self.ps[hh]
                oo = o2[qb % 2]
                S.op("dve", lambda e, po=po, hh=hh: e.reciprocal(out=rr.t[:, hh:hh + 1], in_=po.t[:, 64:65]), reads=[po.r], writes=[rr.r])
                S.op("dve", lambda e, po=po, hh=hh, oo=oo: e.tensor_scalar(out=oo.t[:, hh * 64:(hh + 1) * 64], in0=po.t[:, 0:64], scalar1=rr.t[:, hh:hh + 1],
                                                                       scalar2=None, op0=ALU.mult), reads=[po.r, rr.r], writes=[oo.r])
                if hh == 1:
                    pT = self.ps[6 + qb % 2]
                    S.mm([lambda e, pT=pT, oo=oo: e.transpose(pT.t[:, 0:128], oo.t[:], self.ident.t[:])], reads=[oo.r, self.ident.r], writes=[pT.r])
                    S.op("act", lambda e, pT=pT, qb=qb: e.copy(out=ob_.t[:, qb * 128:(qb + 1) * 128], in_=pT.t[:, 0:128]), reads=[pT.r], writes=[ob_.r])
            scores(0)
            for it in range(len(its)):
                if it + 1 < len(its):
                    scores(it + 1)
                softmax_pv(it)
                if it >= 1:
                    epilogue(it - 1)
            epilogue(len(its) - 1)
            S.dma("sp", self.catT[hc * 128:(hc + 1) * 128, NCTX:], ob_.t[:], reads=[ob_.r], writes=[S.dres("catN", hc)])

    def ph_final(self):
        S = self.S
        self.alloc_pro()
        fw = self.sb("fw", [128, 8])
        S.dma("sp", fw.t[:], self.din["fnormT"], writes=[fw.r])
        yT = [self.sb("yT%d" % a, [128, 8, GS]) for a in range(2)]
        ot = [self.sb("ot%d" % a, [128, D]) for a in range(2)]
        gi = 0
        oi = 0
        for (t0, n, s) in self.groups(False):
            xt = self.prologue(1, 2, t0, n, s, 6, want_h=False)
            y = yT[gi % 2]
            gi += 1
            for c in range(8):
                S.op("dve", lambda e, c=c, y=y, xt=xt: e.scalar_tensor_tensor(
                    out=y.t[:, c, :n], in0=xt.t[:, c, :n], scalar=fw.t[:, c:c + 1], in1=self.rstd.t[:, :n], op0=ALU.mult, op1=ALU.mult),
                    reads=[xt.r, fw.r, self.rstd.r], writes=[y.r])
            for tt in range(n // 128):
                o = ot[oi % 2]
                oi += 1
                for hf in range(2):
                    ps = self.ps[hf * 2 + (tt % 2)]
                    S.mm([lambda e, c=c, ps=ps, hf=hf, tt=tt, y=y: e.transpose(ps.t[:, c * 128:(c + 1) * 128], y.t[:, hf * 4 + c, tt * 128:(tt + 1) * 128],
                                                                            self.ident.t[:]) for c in range(4)], reads=[y.r, self.ident.r], writes=[ps.r])
                    if hf == 0:
                        S.op("act", lambda e, ps=ps, o=o: e.copy(out=o.t[:, 0:512], in_=ps.t[:, :]), reads=[ps.r], writes=[o.r])
                    else:
                        S.op("dve", lambda e, ps=ps, o=o: e.tensor_copy(out=o.t[:, 512:1024], in_=ps.t[:, :]), reads=[ps.r], writes=[o.r])
                r0 = t0 - NCTX + tt * 128
                S.dma("sp", self.out[r0:r0 + 128, :], o.t[:], reads=[o.r], writes=[S.dres("out", r0)])


def _host_inputs(inputs, b, consts, nab):
    f32 = np.float32
    g = lambda k: np.asarray(inputs[k], dtype=f32)
    m = {}
    m["x"] = np.ascontiguousarray(g("x")[b])
    m["ctxi"] = np.ascontiguousarray(g("ctx")[b])
    sv = np.stack([_fm(g("c")[b], 8), _fm(g("c_ctx"), 8)], axis=-1)
    m["sv"] = np.ascontiguousarray(sv.reshape(128, 16))
    m["mod_w"] = g("mod_w")
    mb = np.stack([_fm(g("mod_b")[l], 72) for l in range(2)], axis=1)
    m["mod_b2"] = np.ascontiguousarray(np.repeat(mb[:, :, None, :], 2, axis=2).reshape(128, 288))
    nw = g("norm_w")
    nt = np.stack([np.stack([_fm(nw[l, k], 8) for k in range(3)], axis=1) for l in range(2)], axis=1)
    m["normT2"] = np.ascontiguousarray(np.repeat(nt[:, :, :, None, :], 2, axis=3).reshape(128, 96))
    m["fnormT"] = _fm(g("final_norm_w"), 8)
    m["ffn_w1"] = g("ffn_w1"); m["ffn_w3"] = g("ffn_w3"); m["ffn_w2"] = g("ffn_w2")
    wi = g("ab_w_in")[0]
    m["ab_w_in"] = wi
    perm = consts["perm"]
    cols = np.concatenate([hc * 128 + perm for hc in range(4)] + [512 + hc * 128 + perm for hc in range(4)])
    m["ab_w_inp"] = np.ascontiguousarray(wi[:, cols])
    m["ab_w_out"] = g("ab_w_out")[0]
    m["lamp"] = np.ascontiguousarray(np.broadcast_to(g("diff_lambda")[0].reshape(1, 256), (128, 256)))
    m["subw"] = np.ascontiguousarray(g("diff_subln_w")[0].reshape(128, 1))
    cw = g("hy_conv_w")[0]
    m["hcw"] = np.ascontiguousarray(np.stack([_fm(cw[j], 12) for j in range(3)], axis=-1).reshape(128, 36))
    m["hcb"] = _fm(g("hy_conv_b")[0], 12)
    m["fw0"] = g("hy_f_w0")[0]; m["fb0"] = np.ascontiguousarray(g("hy_f_b0")[0].reshape(64, 1))
    m["fw1"] = g("hy_f_w1")[0]; m["fb1"] = np.ascontiguousarray(g("hy_f_b1")[0].T)
    m["ffreq"] = np.ascontiguousarray(g("hy_f_freq")[0].reshape(64, 1))
    m["fwout"] = g("hy_f_wout")[0]
    m["hskip"] = np.ascontiguousarray(g("hy_bias")[0].reshape(1, 1024))
    m["na_w_in"] = g("na_w_in")[0]; m["na_w_out"] = g("na_w_out")[0]
    m["nab"] = nab
    for k in ("ident", "ropeC", "ropeS", "dftC", "dftS", "wf", "dftCc", "dftSc", "wfc", "zT", "zTc", "tl", "tlc", "deltas"):
        m[k] = consts[k]
    return m


_PROG = {}


def run(inputs, cores, dbg=None):
    consts = _consts()
    nab, blocks, nt = _na_bias(np.asarray(inputs["na_rpb"], np.float32)[0])
    key = (dbg,)
    if key not in _PROG:
        p = Prog(dbg=dbg, nab_cols=nab.shape[2], na_blocks=blocks)
        p.ntypes = nt
        _PROG[key] = p.build()
    nc = _PROG[key]
    in_maps = [_host_inputs(inputs, b, consts, nab) for b in cores]
    res = run_bass_kernel_spmd(nc, in_maps, core_ids=list(range(len(cores))))
    return res


def kernel(**inputs):
    res = run(inputs, list(range(8)))
    return np.stack([np.asarray(r["out"], dtype=np.float32) for r in res.results], axis=0)
```

```python
import math
from contextlib import ExitStack
import numpy as np
import ml_dtypes
import concourse.bass as bass
import concourse.mybir as mybir
from concourse.bass_utils import run_bass_kernel_spmd

F32 = mybir.dt.float32
BF16 = mybir.dt.bfloat16
AF = mybir.ActivationFunctionType
ALU = mybir.AluOpType
AX = mybir.AxisListType

D = 1024; SEQ = 4096; NCTX = 256; TT = SEQ + NCTX; DFF = 2816; NJ = DFF // 128
GRID_W = 64; EPS = 1e-6
GS = 512
SAME_ENGINE_SYNC = True


class Res:
    __slots__ = ("w", "r")

    def __init__(self):
        self.w = None
        self.r = {}


class Sched:
    NSLOT = 10

    def __init__(self, nc, es):
        self.nc = nc
        self.eng = {"pe": nc.tensor, "act": nc.scalar, "dve": nc.vector, "pool": nc.gpsimd, "sp": nc.sync}
        self.sem = {e: es.enter_context(nc.semaphore("s_" + e)) for e in ("pe", "act", "dve", "pool")}
        self.cnt = {e: 0 for e in self.sem}
        self.seen = {e: {} for e in self.eng}
        self.dsem = {q: [es.enter_context(nc.semaphore("d_%s_%d" % (q, i))) for i in range(self.NSLOT)]
                     for q in ("sp", "pool", "act")}
        self.dcnt = {q: [0] * self.NSLOT for q in self.dsem}
        self.dnext = {q: 0 for q in self.dsem}
        self.dram = {}

    def dres(self, *key):
        r = self.dram.get(key)
        if r is None:
            r = self.dram[key] = Res()
        return r

    def _semof(self, key):
        return self.sem[key[1]] if key[0] == "c" else self.dsem[key[1]][key[2]]

    def _wait(self, eng, key, val):
        if self.seen[eng].get(key, 0) >= val:
            return
        self.seen[eng][key] = val
        self.eng[eng].wait_ge(self._semof(key), val)

    def _deps(self, eng, reads, writes):
        best = {}
        for r in reads:
            if r.w is not None:
                k, v = r.w
                if best.get(k, 0) < v:
                    best[k] = v
        for w in writes:
            if w.w is not None:
                k, v = w.w
                if best.get(k, 0) < v:
                    best[k] = v
            for k, v in w.r.items():
                if best.get(k, 0) < v:
                    best[k] = v
        for k, v in best.items():
            if k == ("c", eng) and (eng == "pe" or not SAME_ENGINE_SYNC):
                continue
            self._wait(eng, k, v)

    def _mark(self, key, val, reads, writes):
        for r in reads:
            if r.r.get(key, 0) < val:
                r.r[key] = val
        for w in writes:
            w.w = (key, val)
            w.r = {}

    def op(self, eng, fn, reads=(), writes=()):
        self._deps(eng, reads, writes)
        self.cnt[eng] += 1
        fn(self.eng[eng]).then_inc(self.sem[eng], 1)
        self._mark(("c", eng), self.cnt[eng], reads, writes)

    def mm(self, fns, reads=(), writes=()):
        self._deps("pe", reads, writes)
        ins = None
        for f in fns:
            ins = f(self.nc.tensor)
        self.cnt["pe"] += 1
        ins.then_inc(self.sem["pe"], 1)
        self._mark(("c", "pe"), self.cnt["pe"], reads, writes)

    def dma(self, q, out, in_, reads=(), writes=(), **kw):
        if q == "st":
            q = "act"
        slot = self.dnext[q]
        self.dnext[q] = (slot + 1) % self.NSLOT
        key = ("d", q, slot)
        if self.dcnt[q][slot] > 0:
            self._wait(q, key, self.dcnt[q][slot])
        self._deps(q, reads, writes)
        self.dcnt[q][slot] += 16
        self.eng[q].dma_start(out=out, in_=in_, **kw).then_inc(self.dsem[q][slot], 16)
        self._mark(key, self.dcnt[q][slot], reads, writes)

    def barrier(self):
        keys = [(("c", e), self.cnt[e]) for e in self.cnt]
        for q in self.dsem:
            for i in range(self.NSLOT):
                keys.append((("d", q, i), self.dcnt[q][i]))
        for e in self.eng:
            for k, v in keys:
                if v > 0:
                    self._wait(e, k, v)


class Tl:
    def __init__(self, t, nres=1):
        self.t = t
        self.res = [Res() for _ in range(nres)]

    @property
    def r(self):
        return self.res[0]


def _bf(a):
    return np.asarray(a, dtype=np.float32).astype(ml_dtypes.bfloat16)


def _dft_blocks(n):
    ne = n + 128
    idx = np.arange(n, dtype=np.int64)
    m = (idx[:, None] * idx[None, :]) % (2 * n)
    ang = m.astype(np.float64) * (math.pi / n)
    C = np.zeros((ne, ne), np.float64)
    S = np.zeros((ne, ne), np.float64)
    C[:n, :n] = np.cos(ang)
    S[:n, :n] = np.sin(ang)
    alt = np.where(idx % 2 == 0, 1.0, -1.0)
    C[:n, n] = alt
    C[n, :n] = alt
    nt = ne // 128

    def blk(M):
        return np.ascontiguousarray(M.reshape(nt, 128, nt, 128).transpose(2, 1, 0, 3)).reshape(nt, 128, nt * 128)
    wf = np.full((ne,), 1.0 / n, np.float32)
    wf[0] = 0.5 / n
    wf[n] = 0.5 / n
    wf[n + 1:] = 0.0
    return _bf(blk(C)), _bf(blk(S)), np.ascontiguousarray(wf.reshape(nt, 128).T)


def _filter_consts(n):
    t = np.linspace(0.0, 1.0, n, dtype=np.float32)[:, None]
    w = (2.0 * math.pi / n) * np.arange(n, dtype=np.float32)[:, None]
    bands = np.linspace(1e-4, 15, 16, dtype=np.float32)[None, :]
    z = np.concatenate([t, np.cos(bands * w), -np.sin(bands * w)], axis=-1).astype(np.float32)
    tl = np.ascontiguousarray((-t[:, 0]).reshape(n // 128, 128).T).astype(np.float32)
    return np.ascontiguousarray(z.T), tl


def _rope_tables():
    t = np.arange(SEQ)
    pos = (t // GRID_W, t % GRID_W)
    inv = (10000.0 ** (-np.arange(16, dtype=np.float32) / 16)).astype(np.float32)
    C = np.zeros((128, SEQ), np.float32)
    Sg = np.zeros((128, SEQ), np.float32)
    for m in range(2):
        for a in range(2):
            ang = pos[a].astype(np.float32)[None, :] * inv[:, None]
            c = np.cos(ang).astype(np.float32)
            s = np.sin(ang).astype(np.float32)
            b = m * 64 + a * 32
            C[b:b + 16] = c
            C[b + 16:b + 32] = c
            Sg[b:b + 16] = -s
            Sg[b + 16:b + 32] = s
    perm = np.zeros(128, np.int64)
    for m in range(2):
        for a in range(2):
            b = m * 64 + a * 32
            perm[b:b + 16] = np.arange(b + 16, b + 32)
            perm[b + 16:b + 32] = np.arange(b, b + 16)
    return C, Sg, perm


def _na_geometry():
    blocks = []
    types = {}
    for qb in range(32):
        r0 = 2 * qb
        rs = [min(max(r - 4, 0), 56) for r in (r0, r0 + 1)]
        lo, hi = min(rs), max(rs) + 8
        sig = (hi - lo, rs[0] - lo, rs[1] - lo, r0 - lo)
        if sig not in types:
            types[sig] = len(types)
        blocks.append((lo, hi, types[sig]))
    return blocks, types


def _na_bias(rpb):
    blocks, types = _na_geometry()
    nt = len(types)
    out = np.zeros((nt, 16, 7 * 128, 128), np.float32)
    out[:, :, 256:, :] = -30000.0
    kk = np.arange(640)
    ki, kc = kk // 64, kk % 64
    qq = np.arange(128)
    qj, qc = qq // 64, qq % 64
    cs = np.clip(qc - 8, 0, 48)
    for sig, ti in types.items():
        nrows, rs0, rs1, r0l = sig
        rs = np.array([rs0, rs1])[qj]
        qr = r0l + qj
        valid = (ki[:, None] < nrows) & (ki[:, None] >= rs[None, :]) & (ki[:, None] < rs[None, :] + 8) \
            & (kc[:, None] >= cs[None, :]) & (kc[:, None] < cs[None, :] + 16)
        dr = np.clip(ki[:, None] - qr[None, :] + 7, 0, 14)
        dc = np.clip(kc[:, None] - qc[None, :] + 15, 0, 30)
        g = rpb[:, dr, dc]
        out[ti, :, 256:, :] = np.where(valid[None], g, np.float32(-30000.0))
    o = out.reshape(nt, 8, 2, 7, 128, 128).transpose(1, 4, 0, 2, 3, 5)
    return np.ascontiguousarray(o).reshape(8, 128, nt * 2 * 7 * 128), blocks, nt


_CONSTS = {}


def _consts():
    if _CONSTS:
        return _CONSTS
    c = _CONSTS
    c["dftC"], c["dftS"], c["wf"] = _dft_blocks(SEQ)
    c["dftCc"], c["dftSc"], c["wfc"] = _dft_blocks(NCTX)
    c["zT"], c["tl"] = _filter_consts(SEQ)
    c["zTc"], c["tlc"] = _filter_consts(NCTX)
    hy_min = math.log(1e-2) / 1.5
    hy_max = math.log(1e-2) / 0.3
    deltas = np.abs(np.linspace(hy_min, hy_max, 512, dtype=np.float32))
    c["deltas"] = np.ascontiguousarray(np.broadcast_to(deltas[None, :], (128, 512))).astype(np.float32)
    c["ropeC"], c["ropeS"], c["perm"] = _rope_tables()
    c["ident"] = np.eye(128, dtype=np.float32)
    return c


def _fm(v, nch):
    return np.ascontiguousarray(np.asarray(v, np.float32).reshape(nch, 128).T)


class Prog:
    def __init__(self, dbg=None, nab_cols=0, na_blocks=None):
        self.dbg = dbg
        self.nab_cols = nab_cols
        self.na_blocks = na_blocks
        self.nc = bass.Bass("TRN2", target_bir_lowering=False)
        self.din = {}

    def inp(self, name, shape, dt=F32):
        self.din[name] = self.nc.dram_tensor(name, list(shape), dt, kind="ExternalInput").ap()
        return self.din[name]

    def scr(self, name, shape, dt):
        return self.nc.dram_tensor(name, list(shape), dt).ap()

    def sb(self, name, shape, dt=F32, nres=1):
        self.uid = getattr(self, "uid", 0) + 1
        return Tl(self.es.enter_context(self.nc.sbuf_tensor("sb%d_%s" % (self.uid, name), list(shape), dt)), nres)

    def build(self):
        nc = self.nc
        I = self.inp
        x = I("x", [SEQ, D]); ctxi = I("ctxi", [NCTX, D])
        I("sv", [128, 16]); I("mod_w", [2, D, 9 * D]); I("mod_b2", [128, 2 * 2 * 72]); I("normT2", [128, 2 * 3 * 2 * 8])
        I("fnormT", [128, 8])
        I("ffn_w1", [2, 2, D, DFF]); I("ffn_w3", [2, 2, D, DFF]); I("ffn_w2", [2, 2, DFF, D])
        I("ab_w_in", [D, 3072]); I("ab_w_inp", [D, 1024]); I("ab_w_out", [D, D])
        I("lamp", [128, 256]); I("subw", [128, 1])
        I("hcw", [128, 36]); I("hcb", [128, 12])
        I("fw0", [33, 64]); I("fb0", [64, 1]); I("fw1", [2, 64, 64]); I("fb1", [64, 2]); I("ffreq", [64, 1])
        I("fwout", [64, 2048]); I("hskip", [1, 1024])
        I("na_w_in", [D, 3072]); I("na_w_out", [D, D]); I("nab", [8, 128, self.nab_cols])
        I("ident", [128, 128]); I("ropeC", [128, SEQ]); I("ropeS", [128, SEQ])
        I("dftC", [33, 128, 33 * 128], BF16); I("dftS", [33, 128, 33 * 128], BF16); I("wf", [128, 33])
        I("dftCc", [3, 128, 3 * 128], BF16); I("dftSc", [3, 128, 3 * 128], BF16); I("wfc", [128, 3])
        I("zT", [33, SEQ]); I("zTc", [33, NCTX]); I("tl", [128, 32]); I("tlc", [128, 2]); I("deltas", [128, 512])
        self.out = nc.dram_tensor("out", [SEQ, D], F32, kind="ExternalOutput").ap()
        if self.dbg:
            self.dbgo = nc.dram_tensor("dbg", [D, TT], F32, kind="ExternalOutput").ap()
        S_ = self.scr
        self.latT = S_("latT", [D, TT], F32)
        self.w1b = [[S_("w1b%d%d" % (l, i), [D, DFF], BF16) for i in range(2)] for l in range(2)]
        self.w3b = [[S_("w3b%d%d" % (l, i), [D, DFF], BF16) for i in range(2)] for l in range(2)]
        self.w2b = [[S_("w2b%d%d" % (l, i), [DFF, D], BF16) for i in range(2)] for l in range(2)]
        self.abinb = S_("abinb", [D, 3072], BF16); self.abinpb = S_("abinpb", [D, 1024], BF16)
        self.aboutb = S_("aboutb", [D, D], BF16)
        self.nainb = S_("nainb", [D, 3072], BF16); self.naoutb = S_("naoutb", [D, D], BF16)
        self.QT = S_("QT", [D, TT], BF16); self.KT = S_("KT", [D, TT], BF16)
        self.VA = S_("VA", [TT, 1040], BF16)
        self.UH = S_("UH", [1536, TT], F32)
        self.UTM = S_("UTM", [TT, 1024], F32)
        self.VTM = S_("VTM", [TT, 512], BF16)
        self.HSD = S_("HSD", [2, 2, SEQ, 512], BF16)
        self.HCS = S_("HCS", [2, 2, 33 * 128, 512], F32)
        self.catT = S_("catT", [D, TT], BF16)
        with ExitStack() as es:
            self.es = es
            self.S = Sched(nc, es)
            self.ps = [Tl(es.enter_context(nc.psum_tensor("ps%d" % i, [128, 512], F32))) for i in range(8)]
            self.persist()
            self.phase(self.ph_setup)
            for l in range(2):
                self.phase(lambda: self.ph_ffn(l, 0))
                if self.dbg == "ffn%d0" % l:
                    break
                if l == 0:
                    self.phase(self.ph_abproj)
                    self.phase(self.ph_hyena_prep)
                    self.phase(self.ph_diffattn)
                    for nn in (NCTX, SEQ):
                        self.phase(lambda: self.ph_filters(nn))
                        self.phase(lambda: self.ph_hyena(nn))
                    if self.dbg == "cat":
                        break
                    self.phase(lambda: self.ph_outproj(0))
                else:
                    self.phase(self.ph_naproj)
                    self.phase(self.ph_na)
                    self.phase(lambda: self.ph_outproj(1))
                if self.dbg == "mix%d" % l:
                    break
                self.phase(lambda: self.ph_ffn(l, 1))
                if self.dbg == "ffn%d1" % l:
                    break
            if self.dbg:
                src = self.latT
                if self.dbg == "cat":
                    src = None
                if src is not None:
                    self.S.dma("sp", self.dbgo, src, reads=[], writes=[self.S.dres("dbgo")])
                else:
                    self.S.dma("pool", self.dbgo, self.catT, reads=[], writes=[self.S.dres("dbgo")])
            else:
                self.phase(self.ph_final)
            self.S.barrier()
        return nc

    def phase(self, fn):
        with ExitStack() as es:
            old = self.es
            self.es = es
            fn()
            self.S.barrier()
            self.es = old

    def persist(self):
        S = self.S
        self.ident = self.sb("ident", [128, 128])
        self.onesb = self.sb("onesb", [128, 128], BF16)
        self.msc = self.sb("msc", [128, 2 * 3 * 3 * 2 * 8])
        S.dma("sp", self.ident.t[:], self.din["ident"], writes=[self.ident.r])
        S.op("dve", lambda e: e.memset(self.onesb.t[:], 1.0), writes=[self.onesb.r])
        self.svs = self.sb("svs", [128, 16]); self.modT = self.sb("modT", [128, 2 * 2 * 72])
        self.mb2 = self.sb("mb2", [128, 2 * 2 * 72]); self.nT2 = self.sb("nT2", [128, 96])
        S.dma("sp", self.svs.t[:], self.din["sv"], writes=[self.svs.r])
        S.dma("sp", self.mb2.t[:], self.din["mod_b2"], writes=[self.mb2.r])
        S.dma("sp", self.nT2.t[:], self.din["normT2"], writes=[self.nT2.r])

    def mcol(self, l, k, ty, s):
        o = (((l * 3 + k) * 3 + ty) * 2 + s) * 8
        return self.msc.t[:, o:o + 8]

    def cast_weights(self, l):
        S, din = self.S, self.din
        for i in range(2):
            S.dma("pool", self.w1b[l][i], din["ffn_w1"][l, i], writes=[S.dres("w1b", l, i)])
            S.dma("pool", self.w3b[l][i], din["ffn_w3"][l, i], writes=[S.dres("w3b", l, i)])
            S.dma("pool", self.w2b[l][i], din["ffn_w2"][l, i], writes=[S.dres("w2b", l, i)])
        pairs = ((self.abinb, "ab_w_in"), (self.abinpb, "ab_w_inp"), (self.aboutb, "ab_w_out")) if l == 0 else \
            ((self.nainb, "na_w_in"), (self.naoutb, "na_w_out"))
        for dst, src in pairs:
            S.dma("pool", dst, din[src], writes=[S.dres(src)])

    def mod_alloc(self):
        self.mw = [self.sb("mw%d" % i, [128, 8, 1024]) for i in range(2)]
        self.mwi = 0

    def mod_block(self, l, b):
        S, din = self.S, self.din
        svs, modT, mb2 = self.svs, self.modT, self.mb2
        svv = svs.t[:].rearrange("p (k s) -> p k s", s=2)
        mwv = din["mod_w"][l].rearrange("(k p) n -> p k n", p=128)
        buf = self.mw[self.mwi % 2]
        self.mwi += 1
        S.dma("sp", buf.t[:], mwv[:, :, b * 1024:(b + 1) * 1024], writes=[buf.r])
        ps = self.ps[7]
        fns = []
        for jj in range(8):
            for k in range(8):
                fns.append(lambda e, jj=jj, k=k, buf=buf, ps=ps: e.matmul(
                    ps.t[:, 2 * jj:2 * jj + 2], lhsT=buf.t[:, k, jj * 128:(jj + 1) * 128], rhs=svv[:, k, :],
                    start=(k == 0), stop=(k == 7)))
        S.mm(fns, reads=[buf.r, svs.r], writes=[ps.r])
        psv = ps.t[:, 0:16].rearrange("p (j s) -> p s j", s=2)
        for s in range(2):
            o = (l * 2 + s) * 72 + b * 8
            S.op("dve", lambda e, s=s, o=o, psv=psv: e.tensor_tensor(
                out=modT.t[:, o:o + 8], in0=psv[:, s, :], in1=mb2.t[:, o:o + 8], op=ALU.add),
                reads=[ps.r, mb2.r], writes=[modT.r])

    def mod_finish(self, l):
        S = self.S
        modT, nT2 = self.modT, self.nT2
        for k in range(3):
            for s in range(2):
                mo = (l * 2 + s) * 72
                no = ((l * 3 + k) * 2 + s) * 8
                sc = modT.t[:, mo + (3 * k + 1) * 8: mo + (3 * k + 2) * 8]
                sh = modT.t[:, mo + (3 * k) * 8: mo + (3 * k + 1) * 8]
                gt = modT.t[:, mo + (3 * k + 2) * 8: mo + (3 * k + 3) * 8]
                S.op("dve", lambda e, sc=sc, no=no, l=l, k=k, s=s: e.scalar_tensor_tensor(
                    out=self.mcol(l, k, 0, s), in0=sc, scalar=1.0, in1=nT2.t[:, no:no + 8],
                    op0=ALU.add, op1=ALU.mult), reads=[modT.r, nT2.r], writes=[self.msc.r])
                S.op("dve", lambda e, sh=sh, l=l, k=k, s=s: e.tensor_copy(out=self.mcol(l, k, 1, s), in_=sh),
                     reads=[modT.r], writes=[self.msc.r])
                S.op("dve", lambda e, gt=gt, l=l, k=k, s=s: e.tensor_scalar(
                    out=self.mcol(l, k, 2, s), in0=gt, scalar1=(1.0 if k == 1 else 0.5), scalar2=None,
                    op0=ALU.mult), reads=[modT.r], writes=[self.msc.r])

    def ph_setup(self):
        S, din = self.S, self.din
        self.cast_weights(0)
        S.op("act", lambda e: e.activation(out=self.svs.t[:], in_=self.svs.t[:], func=AF.Silu), reads=[self.svs.r], writes=[self.svs.r])
        self.mod_alloc()
        self.xT_alloc()
        ti = 0
        for b in range(9):
            self.mod_block(0, b)
            for _ in range(4 if b < 8 else 2):
                self.xT_tile(ti)
                ti += 1
        self.mod_finish(0)

    def xT_alloc(self):
        self.xin = [self.sb("xin%d" % i, [128, D]) for i in range(2)]
        self.xo = [self.sb("xo%d" % i, [128, 8, 128]) for i in range(2)]

    def xT_tile(self, ti):
        S = self.S
        src = self.din["ctxi"][ti * 128:(ti + 1) * 128, :] if ti < 2 else self.din["x"][(ti - 2) * 128:(ti - 1) * 128, :]
        xi, o = self.xin[ti % 2], self.xo[ti % 2]
        S.dma("sp", xi.t[:], src, writes=[xi.r])
        for h in range(2):
            ps = self.ps[(ti % 2) * 2 + h]
            S.mm([lambda e, c=c, ps=ps, xi=xi, h=h: e.transpose(
                ps.t[:, c * 128:(c + 1) * 128], xi.t[:, (h * 4 + c) * 128:(h * 4 + c + 1) * 128], self.ident.t[:])
                for c in range(4)], reads=[xi.r, self.ident.r], writes=[ps.r])
            ov = o.t[:, h * 4:(h + 1) * 4, :]
            pv = ps.t[:, :].rearrange("p (c t) -> p c t", t=128)
            if h == 0:
                S.op("act", lambda e, ov=ov, pv=pv: e.copy(out=ov, in_=pv), reads=[ps.r], writes=[o.r])
            else:
                S.op("dve", lambda e, ov=ov, pv=pv: e.tensor_copy(out=ov, in_=pv), reads=[ps.r], writes=[o.r])
        dst = self.latT.rearrange("(c p) t -> p c t", p=128)[:, :, ti * 128:(ti + 1) * 128]
        S.dma("st", dst, o.t[:], reads=[o.r], writes=[S.dres("latT", ti)])

    def groups(self, with_ctx=True):
        g = [(0, NCTX, 1)] if with_ctx else []
        return g + [(NCTX + GS * i, GS, 0) for i in range(SEQ // GS)]

    def alloc_pro(self):
        self.xt = [self.sb("xt%d" % i, [128, 8, GS]) for i in range(2)]
        self.sq = self.sb("sq", [128, 8, GS], BF16)
        self.rstd = self.sb("rstd", [128, GS])
        self.ptmp = [self.sb("ptmp%d" % i, [128, GS]) for i in range(2)]
        self.hTs = [self.sb("hT%d" % i, [128, 8, GS], BF16) for i in range(2)]
        self.hT = self.hTs[0]
        self.gi = 0

    def prologue(self, l, k, t0, n, s, psb, want_h=True):
        S = self.S
        xt = self.xt[self.gi % 2]
        self.hT = self.hTs[self.gi % 2]
        self.gi += 1
        lv = self.latT.rearrange("(c p) t -> p c t", p=128)[:, :, t0:t0 + n]
        S.dma("sp", xt.t[:, :, :n], lv, writes=[xt.r])
        sq, rstd, hT = self.sq, self.rstd, self.hT
        S.op("act", lambda e: e.activation(out=sq.t[:, :, :n], in_=xt.t[:, :, :n], func=AF.Square),
             reads=[xt.r], writes=[sq.r])
        ps = self.ps[psb]
        S.mm([lambda e, c=c: e.matmul(ps.t[:, :n], lhsT=self.onesb.t[:], rhs=sq.t[:, c, :n], start=(c == 0), stop=(c == 7))
              for c in range(8)], reads=[sq.r, self.onesb.r], writes=[ps.r])
        S.op("act", lambda e: e.activation(out=rstd.t[:, :n], in_=ps.t[:, :n], func=AF.Sqrt, bias=EPS, scale=1.0 / D),
             reads=[ps.r], writes=[rstd.r])
        S.op("dve", lambda e: e.reciprocal(out=rstd.t[:, :n], in_=rstd.t[:, :n]), reads=[rstd.r], writes=[rstd.r])
        if want_h:
            gs, sh = self.mcol(l, k, 0, s), self.mcol(l, k, 1, s)
            for c in range(8):
                tmp = self.ptmp[c % 2]
                S.op("dve", lambda e, c=c, tmp=tmp: e.scalar_tensor_tensor(
                    out=tmp.t[:, :n], in0=xt.t[:, c, :n], scalar=gs[:, c:c + 1], in1=rstd.t[:, :n],
                    op0=ALU.mult, op1=ALU.mult), reads=[xt.r, rstd.r, self.msc.r], writes=[tmp.r])
                S.op("act", lambda e, c=c, tmp=tmp: e.activation(
                    out=hT.t[:, c, :n], in_=tmp.t[:, :n], func=AF.Identity, bias=sh[:, c:c + 1], scale=1.0),
                    reads=[tmp.r, self.msc.r], writes=[hT.r])
        return xt

    def store_group(self, xt, t0, n):
        lv = self.latT.rearrange("(c p) t -> p c t", p=128)[:, :, t0:t0 + n]
        self.S.dma("st", lv, xt.t[:, :, :n], reads=[xt.r], writes=[self.S.dres("latT", t0)])

    def ph_ffn(self, l, i):
        S = self.S
        k = 0 if i == 0 else 2
        self.alloc_pro()
        w2t = self.sb("w2t", [128, NJ, D], BF16)
        w2v = self.w2b[l][i].rearrange("(j p) n -> p j n", p=128)
        for a in range(2):
            S.dma("sp", w2t.t[:, a * 11:(a + 1) * 11, :], w2v[:, a * 11:(a + 1) * 11, :], writes=[w2t.r] if a == 0 else [w2t.r])
        wb1 = [self.sb("wb1_%d" % a, [128, 8, 256], BF16) for a in range(3)]
        wb3 = [self.sb("wb3_%d" % a, [128, 8, 256], BF16) for a in range(3)]
        actT = self.sb("actT", [128, NJ, GS], BF16)
        sg = [self.sb("sg%d" % a, [128, GS]) for a in range(2)]
        w1v = self.w1b[l][i].rearrange("(k p) n -> p k n", p=128)
        w3v = self.w3b[l][i].rearrange("(k p) n -> p k n", p=128)
        need_ctx = (l == 0) or (i == 0)
        bi = 0
        grps = self.groups(need_ctx)
        nxt = (self.prologue(l, k, grps[0][0], grps[0][1], grps[0][2], 6), self.hT)
        for gidx, (t0, n, s) in enumerate(grps):
            xt, hT = nxt
            for nb in range(11):
                b1, b3 = wb1[bi % 3], wb3[bi % 3]
                bi += 1
                S.dma("sp", b1.t[:], w1v[:, :, nb * 256:(nb + 1) * 256], writes=[b1.r])
                S.dma("sp", b3.t[:], w3v[:, :, nb * 256:(nb + 1) * 256], writes=[b3.r])
                for jj in range(2):
                    j = nb * 2 + jj
                    pg, pu = self.ps[2 * (j % 2)], self.ps[2 * (j % 2) + 1]
                    S.mm([lambda e, kc=kc, b1=b1, pg=pg, jj=jj: e.matmul(
                        pg.t[:, :n], lhsT=b1.t[:, kc, jj * 128:(jj + 1) * 128], rhs=hT.t[:, kc, :n],
                        start=(kc == 0), stop=(kc == 7)) for kc in range(8)], reads=[b1.r, hT.r], writes=[pg.r])
                    S.mm([lambda e, kc=kc, b3=b3, pu=pu, jj=jj: e.matmul(
                        pu.t[:, :n], lhsT=b3.t[:, kc, jj * 128:(jj + 1) * 128], rhs=hT.t[:, kc, :n],
                        start=(kc == 0), stop=(kc == 7)) for kc in range(8)], reads=[b3.r, hT.r], writes=[pu.r])
                    sgt = sg[j % 2]
                    S.op("act", lambda e, pg=pg, sgt=sgt: e.activation(out=sgt.t[:, :n], in_=pg.t[:, :n], func=AF.Silu),
                         reads=[pg.r], writes=[sgt.r])
                    S.op("dve", lambda e, pu=pu, sgt=sgt, j=j: e.tensor_tensor(
                        out=actT.t[:, j, :n], in0=pu.t[:, :n], in1=sgt.t[:, :n], op=ALU.mult),
                        reads=[pu.r, sgt.r], writes=[actT.r])
            if gidx + 1 < len(grps):
                g2 = grps[gidx + 1]
                nxt = (self.prologue(l, k, g2[0], g2[1], g2[2], 6), self.hT)
            gate = self.mcol(l, k, 2, s)
            for dc in range(8):
                py = self.ps[4 + dc % 2]
                S.mm([lambda e, j=j, py=py, dc=dc: e.matmul(
                    py.t[:, :n], lhsT=w2t.t[:, j, dc * 128:(dc + 1) * 128], rhs=actT.t[:, j, :n],
                    start=(j == 0), stop=(j == NJ - 1)) for j in range(NJ)], reads=[w2t.r, actT.r], writes=[py.r])
                S.op("dve", lambda e, py=py, dc=dc: e.scalar_tensor_tensor(
                    out=xt.t[:, dc, :n], in0=py.t[:, :n], scalar=gate[:, dc:dc + 1], in1=xt.t[:, dc, :n],
                    op0=ALU.mult, op1=ALU.add), reads=[py.r, xt.r, self.msc.r], writes=[xt.r])
            self.store_group(xt, t0, n)

    def ph_abproj(self):
        S, din = self.S, self.din
        self.alloc_pro()
        win = self.sb("win", [128, 8, 3072], BF16)
        winp = self.sb("winp", [128, 8, 1024], BF16)
        wv = self.abinb.rearrange("(k p) n -> p k n", p=128)
        for a in range(3):
            S.dma("sp", win.t[:, :, a * 1024:(a + 1) * 1024], wv[:, :, a * 1024:(a + 1) * 1024], writes=[win.r])
        S.dma("sp", winp.t[:], self.abinpb.rearrange("(k p) n -> p k n", p=128), writes=[winp.r])
        rc = self.sb("rc", [128, GS]); rs = self.sb("rs", [128, GS])
        qo = [self.sb("qo%d" % a, [128, GS], BF16) for a in range(4)]
        t1 = [self.sb("t1_%d" % a, [128, GS]) for a in range(2)]
        t2 = [self.sb("t2_%d" % a, [128, GS]) for a in range(2)]
        vo = [self.sb("vo%d" % a, [128, 4, 129], BF16) for a in range(2)]
        uo = [self.sb("uo%d" % a, [128, GS]) for a in range(4)]
        for v in vo:
            S.op("dve", lambda e, v=v: e.memset(v.t[:], 1.0), writes=[v.r])
        ci = 0
        for (t0, n, s) in self.groups(True):
            self.prologue(0, 1, t0, n, s, 6)
            hT = self.hT
            if s == 0:
                S.dma("sp", rc.t[:, :n], din["ropeC"][:, t0 - NCTX:t0 - NCTX + n], writes=[rc.r])
                S.dma("sp", rs.t[:, :n], din["ropeS"][:, t0 - NCTX:t0 - NCTX + n], writes=[rs.r])
            for which in range(2):
                for hc in range(4):
                    col = which * 512 + hc * 128
                    pa, pb = self.ps[(ci % 2) * 2], self.ps[(ci % 2) * 2 + 1]
                    q = qo[ci % 4]; a1 = t1[ci % 2]; a2 = t2[ci % 2]
                    ci += 1
                    S.mm([lambda e, k=k, pa=pa, col=col: e.matmul(
                        pa.t[:, :n], lhsT=win.t[:, k, col:col + 128], rhs=hT.t[:, k, :n], start=(k == 0), stop=(k == 7))
                        for k in range(8)], reads=[win.r, hT.r], writes=[pa.r])
                    if s == 0:
                        S.mm([lambda e, k=k, pb=pb, col=col: e.matmul(
                            pb.t[:, :n], lhsT=winp.t[:, k, col:col + 128], rhs=hT.t[:, k, :n], start=(k == 0), stop=(k == 7))
                            for k in range(8)], reads=[winp.r, hT.r], writes=[pb.r])
                        S.op("dve", lambda e, pa=pa, a1=a1: e.tensor_tensor(out=a1.t[:, :n], in0=pa.t[:, :n], in1=rc.t[:, :n], op=ALU.mult),
                             reads=[pa.r, rc.r], writes=[a1.r])
                        S.op("dve", lambda e, pb=pb, a2=a2: e.tensor_tensor(out=a2.t[:, :n], in0=pb.t[:, :n], in1=rs.t[:, :n], op=ALU.mult),
                             reads=[pb.r, rs.r], writes=[a2.r])
                        S.op("pool", lambda e, q=q, a1=a1, a2=a2: e.tensor_tensor(out=q.t[:, :n], in0=a1.t[:, :n], in1=a2.t[:, :n], op=ALU.add),
                             reads=[a1.r, a2.r], writes=[q.r])
                    else:
                        S.op("act", lambda e, q=q, pa=pa: e.copy(out=q.t[:, :n], in_=pa.t[:, :n]), reads=[pa.r], writes=[q.r])
                    dst = (self.QT if which == 0 else self.KT)[hc * 128:(hc + 1) * 128, t0:t0 + n]
                    S.dma("st", dst, q.t[:, :n], reads=[q.r], writes=[S.dres("qk", which, hc, t0)])
            for tt in range(n // 128):
                pv = self.ps[4 + tt % 2]
                v = vo[tt % 2]
                S.mm([lambda e, k=k, pv=pv, tt=tt: e.matmul(
                    pv.t[:, :512], lhsT=hT.t[:, k, tt * 128:(tt + 1) * 128], rhs=win.t[:, k, 1024:1536], start=(k == 0), stop=(k == 7))
                    for k in range(8)], reads=[win.r, hT.r], writes=[pv.r])
                S.op("act", lambda e, pv=pv, v=v: e.copy(out=v.t[:, :, 0:128], in_=pv.t[:, :].rearrange("p (h d) -> p h d", d=128)),
                     reads=[pv.r], writes=[v.r])
                S.dma("st", self.VA[t0 + tt * 128:t0 + (tt + 1) * 128, 0:516].rearrange("p (h d) -> p h d", d=129), v.t[:],
                      reads=[v.r], writes=[S.dres("va", t0, tt)])
            for c in range(12):
                pu = self.ps[(c % 4)]
                u = uo[c % 4]
                S.mm([lambda e, k=k, pu=pu, c=c: e.matmul(
                    pu.t[:, :n], lhsT=win.t[:, k, 1536 + c * 128:1536 + (c + 1) * 128], rhs=hT.t[:, k, :n], start=(k == 0), stop=(k == 7))
                    for k in range(8)], reads=[win.r, hT.r], writes=[pu.r])
                if c % 2 == 0:
                    S.op("act", lambda e, pu=pu, u=u: e.copy(out=u.t[:, :n], in_=pu.t[:, :n]), reads=[pu.r], writes=[u.r])
                else:
                    S.op("dve", lambda e, pu=pu, u=u: e.tensor_copy(out=u.t[:, :n], in_=pu.t[:, :n]), reads=[pu.r], writes=[u.r])
                S.dma("st", self.UH[c * 128:(c + 1) * 128, t0:t0 + n], u.t[:, :n], reads=[u.r], writes=[S.dres("uh", c, t0)])

    def ph_hyena_prep(self):
        S, din = self.S, self.din
        cw = self.sb("cw", [128, 36]); cb = self.sb("cb", [128, 12])
        S.dma("sp", cw.t[:], din["hcw"], writes=[cw.r]); S.dma("sp", cb.t[:], din["hcb"], writes=[cb.r])
        ub = [self.sb("ub%d" % a, [128, SEQ + 2]) for a in range(2)]
        uc = [self.sb("uc%d" % a, [128, SEQ]) for a in range(2)]
        st = [self.sb("st%d" % a, [128, 4, 128]) for a in range(2)]
        stb = [self.sb("stb%d" % a, [128, 4, 128], BF16) for a in range(2)]
        ci = 0
        gi = 0
        self.mod_alloc()
        mblk = 0
        for (T0, n) in ((0, NCTX), (NCTX, SEQ)):
            for c in range(12):
                if n == SEQ and mblk < 9:
                    self.mod_block(1, mblk)
                    mblk += 1
                    if mblk == 9:
                        self.mod_finish(1)
                b, u = ub[ci % 2], uc[ci % 2]
                ci += 1
                S.op("dve", lambda e, b=b: e.memset(b.t[:, 0:1], 0.0), writes=[b.r])
                S.op("dve", lambda e, b=b: e.memset(b.t[:, n + 1:n + 2], 0.0), writes=[b.r])
                S.dma("sp", b.t[:, 1:n + 1], self.UH[c * 128:(c + 1) * 128, T0:T0 + n], writes=[b.r])
                S.op("act", lambda e, b=b, u=u, c=c: e.activation(out=u.t[:, :n], in_=b.t[:, 1:n + 1], func=AF.Identity,
                                                                 bias=cb.t[:, c:c + 1], scale=cw.t[:, 3 * c + 1:3 * c + 2]),
                     reads=[b.r, cw.r, cb.r], writes=[u.r])
                S.op("dve", lambda e, b=b, u=u, c=c: e.scalar_tensor_tensor(out=u.t[:, :n], in0=b.t[:, 0:n], scalar=cw.t[:, 3 * c:3 * c + 1],
                                                                        in1=u.t[:, :n], op0=ALU.mult, op1=ALU.add),
                     reads=[b.r, u.r, cw.r], writes=[u.r])
                S.op("dve", lambda e, b=b, u=u, c=c: e.scalar_tensor_tensor(out=u.t[:, :n], in0=b.t[:, 2:n + 2], scalar=cw.t[:, 3 * c + 2:3 * c + 3],
                                                                        in1=u.t[:, :n], op0=ALU.mult, op1=ALU.add),
                     reads=[b.r, u.r, cw.r], writes=[u.r])
                for g4 in range(n // 512 if n >= 512 else 1):
                    ntl = min(4, n // 128)
                    ps = self.ps[gi % 4]
                    S.mm([lambda e, a=a, ps=ps, u=u, g4=g4: e.transpose(
                        ps.t[:, a * 128:(a + 1) * 128], u.t[:, (g4 * 4 + a) * 128:(g4 * 4 + a + 1) * 128], self.ident.t[:])
                        for a in range(ntl)], reads=[u.r, self.ident.r], writes=[ps.r])
                    pvw = ps.t[:, :ntl * 128].rearrange("p (a w) -> p a w", w=128)
                    r0 = T0 + g4 * 512
                    if c < 8:
                        sx = st[gi % 2]
                        S.op("act" if gi % 2 == 0 else "dve",
                             (lambda e, sx=sx, pvw=pvw: e.copy(out=sx.t[:, :ntl, :], in_=pvw)) if gi % 2 == 0 else
                             (lambda e, sx=sx, pvw=pvw: e.tensor_copy(out=sx.t[:, :ntl, :], in_=pvw)),
                             reads=[ps.r], writes=[sx.r])
                        dst = self.UTM[r0:r0 + ntl * 128, c * 128:(c + 1) * 128].rearrange("(a p) w -> p a w", p=128)
                        S.dma("st", dst, sx.t[:, :ntl, :], reads=[sx.r], writes=[S.dres("utm", c, r0)])
                    else:
                        sx = stb[gi % 2]
                        S.op("act" if gi % 2 == 0 else "dve",
                             (lambda e, sx=sx, pvw=pvw: e.copy(out=sx.t[:, :ntl, :], in_=pvw)) if gi % 2 == 0 else
                             (lambda e, sx=sx, pvw=pvw: e.tensor_copy(out=sx.t[:, :ntl, :], in_=pvw)),
                             reads=[ps.r], writes=[sx.r])
                        dst = self.VTM[r0:r0 + ntl * 128, (c - 8) * 128:(c - 7) * 128].rearrange("(a p) w -> p a w", p=128)
                        S.dma("st", dst, sx.t[:, :ntl, :], reads=[sx.r], writes=[S.dres("vtm", c, r0)])
                    gi += 1

    def ph_diffattn(self):
        S, din = self.S, self.din
        self.cast_weights(1)
        lam_init = 0.8 - 0.6 * math.exp(-0.3 * 0)
        lp = self.sb("lp", [128, 256]); pr = self.sb("pr", [128, 128]); sm = self.sb("sm", [128, 2])
        nlam = self.sb("nlam", [128, 1]); wsub = self.sb("wsub", [128, 1])
        S.dma("sp", lp.t[:], din["lamp"], writes=[lp.r]); S.dma("sp", wsub.t[:], din["subw"], writes=[wsub.r])
        S.op("dve", lambda e: e.tensor_tensor(out=pr.t[:, 0:64], in0=lp.t[:, 0:64], in1=lp.t[:, 64:128], op=ALU.mult), reads=[lp.r], writes=[pr.r])
        S.op("dve", lambda e: e.tensor_tensor(out=pr.t[:, 64:128], in0=lp.t[:, 128:192], in1=lp.t[:, 192:256], op=ALU.mult), reads=[lp.r], writes=[pr.r])
        S.op("dve", lambda e: e.reduce_sum(out=sm.t[:, 0:2], in_=pr.t[:, :].rearrange("p (a b) -> p a b", b=64), axis=AX.X), reads=[pr.r], writes=[sm.r])
        S.op("act", lambda e: e.activation(out=sm.t[:], in_=sm.t[:], func=AF.Exp), reads=[sm.r], writes=[sm.r])
        S.op("dve", lambda e: e.tensor_tensor(out=nlam.t[:], in0=sm.t[:, 1:2], in1=sm.t[:, 0:1], op=ALU.subtract), reads=[sm.r], writes=[nlam.r])
        S.op("dve", lambda e: e.tensor_scalar(out=nlam.t[:], in0=nlam.t[:], scalar1=-lam_init, scalar2=None, op0=ALU.add), reads=[nlam.r], writes=[nlam.r])
        S.op("dve", lambda e: e.tensor_scalar(out=wsub.t[:], in0=wsub.t[:], scalar1=1.0 - lam_init, scalar2=None, op0=ALU.mult), reads=[wsub.r], writes=[wsub.r])
        ktb = [self.sb("ktb%d" % a, [128, TT], BF16) for a in range(2)]
        qtb = [self.sb("qtb%d" % a, [128, TT], BF16) for a in range(2)]
        vab = [self.sb("vab%d" % a, [128, TT // 128, 129], BF16) for a in range(2)]
        cab = [self.sb("cab%d" % a, [128, TT], BF16) for a in range(2)]
        pts = [self.sb("pt%d" % a, [128, 512], BF16) for a in range(6)]
        sacc = [self.sb("sacc%d" % a, [128, 512]) for a in range(2)]
        rsb = self.sb("rsb", [128, 512]); osb = self.sb("osb", [128, 512]); o1 = self.sb("o1", [128, 512])
        sqb = self.sb("sqb", [128, 512], BF16); rst = self.sb("rst", [128, 512])
        onesf = self.sb("onesf", [128, 128])
        S.op("dve", lambda e: e.memset(onesf.t[:], 1.0), writes=[onesf.r])
        cnt = 0
        mi = 0
        for h in range(4):
            kt_, qt_, va_, ca_ = ktb[h % 2], qtb[h % 2], vab[h % 2], cab[h % 2]
            S.dma("sp", kt_.t[:], self.KT[h * 128:(h + 1) * 128, :], writes=[kt_.r])
            S.dma("sp", qt_.t[:], self.QT[h * 128:(h + 1) * 128, :], writes=[qt_.r])
            S.dma("sp", va_.t[:], self.VA[:, h * 129:(h + 1) * 129].rearrange("(i p) w -> p i w", p=128), writes=[va_.r])
            qgroups = [(0, NCTX, list(range(2)))] + [(NCTX + 512 * g, 512, list(range(TT // 128))) for g in range(SEQ // 512)]
            for (q0, qn, kts) in qgroups:
                OT = [self.ps[0], self.ps[1]]

                def score(i):
                    kt = kts[i]
                    for m in range(2):
                        pS = self.ps[4 + 2 * m + i % 2]
                        S.mm([lambda e, pS=pS, kt=kt, m=m: e.matmul(
                            pS.t[:, :qn], lhsT=kt_.t[m * 64:(m + 1) * 64, kt * 128:(kt + 1) * 128],
                            rhs=qt_.t[m * 64:(m + 1) * 64, q0:q0 + qn], start=True, stop=True)],
                            reads=[kt_.r, qt_.r], writes=[pS.r])
                score(0)
                for i, kt in enumerate(kts):
                    if i + 1 < len(kts):
                        score(i + 1)
                    for m in range(2):
                        pS = self.ps[4 + 2 * m + i % 2]
                        pt = pts[cnt % 6]
                        cnt += 1
                        sa = sacc[m]
                        S.op("act", lambda e, pS=pS, pt=pt: e.activation(out=pt.t[:, :qn], in_=pS.t[:, :qn], func=AF.Exp, scale=0.125),
                             reads=[pS.r], writes=[pt.r])
                        S.mm([lambda e, pt=pt, kt=kt, m=m: e.matmul(
                            OT[m].t[:, :qn], lhsT=va_.t[:, kt, 0:128], rhs=pt.t[:, :qn], start=(kt == kts[0]), stop=(kt == kts[-1]))],
                            reads=[pt.r, va_.r], writes=[OT[m].r])
                        aeng = "dve"
                        if i == 0:
                            S.op(aeng, lambda e, pt=pt, sa=sa: e.tensor_copy(out=sa.t[:, :qn], in_=pt.t[:, :qn]), reads=[pt.r], writes=[sa.r])
                        else:
                            S.op(aeng, lambda e, pt=pt, sa=sa: e.tensor_tensor(out=sa.t[:, :qn], in0=sa.t[:, :qn], in1=pt.t[:, :qn], op=ALU.add),
                                 reads=[pt.r, sa.r], writes=[sa.r])
                for m in range(2):
                    SM = self.ps[2]
                    sa = sacc[m]
                    S.mm([lambda e, SM=SM, sa=sa: e.matmul(SM.t[:, :qn], lhsT=onesf.t[:], rhs=sa.t[:, :qn], start=True, stop=True)],
                         reads=[onesf.r, sa.r], writes=[SM.r])
                    S.op("dve", lambda e, SM=SM: e.reciprocal(out=rsb.t[:, :qn], in_=SM.t[:, :qn]), reads=[SM.r], writes=[rsb.r])
                    if m == 0:
                        S.op("dve", lambda e: e.tensor_tensor(out=osb.t[:, :qn], in0=OT[0].t[:, :qn], in1=rsb.t[:, :qn], op=ALU.mult),
                             reads=[OT[0].r, rsb.r], writes=[osb.r])
                    else:
                        S.op("dve", lambda e: e.tensor_tensor(out=o1.t[:, :qn], in0=OT[1].t[:, :qn], in1=rsb.t[:, :qn], op=ALU.mult),
                             reads=[OT[1].r, rsb.r], writes=[o1.r])
                S.op("dve", lambda e: e.scalar_tensor_tensor(out=osb.t[:, :qn], in0=o1.t[:, :qn], scalar=nlam.t[:, 0:1], in1=osb.t[:, :qn],
                                                             op0=ALU.mult, op1=ALU.add), reads=[o1.r, nlam.r, osb.r], writes=[osb.r])
                S.op("act", lambda e: e.activation(out=sqb.t[:, :qn], in_=osb.t[:, :qn], func=AF.Square), reads=[osb.r], writes=[sqb.r])
                SS = self.ps[3]
                S.mm([lambda e, SS=SS: e.matmul(SS.t[:, :qn], lhsT=self.onesb.t[:], rhs=sqb.t[:, :qn], start=True, stop=True)],
                     reads=[self.onesb.r, sqb.r], writes=[SS.r])
                S.op("act", lambda e, SS=SS: e.activation(out=rst.t[:, :qn], in_=SS.t[:, :qn], func=AF.Sqrt, bias=EPS, scale=1.0 / 128),
                     reads=[SS.r], writes=[rst.r])
                S.op("dve", lambda e: e.reciprocal(out=rst.t[:, :qn], in_=rst.t[:, :qn]), reads=[rst.r], writes=[rst.r])
                S.op("dve", lambda e: e.scalar_tensor_tensor(out=ca_.t[:, q0:q0 + qn], in0=osb.t[:, :qn], scalar=wsub.t[:, 0:1], in1=rst.t[:, :qn],
                                                             op0=ALU.mult, op1=ALU.mult), reads=[osb.r, wsub.r, rst.r], writes=[ca_.r])
            S.dma("st", self.catT[h * 128:(h + 1) * 128, :], ca_.t[:], reads=[ca_.r], writes=[S.dres("catA", h)])

    def ph_filters(self, n):
        S, din = self.S, self.din
        cm = (n == NCTX)
        zt = self.sb("zt", [33, n]); w0 = self.sb("w0", [33, 64]); b0 = self.sb("b0", [64, 1])
        w1 = self.sb("w1", [64, 2, 64]); b1 = self.sb("b1", [64, 2]); fr = self.sb("fr", [64, 1]); wo = self.sb("wo", [64, 2048])
        skp = self.sb("skp", [1, 1024]); dl = self.sb("dl", [128, 512]); tl = self.sb("tl", [128, n // 128])
        S.dma("sp", zt.t[:], din["zTc" if cm else "zT"], writes=[zt.r])
        S.dma("sp", w0.t[:], din["fw0"], writes=[w0.r]); S.dma("sp", b0.t[:], din["fb0"], writes=[b0.r])
        S.dma("sp", w1.t[:], din["fw1"].rearrange("i k m -> k i m"), writes=[w1.r]); S.dma("sp", b1.t[:], din["fb1"], writes=[b1.r])
        S.dma("sp", fr.t[:], din["ffreq"], writes=[fr.r]); S.dma("sp", wo.t[:], din["fwout"], writes=[wo.r])
        S.op("dve", lambda e: e.tensor_scalar(out=fr.t[:], in0=fr.t[:], scalar1=1.0 / (2.0 * math.pi), scalar2=None, op0=ALU.mult), reads=[fr.r], writes=[fr.r])
        S.dma("sp", skp.t[:], din["hskip"], writes=[skp.r]); S.dma("sp", dl.t[:], din["deltas"], writes=[dl.r])
        S.dma("sp", tl.t[:], din["tlc" if cm else "tl"], writes=[tl.r])
        hid = [self.sb("hid%d" % a, [64, n]) for a in range(2)]
        tmp = [self.sb("ftmp%d" % a, [64, 512]) for a in range(2)]
        gsz = min(512, n)
        TWO_PI = 2.0 * math.pi
        OFFS = math.pi + 16.0 * math.pi
        ci = 0
        for layer in range(3):
            src = zt if layer == 0 else hid[(layer - 1) % 2]
            dst = hid[layer % 2]
            bias = b0.t[:, 0:1] if layer == 0 else b1.t[:, layer - 1:layer]
            for g in range(n // gsz):
                ps = self.ps[ci % 2]; tm = tmp[ci % 2]
                ci += 1
                if layer == 0:
                    S.mm([lambda e, ps=ps, g=g: e.matmul(ps.t[:64, :gsz], lhsT=w0.t[:, :], rhs=zt.t[:, g * gsz:(g + 1) * gsz], start=True, stop=True)],
                         reads=[w0.r, zt.r], writes=[ps.r])
                else:
                    S.mm([lambda e, ps=ps, g=g, src=src, layer=layer: e.matmul(ps.t[:64, :gsz], lhsT=w1.t[:, layer - 1, :], rhs=src.t[:, g * gsz:(g + 1) * gsz],
                                                                             start=True, stop=True)], reads=[w1.r, src.r], writes=[ps.r])
                S.op("dve", lambda e, ps=ps, tm=tm, bias=bias: e.tensor_scalar(out=tm.t[:, :gsz], in0=ps.t[:64, :gsz], scalar1=bias, scalar2=fr.t[:, 0:1],
                                                                           op0=ALU.add, op1=ALU.mult), reads=[ps.r, b0.r, b1.r, fr.r], writes=[tm.r])
                for rnd in range(2):
                    S.op("dve", lambda e, tm=tm: e.scalar_tensor_tensor(out=tm.t[:, :gsz], in0=tm.t[:, :gsz], scalar=-0.5, in1=tm.t[:, :gsz],
                                                                    op0=ALU.is_lt, op1=ALU.add), reads=[tm.r], writes=[tm.r])
                    S.op("dve", lambda e, tm=tm: e.scalar_tensor_tensor(out=tm.t[:, :gsz], in0=tm.t[:, :gsz], scalar=0.5, in1=tm.t[:, :gsz],
                                                                    op0=ALU.is_gt, op1=ALU.subtract), reads=[tm.r], writes=[tm.r])
                S.op("act", lambda e, tm=tm, dst=dst, g=g: e.activation(out=dst.t[:, g * gsz:(g + 1) * gsz], in_=tm.t[:, :gsz], func=AF.Sin, scale=TWO_PI),
                     reads=[tm.r], writes=[dst.r])
        hfin = hid[0]
        wnd = [self.sb("wnd%d" % a, [128, 512]) for a in range(2)]
        ff = [self.sb("ff%d" % a, [128, 512]) for a in range(4)]
        ho = [self.sb("ho%d" % a, [128, 512], BF16) for a in range(4)]
        hi_ = 0
        for tt in range(n // 128):
            wn = wnd[tt % 2]
            S.op("act", lambda e, wn=wn, tt=tt: e.activation(out=wn.t[:], in_=dl.t[:], func=AF.Exp, scale=tl.t[:, tt:tt + 1]),
                 reads=[dl.r, tl.r], writes=[wn.r])
            for o in range(2):
                for d in range(2):
                    cb = o * 2 + d
                    ps = self.ps[2 + cb]
                    S.mm([lambda e, ps=ps, cb=cb, tt=tt: e.matmul(ps.t[:, :512], lhsT=hfin.t[:, tt * 128:(tt + 1) * 128], rhs=wo.t[:, cb * 512:(cb + 1) * 512],
                                                                 start=True, stop=True)], reads=[hfin.r, wo.r], writes=[ps.r])
                    f = ff[cb]
                    S.op("dve", lambda e, ps=ps, f=f, wn=wn: e.tensor_tensor(out=f.t[:], in0=ps.t[:, :512], in1=wn.t[:], op=ALU.mult),
                         reads=[ps.r, wn.r], writes=[f.r])
                    if tt == 0:
                        if d == 0:
                            S.op("dve", lambda e, f=f, o=o: e.tensor_tensor(out=f.t[0:1, :], in0=f.t[0:1, :], in1=skp.t[0:1, o * 512:(o + 1) * 512], op=ALU.add),
                                 reads=[f.r, skp.r], writes=[f.r])
                        else:
                            S.op("dve", lambda e, f=f: e.memset(f.t[0:1, :], 0.0), reads=[f.r], writes=[f.r])
                f0, f1 = ff[o * 2], ff[o * 2 + 1]
                for sd in range(2):
                    h_ = ho[hi_ % 4]
                    hi_ += 1
                    S.op("pool", lambda e, h_=h_, f0=f0, f1=f1, sd=sd: e.tensor_tensor(out=h_.t[:], in0=f0.t[:], in1=f1.t[:],
                                                                                    op=(ALU.add if sd == 0 else ALU.subtract)),
                         reads=[f0.r, f1.r], writes=[h_.r])
                    S.dma("st", self.HSD[o, sd, tt * 128:(tt + 1) * 128, :], h_.t[:], reads=[h_.r], writes=[S.dres("hsd", n, o, sd, tt)])

    def ph_hyena(self, n):
        S, din = self.S, self.din
        cm = (n == NCTX)
        T0 = 0 if cm else NCTX
        nt = n // 128
        nf = nt + 1
        dC, dS = (din["dftCc"], din["dftSc"]) if cm else (din["dftC"], din["dftS"])
        wft = self.sb("wft", [128, nf])
        S.dma("sp", wft.t[:], din["wfc" if cm else "wf"], writes=[wft.r])
        vt = self.sb("vt", [128, nt, 512], BF16)
        Y = [self.sb("Y%d" % a, [128, nf, 512], BF16) for a in range(2)]
        blk = [self.sb("blk%d" % a, [128, nf, 128], BF16) for a in range(4)]
        hst = [self.sb("hst%d" % a, [128, 512]) for a in range(4)]
        bi = 0

        def load_blk(src, b, rows):
            nonlocal bi
            t = blk[bi % 4]
            bi += 1
            S.dma("sp", t.t[:, :rows, :], src[b][:, 0:rows * 128].rearrange("p (i w) -> p i w", w=128), writes=[t.r])
            return t
        pi_ = 0
        for o in range(2):
            for cs in range(2):
                S.dma("sp", vt.t[:], self.HSD[o, cs, 0:n, :].rearrange("(i p) c -> p i c", p=128), writes=[vt.r])
                for fb in range(nf):
                    t = load_blk(dC if cs == 0 else dS, fb, nt)
                    ps = self.ps[pi_ % 4]; hs_ = hst[pi_ % 4]
                    pi_ += 1
                    S.mm([lambda e, i=i, t=t, ps=ps: e.matmul(ps.t[:, :512], lhsT=t.t[:, i, :], rhs=vt.t[:, i, :], start=(i == 0), stop=(i == nt - 1))
                          for i in range(nt)], reads=[t.r, vt.r], writes=[ps.r])
                    S.op("act", lambda e, ps=ps, hs_=hs_, fb=fb: e.activation(out=hs_.t[:], in_=ps.t[:, :512], func=AF.Copy, scale=wft.t[:, fb:fb + 1]),
                         reads=[ps.r, wft.r], writes=[hs_.r])
                    S.dma("st", self.HCS[o, cs, fb * 128:(fb + 1) * 128, :], hs_.t[:], reads=[hs_.r], writes=[S.dres("hcs", o, cs, fb)])
        S.dma("sp", vt.t[:], self.VTM[T0:T0 + n, :].rearrange("(i p) c -> p i c", p=128), reads=[S.dres("hcs", 1, 1, nf - 1)], writes=[vt.r])
        Hc = [self.sb("Hc%d" % a, [128, 512]) for a in range(2)]
        Hs = [self.sb("Hs%d" % a, [128, 512]) for a in range(2)]
        tq = [self.sb("tq%d" % a, [128, 512]) for a in range(4)]
        xs = [self.sb("xs%d" % a, [128, 512]) for a in range(2)]
        bt = self.sb("bt", [128, 512])
        bst = [self.sb("bst%d" % a, [128, 4, 128], BF16) for a in range(2)]
        for o in range(2):
            for fb in range(nf):
                tC = load_blk(dC, fb, nt)
                tS = load_blk(dS, fb, nt)
                hc, hs = Hc[fb % 2], Hs[fb % 2]
                S.dma("sp", hc.t[:], self.HCS[o, 0, fb * 128:(fb + 1) * 128, :], reads=[S.dres("hcs", o, 0, fb)], writes=[hc.r])
                S.dma("sp", hs.t[:], self.HCS[o, 1, fb * 128:(fb + 1) * 128, :], reads=[S.dres("hcs", o, 1, fb)], writes=[hs.r])
                pC, pS = self.ps[2 * (fb % 2)], self.ps[2 * (fb % 2) + 1]
                S.mm([lambda e, i=i, tC=tC, pC=pC: e.matmul(pC.t[:, :512], lhsT=tC.t[:, i, :], rhs=vt.t[:, i, :], start=(i == 0), stop=(i == nt - 1))
                      for i in range(nt)], reads=[tC.r, vt.r], writes=[pC.r])
                S.mm([lambda e, i=i, tS=tS, pS=pS: e.matmul(pS.t[:, :512], lhsT=tS.t[:, i, :], rhs=vt.t[:, i, :], start=(i == 0), stop=(i == nt - 1))
                      for i in range(nt)], reads=[tS.r, vt.r], writes=[pS.r])
                a1, a2, a3, a4 = tq
                S.op("dve", lambda e, pC=pC, hc=hc: e.tensor_tensor(out=a1.t[:], in0=pC.t[:, :512], in1=hc.t[:], op=ALU.mult), reads=[pC.r, hc.r], writes=[a1.r])
                S.op("dve", lambda e, pS=pS, hs=hs: e.tensor_tensor(out=a2.t[:], in0=pS.t[:, :512], in1=hs.t[:], op=ALU.mult), reads=[pS.r, hs.r], writes=[a2.r])
                S.op("pool", lambda e, fb=fb: e.tensor_tensor(out=Y[0].t[:, fb, :], in0=a1.t[:], in1=a2.t[:], op=ALU.subtract), reads=[a1.r, a2.r], writes=[Y[0].r])
                S.op("dve", lambda e, pC=pC, hs=hs: e.tensor_tensor(out=a3.t[:], in0=pC.t[:, :512], in1=hs.t[:], op=ALU.mult), reads=[pC.r, hs.r], writes=[a3.r])
                S.op("dve", lambda e, pS=pS, hc=hc: e.tensor_tensor(out=a4.t[:], in0=pS.t[:, :512], in1=hc.t[:], op=ALU.mult), reads=[pS.r, hc.r], writes=[a4.r])
                S.op("pool", lambda e, fb=fb: e.tensor_tensor(out=Y[1].t[:, fb, :], in0=a3.t[:], in1=a4.t[:], op=ALU.add), reads=[a3.r, a4.r], writes=[Y[1].r])
            for tb in range(nt):
                tC = load_blk(dC, tb, nf)
                tS = load_blk(dS, tb, nf)
                py = self.ps[4 + tb % 2]
                x_ = xs[tb % 2]
                S.dma("sp", x_.t[:], self.UTM[T0 + tb * 128:T0 + (tb + 1) * 128, o * 512:(o + 1) * 512], writes=[x_.r])
                fns = []
                for j in range(nf):
                    fns.append(lambda e, j=j, tC=tC, py=py: e.matmul(py.t[:, :512], lhsT=tC.t[:, j, :], rhs=Y[0].t[:, j, :], start=(j == 0), stop=False))
                    fns.append(lambda e, j=j, tS=tS, py=py: e.matmul(py.t[:, :512], lhsT=tS.t[:, j, :], rhs=Y[1].t[:, j, :], start=False, stop=(j == nf - 1)))
                S.mm(fns, reads=[tC.r, tS.r, Y[0].r, Y[1].r], writes=[py.r])
                if o == 0:
                    S.op("dve", lambda e, py=py, x_=x_, tb=tb: e.tensor_tensor(out=vt.t[:, tb, :], in0=py.t[:, :512], in1=x_.t[:], op=ALU.mult),
                         reads=[py.r, x_.r], writes=[vt.r])
                else:
                    S.op("dve", lambda e, py=py, x_=x_: e.tensor_tensor(out=bt.t[:], in0=py.t[:, :512], in1=x_.t[:], op=ALU.mult),
                         reads=[py.r, x_.r], writes=[bt.r])
                    pT = self.ps[6 + tb % 2]
                    S.mm([lambda e, c=c, pT=pT: e.transpose(pT.t[:, c * 128:(c + 1) * 128], bt.t[:, c * 128:(c + 1) * 128], self.ident.t[:])
                          for c in range(4)], reads=[bt.r, self.ident.r], writes=[pT.r])
                    b_ = bst[tb % 2]
                    S.op("act", lambda e, pT=pT, b_=b_: e.copy(out=b_.t[:], in_=pT.t[:, :].rearrange("p (c w) -> p c w", w=128)), reads=[pT.r], writes=[b_.r])
                    dst = self.catT[512:1024, T0 + tb * 128:T0 + (tb + 1) * 128].rearrange("(c p) t -> p c t", p=128)
                    S.dma("st", dst, b_.t[:], reads=[b_.r], writes=[S.dres("catB", n, tb)])

    def ph_outproj(self, l):
        S = self.S
        wo = self.sb("wo", [128, 8, D], BF16)
        S.dma("sp", wo.t[:], (self.aboutb if l == 0 else self.naoutb).rearrange("(k p) n -> p k n", p=128), writes=[wo.r])
        xts = [self.sb("oxt%d" % a, [128, 8, GS]) for a in range(2)]
        cts = [self.sb("oct%d" % a, [128, 8, GS], BF16) for a in range(2)]
        lv = self.latT.rearrange("(c p) t -> p c t", p=128)
        cv = self.catT.rearrange("(c p) t -> p c t", p=128)
        gi = 0
        for (t0, n, s) in self.groups(l == 0):
            xt, ct = xts[gi % 2], cts[gi % 2]
            gi += 1
            S.dma("sp", xt.t[:, :, :n], lv[:, :, t0:t0 + n], writes=[xt.r])
            S.dma("sp", ct.t[:, :, :n], cv[:, :, t0:t0 + n], writes=[ct.r])
            gate = self.mcol(l, 1, 2, s)
            for dc in range(8):
                py = self.ps[dc % 4]
                S.mm([lambda e, k=k, py=py, dc=dc: e.matmul(py.t[:, :n], lhsT=wo.t[:, k, dc * 128:(dc + 1) * 128], rhs=ct.t[:, k, :n],
                                                          start=(k == 0), stop=(k == 7)) for k in range(8)], reads=[wo.r, ct.r], writes=[py.r])
                S.op("dve", lambda e, py=py, dc=dc: e.scalar_tensor_tensor(out=xt.t[:, dc, :n], in0=py.t[:, :n], scalar=gate[:, dc:dc + 1], in1=xt.t[:, dc, :n],
                                                                       op0=ALU.mult, op1=ALU.add), reads=[py.r, xt.r, self.msc.r], writes=[xt.r])
            self.store_group(xt, t0, n)

    def ph_naproj(self):
        S = self.S
        self.alloc_pro()
        win = self.sb("win", [128, 8, 3072], BF16)
        wv = self.nainb.rearrange("(k p) n -> p k n", p=128)
        for a in range(3):
            S.dma("sp", win.t[:, :, a * 1024:(a + 1) * 1024], wv[:, :, a * 1024:(a + 1) * 1024], writes=[win.r])
        qo = [self.sb("qo%d" % a, [128, GS], BF16) for a in range(4)]
        vo = [self.sb("vo%d" % a, [128, 16, 65], BF16) for a in range(2)]
        for v in vo:
            S.op("dve", lambda e, v=v: e.memset(v.t[:], 1.0), writes=[v.r])
        ci = 0
        for (t0, n, s) in self.groups(True):
            self.prologue(1, 1, t0, n, s, 6)
            hT = self.hT
            for which in range(2):
                if which == 0 and s == 1:
                    continue
                for hc in range(8):
                    col = which * 1024 + hc * 128
                    pa = self.ps[ci % 4]; q = qo[ci % 4]
                    ci += 1
                    S.mm([lambda e, k=k, pa=pa, col=col: e.matmul(pa.t[:, :n], lhsT=win.t[:, k, col:col + 128], rhs=hT.t[:, k, :n],
                                                                start=(k == 0), stop=(k == 7)) for k in range(8)], reads=[win.r, hT.r], writes=[pa.r])
                    if ci % 2 == 0:
                        S.op("act", lambda e, q=q, pa=pa: e.copy(out=q.t[:, :n], in_=pa.t[:, :n]), reads=[pa.r], writes=[q.r])
                    else:
                        S.op("dve", lambda e, q=q, pa=pa: e.tensor_copy(out=q.t[:, :n], in_=pa.t[:, :n]), reads=[pa.r], writes=[q.r])
                    dst = (self.QT if which == 0 else self.KT)[hc * 128:(hc + 1) * 128, t0:t0 + n]
                    S.dma("st", dst, q.t[:, :n], reads=[q.r], writes=[S.dres("qk2", which, hc, t0)])
            for tt in range(n // 128):
                v = vo[tt % 2]
                for hf in range(2):
                    pv = self.ps[4 + hf]
                    S.mm([lambda e, k=k, pv=pv, tt=tt, hf=hf: e.matmul(pv.t[:, :512], lhsT=hT.t[:, k, tt * 128:(tt + 1) * 128],
                                                                      rhs=win.t[:, k, 2048 + hf * 512:2048 + (hf + 1) * 512], start=(k == 0), stop=(k == 7))
                          for k in range(8)], reads=[win.r, hT.r], writes=[pv.r])
                    S.op("act" if hf == 0 else "dve",
                         (lambda e, pv=pv, v=v, hf=hf: e.copy(out=v.t[:, hf * 8:(hf + 1) * 8, 0:64], in_=pv.t[:, :].rearrange("p (h d) -> p h d", d=64))) if hf == 0 else
                         (lambda e, pv=pv, v=v, hf=hf: e.tensor_copy(out=v.t[:, hf * 8:(hf + 1) * 8, 0:64], in_=pv.t[:, :].rearrange("p (h d) -> p h d", d=64))),
                         reads=[pv.r], writes=[v.r])
                S.dma("st", self.VA[t0 + tt * 128:t0 + (tt + 1) * 128, :].rearrange("p (h d) -> p h d", d=65), v.t[:],
                      reads=[v.r], writes=[S.dres("va2", t0, tt)])

    def ph_na(self):
        S, din = self.S, self.din
        blocks = self.na_blocks
        ntp = self.ntypes
        ktb = [self.sb("ktb%d" % a, [128, TT], BF16) for a in range(2)]
        qtb = [self.sb("qtb%d" % a, [128, TT], BF16) for a in range(2)]
        vab = [self.sb("vab%d" % a, [128, TT // 128, 130], BF16) for a in range(2)]
        bib = [self.sb("bib%d" % a, [128, ntp * 2 * 7 * 128]) for a in range(2)]
        obb = [self.sb("obb%d" % a, [128, SEQ], BF16) for a in range(2)]
        tmA = [self.sb("tmA%d" % a, [128, 512]) for a in range(2)]
        tmB = [self.sb("tmB%d" % a, [128, 384]) for a in range(2)]
        PA = [self.sb("PA%d" % a, [128, 512], BF16) for a in range(2)]
        PB = [self.sb("PB%d" % a, [128, 384], BF16) for a in range(2)]
        o2 = [self.sb("o2_%d" % a, [128, 128]) for a in range(2)]
        rr = self.sb("rr", [128, 2])
        its = [(qb, hh) for qb in range(32) for hh in range(2)]

        def geom(qb):
            lo, hi, ty = blocks[qb]
            nk = (hi - lo) * 64
            tile0 = (NCTX + lo * 64) // 128
            nfull = nk // 128
            return ty, tile0, nfull, (nk % 128 != 0)
        for hc in range(8):
            kt_, qt_, va_, bi_, ob_ = ktb[hc % 2], qtb[hc % 2], vab[hc % 2], bib[hc % 2], obb[hc % 2]
            S.dma("sp", kt_.t[:], self.KT[hc * 128:(hc + 1) * 128, :], writes=[kt_.r])
            S.dma("sp", qt_.t[:, NCTX:], self.QT[hc * 128:(hc + 1) * 128, NCTX:], writes=[qt_.r])
            S.dma("sp", va_.t[:], self.VA[:, hc * 130:(hc + 1) * 130].rearrange("(i p) w -> p i w", p=128), writes=[va_.r])
            S.dma("sp", bi_.t[:], din["nab"][hc], writes=[bi_.r])

            def tiles(qb):
                ty, tile0, nfull, half = geom(qb)
                tl = [(0, 128), (1, 128)] + [(tile0 + a, 128) for a in range(nfull)]
                if half:
                    tl.append((tile0 + nfull, 64))
                return tl

            def scores(it):
                qb, hh = its[it]
                A, B = self.ps[2 + (it % 2) * 2], self.ps[3 + (it % 2) * 2]
                q0 = NCTX + qb * 128
                fns = []
                for idx, (tile, sz) in enumerate(tiles(qb)):
                    dst = A.t[:sz, idx * 128:(idx + 1) * 128] if idx < 4 else B.t[:sz, (idx - 4) * 128:(idx - 3) * 128]
                    fns.append(lambda e, dst=dst, tile=tile, sz=sz, hh=hh, q0=q0: e.matmul(
                        dst, lhsT=kt_.t[hh * 64:(hh + 1) * 64, tile * 128:tile * 128 + sz],
                        rhs=qt_.t[hh * 64:(hh + 1) * 64, q0:q0 + 128], start=True, stop=True))
                S.mm(fns, reads=[kt_.r, qt_.r], writes=[A.r, B.r])
            def softmax_pv(it):
                qb, hh = its[it]
                ty = geom(qb)[0]
                tl = tiles(qb)
                nb = (len(tl) - 4) * 128
                A, B = self.ps[2 + (it % 2) * 2], self.ps[3 + (it % 2) * 2]
                ta, tb_, pa, pb = tmA[it % 2], tmB[it % 2], PA[it % 2], PB[it % 2]
                bo = (ty * 2 + hh) * 896
                S.op("dve", lambda e, A=A, ta=ta, bo=bo: e.scalar_tensor_tensor(
                    out=ta.t[:], in0=A.t[:, 0:512], scalar=0.125, in1=bi_.t[:, bo:bo + 512], op0=ALU.mult, op1=ALU.add),
                    reads=[A.r, bi_.r], writes=[ta.r])
                S.op("act", lambda e, ta=ta, pa=pa: e.activation(out=pa.t[:], in_=ta.t[:], func=AF.Exp), reads=[ta.r], writes=[pa.r])
                S.op("dve", lambda e, B=B, tb_=tb_, bo=bo, nb=nb: e.scalar_tensor_tensor(
                    out=tb_.t[:, :nb], in0=B.t[:, 0:nb], scalar=0.125, in1=bi_.t[:, bo + 512:bo + 512 + nb], op0=ALU.mult, op1=ALU.add),
                    reads=[B.r, bi_.r], writes=[tb_.r])
                S.op("act", lambda e, tb_=tb_, pb=pb, nb=nb: e.activation(out=pb.t[:, :nb], in_=tb_.t[:, :nb], func=AF.Exp), reads=[tb_.r], writes=[pb.r])
                po = self.ps[hh]
                fns = []
                for idx, (tile, sz) in enumerate(tl):
                    src = pa.t[:sz, idx * 128:(idx + 1) * 128] if idx < 4 else pb.t[:sz, (idx - 4) * 128:(idx - 3) * 128]
                    fns.append(lambda e, po=po, src=src, tile=tile, sz=sz, hh=hh, idx=idx, n_=len(tl): e.matmul(
                        po.t[:, 0:65], lhsT=src, rhs=va_.t[:sz, tile, hh * 65:(hh + 1) * 65], start=(idx == 0), stop=(idx == n_ - 1)))
                S.mm(fns, reads=[pa.r, pb.r, va_.r], writes=[po.r])

            def epilogue(it):
                qb, hh = its[it]
                po = self.ps[hh]
                oo = o2[qb % 2]
                S.op("dve", lambda e, po=po, hh=hh: e.reciprocal(out=rr.t[:, hh:hh + 1], in_=po.t[:, 64:65]), reads=[po.r], writes=[rr.r])
                S.op("dve", lambda e, po=po, hh=hh, oo=oo: e.tensor_scalar(out=oo.t[:, hh * 64:(hh + 1) * 64], in0=po.t[:, 0:64], scalar1=rr.t[:, hh:hh + 1],
                                                                       scalar2=None, op0=ALU.mult), reads=[po.r, rr.r], writes=[oo.r])
                if hh == 1:
                    pT = self.ps[6 + qb % 2]
                    S.mm([lambda e, pT=pT, oo=oo: e.transpose(pT.t[:, 0:128], oo.t[:], self.ident.t[:])], reads=[oo.r, self.ident.r], writes=[pT.r])
                    S.op("act", lambda e, pT=pT, qb=qb: e.copy(out=ob_.t[:, qb * 128:(qb + 1) * 128], in_=pT.t[:, 0:128]), reads=[pT.r], writes=[ob_.r])
            scores(0)
            for it in range(len(its)):
                if it + 1 < len(its):
                    scores(it + 1)
                softmax_pv(it)
                if it >= 1:
                    epilogue(it - 1)
            epilogue(len(its) - 1)
            S.dma("st", self.catT[hc * 128:(hc + 1) * 128, NCTX:], ob_.t[:], reads=[ob_.r], writes=[S.dres("catN", hc)])

    def ph_final(self):
        S = self.S
        self.alloc_pro()
        fw = self.sb("fw", [128, 8])
        S.dma("sp", fw.t[:], self.din["fnormT"], writes=[fw.r])
        yT = [self.sb("yT%d" % a, [128, 8, GS]) for a in range(2)]
        ot = [self.sb("ot%d" % a, [128, D]) for a in range(2)]
        gi = 0
        oi = 0
        for (t0, n, s) in self.groups(False):
            xt = self.prologue(1, 2, t0, n, s, 6, want_h=False)
            y = yT[gi % 2]
            gi += 1
            for c in range(8):
                S.op("dve", lambda e, c=c, y=y, xt=xt: e.scalar_tensor_tensor(
                    out=y.t[:, c, :n], in0=xt.t[:, c, :n], scalar=fw.t[:, c:c + 1], in1=self.rstd.t[:, :n], op0=ALU.mult, op1=ALU.mult),
                    reads=[xt.r, fw.r, self.rstd.r], writes=[y.r])
            for tt in range(n // 128):
                o = ot[oi % 2]
                oi += 1
                for hf in range(2):
                    ps = self.ps[hf * 2 + (tt % 2)]
                    S.mm([lambda e, c=c, ps=ps, hf=hf, tt=tt, y=y: e.transpose(ps.t[:, c * 128:(c + 1) * 128], y.t[:, hf * 4 + c, tt * 128:(tt + 1) * 128],
                                                                            self.ident.t[:]) for c in range(4)], reads=[y.r, self.ident.r], writes=[ps.r])
                    if hf == 0:
                        S.op("act", lambda e, ps=ps, o=o: e.copy(out=o.t[:, 0:512], in_=ps.t[:, :]), reads=[ps.r], writes=[o.r])
                    else:
                        S.op("dve", lambda e, ps=ps, o=o: e.tensor_copy(out=o.t[:, 512:1024], in_=ps.t[:, :]), reads=[ps.r], writes=[o.r])
                r0 = t0 - NCTX + tt * 128
                S.dma("st", self.out[r0:r0 + 128, :], o.t[:], reads=[o.r], writes=[S.dres("out", r0)])


def _host_inputs(inputs, b, consts, nab):
    f32 = np.float32
    g = lambda k: np.asarray(inputs[k], dtype=f32)
    m = {}
    m["x"] = np.ascontiguousarray(g("x")[b])
    m["ctxi"] = np.ascontiguousarray(g("ctx")[b])
    sv = np.stack([_fm(g("c")[b], 8), _fm(g("c_ctx"), 8)], axis=-1)
    m["sv"] = np.ascontiguousarray(sv.reshape(128, 16))
    m["mod_w"] = g("mod_w")
    mb = np.stack([_fm(g("mod_b")[l], 72) for l in range(2)], axis=1)
    m["mod_b2"] = np.ascontiguousarray(np.repeat(mb[:, :, None, :], 2, axis=2).reshape(128, 288))
    nw = g("norm_w")
    nt = np.stack([np.stack([_fm(nw[l, k], 8) for k in range(3)], axis=1) for l in range(2)], axis=1)
    m["normT2"] = np.ascontiguousarray(np.repeat(nt[:, :, :, None, :], 2, axis=3).reshape(128, 96))
    m["fnormT"] = _fm(g("final_norm_w"), 8)
    m["ffn_w1"] = g("ffn_w1"); m["ffn_w3"] = g("ffn_w3"); m["ffn_w2"] = g("ffn_w2")
    wi = g("ab_w_in")[0]
    m["ab_w_in"] = wi
    perm = consts["perm"]
    cols = np.concatenate([hc * 128 + perm for hc in range(4)] + [512 + hc * 128 + perm for hc in range(4)])
    m["ab_w_inp"] = np.ascontiguousarray(wi[:, cols])
    m["ab_w_out"] = g("ab_w_out")[0]
    m["lamp"] = np.ascontiguousarray(np.broadcast_to(g("diff_lambda")[0].reshape(1, 256), (128, 256)))
    m["subw"] = np.ascontiguousarray(g("diff_subln_w")[0].reshape(128, 1))
    cw = g("hy_conv_w")[0]
    m["hcw"] = np.ascontiguousarray(np.stack([_fm(cw[j], 12) for j in range(3)], axis=-1).reshape(128, 36))
    m["hcb"] = _fm(g("hy_conv_b")[0], 12)
    m["fw0"] = g("hy_f_w0")[0]; m["fb0"] = np.ascontiguousarray(g("hy_f_b0")[0].reshape(64, 1))
    m["fw1"] = g("hy_f_w1")[0]; m["fb1"] = np.ascontiguousarray(g("hy_f_b1")[0].T)
    m["ffreq"] = np.ascontiguousarray(g("hy_f_freq")[0].reshape(64, 1))
    m["fwout"] = g("hy_f_wout")[0]
    m["hskip"] = np.ascontiguousarray(g("hy_bias")[0].reshape(1, 1024))
    m["na_w_in"] = g("na_w_in")[0]; m["na_w_out"] = g("na_w_out")[0]
    m["nab"] = nab
    for k in ("ident", "ropeC", "ropeS", "dftC", "dftS", "wf", "dftCc", "dftSc", "wfc", "zT", "zTc", "tl", "tlc", "deltas"):
        m[k] = consts[k]
    return m


_PROG = {}


def run(inputs, cores, dbg=None):
    consts = _consts()
    nab, blocks, nt = _na_bias(np.asarray(inputs["na_rpb"], np.float32)[0])
    key = (dbg,)
    if key not in _PROG:
        p = Prog(dbg=dbg, nab_cols=nab.shape[2], na_blocks=blocks)
        p.ntypes = nt
        _PROG[key] = p.build()
    nc = _PROG[key]
    in_maps = [_host_inputs(inputs, b, consts, nab) for b in cores]
    res = run_bass_kernel_spmd(nc, in_maps, core_ids=list(range(len(cores))))
    return res


def kernel(**inputs):
    res = run(inputs, list(range(8)))
    return np.stack([np.asarray(r["out"], dtype=np.float32) for r in res.results], axis=0)
```

```python
import math
from contextlib import ExitStack
import numpy as np
import ml_dtypes
import concourse.bass as bass
import concourse.mybir as mybir
from concourse.bass_utils import run_bass_kernel_spmd

F32 = mybir.dt.float32
BF16 = mybir.dt.bfloat16
AF = mybir.ActivationFunctionType
ALU = mybir.AluOpType
AX = mybir.AxisListType

D = 1024; SEQ = 4096; NCTX = 256; TT = SEQ + NCTX; DFF = 2816; NJ = DFF // 128
GRID_W = 64; EPS = 1e-6
GS = 512
SAME_ENGINE_SYNC = True


class Res:
    __slots__ = ("w", "r")

    def __init__(self):
        self.w = None
        self.r = {}


class Sched:
    NSLOT = 10

    def __init__(self, nc, es):
        self.nc = nc
        self.eng = {"pe": nc.tensor, "act": nc.scalar, "dve": nc.vector, "pool": nc.gpsimd, "sp": nc.sync}
        self.sem = {e: es.enter_context(nc.semaphore("s_" + e)) for e in ("pe", "act", "dve", "pool")}
        self.cnt = {e: 0 for e in self.sem}
        self.seen = {e: {} for e in self.eng}
        self.dsem = {q: [es.enter_context(nc.semaphore("d_%s_%d" % (q, i))) for i in range(self.NSLOT)]
                     for q in ("sp", "pool", "act")}
        self.dcnt = {q: [0] * self.NSLOT for q in self.dsem}
        self.dnext = {q: 0 for q in self.dsem}
        self.dram = {}

    def dres(self, *key):
        r = self.dram.get(key)
        if r is None:
            r = self.dram[key] = Res()
        return r

    def _semof(self, key):
        return self.sem[key[1]] if key[0] == "c" else self.dsem[key[1]][key[2]]

    def _wait(self, eng, key, val):
        if self.seen[eng].get(key, 0) >= val:
            return
        self.seen[eng][key] = val
        self.eng[eng].wait_ge(self._semof(key), val)

    def _deps(self, eng, reads, writes):
        best = {}
        for r in reads:
            if r.w is not None:
                k, v = r.w
                if best.get(k, 0) < v:
                    best[k] = v
        for w in writes:
            if w.w is not None:
                k, v = w.w
                if best.get(k, 0) < v:
                    best[k] = v
            for k, v in w.r.items():
                if best.get(k, 0) < v:
                    best[k] = v
        for k, v in best.items():
            if k == ("c", eng) and (eng == "pe" or not SAME_ENGINE_SYNC):
                continue
            self._wait(eng, k, v)

    def _mark(self, key, val, reads, writes):
        for r in reads:
            if r.r.get(key, 0) < val:
                r.r[key] = val
        for w in writes:
            w.w = (key, val)
            w.r = {}

    def op(self, eng, fn, reads=(), writes=()):
        self._deps(eng, reads, writes)
        self.cnt[eng] += 1
        fn(self.eng[eng]).then_inc(self.sem[eng], 1)
        self._mark(("c", eng), self.cnt[eng], reads, writes)

    def mm(self, fns, reads=(), writes=()):
        self._deps("pe", reads, writes)
        ins = None
        for f in fns:
            ins = f(self.nc.tensor)
        self.cnt["pe"] += 1
        ins.then_inc(self.sem["pe"], 1)
        self._mark(("c", "pe"), self.cnt["pe"], reads, writes)

    def dma(self, q, out, in_, reads=(), writes=(), **kw):
        if q == "st":
            q = "act"
        slot = self.dnext[q]
        self.dnext[q] = (slot + 1) % self.NSLOT
        key = ("d", q, slot)
        if self.dcnt[q][slot] > 0:
            self._wait(q, key, self.dcnt[q][slot])
        self._deps(q, reads, writes)
        self.dcnt[q][slot] += 16
        self.eng[q].dma_start(out=out, in_=in_, **kw).then_inc(self.dsem[q][slot], 16)
        self._mark(key, self.dcnt[q][slot], reads, writes)

    def barrier(self):
        keys = [(("c", e), self.cnt[e]) for e in self.cnt]
        for q in self.dsem:
            for i in range(self.NSLOT):
                keys.append((("d", q, i), self.dcnt[q][i]))
        for e in self.eng:
            for k, v in keys:
                if v > 0:
                    self._wait(e, k, v)


class Tl:
    def __init__(self, t, nres=1):
        self.t = t
        self.res = [Res() for _ in range(nres)]

    @property
    def r(self):
        return self.res[0]


def _bf(a):
    return np.asarray(a, dtype=np.float32).astype(ml_dtypes.bfloat16)


def _dft_blocks(n):
    ne = n + 128
    idx = np.arange(n, dtype=np.int64)
    m = (idx[:, None] * idx[None, :]) % (2 * n)
    ang = m.astype(np.float64) * (math.pi / n)
    C = np.zeros((ne, ne), np.float64)
    S = np.zeros((ne, ne), np.float64)
    C[:n, :n] = np.cos(ang)
    S[:n, :n] = np.sin(ang)
    alt = np.where(idx % 2 == 0, 1.0, -1.0)
    C[:n, n] = alt
    C[n, :n] = alt
    nt = ne // 128

    def blk(M):
        return np.ascontiguousarray(M.reshape(nt, 128, nt, 128).transpose(2, 1, 0, 3)).reshape(nt, 128, nt * 128)
    wf = np.full((ne,), 1.0 / n, np.float32)
    wf[0] = 0.5 / n
    wf[n] = 0.5 / n
    wf[n + 1:] = 0.0
    return _bf(blk(C)), _bf(blk(S)), np.ascontiguousarray(wf.reshape(nt, 128).T)


def _filter_consts(n):
    t = np.linspace(0.0, 1.0, n, dtype=np.float32)[:, None]
    w = (2.0 * math.pi / n) * np.arange(n, dtype=np.float32)[:, None]
    bands = np.linspace(1e-4, 15, 16, dtype=np.float32)[None, :]
    z = np.concatenate([t, np.cos(bands * w), -np.sin(bands * w)], axis=-1).astype(np.float32)
    tl = np.ascontiguousarray((-t[:, 0]).reshape(n // 128, 128).T).astype(np.float32)
    return np.ascontiguousarray(z.T), tl


def _rope_tables():
    t = np.arange(SEQ)
    pos = (t // GRID_W, t % GRID_W)
    inv = (10000.0 ** (-np.arange(16, dtype=np.float32) / 16)).astype(np.float32)
    C = np.zeros((128, SEQ), np.float32)
    Sg = np.zeros((128, SEQ), np.float32)
    for m in range(2):
        for a in range(2):
            ang = pos[a].astype(np.float32)[None, :] * inv[:, None]
            c = np.cos(ang).astype(np.float32)
            s = np.sin(ang).astype(np.float32)
            b = m * 64 + a * 32
            C[b:b + 16] = c
            C[b + 16:b + 32] = c
            Sg[b:b + 16] = -s
            Sg[b + 16:b + 32] = s
    perm = np.zeros(128, np.int64)
    for m in range(2):
        for a in range(2):
            b = m * 64 + a * 32
            perm[b:b + 16] = np.arange(b + 16, b + 32)
            perm[b + 16:b + 32] = np.arange(b, b + 16)
    return C, Sg, perm


def _na_geometry():
    blocks = []
    types = {}
    for qb in range(32):
        r0 = 2 * qb
        rs = [min(max(r - 4, 0), 56) for r in (r0, r0 + 1)]
        lo, hi = min(rs), max(rs) + 8
        sig = (hi - lo, rs[0] - lo, rs[1] - lo, r0 - lo)
        if sig not in types:
            types[sig] = len(types)
        blocks.append((lo, hi, types[sig]))
    return blocks, types


def _na_bias(rpb):
    blocks, types = _na_geometry()
    nt = len(types)
    out = np.zeros((nt, 16, 7 * 128, 128), np.float32)
    out[:, :, 256:, :] = -30000.0
    kk = np.arange(640)
    ki, kc = kk // 64, kk % 64
    qq = np.arange(128)
    qj, qc = qq // 64, qq % 64
    cs = np.clip(qc - 8, 0, 48)
    for sig, ti in types.items():
        nrows, rs0, rs1, r0l = sig
        rs = np.array([rs0, rs1])[qj]
        qr = r0l + qj
        valid = (ki[:, None] < nrows) & (ki[:, None] >= rs[None, :]) & (ki[:, None] < rs[None, :] + 8) \
            & (kc[:, None] >= cs[None, :]) & (kc[:, None] < cs[None, :] + 16)
        dr = np.clip(ki[:, None] - qr[None, :] + 7, 0, 14)
        dc = np.clip(kc[:, None] - qc[None, :] + 15, 0, 30)
        g = rpb[:, dr, dc]
        out[ti, :, 256:, :] = np.where(valid[None], g, np.float32(-30000.0))
    o = out.reshape(nt, 8, 2, 7, 128, 128).transpose(1, 4, 0, 2, 3, 5)
    return np.ascontiguousarray(o).reshape(8, 128, nt * 2 * 7 * 128), blocks, nt


_CONSTS = {}


def _consts():
    if _CONSTS:
        return _CONSTS
    c = _CONSTS
    c["dftC"], c["dftS"], c["wf"] = _dft_blocks(SEQ)
    c["dftCc"], c["dftSc"], c["wfc"] = _dft_blocks(NCTX)
    c["zT"], c["tl"] = _filter_consts(SEQ)
    c["zTc"], c["tlc"] = _filter_consts(NCTX)
    hy_min = math.log(1e-2) / 1.5
    hy_max = math.log(1e-2) / 0.3
    deltas = np.abs(np.linspace(hy_min, hy_max, 512, dtype=np.float32))
    c["deltas"] = np.ascontiguousarray(np.broadcast_to(deltas[None, :], (128, 512))).astype(np.float32)
    c["ropeC"], c["ropeS"], c["perm"] = _rope_tables()
    c["ident"] = np.eye(128, dtype=np.float32)
    return c


def _fm(v, nch):
    return np.ascontiguousarray(np.asarray(v, np.float32).reshape(nch, 128).T)


class Prog:
    def __init__(self, dbg=None, nab_cols=0, na_blocks=None):
        self.dbg = dbg
        self.nab_cols = nab_cols
        self.na_blocks = na_blocks
        self.nc = bass.Bass("TRN2", target_bir_lowering=False)
        self.din = {}

    def inp(self, name, shape, dt=F32):
        self.din[name] = self.nc.dram_tensor(name, list(shape), dt, kind="ExternalInput").ap()
        return self.din[name]

    def scr(self, name, shape, dt):
        return self.nc.dram_tensor(name, list(shape), dt).ap()

    def sb(self, name, shape, dt=F32, nres=1):
        self.uid = getattr(self, "uid", 0) + 1
        return Tl(self.es.enter_context(self.nc.sbuf_tensor("sb%d_%s" % (self.uid, name), list(shape), dt)), nres)

    def build(self):
        nc = self.nc
        I = self.inp
        x = I("x", [SEQ, D]); ctxi = I("ctxi", [NCTX, D])
        I("sv", [128, 16]); I("mod_w", [2, D, 9 * D]); I("mod_b2", [128, 2 * 2 * 72]); I("normT2", [128, 2 * 3 * 2 * 8])
        I("fnormT", [128, 8])
        I("ffn_w1", [2, 2, D, DFF]); I("ffn_w3", [2, 2, D, DFF]); I("ffn_w2", [2, 2, DFF, D])
        I("ab_w_in", [D, 3072]); I("ab_w_inp", [D, 1024]); I("ab_w_out", [D, D])
        I("lamp", [128, 256]); I("subw", [128, 1])
        I("hcw", [128, 36]); I("hcb", [128, 12])
        I("fw0", [33, 64]); I("fb0", [64, 1]); I("fw1", [2, 64, 64]); I("fb1", [64, 2]); I("ffreq", [64, 1])
        I("fwout", [64, 2048]); I("hskip", [1, 1024])
        I("na_w_in", [D, 3072]); I("na_w_out", [D, D]); I("nab", [8, 128, self.nab_cols])
        I("ident", [128, 128]); I("ropeC", [128, SEQ]); I("ropeS", [128, SEQ])
        I("dftC", [33, 128, 33 * 128], BF16); I("dftS", [33, 128, 33 * 128], BF16); I("wf", [128, 33])
        I("dftCc", [3, 128, 3 * 128], BF16); I("dftSc", [3, 128, 3 * 128], BF16); I("wfc", [128, 3])
        I("zT", [33, SEQ]); I("zTc", [33, NCTX]); I("tl", [128, 32]); I("tlc", [128, 2]); I("deltas", [128, 512])
        self.out = nc.dram_tensor("out", [SEQ, D], F32, kind="ExternalOutput").ap()
        if self.dbg:
            self.dbgo = nc.dram_tensor("dbg", [D, TT], F32, kind="ExternalOutput").ap()
        S_ = self.scr
        self.latT = S_("latT", [D, TT], F32)
        self.w1b = [[S_("w1b%d%d" % (l, i), [D, DFF], BF16) for i in range(2)] for l in range(2)]
        self.w3b = [[S_("w3b%d%d" % (l, i), [D, DFF], BF16) for i in range(2)] for l in range(2)]
        self.w2b = [[S_("w2b%d%d" % (l, i), [DFF, D], BF16) for i in range(2)] for l in range(2)]
        self.abinb = S_("abinb", [D, 3072], BF16); self.abinpb = S_("abinpb", [D, 1024], BF16)
        self.aboutb = S_("aboutb", [D, D], BF16)
        self.nainb = S_("nainb", [D, 3072], BF16); self.naoutb = S_("naoutb", [D, D], BF16)
        self.QT = S_("QT", [D, TT], BF16); self.KT = S_("KT", [D, TT], BF16)
        self.VA = S_("VA", [TT, 1040], BF16)
        self.UH = S_("UH", [1536, TT], F32)
        self.UTM = S_("UTM", [TT, 1024], F32)
        self.VTM = S_("VTM", [TT, 512], BF16)
        self.HSD = S_("HSD", [2, 2, SEQ, 512], BF16)
        self.HCS = S_("HCS", [2, 2, 33 * 128, 512], F32)
        self.catT = S_("catT", [D, TT], BF16)
        with ExitStack() as es:
            self.es = es
            self.S = Sched(nc, es)
            self.ps = [Tl(es.enter_context(nc.psum_tensor("ps%d" % i, [128, 512], F32))) for i in range(8)]
            self.persist()
            self.phase(self.ph_setup)
            for l in range(2):
                self.phase(lambda: self.ph_ffn(l, 0))
                if self.dbg == "ffn%d0" % l:
                    break
                if l == 0:
                    self.phase(self.ph_abproj)
                    self.phase(self.ph_hyena_prep)
                    self.phase(self.ph_diffattn)
                    for nn in (NCTX, SEQ):
                        self.phase(lambda: self.ph_filters(nn))
                        self.phase(lambda: self.ph_hyena(nn))
                    if self.dbg == "cat":
                        break
                    self.phase(lambda: self.ph_outproj(0))
                else:
                    self.phase(self.ph_naproj)
                    self.phase(self.ph_na)
                    self.phase(lambda: self.ph_outproj(1))
                if self.dbg == "mix%d" % l:
                    break
                self.phase(lambda: self.ph_ffn(l, 1))
                if self.dbg == "ffn%d1" % l:
                    break
            if self.dbg:
                src = self.latT
                if self.dbg == "cat":
                    src = None
                if src is not None:
                    self.S.dma("sp", self.dbgo, src, reads=[], writes=[self.S.dres("dbgo")])
                else:
                    self.S.dma("pool", self.dbgo, self.catT, reads=[], writes=[self.S.dres("dbgo")])
            else:
                self.phase(self.ph_final)
            self.S.barrier()
        return nc

    def phase(self, fn):
        with ExitStack() as es:
            old = self.es
            self.es = es
            fn()
            self.S.barrier()
            self.es = old

    def persist(self):
        S = self.S
        self.ident = self.sb("ident", [128, 128])
        self.onesb = self.sb("onesb", [128, 128], BF16)
        self.msc = self.sb("msc", [128, 2 * 3 * 3 * 2 * 8])
        S.dma("sp", self.ident.t[:], self.din["ident"], writes=[self.ident.r])
        S.op("dve", lambda e: e.memset(self.onesb.t[:], 1.0), writes=[self.onesb.r])
        self.svs = self.sb("svs", [128, 16]); self.modT = self.sb("modT", [128, 2 * 2 * 72])
        self.mb2 = self.sb("mb2", [128, 2 * 2 * 72]); self.nT2 = self.sb("nT2", [128, 96])
        S.dma("sp", self.svs.t[:], self.din["sv"], writes=[self.svs.r])
        S.dma("sp", self.mb2.t[:], self.din["mod_b2"], writes=[self.mb2.r])
        S.dma("sp", self.nT2.t[:], self.din["normT2"], writes=[self.nT2.r])

    def mcol(self, l, k, ty, s):
        o = (((l * 3 + k) * 3 + ty) * 2 + s) * 8
        return self.msc.t[:, o:o + 8]

    def cast_weights(self, l, part):
        S, din = self.S, self.din
        for i in ((0,) if part == 0 else (1,)):
            S.dma("pool", self.w1b[l][i], din["ffn_w1"][l, i], writes=[S.dres("w1b", l, i)])
            S.dma("pool", self.w3b[l][i], din["ffn_w3"][l, i], writes=[S.dres("w3b", l, i)])
            S.dma("pool", self.w2b[l][i], din["ffn_w2"][l, i], writes=[S.dres("w2b", l, i)])
        if part == 0:
            return
        pairs = ((self.abinb, "ab_w_in"), (self.abinpb, "ab_w_inp"), (self.aboutb, "ab_w_out")) if l == 0 else \
            ((self.nainb, "na_w_in"), (self.naoutb, "na_w_out"))
        for dst, src in pairs:
            S.dma("pool", dst, din[src], writes=[S.dres(src)])

    def mod_alloc(self):
        self.mw = [self.sb("mw%d" % i, [128, 8, 1024]) for i in range(2)]
        self.mwi = 0

    def mod_block(self, l, b):
        S, din = self.S, self.din
        svs, modT, mb2 = self.svs, self.modT, self.mb2
        svv = svs.t[:].rearrange("p (k s) -> p k s", s=2)
        mwv = din["mod_w"][l].rearrange("(k p) n -> p k n", p=128)
        buf = self.mw[self.mwi % 2]
        self.mwi += 1
        S.dma("sp", buf.t[:], mwv[:, :, b * 1024:(b + 1) * 1024], writes=[buf.r])
        ps = self.ps[7]
        fns = []
        for jj in range(8):
            for k in range(8):
                fns.append(lambda e, jj=jj, k=k, buf=buf, ps=ps: e.matmul(
                    ps.t[:, 2 * jj:2 * jj + 2], lhsT=buf.t[:, k, jj * 128:(jj + 1) * 128], rhs=svv[:, k, :],
                    start=(k == 0), stop=(k == 7)))
        S.mm(fns, reads=[buf.r, svs.r], writes=[ps.r])
        psv = ps.t[:, 0:16].rearrange("p (j s) -> p s j", s=2)
        for s in range(2):
            o = (l * 2 + s) * 72 + b * 8
            S.op("dve", lambda e, s=s, o=o, psv=psv: e.tensor_tensor(
                out=modT.t[:, o:o + 8], in0=psv[:, s, :], in1=mb2.t[:, o:o + 8], op=ALU.add),
                reads=[ps.r, mb2.r], writes=[modT.r])

    def mod_finish(self, l):
        S = self.S
        modT, nT2 = self.modT, self.nT2
        for k in range(3):
            for s in range(2):
                mo = (l * 2 + s) * 72
                no = ((l * 3 + k) * 2 + s) * 8
                sc = modT.t[:, mo + (3 * k + 1) * 8: mo + (3 * k + 2) * 8]
                sh = modT.t[:, mo + (3 * k) * 8: mo + (3 * k + 1) * 8]
                gt = modT.t[:, mo + (3 * k + 2) * 8: mo + (3 * k + 3) * 8]
                S.op("dve", lambda e, sc=sc, no=no, l=l, k=k, s=s: e.scalar_tensor_tensor(
                    out=self.mcol(l, k, 0, s), in0=sc, scalar=1.0, in1=nT2.t[:, no:no + 8],
                    op0=ALU.add, op1=ALU.mult), reads=[modT.r, nT2.r], writes=[self.msc.r])
                S.op("dve", lambda e, sh=sh, l=l, k=k, s=s: e.tensor_copy(out=self.mcol(l, k, 1, s), in_=sh),
                     reads=[modT.r], writes=[self.msc.r])
                S.op("dve", lambda e, gt=gt, l=l, k=k, s=s: e.tensor_scalar(
                    out=self.mcol(l, k, 2, s), in0=gt, scalar1=(1.0 if k == 1 else 0.5), scalar2=None,
                    op0=ALU.mult), reads=[modT.r], writes=[self.msc.r])

    def ph_setup(self):
        S, din = self.S, self.din
        self.cast_weights(0, 0)
        S.op("act", lambda e: e.activation(out=self.svs.t[:], in_=self.svs.t[:], func=AF.Silu), reads=[self.svs.r], writes=[self.svs.r])
        self.mod_alloc()
        self.xT_alloc()
        ti = 0
        for b in range(9):
            self.mod_block(0, b)
            for _ in range(4 if b < 8 else 2):
                self.xT_tile(ti)
                ti += 1
        self.mod_finish(0)

    def xT_alloc(self):
        self.xin = [self.sb("xin%d" % i, [128, D]) for i in range(2)]
        self.xo = [self.sb("xo%d" % i, [128, 8, 128]) for i in range(2)]

    def xT_tile(self, ti):
        S = self.S
        src = self.din["ctxi"][ti * 128:(ti + 1) * 128, :] if ti < 2 else self.din["x"][(ti - 2) * 128:(ti - 1) * 128, :]
        xi, o = self.xin[ti % 2], self.xo[ti % 2]
        S.dma("sp", xi.t[:], src, writes=[xi.r])
        for h in range(2):
            ps = self.ps[(ti % 2) * 2 + h]
            S.mm([lambda e, c=c, ps=ps, xi=xi, h=h: e.transpose(
                ps.t[:, c * 128:(c + 1) * 128], xi.t[:, (h * 4 + c) * 128:(h * 4 + c + 1) * 128], self.ident.t[:])
                for c in range(4)], reads=[xi.r, self.ident.r], writes=[ps.r])
            ov = o.t[:, h * 4:(h + 1) * 4, :]
            pv = ps.t[:, :].rearrange("p (c t) -> p c t", t=128)
            if h == 0:
                S.op("act", lambda e, ov=ov, pv=pv: e.copy(out=ov, in_=pv), reads=[ps.r], writes=[o.r])
            else:
                S.op("dve", lambda e, ov=ov, pv=pv: e.tensor_copy(out=ov, in_=pv), reads=[ps.r], writes=[o.r])
        dst = self.latT.rearrange("(c p) t -> p c t", p=128)[:, :, ti * 128:(ti + 1) * 128]
        S.dma("st", dst, o.t[:], reads=[o.r], writes=[S.dres("latT", ti)])

    def groups(self, with_ctx=True):
        g = [(0, NCTX, 1)] if with_ctx else []
        return g + [(NCTX + GS * i, GS, 0) for i in range(SEQ // GS)]

    def alloc_pro(self):
        self.xt = [self.sb("xt%d" % i, [128, 8, GS]) for i in range(2)]
        self.sq = self.sb("sq", [128, 8, GS], BF16)
        self.rstd = self.sb("rstd", [128, GS])
        self.ptmp = [self.sb("ptmp%d" % i, [128, GS]) for i in range(2)]
        self.hTs = [self.sb("hT%d" % i, [128, 8, GS], BF16) for i in range(2)]
        self.hT = self.hTs[0]
        self.gi = 0

    def prologue(self, l, k, t0, n, s, psb, want_h=True):
        S = self.S
        xt = self.xt[self.gi % 2]
        self.hT = self.hTs[self.gi % 2]
        self.gi += 1
        lv = self.latT.rearrange("(c p) t -> p c t", p=128)[:, :, t0:t0 + n]
        S.dma("sp", xt.t[:, :, :n], lv, writes=[xt.r])
        sq, rstd, hT = self.sq, self.rstd, self.hT
        S.op("act", lambda e: e.activation(out=sq.t[:, :, :n], in_=xt.t[:, :, :n], func=AF.Square),
             reads=[xt.r], writes=[sq.r])
        ps = self.ps[psb]
        S.mm([lambda e, c=c: e.matmul(ps.t[:, :n], lhsT=self.onesb.t[:], rhs=sq.t[:, c, :n], start=(c == 0), stop=(c == 7))
              for c in range(8)], reads=[sq.r, self.onesb.r], writes=[ps.r])
        S.op("act", lambda e: e.activation(out=rstd.t[:, :n], in_=ps.t[:, :n], func=AF.Sqrt, bias=EPS, scale=1.0 / D),
             reads=[ps.r], writes=[rstd.r])
        S.op("dve", lambda e: e.reciprocal(out=rstd.t[:, :n], in_=rstd.t[:, :n]), reads=[rstd.r], writes=[rstd.r])
        if want_h:
            gs, sh = self.mcol(l, k, 0, s), self.mcol(l, k, 1, s)
            for c in range(8):
                tmp = self.ptmp[c % 2]
                S.op("dve", lambda e, c=c, tmp=tmp: e.scalar_tensor_tensor(
                    out=tmp.t[:, :n], in0=xt.t[:, c, :n], scalar=gs[:, c:c + 1], in1=rstd.t[:, :n],
                    op0=ALU.mult, op1=ALU.mult), reads=[xt.r, rstd.r, self.msc.r], writes=[tmp.r])
                S.op("act", lambda e, c=c, tmp=tmp: e.activation(
                    out=hT.t[:, c, :n], in_=tmp.t[:, :n], func=AF.Identity, bias=sh[:, c:c + 1], scale=1.0),
                    reads=[tmp.r, self.msc.r], writes=[hT.r])
        return xt

    def store_group(self, xt, t0, n):
        lv = self.latT.rearrange("(c p) t -> p c t", p=128)[:, :, t0:t0 + n]
        self.S.dma("st", lv, xt.t[:, :, :n], reads=[xt.r], writes=[self.S.dres("latT", t0)])

    def ph_ffn(self, l, i):
        S = self.S
        k = 0 if i == 0 else 2
        if l == 0 and i == 0:
            self.cast_weights(0, 1)
        self.alloc_pro()
        w2t = self.sb("w2t", [128, NJ, D], BF16)
        w2v = self.w2b[l][i].rearrange("(j p) n -> p j n", p=128)
        for a in range(2):
            S.dma("sp", w2t.t[:, a * 11:(a + 1) * 11, :], w2v[:, a * 11:(a + 1) * 11, :], writes=[w2t.r] if a == 0 else [w2t.r])
        wb1 = [self.sb("wb1_%d" % a, [128, 8, 256], BF16) for a in range(3)]
        wb3 = [self.sb("wb3_%d" % a, [128, 8, 256], BF16) for a in range(3)]
        actT = self.sb("actT", [128, NJ, GS], BF16)
        sg = [self.sb("sg%d" % a, [128, GS]) for a in range(2)]
        w1v = self.w1b[l][i].rearrange("(k p) n -> p k n", p=128)
        w3v = self.w3b[l][i].rearrange("(k p) n -> p k n", p=128)
        need_ctx = (l == 0) or (i == 0)
        bi = 0
        grps = self.groups(need_ctx)
        nxt = (self.prologue(l, k, grps[0][0], grps[0][1], grps[0][2], 6), self.hT)
        for gidx, (t0, n, s) in enumerate(grps):
            xt, hT = nxt
            for nb in range(11):
                b1, b3 = wb1[bi % 3], wb3[bi % 3]
                bi += 1
                S.dma("sp", b1.t[:], w1v[:, :, nb * 256:(nb + 1) * 256], writes=[b1.r])
                S.dma("sp", b3.t[:], w3v[:, :, nb * 256:(nb + 1) * 256], writes=[b3.r])
                for jj in range(2):
                    j = nb * 2 + jj
                    pg, pu = self.ps[2 * (j % 2)], self.ps[2 * (j % 2) + 1]
                    S.mm([lambda e, kc=kc, b1=b1, pg=pg, jj=jj: e.matmul(
                        pg.t[:, :n], lhsT=b1.t[:, kc, jj * 128:(jj + 1) * 128], rhs=hT.t[:, kc, :n],
                        start=(kc == 0), stop=(kc == 7)) for kc in range(8)], reads=[b1.r, hT.r], writes=[pg.r])
                    S.mm([lambda e, kc=kc, b3=b3, pu=pu, jj=jj: e.matmul(
                        pu.t[:, :n], lhsT=b3.t[:, kc, jj * 128:(jj + 1) * 128], rhs=hT.t[:, kc, :n],
                        start=(kc == 0), stop=(kc == 7)) for kc in range(8)], reads=[b3.r, hT.r], writes=[pu.r])
                    sgt = sg[j % 2]
                    S.op("act", lambda e, pg=pg, sgt=sgt: e.activation(out=sgt.t[:, :n], in_=pg.t[:, :n], func=AF.Silu),
                         reads=[pg.r], writes=[sgt.r])
                    S.op("dve", lambda e, pu=pu, sgt=sgt, j=j: e.tensor_tensor(
                        out=actT.t[:, j, :n], in0=pu.t[:, :n], in1=sgt.t[:, :n], op=ALU.mult),
                        reads=[pu.r, sgt.r], writes=[actT.r])
            if gidx + 1 < len(grps):
                g2 = grps[gidx + 1]
                nxt = (self.prologue(l, k, g2[0], g2[1], g2[2], 6), self.hT)
            gate = self.mcol(l, k, 2, s)
            for dc in range(8):
                py = self.ps[4 + dc % 2]
                S.mm([lambda e, j=j, py=py, dc=dc: e.matmul(
                    py.t[:, :n], lhsT=w2t.t[:, j, dc * 128:(dc + 1) * 128], rhs=actT.t[:, j, :n],
                    start=(j == 0), stop=(j == NJ - 1)) for j in range(NJ)], reads=[w2t.r, actT.r], writes=[py.r])
                S.op("dve", lambda e, py=py, dc=dc: e.scalar_tensor_tensor(
                    out=xt.t[:, dc, :n], in0=py.t[:, :n], scalar=gate[:, dc:dc + 1], in1=xt.t[:, dc, :n],
                    op0=ALU.mult, op1=ALU.add), reads=[py.r, xt.r, self.msc.r], writes=[xt.r])
            self.store_group(xt, t0, n)

    def ph_abproj(self):
        S, din = self.S, self.din
        self.alloc_pro()
        win = self.sb("win", [128, 8, 3072], BF16)
        winp = self.sb("winp", [128, 8, 1024], BF16)
        wv = self.abinb.rearrange("(k p) n -> p k n", p=128)
        for a in range(3):
            S.dma("sp", win.t[:, :, a * 1024:(a + 1) * 1024], wv[:, :, a * 1024:(a + 1) * 1024], writes=[win.r])
        S.dma("sp", winp.t[:], self.abinpb.rearrange("(k p) n -> p k n", p=128), writes=[winp.r])
        rc = self.sb("rc", [128, GS]); rs = self.sb("rs", [128, GS])
        qo = [self.sb("qo%d" % a, [128, GS], BF16) for a in range(4)]
        t1 = [self.sb("t1_%d" % a, [128, GS]) for a in range(2)]
        t2 = [self.sb("t2_%d" % a, [128, GS]) for a in range(2)]
        vo = [self.sb("vo%d" % a, [128, 4, 129], BF16) for a in range(2)]
        uo = [self.sb("uo%d" % a, [128, GS]) for a in range(4)]
        for v in vo:
            S.op("dve", lambda e, v=v: e.memset(v.t[:], 1.0), writes=[v.r])
        ci = 0
        for (t0, n, s) in self.groups(True):
            self.prologue(0, 1, t0, n, s, 6)
            hT = self.hT
            if s == 0:
                S.dma("sp", rc.t[:, :n], din["ropeC"][:, t0 - NCTX:t0 - NCTX + n], writes=[rc.r])
                S.dma("sp", rs.t[:, :n], din["ropeS"][:, t0 - NCTX:t0 - NCTX + n], writes=[rs.r])
            for which in range(2):
                for hc in range(4):
                    col = which * 512 + hc * 128
                    pa, pb = self.ps[(ci % 2) * 2], self.ps[(ci % 2) * 2 + 1]
                    q = qo[ci % 4]; a1 = t1[ci % 2]; a2 = t2[ci % 2]
                    ci += 1
                    S.mm([lambda e, k=k, pa=pa, col=col: e.matmul(
                        pa.t[:, :n], lhsT=win.t[:, k, col:col + 128], rhs=hT.t[:, k, :n], start=(k == 0), stop=(k == 7))
                        for k in range(8)], reads=[win.r, hT.r], writes=[pa.r])
                    if s == 0:
                        S.mm([lambda e, k=k, pb=pb, col=col: e.matmul(
                            pb.t[:, :n], lhsT=winp.t[:, k, col:col + 128], rhs=hT.t[:, k, :n], start=(k == 0), stop=(k == 7))
                            for k in range(8)], reads=[winp.r, hT.r], writes=[pb.r])
                        S.op("dve", lambda e, pa=pa, a1=a1: e.tensor_tensor(out=a1.t[:, :n], in0=pa.t[:, :n], in1=rc.t[:, :n], op=ALU.mult),
                             reads=[pa.r, rc.r], writes=[a1.r])
                        S.op("dve", lambda e, pb=pb, a2=a2: e.tensor_tensor(out=a2.t[:, :n], in0=pb.t[:, :n], in1=rs.t[:, :n], op=ALU.mult),
                             reads=[pb.r, rs.r], writes=[a2.r])
                        S.op("pool", lambda e, q=q, a1=a1, a2=a2: e.tensor_tensor(out=q.t[:, :n], in0=a1.t[:, :n], in1=a2.t[:, :n], op=ALU.add),
                             reads=[a1.r, a2.r], writes=[q.r])
                    else:
                        S.op("act", lambda e, q=q, pa=pa: e.copy(out=q.t[:, :n], in_=pa.t[:, :n]), reads=[pa.r], writes=[q.r])
                    dst = (self.QT if which == 0 else self.KT)[hc * 128:(hc + 1) * 128, t0:t0 + n]
                    S.dma("st", dst, q.t[:, :n], reads=[q.r], writes=[S.dres("qk", which, hc, t0)])
            for tt in range(n // 128):
                pv = self.ps[4 + tt % 2]
                v = vo[tt % 2]
                S.mm([lambda e, k=k, pv=pv, tt=tt: e.matmul(
                    pv.t[:, :512], lhsT=hT.t[:, k, tt * 128:(tt + 1) * 128], rhs=win.t[:, k, 1024:1536], start=(k == 0), stop=(k == 7))
                    for k in range(8)], reads=[win.r, hT.r], writes=[pv.r])
                S.op("act", lambda e, pv=pv, v=v: e.copy(out=v.t[:, :, 0:128], in_=pv.t[:, :].rearrange("p (h d) -> p h d", d=128)),
                     reads=[pv.r], writes=[v.r])
                S.dma("st", self.VA[t0 + tt * 128:t0 + (tt + 1) * 128, 0:516].rearrange("p (h d) -> p h d", d=129), v.t[:],
                      reads=[v.r], writes=[S.dres("va", t0, tt)])
            for c in range(12):
                pu = self.ps[(c % 4)]
                u = uo[c % 4]
                S.mm([lambda e, k=k, pu=pu, c=c: e.matmul(
                    pu.t[:, :n], lhsT=win.t[:, k, 1536 + c * 128:1536 + (c + 1) * 128], rhs=hT.t[:, k, :n], start=(k == 0), stop=(k == 7))
                    for k in range(8)], reads=[win.r, hT.r], writes=[pu.r])
                if c % 2 == 0:
                    S.op("act", lambda e, pu=pu, u=u: e.copy(out=u.t[:, :n], in_=pu.t[:, :n]), reads=[pu.r], writes=[u.r])
                else:
                    S.op("dve", lambda e, pu=pu, u=u: e.tensor_copy(out=u.t[:, :n], in_=pu.t[:, :n]), reads=[pu.r], writes=[u.r])
                S.dma("st", self.UH[c * 128:(c + 1) * 128, t0:t0 + n], u.t[:, :n], reads=[u.r], writes=[S.dres("uh", c, t0)])

    def ph_hyena_prep(self):
        S, din = self.S, self.din
        cw = self.sb("cw", [128, 36]); cb = self.sb("cb", [128, 12])
        S.dma("sp", cw.t[:], din["hcw"], writes=[cw.r]); S.dma("sp", cb.t[:], din["hcb"], writes=[cb.r])
        ub = [self.sb("ub%d" % a, [128, SEQ + 2]) for a in range(2)]
        uc = [self.sb("uc%d" % a, [128, SEQ]) for a in range(2)]
        st = [self.sb("st%d" % a, [128, 4, 128]) for a in range(2)]
        stb = [self.sb("stb%d" % a, [128, 4, 128], BF16) for a in range(2)]
        ci = 0
        gi = 0
        self.mod_alloc()
        mblk = 0
        for (T0, n) in ((0, NCTX), (NCTX, SEQ)):
            for c in range(12):
                if n == SEQ and mblk < 9:
                    self.mod_block(1, mblk)
                    mblk += 1
                    if mblk == 9:
                        self.mod_finish(1)
                b, u = ub[ci % 2], uc[ci % 2]
                ci += 1
                S.op("dve", lambda e, b=b: e.memset(b.t[:, 0:1], 0.0), writes=[b.r])
                S.op("dve", lambda e, b=b: e.memset(b.t[:, n + 1:n + 2], 0.0), writes=[b.r])
                S.dma("sp", b.t[:, 1:n + 1], self.UH[c * 128:(c + 1) * 128, T0:T0 + n], writes=[b.r])
                S.op("act", lambda e, b=b, u=u, c=c: e.activation(out=u.t[:, :n], in_=b.t[:, 1:n + 1], func=AF.Identity,
                                                                 bias=cb.t[:, c:c + 1], scale=cw.t[:, 3 * c + 1:3 * c + 2]),
                     reads=[b.r, cw.r, cb.r], writes=[u.r])
                S.op("dve", lambda e, b=b, u=u, c=c: e.scalar_tensor_tensor(out=u.t[:, :n], in0=b.t[:, 0:n], scalar=cw.t[:, 3 * c:3 * c + 1],
                                                                        in1=u.t[:, :n], op0=ALU.mult, op1=ALU.add),
                     reads=[b.r, u.r, cw.r], writes=[u.r])
                S.op("dve", lambda e, b=b, u=u, c=c: e.scalar_tensor_tensor(out=u.t[:, :n], in0=b.t[:, 2:n + 2], scalar=cw.t[:, 3 * c + 2:3 * c + 3],
                                                                        in1=u.t[:, :n], op0=ALU.mult, op1=ALU.add),
                     reads=[b.r, u.r, cw.r], writes=[u.r])
                for g4 in range(n // 512 if n >= 512 else 1):
                    ntl = min(4, n // 128)
                    ps = self.ps[gi % 4]
                    S.mm([lambda e, a=a, ps=ps, u=u, g4=g4: e.transpose(
                        ps.t[:, a * 128:(a + 1) * 128], u.t[:, (g4 * 4 + a) * 128:(g4 * 4 + a + 1) * 128], self.ident.t[:])
                        for a in range(ntl)], reads=[u.r, self.ident.r], writes=[ps.r])
                    pvw = ps.t[:, :ntl * 128].rearrange("p (a w) -> p a w", w=128)
                    r0 = T0 + g4 * 512
                    if c < 8:
                        sx = st[gi % 2]
                        S.op("act" if gi % 2 == 0 else "dve",
                             (lambda e, sx=sx, pvw=pvw: e.copy(out=sx.t[:, :ntl, :], in_=pvw)) if gi % 2 == 0 else
                             (lambda e, sx=sx, pvw=pvw: e.tensor_copy(out=sx.t[:, :ntl, :], in_=pvw)),
                             reads=[ps.r], writes=[sx.r])
                        dst = self.UTM[r0:r0 + ntl * 128, c * 128:(c + 1) * 128].rearrange("(a p) w -> p a w", p=128)
                        S.dma("st", dst, sx.t[:, :ntl, :], reads=[sx.r], writes=[S.dres("utm", c, r0)])
                    else:
                        sx = stb[gi % 2]
                        S.op("act" if gi % 2 == 0 else "dve",
                             (lambda e, sx=sx, pvw=pvw: e.copy(out=sx.t[:, :ntl, :], in_=pvw)) if gi % 2 == 0 else
                             (lambda e, sx=sx, pvw=pvw: e.tensor_copy(out=sx.t[:, :ntl, :], in_=pvw)),
                             reads=[ps.r], writes=[sx.r])
                        dst = self.VTM[r0:r0 + ntl * 128, (c - 8) * 128:(c - 7) * 128].rearrange("(a p) w -> p a w", p=128)
                        S.dma("st", dst, sx.t[:, :ntl, :], reads=[sx.r], writes=[S.dres("vtm", c, r0)])
                    gi += 1

    def ph_diffattn(self):
        S, din = self.S, self.din
        self.cast_weights(1, 0)
        self.cast_weights(1, 1)
        lam_init = 0.8 - 0.6 * math.exp(-0.3 * 0)
        lp = self.sb("lp", [128, 256]); pr = self.sb("pr", [128, 128]); sm = self.sb("sm", [128, 2])
        nlam = self.sb("nlam", [128, 1]); wsub = self.sb("wsub", [128, 1])
        S.dma("sp", lp.t[:], din["lamp"], writes=[lp.r]); S.dma("sp", wsub.t[:], din["subw"], writes=[wsub.r])
        S.op("dve", lambda e: e.tensor_tensor(out=pr.t[:, 0:64], in0=lp.t[:, 0:64], in1=lp.t[:, 64:128], op=ALU.mult), reads=[lp.r], writes=[pr.r])
        S.op("dve", lambda e: e.tensor_tensor(out=pr.t[:, 64:128], in0=lp.t[:, 128:192], in1=lp.t[:, 192:256], op=ALU.mult), reads=[lp.r], writes=[pr.r])
        S.op("dve", lambda e: e.reduce_sum(out=sm.t[:, 0:2], in_=pr.t[:, :].rearrange("p (a b) -> p a b", b=64), axis=AX.X), reads=[pr.r], writes=[sm.r])
        S.op("act", lambda e: e.activation(out=sm.t[:], in_=sm.t[:], func=AF.Exp), reads=[sm.r], writes=[sm.r])
        S.op("dve", lambda e: e.tensor_tensor(out=nlam.t[:], in0=sm.t[:, 1:2], in1=sm.t[:, 0:1], op=ALU.subtract), reads=[sm.r], writes=[nlam.r])
        S.op("dve", lambda e: e.tensor_scalar(out=nlam.t[:], in0=nlam.t[:], scalar1=-lam_init, scalar2=None, op0=ALU.add), reads=[nlam.r], writes=[nlam.r])
        S.op("dve", lambda e: e.tensor_scalar(out=wsub.t[:], in0=wsub.t[:], scalar1=1.0 - lam_init, scalar2=None, op0=ALU.mult), reads=[wsub.r], writes=[wsub.r])
        ktb = [self.sb("ktb%d" % a, [128, TT], BF16) for a in range(2)]
        qtb = [self.sb("qtb%d" % a, [128, TT], BF16) for a in range(2)]
        vab = [self.sb("vab%d" % a, [128, TT // 128, 129], BF16) for a in range(2)]
        cab = [self.sb("cab%d" % a, [128, TT], BF16) for a in range(2)]
        pts = [self.sb("pt%d" % a, [128, 512], BF16) for a in range(6)]
        sacc = [self.sb("sacc%d" % a, [128, 512]) for a in range(2)]
        rsb = self.sb("rsb", [128, 512]); osb = self.sb("osb", [128, 512]); o1 = self.sb("o1", [128, 512])
        sqb = self.sb("sqb", [128, 512], BF16); rst = self.sb("rst", [128, 512])
        onesf = self.sb("onesf", [128, 128])
        S.op("dve", lambda e: e.memset(onesf.t[:], 1.0), writes=[onesf.r])
        cnt = 0
        mi = 0
        for h in range(4):
            kt_, qt_, va_, ca_ = ktb[h % 2], qtb[h % 2], vab[h % 2], cab[h % 2]
            S.dma("sp", kt_.t[:], self.KT[h * 128:(h + 1) * 128, :], writes=[kt_.r])
            S.dma("sp", qt_.t[:], self.QT[h * 128:(h + 1) * 128, :], writes=[qt_.r])
            S.dma("sp", va_.t[:], self.VA[:, h * 129:(h + 1) * 129].rearrange("(i p) w -> p i w", p=128), writes=[va_.r])
            qgroups = [(0, NCTX, list(range(2)))] + [(NCTX + 512 * g, 512, list(range(TT // 128))) for g in range(SEQ // 512)]
            for (q0, qn, kts) in qgroups:
                OT = [self.ps[0], self.ps[1]]

                def score(i):
                    kt = kts[i]
                    for m in range(2):
                        pS = self.ps[4 + 2 * m + i % 2]
                        S.mm([lambda e, pS=pS, kt=kt, m=m: e.matmul(
                            pS.t[:, :qn], lhsT=kt_.t[m * 64:(m + 1) * 64, kt * 128:(kt + 1) * 128],
                            rhs=qt_.t[m * 64:(m + 1) * 64, q0:q0 + qn], start=True, stop=True)],
                            reads=[kt_.r, qt_.r], writes=[pS.r])
                score(0)
                for i, kt in enumerate(kts):
                    if i + 1 < len(kts):
                        score(i + 1)
                    for m in range(2):
                        pS = self.ps[4 + 2 * m + i % 2]
                        pt = pts[cnt % 6]
                        cnt += 1
                        sa = sacc[m]
                        S.op("act", lambda e, pS=pS, pt=pt: e.activation(out=pt.t[:, :qn], in_=pS.t[:, :qn], func=AF.Exp, scale=0.125),
                             reads=[pS.r], writes=[pt.r])
                        S.mm([lambda e, pt=pt, kt=kt, m=m: e.matmul(
                            OT[m].t[:, :qn], lhsT=va_.t[:, kt, 0:128], rhs=pt.t[:, :qn], start=(kt == kts[0]), stop=(kt == kts[-1]))],
                            reads=[pt.r, va_.r], writes=[OT[m].r])
                        if m == 1:
                            S.mm([lambda e, pt=pt, kt=kt: e.matmul(self.ps[2].t[:, :qn], lhsT=self.onesb.t[:], rhs=pt.t[:, :qn],
                                                                    start=(kt == kts[0]), stop=(kt == kts[-1]))],
                                 reads=[pt.r, self.onesb.r], writes=[self.ps[2].r])
                        elif i == 0:
                            S.op("dve", lambda e, pt=pt, sa=sa: e.tensor_copy(out=sa.t[:, :qn], in_=pt.t[:, :qn]), reads=[pt.r], writes=[sa.r])
                        else:
                            S.op("dve", lambda e, pt=pt, sa=sa: e.tensor_tensor(out=sa.t[:, :qn], in0=sa.t[:, :qn], in1=pt.t[:, :qn], op=ALU.add),
                                 reads=[pt.r, sa.r], writes=[sa.r])
                for m in range(2):
                    SM = self.ps[3] if m == 0 else self.ps[2]
                    sa = sacc[m]
                    if m == 0:
                        S.mm([lambda e, SM=SM, sa=sa: e.matmul(SM.t[:, :qn], lhsT=onesf.t[:], rhs=sa.t[:, :qn], start=True, stop=True)],
                             reads=[onesf.r, sa.r], writes=[SM.r])
                    S.op("dve", lambda e, SM=SM: e.reciprocal(out=rsb.t[:, :qn], in_=SM.t[:, :qn]), reads=[SM.r], writes=[rsb.r])
                    if m == 0:
                        S.op("dve", lambda e: e.tensor_tensor(out=osb.t[:, :qn], in0=OT[0].t[:, :qn], in1=rsb.t[:, :qn], op=ALU.mult),
                             reads=[OT[0].r, rsb.r], writes=[osb.r])
                    else:
                        S.op("dve", lambda e: e.tensor_tensor(out=o1.t[:, :qn], in0=OT[1].t[:, :qn], in1=rsb.t[:, :qn], op=ALU.mult),
                             reads=[OT[1].r, rsb.r], writes=[o1.r])
                S.op("dve", lambda e: e.scalar_tensor_tensor(out=osb.t[:, :qn], in0=o1.t[:, :qn], scalar=nlam.t[:, 0:1], in1=osb.t[:, :qn],
                                                             op0=ALU.mult, op1=ALU.add), reads=[o1.r, nlam.r, osb.r], writes=[osb.r])
                S.op("act", lambda e: e.activation(out=sqb.t[:, :qn], in_=osb.t[:, :qn], func=AF.Square), reads=[osb.r], writes=[sqb.r])
                SS = self.ps[3]
                S.mm([lambda e, SS=SS: e.matmul(SS.t[:, :qn], lhsT=self.onesb.t[:], rhs=sqb.t[:, :qn], start=True, stop=True)],
                     reads=[self.onesb.r, sqb.r], writes=[SS.r])
                S.op("act", lambda e, SS=SS: e.activation(out=rst.t[:, :qn], in_=SS.t[:, :qn], func=AF.Sqrt, bias=EPS, scale=1.0 / 128),
                     reads=[SS.r], writes=[rst.r])
                S.op("dve", lambda e: e.reciprocal(out=rst.t[:, :qn], in_=rst.t[:, :qn]), reads=[rst.r], writes=[rst.r])
                S.op("dve", lambda e: e.scalar_tensor_tensor(out=ca_.t[:, q0:q0 + qn], in0=osb.t[:, :qn], scalar=wsub.t[:, 0:1], in1=rst.t[:, :qn],
                                                             op0=ALU.mult, op1=ALU.mult), reads=[osb.r, wsub.r, rst.r], writes=[ca_.r])
            S.dma("st", self.catT[h * 128:(h + 1) * 128, :], ca_.t[:], reads=[ca_.r], writes=[S.dres("catA", h)])

    def ph_filters(self, n):
        S, din = self.S, self.din
        cm = (n == NCTX)
        zt = self.sb("zt", [33, n]); w0 = self.sb("w0", [33, 64]); b0 = self.sb("b0", [64, 1])
        w1 = self.sb("w1", [64, 2, 64]); b1 = self.sb("b1", [64, 2]); fr = self.sb("fr", [64, 1]); wo = self.sb("wo", [64, 2048])
        skp = self.sb("skp", [1, 1024]); dl = self.sb("dl", [128, 512]); tl = self.sb("tl", [128, n // 128])
        S.dma("sp", zt.t[:], din["zTc" if cm else "zT"], writes=[zt.r])
        S.dma("sp", w0.t[:], din["fw0"], writes=[w0.r]); S.dma("sp", b0.t[:], din["fb0"], writes=[b0.r])
        S.dma("sp", w1.t[:], din["fw1"].rearrange("i k m -> k i m"), writes=[w1.r]); S.dma("sp", b1.t[:], din["fb1"], writes=[b1.r])
        S.dma("sp", fr.t[:], din["ffreq"], writes=[fr.r]); S.dma("sp", wo.t[:], din["fwout"], writes=[wo.r])
        S.op("dve", lambda e: e.tensor_scalar(out=fr.t[:], in0=fr.t[:], scalar1=1.0 / (2.0 * math.pi), scalar2=None, op0=ALU.mult), reads=[fr.r], writes=[fr.r])
        S.dma("sp", skp.t[:], din["hskip"], writes=[skp.r]); S.dma("sp", dl.t[:], din["deltas"], writes=[dl.r])
        S.dma("sp", tl.t[:], din["tlc" if cm else "tl"], writes=[tl.r])
        hid = [self.sb("hid%d" % a, [64, n]) for a in range(2)]
        tmp = [self.sb("ftmp%d" % a, [64, 512]) for a in range(2)]
        gsz = min(512, n)
        TWO_PI = 2.0 * math.pi
        OFFS = math.pi + 16.0 * math.pi
        ci = 0
        for layer in range(3):
            src = zt if layer == 0 else hid[(layer - 1) % 2]
            dst = hid[layer % 2]
            bias = b0.t[:, 0:1] if layer == 0 else b1.t[:, layer - 1:layer]
            for g in range(n // gsz):
                ps = self.ps[ci % 2]; tm = tmp[ci % 2]
                ci += 1
                if layer == 0:
                    S.mm([lambda e, ps=ps, g=g: e.matmul(ps.t[:64, :gsz], lhsT=w0.t[:, :], rhs=zt.t[:, g * gsz:(g + 1) * gsz], start=True, stop=True)],
                         reads=[w0.r, zt.r], writes=[ps.r])
                else:
                    S.mm([lambda e, ps=ps, g=g, src=src, layer=layer: e.matmul(ps.t[:64, :gsz], lhsT=w1.t[:, layer - 1, :], rhs=src.t[:, g * gsz:(g + 1) * gsz],
                                                                             start=True, stop=True)], reads=[w1.r, src.r], writes=[ps.r])
                S.op("dve", lambda e, ps=ps, tm=tm, bias=bias: e.tensor_scalar(out=tm.t[:, :gsz], in0=ps.t[:64, :gsz], scalar1=bias, scalar2=fr.t[:, 0:1],
                                                                           op0=ALU.add, op1=ALU.mult), reads=[ps.r, b0.r, b1.r, fr.r], writes=[tm.r])
                for rnd in range(2):
                    S.op("dve", lambda e, tm=tm: e.scalar_tensor_tensor(out=tm.t[:, :gsz], in0=tm.t[:, :gsz], scalar=-0.5, in1=tm.t[:, :gsz],
                                                                    op0=ALU.is_lt, op1=ALU.add), reads=[tm.r], writes=[tm.r])
                    S.op("dve", lambda e, tm=tm: e.scalar_tensor_tensor(out=tm.t[:, :gsz], in0=tm.t[:, :gsz], scalar=0.5, in1=tm.t[:, :gsz],
                                                                    op0=ALU.is_gt, op1=ALU.subtract), reads=[tm.r], writes=[tm.r])
                S.op("act", lambda e, tm=tm, dst=dst, g=g: e.activation(out=dst.t[:, g * gsz:(g + 1) * gsz], in_=tm.t[:, :gsz], func=AF.Sin, scale=TWO_PI),
                     reads=[tm.r], writes=[dst.r])
        hfin = hid[0]
        wnd = [self.sb("wnd%d" % a, [128, 512]) for a in range(2)]
        ff = [self.sb("ff%d" % a, [128, 512]) for a in range(4)]
        ho = [self.sb("ho%d" % a, [128, 512], BF16) for a in range(4)]
        hi_ = 0
        for tt in range(n // 128):
            wn = wnd[tt % 2]
            S.op("act", lambda e, wn=wn, tt=tt: e.activation(out=wn.t[:], in_=dl.t[:], func=AF.Exp, scale=tl.t[:, tt:tt + 1]),
                 reads=[dl.r, tl.r], writes=[wn.r])
            for o in range(2):
                for d in range(2):
                    cb = o * 2 + d
                    ps = self.ps[2 + cb]
                    S.mm([lambda e, ps=ps, cb=cb, tt=tt: e.matmul(ps.t[:, :512], lhsT=hfin.t[:, tt * 128:(tt + 1) * 128], rhs=wo.t[:, cb * 512:(cb + 1) * 512],
                                                                 start=True, stop=True)], reads=[hfin.r, wo.r], writes=[ps.r])
                    f = ff[cb]
                    S.op("dve", lambda e, ps=ps, f=f, wn=wn: e.tensor_tensor(out=f.t[:], in0=ps.t[:, :512], in1=wn.t[:], op=ALU.mult),
                         reads=[ps.r, wn.r], writes=[f.r])
                    if tt == 0:
                        if d == 0:
                            S.op("dve", lambda e, f=f, o=o: e.tensor_tensor(out=f.t[0:1, :], in0=f.t[0:1, :], in1=skp.t[0:1, o * 512:(o + 1) * 512], op=ALU.add),
                                 reads=[f.r, skp.r], writes=[f.r])
                        else:
                            S.op("dve", lambda e, f=f: e.memset(f.t[0:1, :], 0.0), reads=[f.r], writes=[f.r])
                f0, f1 = ff[o * 2], ff[o * 2 + 1]
                for sd in range(2):
                    h_ = ho[hi_ % 4]
                    hi_ += 1
                    S.op("pool", lambda e, h_=h_, f0=f0, f1=f1, sd=sd: e.tensor_tensor(out=h_.t[:], in0=f0.t[:], in1=f1.t[:],
                                                                                    op=(ALU.add if sd == 0 else ALU.subtract)),
                         reads=[f0.r, f1.r], writes=[h_.r])
                    S.dma("st", self.HSD[o, sd, tt * 128:(tt + 1) * 128, :], h_.t[:], reads=[h_.r], writes=[S.dres("hsd", n, o, sd, tt)])

    def ph_hyena(self, n):
        S, din = self.S, self.din
        cm = (n == NCTX)
        T0 = 0 if cm else NCTX
        nt = n // 128
        nf = nt + 1
        dC, dS = (din["dftCc"], din["dftSc"]) if cm else (din["dftC"], din["dftS"])
        wft = self.sb("wft", [128, nf])
        S.dma("sp", wft.t[:], din["wfc" if cm else "wf"], writes=[wft.r])
        vt = self.sb("vt", [128, nt, 512], BF16)
        Y = [self.sb("Y%d" % a, [128, nf, 512], BF16) for a in range(2)]
        blk = [self.sb("blk%d" % a, [128, nf, 128], BF16) for a in range(4)]
        hst = [self.sb("hst%d" % a, [128, 512]) for a in range(4)]
        bi = 0

        def load_blk(src, b, rows):
            nonlocal bi
            t = blk[bi % 4]
            bi += 1
            S.dma("sp", t.t[:, :rows, :], src[b][:, 0:rows * 128].rearrange("p (i w) -> p i w", w=128), writes=[t.r])
            return t
        pi_ = 0
        for o in range(2):
            for cs in range(2):
                S.dma("sp", vt.t[:], self.HSD[o, cs, 0:n, :].rearrange("(i p) c -> p i c", p=128), writes=[vt.r])
                for fb in range(nf):
                    t = load_blk(dC if cs == 0 else dS, fb, nt)
                    ps = self.ps[pi_ % 4]; hs_ = hst[pi_ % 4]
                    pi_ += 1
                    S.mm([lambda e, i=i, t=t, ps=ps: e.matmul(ps.t[:, :512], lhsT=t.t[:, i, :], rhs=vt.t[:, i, :], start=(i == 0), stop=(i == nt - 1))
                          for i in range(nt)], reads=[t.r, vt.r], writes=[ps.r])
                    S.op("act", lambda e, ps=ps, hs_=hs_, fb=fb: e.activation(out=hs_.t[:], in_=ps.t[:, :512], func=AF.Copy, scale=wft.t[:, fb:fb + 1]),
                         reads=[ps.r, wft.r], writes=[hs_.r])
                    S.dma("st", self.HCS[o, cs, fb * 128:(fb + 1) * 128, :], hs_.t[:], reads=[hs_.r], writes=[S.dres("hcs", o, cs, fb)])
        S.dma("sp", vt.t[:], self.VTM[T0:T0 + n, :].rearrange("(i p) c -> p i c", p=128), reads=[S.dres("hcs", 1, 1, nf - 1)], writes=[vt.r])
        Hc = [self.sb("Hc%d" % a, [128, 512]) for a in range(2)]
        Hs = [self.sb("Hs%d" % a, [128, 512]) for a in range(2)]
        tq = [self.sb("tq%d" % a, [128, 512]) for a in range(4)]
        xs = [self.sb("xs%d" % a, [128, 512]) for a in range(2)]
        bt = self.sb("bt", [128, 512])
        bst = [self.sb("bst%d" % a, [128, 4, 128], BF16) for a in range(2)]
        for o in range(2):
            for fb in range(nf):
                tC = load_blk(dC, fb, nt)
                tS = load_blk(dS, fb, nt)
                hc, hs = Hc[fb % 2], Hs[fb % 2]
                S.dma("sp", hc.t[:], self.HCS[o, 0, fb * 128:(fb + 1) * 128, :], reads=[S.dres("hcs", o, 0, fb)], writes=[hc.r])
                S.dma("sp", hs.t[:], self.HCS[o, 1, fb * 128:(fb + 1) * 128, :], reads=[S.dres("hcs", o, 1, fb)], writes=[hs.r])
                pC, pS = self.ps[2 * (fb % 2)], self.ps[2 * (fb % 2) + 1]
                S.mm([lambda e, i=i, tC=tC, pC=pC: e.matmul(pC.t[:, :512], lhsT=tC.t[:, i, :], rhs=vt.t[:, i, :], start=(i == 0), stop=(i == nt - 1))
                      for i in range(nt)], reads=[tC.r, vt.r], writes=[pC.r])
                S.mm([lambda e, i=i, tS=tS, pS=pS: e.matmul(pS.t[:, :512], lhsT=tS.t[:, i, :], rhs=vt.t[:, i, :], start=(i == 0), stop=(i == nt - 1))
                      for i in range(nt)], reads=[tS.r, vt.r], writes=[pS.r])
                a1, a2, a3, a4 = tq
                S.op("dve", lambda e, pC=pC, hc=hc: e.tensor_tensor(out=a1.t[:], in0=pC.t[:, :512], in1=hc.t[:], op=ALU.mult), reads=[pC.r, hc.r], writes=[a1.r])
                S.op("dve", lambda e, pS=pS, hs=hs: e.tensor_tensor(out=a2.t[:], in0=pS.t[:, :512], in1=hs.t[:], op=ALU.mult), reads=[pS.r, hs.r], writes=[a2.r])
                S.op("pool", lambda e, fb=fb: e.tensor_tensor(out=Y[0].t[:, fb, :], in0=a1.t[:], in1=a2.t[:], op=ALU.subtract), reads=[a1.r, a2.r], writes=[Y[0].r])
                S.op("dve", lambda e, pC=pC, hs=hs: e.tensor_tensor(out=a3.t[:], in0=pC.t[:, :512], in1=hs.t[:], op=ALU.mult), reads=[pC.r, hs.r], writes=[a3.r])
                S.op("dve", lambda e, pS=pS, hc=hc: e.tensor_tensor(out=a4.t[:], in0=pS.t[:, :512], in1=hc.t[:], op=ALU.mult), reads=[pS.r, hc.r], writes=[a4.r])
                S.op("pool", lambda e, fb=fb: e.tensor_tensor(out=Y[1].t[:, fb, :], in0=a3.t[:], in1=a4.t[:], op=ALU.add), reads=[a3.r, a4.r], writes=[Y[1].r])
            for tb in range(nt):
                tC = load_blk(dC, tb, nf)
                tS = load_blk(dS, tb, nf)
                py = self.ps[4 + tb % 2]
                x_ = xs[tb % 2]
                S.dma("sp", x_.t[:], self.UTM[T0 + tb * 128:T0 + (tb + 1) * 128, o * 512:(o + 1) * 512], writes=[x_.r])
                fns = []
                for j in range(nf):
                    fns.append(lambda e, j=j, tC=tC, py=py: e.matmul(py.t[:, :512], lhsT=tC.t[:, j, :], rhs=Y[0].t[:, j, :], start=(j == 0), stop=False))
                    fns.append(lambda e, j=j, tS=tS, py=py: e.matmul(py.t[:, :512], lhsT=tS.t[:, j, :], rhs=Y[1].t[:, j, :], start=False, stop=(j == nf - 1)))
                S.mm(fns, reads=[tC.r, tS.r, Y[0].r, Y[1].r], writes=[py.r])
                if o == 0:
                    S.op("dve", lambda e, py=py, x_=x_, tb=tb: e.tensor_tensor(out=vt.t[:, tb, :], in0=py.t[:, :512], in1=x_.t[:], op=ALU.mult),
                         reads=[py.r, x_.r], writes=[vt.r])
                else:
                    S.op("dve", lambda e, py=py, x_=x_: e.tensor_tensor(out=bt.t[:], in0=py.t[:, :512], in1=x_.t[:], op=ALU.mult),
                         reads=[py.r, x_.r], writes=[bt.r])
                    pT = self.ps[6 + tb % 2]
                    S.mm([lambda e, c=c, pT=pT: e.transpose(pT.t[:, c * 128:(c + 1) * 128], bt.t[:, c * 128:(c + 1) * 128], self.ident.t[:])
                          for c in range(4)], reads=[bt.r, self.ident.r], writes=[pT.r])
                    b_ = bst[tb % 2]
                    S.op("act", lambda e, pT=pT, b_=b_: e.copy(out=b_.t[:], in_=pT.t[:, :].rearrange("p (c w) -> p c w", w=128)), reads=[pT.r], writes=[b_.r])
                    dst = self.catT[512:1024, T0 + tb * 128:T0 + (tb + 1) * 128].rearrange("(c p) t -> p c t", p=128)
                    S.dma("st", dst, b_.t[:], reads=[b_.r], writes=[S.dres("catB", n, tb)])

    def ph_outproj(self, l):
        S = self.S
        wo = self.sb("wo", [128, 8, D], BF16)
        S.dma("sp", wo.t[:], (self.aboutb if l == 0 else self.naoutb).rearrange("(k p) n -> p k n", p=128), writes=[wo.r])
        xts = [self.sb("oxt%d" % a, [128, 8, GS]) for a in range(2)]
        cts = [self.sb("oct%d" % a, [128, 8, GS], BF16) for a in range(2)]
        lv = self.latT.rearrange("(c p) t -> p c t", p=128)
        cv = self.catT.rearrange("(c p) t -> p c t", p=128)
        gi = 0
        for (t0, n, s) in self.groups(l == 0):
            xt, ct = xts[gi % 2], cts[gi % 2]
            gi += 1
            S.dma("sp", xt.t[:, :, :n], lv[:, :, t0:t0 + n], writes=[xt.r])
            S.dma("sp", ct.t[:, :, :n], cv[:, :, t0:t0 + n], writes=[ct.r])
            gate = self.mcol(l, 1, 2, s)
            for dc in range(8):
                py = self.ps[dc % 4]
                S.mm([lambda e, k=k, py=py, dc=dc: e.matmul(py.t[:, :n], lhsT=wo.t[:, k, dc * 128:(dc + 1) * 128], rhs=ct.t[:, k, :n],
                                                          start=(k == 0), stop=(k == 7)) for k in range(8)], reads=[wo.r, ct.r], writes=[py.r])
                S.op("dve", lambda e, py=py, dc=dc: e.scalar_tensor_tensor(out=xt.t[:, dc, :n], in0=py.t[:, :n], scalar=gate[:, dc:dc + 1], in1=xt.t[:, dc, :n],
                                                                       op0=ALU.mult, op1=ALU.add), reads=[py.r, xt.r, self.msc.r], writes=[xt.r])
            self.store_group(xt, t0, n)

    def ph_naproj(self):
        S = self.S
        self.alloc_pro()
        win = self.sb("win", [128, 8, 3072], BF16)
        wv = self.nainb.rearrange("(k p) n -> p k n", p=128)
        for a in range(3):
            S.dma("sp", win.t[:, :, a * 1024:(a + 1) * 1024], wv[:, :, a * 1024:(a + 1) * 1024], writes=[win.r])
        qo = [self.sb("qo%d" % a, [128, GS], BF16) for a in range(4)]
        vo = [self.sb("vo%d" % a, [128, 16, 65], BF16) for a in range(2)]
        for v in vo:
            S.op("dve", lambda e, v=v: e.memset(v.t[:], 1.0), writes=[v.r])
        ci = 0
        for (t0, n, s) in self.groups(True):
            self.prologue(1, 1, t0, n, s, 6)
            hT = self.hT
            for which in range(2):
                if which == 0 and s == 1:
                    continue
                for hc in range(8):
                    col = which * 1024 + hc * 128
                    pa = self.ps[ci % 4]; q = qo[ci % 4]
                    ci += 1
                    S.mm([lambda e, k=k, pa=pa, col=col: e.matmul(pa.t[:, :n], lhsT=win.t[:, k, col:col + 128], rhs=hT.t[:, k, :n],
                                                                start=(k == 0), stop=(k == 7)) for k in range(8)], reads=[win.r, hT.r], writes=[pa.r])
                    if ci % 2 == 0:
                        S.op("act", lambda e, q=q, pa=pa: e.copy(out=q.t[:, :n], in_=pa.t[:, :n]), reads=[pa.r], writes=[q.r])
                    else:
                        S.op("dve", lambda e, q=q, pa=pa: e.tensor_copy(out=q.t[:, :n], in_=pa.t[:, :n]), reads=[pa.r], writes=[q.r])
                    dst = (self.QT if which == 0 else self.KT)[hc * 128:(hc + 1) * 128, t0:t0 + n]
                    S.dma("st", dst, q.t[:, :n], reads=[q.r], writes=[S.dres("qk2", which, hc, t0)])
            for tt in range(n // 128):
                v = vo[tt % 2]
                for hf in range(2):
                    pv = self.ps[4 + hf]
                    S.mm([lambda e, k=k, pv=pv, tt=tt, hf=hf: e.matmul(pv.t[:, :512], lhsT=hT.t[:, k, tt * 128:(tt + 1) * 128],
                                                                      rhs=win.t[:, k, 2048 + hf * 512:2048 + (hf + 1) * 512], start=(k == 0), stop=(k == 7))
                          for k in range(8)], reads=[win.r, hT.r], writes=[pv.r])
                    S.op("act" if hf == 0 else "dve",
                         (lambda e, pv=pv, v=v, hf=hf: e.copy(out=v.t[:, hf * 8:(hf + 1) * 8, 0:64], in_=pv.t[:, :].rearrange("p (h d) -> p h d", d=64))) if hf == 0 else
                         (lambda e, pv=pv, v=v, hf=hf: e.tensor_copy(out=v.t[:, hf * 8:(hf + 1) * 8, 0:64], in_=pv.t[:, :].rearrange("p (h d) -> p h d", d=64))),
                         reads=[pv.r], writes=[v.r])
                S.dma("st", self.VA[t0 + tt * 128:t0 + (tt + 1) * 128, :].rearrange("p (h d) -> p h d", d=65), v.t[:],
                      reads=[v.r], writes=[S.dres("va2", t0, tt)])

    def ph_na(self):
        S, din = self.S, self.din
        blocks = self.na_blocks
        ntp = self.ntypes
        ktb = [self.sb("ktb%d" % a, [128, TT], BF16) for a in range(2)]
        qtb = [self.sb("qtb%d" % a, [128, TT], BF16) for a in range(2)]
        vab = [self.sb("vab%d" % a, [128, TT // 128, 130], BF16) for a in range(2)]
        bib = [self.sb("bib%d" % a, [128, ntp * 2 * 7 * 128]) for a in range(2)]
        obb = [self.sb("obb%d" % a, [128, SEQ], BF16) for a in range(2)]
        tmA = [self.sb("tmA%d" % a, [128, 512]) for a in range(2)]
        tmB = [self.sb("tmB%d" % a, [128, 384]) for a in range(2)]
        PA = [self.sb("PA%d" % a, [128, 512], BF16) for a in range(2)]
        PB = [self.sb("PB%d" % a, [128, 384], BF16) for a in range(2)]
        o2 = [self.sb("o2_%d" % a, [128, 128]) for a in range(2)]
        rr = self.sb("rr", [128, 2])
        its = [(qb, hh) for qb in range(32) for hh in range(2)]

        def geom(qb):
            lo, hi, ty = blocks[qb]
            nk = (hi - lo) * 64
            tile0 = (NCTX + lo * 64) // 128
            nfull = nk // 128
            return ty, tile0, nfull, (nk % 128 != 0)
        for hc in range(8):
            kt_, qt_, va_, bi_, ob_ = ktb[hc % 2], qtb[hc % 2], vab[hc % 2], bib[hc % 2], obb[hc % 2]
            S.dma("sp", kt_.t[:], self.KT[hc * 128:(hc + 1) * 128, :], writes=[kt_.r])
            S.dma("sp", qt_.t[:, NCTX:], self.QT[hc * 128:(hc + 1) * 128, NCTX:], writes=[qt_.r])
            S.dma("sp", va_.t[:], self.VA[:, hc * 130:(hc + 1) * 130].rearrange("(i p) w -> p i w", p=128), writes=[va_.r])
            S.dma("sp", bi_.t[:], din["nab"][hc], writes=[bi_.r])

            def tiles(qb):
                ty, tile0, nfull, half = geom(qb)
                tl = [(0, 128), (1, 128)] + [(tile0 + a, 128) for a in range(nfull)]
                if half:
                    tl.append((tile0 + nfull, 64))
                return tl

            def scores(it):
                qb, hh = its[it]
                A, B = self.ps[2 + (it % 2) * 2], self.ps[3 + (it % 2) * 2]
                q0 = NCTX + qb * 128
                fns = []
                for idx, (tile, sz) in enumerate(tiles(qb)):
                    dst = A.t[:sz, idx * 128:(idx + 1) * 128] if idx < 4 else B.t[:sz, (idx - 4) * 128:(idx - 3) * 128]
                    fns.append(lambda e, dst=dst, tile=tile, sz=sz, hh=hh, q0=q0: e.matmul(
                        dst, lhsT=kt_.t[hh * 64:(hh + 1) * 64, tile * 128:tile * 128 + sz],
                        rhs=qt_.t[hh * 64:(hh + 1) * 64, q0:q0 + 128], start=True, stop=True))
                S.mm(fns, reads=[kt_.r, qt_.r], writes=[A.r, B.r])
            def softmax_pv(it):
                qb, hh = its[it]
                ty = geom(qb)[0]
                tl = tiles(qb)
                nb = (len(tl) - 4) * 128
                A, B = self.ps[2 + (it % 2) * 2], self.ps[3 + (it % 2) * 2]
                ta, tb_, pa, pb = tmA[it % 2], tmB[it % 2], PA[it % 2], PB[it % 2]
                bo = (ty * 2 + hh) * 896
                S.op("dve", lambda e, A=A, ta=ta, bo=bo: e.scalar_tensor_tensor(
                    out=ta.t[:], in0=A.t[:, 0:512], scalar=0.125, in1=bi_.t[:, bo:bo + 512], op0=ALU.mult, op1=ALU.add),
                    reads=[A.r, bi_.r], writes=[ta.r])
                S.op("act", lambda e, ta=ta, pa=pa: e.activation(out=pa.t[:], in_=ta.t[:], func=AF.Exp), reads=[ta.r], writes=[pa.r])
                S.op("dve", lambda e, B=B, tb_=tb_, bo=bo, nb=nb: e.scalar_tensor_tensor(
                    out=tb_.t[:, :nb], in0=B.t[:, 0:nb], scalar=0.125, in1=bi_.t[:, bo + 512:bo + 512 + nb], op0=ALU.mult, op1=ALU.add),
                    reads=[B.r, bi_.r], writes=[tb_.r])
                S.op("act", lambda e, tb_=tb_, pb=pb, nb=nb: e.activation(out=pb.t[:, :nb], in_=tb_.t[:, :nb], func=AF.Exp), reads=[tb_.r], writes=[pb.r])
                po = self.ps[hh]
                fns = []
                for idx, (tile, sz) in enumerate(tl):
                    src = pa.t[:sz, idx * 128:(idx + 1) * 128] if idx < 4 else pb.t[:sz, (idx - 4) * 128:(idx - 3) * 128]
                    fns.append(lambda e, po=po, src=src, tile=tile, sz=sz, hh=hh, idx=idx, n_=len(tl): e.matmul(
                        po.t[:, 0:65], lhsT=src, rhs=va_.t[:sz, tile, hh * 65:(hh + 1) * 65], start=(idx == 0), stop=(idx == n_ - 1)))
                S.mm(fns, reads=[pa.r, pb.r, va_.r], writes=[po.r])

            def epilogue(it):
                qb, hh = its[it]
                po = self.ps[hh]
                oo = o2[qb % 2]
                S.op("dve", lambda e, po=po, hh=hh: e.reciprocal(out=rr.t[:, hh:hh + 1], in_=po.t[:, 64:65]), reads=[po.r], writes=[rr.r])
                S.op("dve", lambda e, po=po, hh=hh, oo=oo: e.tensor_scalar(out=oo.t[:, hh * 64:(hh + 1) * 64], in0=po.t[:, 0:64], scalar1=rr.t[:, hh:hh + 1],
                                                                       scalar2=None, op0=ALU.mult), reads=[po.r, rr.r], writes=[oo.r])
                if hh == 1:
                    pT = self.ps[6 + qb % 2]
                    S.mm([lambda e, pT=pT, oo=oo: e.transpose(pT.t[:, 0:128], oo.t[:], self.ident.t[:])], reads=[oo.r, self.ident.r], writes=[pT.r])
                    S.op("act", lambda e, pT=pT, qb=qb: e.copy(out=ob_.t[:, qb * 128:(qb + 1) * 128], in_=pT.t[:, 0:128]), reads=[pT.r], writes=[ob_.r])
            scores(0)
            for it in range(len(its)):
                if it + 1 < len(its):
                    scores(it + 1)
                softmax_pv(it)
                if it >= 1:
                    epilogue(it - 1)
            epilogue(len(its) - 1)
            S.dma("st", self.catT[hc * 128:(hc + 1) * 128, NCTX:], ob_.t[:], reads=[ob_.r], writes=[S.dres("catN", hc)])

    def ph_final(self):
        S = self.S
        self.alloc_pro()
        fw = self.sb("fw", [128, 8])
        S.dma("sp", fw.t[:], self.din["fnormT"], writes=[fw.r])
        yT = [self.sb("yT%d" % a, [128, 8, GS]) for a in range(2)]
        ot = [self.sb("ot%d" % a, [128, D]) for a in range(2)]
        gi = 0
        oi = 0
        for (t0, n, s) in self.groups(False):
            xt = self.prologue(1, 2, t0, n, s, 6, want_h=False)
            y = yT[gi % 2]
            gi += 1
            for c in range(8):
                S.op("dve", lambda e, c=c, y=y, xt=xt: e.scalar_tensor_tensor(
                    out=y.t[:, c, :n], in0=xt.t[:, c, :n], scalar=fw.t[:, c:c + 1], in1=self.rstd.t[:, :n], op0=ALU.mult, op1=ALU.mult),
                    reads=[xt.r, fw.r, self.rstd.r], writes=[y.r])
            for tt in range(n // 128):
                o = ot[oi % 2]
                oi += 1
                for hf in range(2):
                    ps = self.ps[hf * 2 + (tt % 2)]
                    S.mm([lambda e, c=c, ps=ps, hf=hf, tt=tt, y=y: e.transpose(ps.t[:, c * 128:(c + 1) * 128], y.t[:, hf * 4 + c, tt * 128:(tt + 1) * 128],
                                                                            self.ident.t[:]) for c in range(4)], reads=[y.r, self.ident.r], writes=[ps.r])
                    if hf == 0:
                        S.op("act", lambda e, ps=ps, o=o: e.copy(out=o.t[:, 0:512], in_=ps.t[:, :]), reads=[ps.r], writes=[o.r])
                    else:
                        S.op("dve", lambda e, ps=ps, o=o: e.tensor_copy(out=o.t[:, 512:1024], in_=ps.t[:, :]), reads=[ps.r], writes=[o.r])
                r0 = t0 - NCTX + tt * 128
                S.dma("st", self.out[r0:r0 + 128, :], o.t[:], reads=[o.r], writes=[S.dres("out", r0)])


def _host_inputs(inputs, b, consts, nab):
    f32 = np.float32
    g = lambda k: np.asarray(inputs[k], dtype=f32)
    m = {}
    m["x"] = np.ascontiguousarray(g("x")[b])
    m["ctxi"] = np.ascontiguousarray(g("ctx")[b])
    sv = np.stack([_fm(g("c")[b], 8), _fm(g("c_ctx"), 8)], axis=-1)
    m["sv"] = np.ascontiguousarray(sv.reshape(128, 16))
    m["mod_w"] = g("mod_w")
    mb = np.stack([_fm(g("mod_b")[l], 72) for l in range(2)], axis=1)
    m["mod_b2"] = np.ascontiguousarray(np.repeat(mb[:, :, None, :], 2, axis=2).reshape(128, 288))
    nw = g("norm_w")
    nt = np.stack([np.stack([_fm(nw[l, k], 8) for k in range(3)], axis=1) for l in range(2)], axis=1)
    m["normT2"] = np.ascontiguousarray(np.repeat(nt[:, :, :, None, :], 2, axis=3).reshape(128, 96))
    m["fnormT"] = _fm(g("final_norm_w"), 8)
    m["ffn_w1"] = g("ffn_w1"); m["ffn_w3"] = g("ffn_w3"); m["ffn_w2"] = g("ffn_w2")
    wi = g("ab_w_in")[0]
    m["ab_w_in"] = wi
    perm = consts["perm"]
    cols = np.concatenate([hc * 128 + perm for hc in range(4)] + [512 + hc * 128 + perm for hc in range(4)])
    m["ab_w_inp"] = np.ascontiguousarray(wi[:, cols])
    m["ab_w_out"] = g("ab_w_out")[0]
    m["lamp"] = np.ascontiguousarray(np.broadcast_to(g("diff_lambda")[0].reshape(1, 256), (128, 256)))
    m["subw"] = np.ascontiguousarray(g("diff_subln_w")[0].reshape(128, 1))
    cw = g("hy_conv_w")[0]
    m["hcw"] = np.ascontiguousarray(np.stack([_fm(cw[j], 12) for j in range(3)], axis=-1).reshape(128, 36))
    m["hcb"] = _fm(g("hy_conv_b")[0], 12)
    m["fw0"] = g("hy_f_w0")[0]; m["fb0"] = np.ascontiguousarray(g("hy_f_b0")[0].reshape(64, 1))
    m["fw1"] = g("hy_f_w1")[0]; m["fb1"] = np.ascontiguousarray(g("hy_f_b1")[0].T)
    m["ffreq"] = np.ascontiguousarray(g("hy_f_freq")[0].reshape(64, 1))
    m["fwout"] = g("hy_f_wout")[0]
    m["hskip"] = np.ascontiguousarray(g("hy_bias")[0].reshape(1, 1024))
    m["na_w_in"] = g("na_w_in")[0]; m["na_w_out"] = g("na_w_out")[0]
    m["nab"] = nab
    for k in ("ident", "ropeC", "ropeS", "dftC", "dftS", "wf", "dftCc", "dftSc", "wfc", "zT", "zTc", "tl", "tlc", "deltas"):
        m[k] = consts[k]
    return m


_PROG = {}


def run(inputs, cores, dbg=None):
    consts = _consts()
    nab, blocks, nt = _na_bias(np.asarray(inputs["na_rpb"], np.float32)[0])
    key = (dbg,)
    if key not in _PROG:
        p = Prog(dbg=dbg, nab_cols=nab.shape[2], na_blocks=blocks)
        p.ntypes = nt
        _PROG[key] = p.build()
    nc = _PROG[key]
    in_maps = [_host_inputs(inputs, b, consts, nab) for b in cores]
    res = run_bass_kernel_spmd(nc, in_maps, core_ids=list(range(len(cores))))
    return res


def kernel(**inputs):
    res = run(inputs, list(range(8)))
    return np.stack([np.asarray(r["out"], dtype=np.float32) for r in res.results], axis=0)
```
